# Optimizing a Trainium2 kernel written in Bass

```python
import math
import jax
import jax.numpy as jnp
from jax import lax
import numpy as np

D_MODEL = 2048
BATCH = 8
SEQ = 2048
DEPTH = 4
DEC_BATCH = 8
DEC_SEQ = 16
PAST_LEN = 4096

CHUNK = 64
DK = 128
DV = 128
NK = D_MODEL // DK
NV = 2 * NK
KEY_DIM = NK * DK
VAL_DIM = NV * DV
QKV_DIM = 2 * KEY_DIM + VAL_DIM
IN_DIM = QKV_DIM + VAL_DIM + 2 * NV
CONV_W = 4
POOL_WINDOWS = (2, 4, 8, 16)
N_GROUPS = 4
GC = D_MODEL // N_GROUPS
POOL_BUF = max(POOL_WINDOWS) - 1
FFN_HIDDEN = -(-8 * D_MODEL // 768) * 256
N_A = (DEPTH + 1) // 2
N_B = DEPTH // 2
EPS = 1e-6

kernel_name = 'hybrid_gdn_pool_stream_step'


def _rms_norm(x, w):
    xf = x.astype(jnp.float32)
    y = xf * lax.rsqrt(jnp.mean(xf * xf, axis=-1, keepdims=True) + EPS)
    return (y * w.astype(jnp.float32)).astype(x.dtype)


def _l2norm(x):
    return x * lax.rsqrt(jnp.sum(x * x, axis=-1, keepdims=True) + EPS)


def _to_chunks(a, n, c):
    b, _, h = a.shape[:3]
    a = a.reshape((b, n, c, h) + a.shape[3:])
    return jnp.moveaxis(a, 3, 1)


def _gated_delta_rule(q, k, v, g, beta, S0):
    B, T, H, _ = q.shape
    C = min(CHUNK, T)
    pad = (-T) % C
    if pad:
        p4 = ((0, 0), (0, pad), (0, 0), (0, 0))
        q, k, v = jnp.pad(q, p4), jnp.pad(k, p4), jnp.pad(v, p4)
        g, beta = jnp.pad(g, p4[:3]), jnp.pad(beta, p4[:3])
    N = (T + pad) // C
    q, k, v = _to_chunks(q, N, C), _to_chunks(k, N, C), _to_chunks(v, N, C)
    g, beta = _to_chunks(g, N, C), _to_chunks(beta, N, C)
    gc = jnp.cumsum(g, axis=-1)
    causal = jnp.tril(jnp.ones((C, C), dtype=bool))
    strict = jnp.tril(jnp.ones((C, C), dtype=bool), -1)
    diff = gc[..., :, None] - gc[..., None, :]
    decay = jnp.where(causal, jnp.exp(jnp.where(causal, diff, 0.0)), 0.0)
    kb = k * beta[..., None]
    L = jnp.where(strict, jnp.einsum('bhncd,bhnsd->bhncs', kb, k) * decay, 0.0)
    eye = jnp.eye(C, dtype=jnp.float32)
    Tm = lax.linalg.triangular_solve(eye + L, jnp.broadcast_to(eye, L.shape), left_side=True, lower=True)
    w = jnp.einsum('bhncs,bhnsd->bhncd', Tm, kb * jnp.exp(gc)[..., None])
    u = jnp.einsum('bhncs,bhnsd->bhncd', Tm, v * beta[..., None])
    A = jnp.where(causal, jnp.einsum('bhncd,bhnsd->bhncs', q, k) * decay, 0.0)
    qg = q * jnp.exp(gc)[..., None]
    kd = k * jnp.exp(gc[..., -1:] - gc)[..., None]
    gl = jnp.exp(gc[..., -1])

    def step(S, xs):
        w_i, u_i, A_i, qg_i, kd_i, gl_i = xs
        v_new = u_i - jnp.einsum('bhck,bhkv->bhcv', w_i, S)
        o_i = jnp.einsum('bhck,bhkv->bhcv', qg_i, S) + jnp.einsum('bhcs,bhsv->bhcv', A_i, v_new)
        S = S * gl_i[..., None, None] + jnp.einsum('bhck,bhcv->bhkv', kd_i, v_new)
        return S, o_i

    xs = tuple(jnp.moveaxis(t, 2, 0) for t in (w, u, A, qg, kd, gl))
    S, o = lax.scan(step, S0, xs)
    o = jnp.moveaxis(o, 0, 2)
    o = jnp.moveaxis(o, 1, 3).reshape(B, N * C, H, -1)[:, :T]
    return o, S


def _gdn_mixer(h, conv_hist, S0, w_in, conv_w, A_log, dt_bias, norm_w, w_out):
    B, T, _ = h.shape
    proj = h @ w_in
    qkv = proj[..., :QKV_DIM]
    z = proj[..., QKV_DIM:QKV_DIM + VAL_DIM]
    b_logit = proj[..., QKV_DIM + VAL_DIM:QKV_DIM + VAL_DIM + NV]
    a_logit = proj[..., QKV_DIM + VAL_DIM + NV:]
    xc = jnp.concatenate([conv_hist.astype(qkv.dtype), qkv], axis=1)
    acc = xc[:, 0:T] * conv_w[0]
    for j in range(1, CONV_W):
        acc = acc + xc[:, j:j + T] * conv_w[j]
    qkv_c = jax.nn.silu(acc).astype(jnp.float32)
    q = _l2norm(qkv_c[..., :KEY_DIM].reshape(B, T, NK, DK)) * (DK ** -0.5)
    k = _l2norm(qkv_c[..., KEY_DIM:2 * KEY_DIM].reshape(B, T, NK, DK))
    v = qkv_c[..., 2 * KEY_DIM:].reshape(B, T, NV, DV)
    q = jnp.repeat(q, NV // NK, axis=2)
    k = jnp.repeat(k, NV // NK, axis=2)
    beta = jax.nn.sigmoid(b_logit.astype(jnp.float32))
    g = -jnp.exp(A_log.astype(jnp.float32)) * jax.nn.softplus(a_logit.astype(jnp.float32) + dt_bias.astype(jnp.float32))
    o, S = _gated_delta_rule(q, k, v, g, beta, S0.astype(jnp.float32))
    zf = z.reshape(B, T, NV, DV).astype(jnp.float32)
    o = o * lax.rsqrt(jnp.mean(o * o, axis=-1, keepdims=True) + EPS) * norm_w.astype(jnp.float32) * jax.nn.silu(zf)
    out = o.reshape(B, T, VAL_DIM).astype(h.dtype) @ w_out
    return out, xc[:, -(CONV_W - 1):], S


def _pool_mixer(h, hist, w, scale):
    B, T, D = h.shape
    P = hist.shape[1]
    xc = jnp.concatenate([hist.astype(h.dtype), h], axis=1)
    xf = xc.astype(jnp.float32)
    cs = jnp.concatenate([jnp.zeros((B, 1, D), jnp.float32), jnp.cumsum(xf, axis=1)], axis=1)
    end = np.arange(T) + P + 1
    groups = []
    for gi, wlen in enumerate(POOL_WINDOWS):
        sl = slice(gi * GC, (gi + 1) * GC)
        lo_idx = np.maximum(end - wlen, 0)
        cnt = jnp.asarray((end - lo_idx)[:, None], jnp.float32)
        win_sum = cs[:, P + 1:, sl] - cs[:, lo_idx, sl]
        groups.append(win_sum / cnt - xf[:, P:, sl])
    d = jnp.stack(groups, axis=2).astype(h.dtype)
    y = jnp.einsum('btgc,gce->btge', d, w).reshape(B, T, D) * scale
    return y, xc[:, -POOL_BUF:]


def _swiglu(h, w_gu, w_down):
    gu = h @ w_gu
    return (jax.nn.silu(gu[..., :FFN_HIDDEN]) * gu[..., FFN_HIDDEN:]) @ w_down


def _trunk(x, conv_hist, S_hist, pool_hist, norm_mix_w, norm_ffn_w, final_norm_w, gdn_w_in, gdn_conv_w,
           gdn_A_log, gdn_dt_bias, gdn_norm_w, gdn_w_out, pool_w, pool_scale, ffn_w_gu, ffn_w_down):
    new_conv, new_S, new_pool = [], [], []
    for i in range(DEPTH):
        j = i // 2
        h = _rms_norm(x, norm_mix_w[i])
        if i % 2 == 0:
            out, c, S = _gdn_mixer(h, conv_hist[j], S_hist[j], gdn_w_in[j], gdn_conv_w[j], gdn_A_log[j],
                                   gdn_dt_bias[j], gdn_norm_w[j], gdn_w_out[j])
            new_conv.append(c)
            new_S.append(S)
        else:
            out, p = _pool_mixer(h, pool_hist[j], pool_w[j], pool_scale[j])
            new_pool.append(p)
        x = x + out.astype(x.dtype)
        x = x + _swiglu(_rms_norm(x, norm_ffn_w[i]), ffn_w_gu[i], ffn_w_down[i]).astype(x.dtype)
    y = _rms_norm(x, final_norm_w)
    return y, jnp.stack(new_conv), jnp.stack(new_S), jnp.stack(new_pool)


def setup_inputs(seed: int = 0) -> dict:
    key = jax.random.key(seed)
    ks = jax.random.split(key, 20)
    f32 = jnp.float32

    def nrm(k, shape, s):
        return jax.random.normal(k, shape, f32) * s

    n_pool_hist = min(POOL_BUF, PAST_LEN)
    A = jax.random.uniform(ks[8], (N_A, NV), f32, 1.0, 16.0)
    dt = jnp.exp(jax.random.uniform(ks[9], (N_A, NV), f32, math.log(1e-3), math.log(1e-1)))
    dt_bias = dt + jnp.log(-jnp.expm1(-dt))
    return {
        'x_prompt': nrm(ks[0], (BATCH, SEQ, D_MODEL), 1.0),
        'x_sample': nrm(ks[1], (DEC_BATCH, DEC_SEQ, D_MODEL), 1.0),
        'state_gdn_conv': nrm(ks[2], (N_A, DEC_BATCH, CONV_W - 1, QKV_DIM), 1.0),
        'state_gdn_S': nrm(ks[3], (N_A, DEC_BATCH, NV, DK, DV), 0.1),
        'state_pool': nrm(ks[4], (N_B, DEC_BATCH, n_pool_hist, D_MODEL), 1.0),
        'norm_mix_w': 1.0 + nrm(ks[5], (DEPTH, D_MODEL), 0.02),
        'norm_ffn_w': 1.0 + nrm(ks[6], (DEPTH, D_MODEL), 0.02),
        'final_norm_w': 1.0 + nrm(ks[7], (D_MODEL,), 0.02),
        'gdn_w_in': nrm(ks[10], (N_A, D_MODEL, IN_DIM), D_MODEL ** -0.5),
        'gdn_conv_w': nrm(ks[11], (N_A, CONV_W, QKV_DIM), CONV_W ** -0.5),
        'gdn_A_log': jnp.log(A),
        'gdn_dt_bias': dt_bias,
        'gdn_norm_w': 1.0 + nrm(ks[12], (N_A, DV), 0.02),
        'gdn_w_out': nrm(ks[13], (N_A, VAL_DIM, D_MODEL), VAL_DIM ** -0.5),
        'pool_w': nrm(ks[14], (N_B, N_GROUPS, GC, GC), GC ** -0.5),
        'pool_scale': 1.0 + nrm(ks[15], (N_B, D_MODEL), 0.02),
        'ffn_w_gu': nrm(ks[16], (DEPTH, D_MODEL, 2 * FFN_HIDDEN), D_MODEL ** -0.5),
        'ffn_w_down': nrm(ks[17], (DEPTH, FFN_HIDDEN, D_MODEL), FFN_HIDDEN ** -0.5),
    }


def reference(x_prompt, x_sample, state_gdn_conv, state_gdn_S, state_pool, norm_mix_w, norm_ffn_w,
              final_norm_w, gdn_w_in, gdn_conv_w, gdn_A_log, gdn_dt_bias, gdn_norm_w, gdn_w_out, pool_w,
              pool_scale, ffn_w_gu, ffn_w_down):
    conv0 = jnp.zeros((N_A, BATCH, CONV_W - 1, QKV_DIM), x_prompt.dtype)
    S00 = jnp.zeros((N_A, BATCH, NV, DK, DV), jnp.float32)
    pool0 = jnp.zeros((N_B, BATCH, 0, D_MODEL), x_prompt.dtype)
    y_prompt, conv_p, S_p, pool_p = _trunk(
        x_prompt, conv0, S00, pool0, norm_mix_w, norm_ffn_w, final_norm_w, gdn_w_in, gdn_conv_w,
        gdn_A_log, gdn_dt_bias, gdn_norm_w, gdn_w_out, pool_w, pool_scale, ffn_w_gu, ffn_w_down)
    y_sample, conv_s, S_s, pool_s = _trunk(
        x_sample, state_gdn_conv, state_gdn_S, state_pool, norm_mix_w, norm_ffn_w, final_norm_w, gdn_w_in,
        gdn_conv_w, gdn_A_log, gdn_dt_bias, gdn_norm_w, gdn_w_out, pool_w, pool_scale, ffn_w_gu, ffn_w_down)
    S_p = S_p.astype(state_gdn_S.dtype)
    S_s = S_s.astype(state_gdn_S.dtype)
    return (y_prompt, y_sample, conv_p, S_p, pool_p, conv_s, S_s, pool_s)
```

```python
import math
import os
from contextlib import ExitStack
import numpy as np
import concourse.bass as bass
import concourse.mybir as mybir
from concourse.alu_op_type import AluOpType as ALU
from concourse.bass_utils import run_bass_kernel_spmd

F32 = mybir.dt.float32
BF16 = mybir.dt.bfloat16
AF = mybir.ActivationFunctionType

ENGS = ("pe", "dve", "act", "pool", "sp")

D = 2048
KC = 16
TP = 2048
TS = 16
TR = TP + TS
TA = TP + 128
NG = TA // 128
NV = 32
QKV = 8192
VAL = 4096
IN_DIM = 12352
FH = 5632
FC = FH // 128
EPS = 1e-6
TILES = [(0, 512), (512, 512), (1024, 512), (1536, 512), (2048, 16)]
NEG = -1.0e30


class Sched:
    def __init__(self, nc, n_lanes=8):
        self.nc = nc
        self.ops = []
        self.last_w = {}
        self.readers = {}
        self.n_lanes = n_lanes
        self.implicit = {"pe"}

    def op(self, eng, fn, reads=(), writes=(), dma=False):
        nk = lambda k: ("ps", k[1]) if (isinstance(k, tuple) and k and k[0] == "pq") else k
        reads = [nk(k) for k in reads] + ["ALL"]
        writes = [nk(k) for k in writes]
        deps = set()
        for k in reads:
            w = self.last_w.get(k)
            if w is not None:
                deps.add(w)
            if isinstance(k, tuple) and k and k[0] == "ps":
                for r in self.readers.get(k, ()):
                    if self.ops[r]["eng"] != eng:
                        deps.add(r)
        for k in writes:
            w = self.last_w.get(k)
            if w is not None:
                deps.add(w)
            for r in self.readers.get(k, ()):
                deps.add(r)
        idx = len(self.ops)
        self.ops.append(dict(eng=eng, fn=fn, deps=deps, dma=dma))
        for k in reads:
            self.readers.setdefault(k, []).append(idx)
        for k in writes:
            self.last_w[k] = idx
            self.readers[k] = []
        return idx

    def barrier(self):
        for e in ("sp", "pe", "dve", "act", "pool"):
            self.op(e, lambda h: h.nop(), writes=["ALL"])

    def finalize(self, stack):
        nc = self.nc
        ops = self.ops
        needed = set()
        for o in ops:
            if o["eng"] in self.implicit:
                o["deps"] = {d for d in o["deps"] if ops[d]["eng"] != o["eng"] or ops[d]["dma"]}
            needed |= o["deps"]
        csem = {e: stack.enter_context(nc.semaphore(f"c_{e}")) for e in ENGS}
        lanes = {e: [stack.enter_context(nc.semaphore(f"l_{e}_{i}")) for i in range(self.n_lanes)]
                 for e in ("sp", "pool")}
        ccount = {e: 0 for e in ENGS}
        lane_cnt = {e: [0] * self.n_lanes for e in lanes}
        lane_rr = {e: 0 for e in lanes}
        token = [None] * len(ops)
        known = {e: {} for e in ENGS}
        streams = {e: [] for e in ENGS}
        for i, o in enumerate(ops):
            e = o["eng"]
            waits = {}
            for d in o["deps"]:
                s, v = token[d]
                key = id(s)
                if known[e].get(key, 0) >= v:
                    continue
                if key not in waits or waits[key][1] < v:
                    waits[key] = (s, v)
            inc = None
            if o["dma"]:
                ln = lane_rr[e]
                lane_rr[e] = (ln + 1) % self.n_lanes
                s = lanes[e][ln]
                prev = lane_cnt[e][ln]
                if prev > 0 and known[e].get(id(s), 0) < prev:
                    key = id(s)
                    if key not in waits or waits[key][1] < prev:
                        waits[key] = (s, prev)
                lane_cnt[e][ln] = prev + 16
                token[i] = (s, prev + 16)
                inc = (s, 16)
            else:
                if i in needed:
                    ccount[e] += 1
                    token[i] = (csem[e], ccount[e])
                    inc = (csem[e], 1)
                else:
                    token[i] = (csem[e], ccount[e] + 1)
            for key, (s, v) in waits.items():
                known[e][key] = v
            streams[e].append((list(waits.values()), o["fn"], inc))
        self.stats = {e: len(streams[e]) for e in ENGS}
        final_waits = []
        for e in lanes:
            for ln in range(self.n_lanes):
                if lane_cnt[e][ln] > 0:
                    final_waits.append((lanes[e][ln], lane_cnt[e][ln]))
        for e in ENGS:
            if ccount[e] > 0:
                final_waits.append((csem[e], ccount[e]))
        with nc.Block() as block:
            def mk(e):
                def body(engh):
                    for waits, fn, inc in streams[e]:
                        for s, v in waits:
                            engh.wait_ge(s, v)
                        ins = fn(engh)
                        if inc is not None:
                            ins.then_inc(inc[0], inc[1])
                    if e == "sp":
                        for s, v in final_waits:
                            engh.wait_ge(s, v)
                return body
            block.tensor(mk("pe"))
            block.vector(mk("dve"))
            block.scalar(mk("act"))
            block.gpsimd(mk("pool"))
            block.sync(mk("sp"))


class Arena:
    def __init__(self, A, nbytes):
        self.A = A
        self.size = nbytes
        self.base = 0
        self.off = 0

    def alloc(self, dt, shape, name=None):
        esz = 4 if dt == F32 else 2
        free = 1
        for s in shape[1:]:
            free *= s
        nb = free * esz
        off = self.off
        self.off += (nb + 63) // 64 * 64
        assert self.off <= self.size, f"arena overflow {self.off} > {self.size} ({name})"
        ap = self.A[0:shape[0], off // 4:(off + nb + 3) // 4]
        if dt != F32:
            ap = ap.bitcast(dt)
        if len(shape) == 3:
            ap = ap.rearrange("p (a b) -> p a b", b=shape[2])
        elif len(shape) == 4:
            ap = ap.rearrange("p (a b c) -> p a b c", b=shape[2], c=shape[3])
        return ap

    def view(self, off, dt, shape):
        save = self.off
        self.off = off
        ap = self.alloc(dt, shape)
        self.off = save
        return ap

    def mark(self):
        self.base = self.off

    def reset(self):
        self.off = self.base


def build(stages=None, debug=False):
    nc = bass.Bass("TRN2", target_bir_lowering=False)

    def din(name, shape):
        return nc.dram_tensor(name, list(shape), F32, kind="ExternalInput").ap()

    def dout(name, shape):
        return nc.dram_tensor(name, list(shape), F32, kind="ExternalOutput").ap()

    def dscr(name, shape, dt):
        return nc.dram_tensor(name, list(shape), dt, kind="Internal").ap()

    x_p = din("x_p", [TP, D])
    x_s = din("x_s", [TS, D])
    st_conv = din("st_conv", [2, 3, QKV])
    st_S = din("st_S", [2, NV, 128, 128])
    st_pool = din("st_pool", [2, 15, D])
    norm_mix_w = din("norm_mix_w", [4, D])
    norm_ffn_w = din("norm_ffn_w", [4, D])
    final_norm_w = din("final_norm_w", [1, D])
    gdn_w_in = din("gdn_w_in", [2, D, IN_DIM])
    gdn_conv_w = din("gdn_conv_w", [2, 4, QKV])
    gdn_A_log = din("gdn_A_log", [2, NV])
    gdn_dt_bias = din("gdn_dt_bias", [2, NV])
    gdn_norm_w = din("gdn_norm_w", [2, 128])
    gdn_w_out = din("gdn_w_out", [2, VAL, D])
    pool_w = din("pool_w", [2, 4, 512, 512])
    pool_scale = din("pool_scale", [2, D])
    ffn_w_gu = din("ffn_w_gu", [4, D, 2 * FH])
    ffn_w_down = din("ffn_w_down", [4, FH, D])

    y_p = dout("y_p", [TP, D])
    y_s = dout("y_s", [TS, D])
    o_conv_p = dout("o_conv_p", [2, 3, QKV])
    o_S_p = dout("o_S_p", [2, NV, 128, 128])
    o_pool_p = dout("o_pool_p", [2, 15, D])
    o_conv_s = dout("o_conv_s", [2, 3, QKV])
    o_S_s = dout("o_S_s", [2, NV, 128, 128])
    o_pool_s = dout("o_pool_s", [2, 15, D])

    xT = dscr("xT", [D, TA], F32)
    qkvT = dscr("qkvT", [QKV, TA], BF16)
    zT = dscr("zT", [VAL, TA], BF16)
    oT = dscr("oT", [VAL, TA], BF16)

    sc = Sched(nc)
    if stages is None:
        stages = ["gdn0", "ffn0", "pool1", "ffn1", "gdn2", "ffn2", "pool3", "ffn3"]

    with ExitStack() as st:
        ARENA_BYTES = 176 * 1024
        A = st.enter_context(nc.sbuf_tensor("arena", [128, ARENA_BYTES // 4], F32))
        PS = st.enter_context(nc.psum_tensor("psum", [128, 8, 512], F32))
        ar = Arena(A, ARENA_BYTES)

        def bank(i):
            return PS[:, i, :]

        ident_f = ar.alloc(F32, [128, 128])
        ident_b = ar.alloc(BF16, [128, 128])
        ones_f = ar.alloc(F32, [128, 128])
        ones_b = ar.alloc(BF16, [128, 128])
        mneg_b = ar.alloc(BF16, [128, 128])
        zero_f = ar.alloc(F32, [128, 128])
        nmix = ar.alloc(F32, [128, KC, 4])
        nffn = ar.alloc(F32, [128, KC, 4])
        nfin = ar.alloc(F32, [128, KC, 1])
        pscale = ar.alloc(F32, [128, KC, 2])

        sc.op("pool", lambda e: e.memset(ones_f, 1.0), writes=["ones_f"])
        sc.op("pool", lambda e: e.memset(ones_b, 1.0), writes=["ones_b"])
        sc.op("pool", lambda e: e.memset(zero_f, 0.0), writes=["zero_f"])
        sc.op("pool", lambda e: e.affine_select(out=ident_f, in_=ones_f, pattern=[[1, 128]],
                                                compare_op=ALU.is_equal, fill=0.0, base=0,
                                                channel_multiplier=-1),
              reads=["ones_f"], writes=["ident_f"])
        sc.op("pool", lambda e: e.tensor_copy(out=ident_b, in_=ident_f), reads=["ident_f"], writes=["ident_b"])
        sc.op("pool", lambda e: e.affine_select(out=mneg_b, in_=zero_f, pattern=[[1, 128]],
                                                compare_op=ALU.is_ge, fill=NEG, base=0,
                                                channel_multiplier=-1),
              reads=["zero_f"], writes=["mneg_b"])

        rowbuf_box = [None]

        def vec_to_cols(src, R, N, dst, bankno=7):
            nchunk = N // 128
            rowbuf = rowbuf_box[0]
            sc.op("sp", lambda e: e.dma_start(out=rowbuf[0:R, 0:N], in_=src), writes=["rowbuf"], dma=True)
            per = 512 // R
            c = 0
            while c < nchunk:
                m = min(per, nchunk - c)
                pv = bank(bankno)[:, 0:m * R].rearrange("p (a b) -> p a b", b=R)
                for i in range(m):
                    sc.op("pe", lambda e, c=c, i=i, pv=pv: e.transpose(
                        out=pv[:, i, :], in_=rowbuf[0:R, (c + i) * 128:(c + i + 1) * 128], identity=ident_f[0:R, 0:R]),
                        reads=["rowbuf", "ident_f"], writes=[("ps", bankno)])
                sc.op("dve", lambda e, c=c, m=m, pv=pv: e.tensor_copy(out=dst[:, c:c + m, :], in_=pv),
                      reads=[("ps", bankno)], writes=[("cols", id(dst))])
                c += m

        ar.mark()
        rowbuf_box[0] = ar.alloc(F32, [16, QKV])
        vec_to_cols(norm_mix_w, 4, D, nmix)
        vec_to_cols(norm_ffn_w, 4, D, nffn)
        vec_to_cols(final_norm_w, 1, D, nfin)
        vec_to_cols(pool_scale, 2, D, pscale)
        sc.barrier()

        def phase_input():
            ar.reset()
            xin = [ar.alloc(F32, [128, D]) for _ in range(2)]
            xo = [ar.alloc(F32, [128, KC, 128]) for _ in range(2)]
            for g in range(NG):
                b = g % 2
                rows = 128 if g < 16 else TS
                src = x_p[g * 128:(g + 1) * 128, :] if g < 16 else x_s
                sc.op("sp", lambda e, b=b, rows=rows, src=src: e.dma_start(out=xin[b][0:rows, :], in_=src),
                      writes=[("xin", b)], dma=True)
                for q in range(4):
                    bk = (g * 4 + q) % 4
                    for i in range(4):
                        kc = q * 4 + i
                        sc.op("pe", lambda e, b=b, rows=rows, kc=kc, bk=bk, i=i: e.transpose(
                            out=bank(bk)[:, i * 128:i * 128 + rows], in_=xin[b][0:rows, kc * 128:(kc + 1) * 128],
                            identity=ident_f[0:rows, 0:rows]),
                            reads=[("xin", b), "ident_f"], writes=[("ps", bk)])
                    eng = "act" if q % 2 == 0 else "dve"
                    pv = bank(bk).rearrange("p (a b) -> p a b", b=128)[:, :, 0:rows]
                    if eng == "act":
                        sc.op("act", lambda e, b=b, q=q, pv=pv, rows=rows: e.copy(out=xo[b][:, q * 4:(q + 1) * 4, 0:rows], in_=pv),
                              reads=[("ps", bk)], writes=[("xo", b, q)])
                    else:
                        sc.op("dve", lambda e, b=b, q=q, pv=pv, rows=rows: e.tensor_copy(out=xo[b][:, q * 4:(q + 1) * 4, 0:rows], in_=pv),
                              reads=[("ps", bk)], writes=[("xo", b, q)])
                dstv = xT.rearrange("(kc p) t -> p kc t", p=128)[:, :, g * 128:g * 128 + rows]
                sc.op("sp", lambda e, b=b, rows=rows, dstv=dstv: e.dma_start(out=dstv, in_=xo[b][:, :, 0:rows]),
                      reads=[("xo", b, q) for q in range(4)], writes=[("xT", g)], dma=True)
            sc.barrier()

        def norm_tiles(hT, wcols, widx, tiles, tmp_x, tmp_sq, tmp_r, hcol0=0, side=None, hl=None):
            xTv = xT.rearrange("(kc p) t -> p kc t", p=128)
            for ti, (c0, n) in enumerate(tiles):
                b = ti % 2
                xt, sq, rr = tmp_x[b], tmp_sq[b], tmp_r[b]
                kx, kq, kr = ("nx", id(xt)), ("nsq", id(sq)), ("nr", id(rr))
                sc.op("sp", lambda e, xt=xt, c0=c0, n=n: e.dma_start(out=xt[:, :, 0:n], in_=xTv[:, :, c0:c0 + n]),
                      writes=[kx], dma=True)
                sc.op("act", lambda e, xt=xt, sq=sq, n=n: e.activation(out=sq[:, :, 0:n], in_=xt[:, :, 0:n], func=AF.Square),
                      reads=[kx], writes=[kq])
                bk = 6 + b
                for kc in range(KC):
                    sc.op("pe", lambda e, sq=sq, kc=kc, n=n, bk=bk: e.matmul(
                        bank(bk)[:, 0:n], ones_f, sq[:, kc, 0:n], start=(kc == 0), stop=(kc == KC - 1)),
                        reads=[kq, "ones_f"], writes=[("ps", bk)])
                sc.op("act", lambda e, rr=rr, n=n, bk=bk: e.activation(out=rr[:, 0:n], in_=bank(bk)[:, 0:n], func=AF.Ln,
                                                                      scale=1.0 / D, bias=EPS),
                      reads=[("ps", bk)], writes=[kr])
                sc.op("act", lambda e, rr=rr, n=n: e.activation(out=rr[:, 0:n], in_=rr[:, 0:n], func=AF.Exp, scale=-0.5),
                      reads=[kr], writes=[kr])
                for kc in range(KC):
                    sc.op("dve", lambda e, xt=xt, rr=rr, kc=kc, c0=c0, n=n: e.scalar_tensor_tensor(
                        out=hT[:, kc, c0 - hcol0:c0 - hcol0 + n], in0=xt[:, kc, 0:n], scalar=wcols[:, kc, widx:widx + 1],
                        in1=rr[:, 0:n], op0=ALU.mult, op1=ALU.mult),
                        reads=[kx, kr], writes=[("hT", kc, c0)])
                for (ts0, cnt, dcol) in (side or []):
                    if c0 <= ts0 and ts0 + cnt <= c0 + n:
                        for kc in range(KC):
                            sc.op("dve", lambda e, xt=xt, rr=rr, kc=kc, o=ts0 - c0, cnt=cnt, dcol=dcol: e.scalar_tensor_tensor(
                                out=hl[:, kc, dcol:dcol + cnt], in0=xt[:, kc, o:o + cnt], scalar=wcols[:, kc, widx:widx + 1],
                                in1=rr[:, o:o + cnt], op0=ALU.mult, op1=ALU.mult),
                                reads=[kx, kr], writes=[("hl", kc, dcol)])

        def wload(wslot, slot, W, KCn, col0, ncols):
            view = wslot[slot][:, 0:KCn * ncols].rearrange("p (a b) -> p a b", b=ncols)
            src = W[:, col0:col0 + ncols].rearrange("(kc p) m -> p kc m", p=128)
            sc.op("pool", lambda e: e.dma_start(out=view, in_=src), writes=[("w", slot)], dma=True)
            return view

        def phase_ffn(li):
            ar.reset()
            Wgu = ffn_w_gu[li]
            Wd = ffn_w_down[li]
            NB = 1040
            hT = ar.alloc(BF16, [128, KC, NB])
            act_off = ar.off
            act = ar.alloc(BF16, [128, FC, NB])
            wslot = [ar.alloc(BF16, [128, 8192]) for _ in range(2)]
            sgt = [ar.alloc(F32, [128, 512]) for _ in range(2)]
            xres = [ar.alloc(F32, [128, 512]) for _ in range(2)]
            tx = ar.view(act_off, F32, [128, KC, 512])
            tq = ar.view(act_off + 32768, F32, [128, KC, 512])
            tr = [ar.view(act_off + 65536 + i * 2048, F32, [128, 512]) for i in range(2)]
            blocks = [TILES[0:2], TILES[2:5]]
            xTv = xT.rearrange("(kc p) t -> p kc t", p=128)
            wcnt = 0
            for blk in blocks:
                hc0 = blk[0][0]
                norm_tiles(hT, nffn, li, blk, [tx, tx], [tq, tq], tr, hcol0=hc0)
                sc.barrier()
                for jp in range(FC // 2):
                    slot = wcnt % 2
                    wcnt += 1
                    vg = wslot[slot][:, 0:KC * 256].rearrange("p (a b) -> p a b", b=256)
                    vu = wslot[slot][:, KC * 256:KC * 512].rearrange("p (a b) -> p a b", b=256)
                    srcg = Wgu[:, jp * 256:jp * 256 + 256].rearrange("(kc p) m -> p kc m", p=128)
                    srcu = Wgu[:, FH + jp * 256:FH + jp * 256 + 256].rearrange("(kc p) m -> p kc m", p=128)
                    sc.op("pool", lambda e, vg=vg, srcg=srcg: e.dma_start(out=vg, in_=srcg),
                          writes=[("w", slot, 0)], dma=True)
                    sc.op("pool", lambda e, vu=vu, srcu=srcu: e.dma_start(out=vu, in_=srcu),
                          writes=[("w", slot, 1)], dma=True)
                    for jj in range(2):
                        j = jp * 2 + jj
                        for ti, (c0, n) in enumerate(blk):
                            it = j * len(blk) + ti
                            bg = (it % 3) * 2
                            bu = bg + 1
                            for kc in range(KC):
                                sc.op("pe", lambda e, vg=vg, kc=kc, jj=jj, c0=c0, n=n, bg=bg, hc0=hc0: e.matmul(
                                    bank(bg)[:, 0:n], vg[:, kc, jj * 128:(jj + 1) * 128], hT[:, kc, c0 - hc0:c0 - hc0 + n],
                                    start=(kc == 0), stop=(kc == KC - 1)),
                                    reads=[("w", slot, 0), ("hT", kc, c0)], writes=[("ps", bg)])
                            for kc in range(KC):
                                sc.op("pe", lambda e, vu=vu, kc=kc, jj=jj, c0=c0, n=n, bu=bu, hc0=hc0: e.matmul(
                                    bank(bu)[:, 0:n], vu[:, kc, jj * 128:(jj + 1) * 128], hT[:, kc, c0 - hc0:c0 - hc0 + n],
                                    start=(kc == 0), stop=(kc == KC - 1)),
                                    reads=[("w", slot, 1), ("hT", kc, c0)], writes=[("ps", bu)])
                            sb = it % 2
                            sc.op("act", lambda e, sb=sb, bg=bg, n=n: e.activation(out=sgt[sb][:, 0:n], in_=bank(bg)[:, 0:n], func=AF.Silu),
                                  reads=[("ps", bg)], writes=[("sgt", sb)])
                            sc.op("dve", lambda e, sb=sb, bu=bu, n=n, j=j, c0=c0, hc0=hc0: e.tensor_tensor(
                                out=act[:, j, c0 - hc0:c0 - hc0 + n], in0=sgt[sb][:, 0:n], in1=bank(bu)[:, 0:n], op=ALU.mult),
                                reads=[("sgt", sb), ("ps", bu)], writes=[("act", j, c0)])
                for mc in range(KC):
                    slot = wcnt % 2
                    wcnt += 1
                    vw = wslot[slot][:, 0:FC * 128].rearrange("p (a b) -> p a b", b=128)
                    src = Wd[:, mc * 128:(mc + 1) * 128].rearrange("(kc p) m -> p kc m", p=128)
                    sc.op("pool", lambda e, vw=vw, src=src: e.dma_start(out=vw, in_=src),
                          writes=[("w", slot, 0), ("w", slot, 1)], dma=True)
                    for ti, (c0, n) in enumerate(blk):
                        it = mc * len(blk) + ti
                        bk = it % 6
                        xb = it % 2
                        sc.op("sp", lambda e, xb=xb, mc=mc, c0=c0, n=n: e.dma_start(out=xres[xb][:, 0:n], in_=xTv[:, mc, c0:c0 + n]),
                              writes=[("xres", xb)], dma=True)
                        for kc in range(FC):
                            sc.op("pe", lambda e, vw=vw, kc=kc, c0=c0, n=n, bk=bk, hc0=hc0: e.matmul(
                                bank(bk)[:, 0:n], vw[:, kc, :], act[:, kc, c0 - hc0:c0 - hc0 + n],
                                start=(kc == 0), stop=(kc == FC - 1)),
                                reads=[("w", slot, 0), ("w", slot, 1), ("act", kc, c0)], writes=[("ps", bk)])
                        sc.op("dve", lambda e, xb=xb, bk=bk, n=n: e.tensor_tensor(
                            out=xres[xb][:, 0:n], in0=xres[xb][:, 0:n], in1=bank(bk)[:, 0:n], op=ALU.add),
                            reads=[("xres", xb), ("ps", bk)], writes=[("xres", xb)])
                        sc.op("sp", lambda e, xb=xb, mc=mc, c0=c0, n=n: e.dma_start(out=xTv[:, mc, c0:c0 + n], in_=xres[xb][:, 0:n]),
                              reads=[("xres", xb)], writes=[("xTo", mc, c0)], dma=True)
                sc.barrier()

        def phase_pool(li):
            j = li // 2
            ar.reset()
            HP = 2096
            hpad = ar.alloc(BF16, [128, KC, HP])
            hl = ar.alloc(F32, [128, KC, 32])
            icnt = ar.alloc(F32, [128, 16])
            wslot = [ar.alloc(BF16, [128, 2048]) for _ in range(2)]
            xres = [ar.alloc(F32, [128, 512]) for _ in range(2)]
            po = ar.alloc(F32, [16, D])
            big_off = ar.off
            dT = ar.alloc(BF16, [128, KC, TR])
            tA = ar.alloc(F32, [128, HP])
            tB = ar.alloc(F32, [128, HP])
            tC = ar.alloc(F32, [128, 16])
            tx = ar.view(big_off, F32, [128, KC, 512])
            tq = ar.view(big_off + 32768, F32, [128, KC, 512])
            tr = [ar.view(big_off + 65536 + i * 2048, F32, [128, 512]) for i in range(2)]
            rb = ar.view(big_off, F32, [16, D])
            xTv = xT.rearrange("(kc p) t -> p kc t", p=128)
            for t in range(16):
                sc.op("pool", lambda e, t=t: e.memset(icnt[:, t:t + 1], 1.0 / (t + 1)), writes=["icnt"])
            sc.op("pool", lambda e: e.memset(hpad[:, :, 0:16], 0.0), writes=[("hp0",)])
            sc.op("pool", lambda e: e.memset(hpad[:, :, 2064:2065], 0.0), writes=[("hp1",)])
            sc.op("sp", lambda e: e.dma_start(out=rb[0:15, :], in_=st_pool[j]), writes=["rb"], dma=True)
            for q in range(4):
                pv = bank(7)[:, 0:60].rearrange("p (a b) -> p a b", b=15)
                for i in range(4):
                    sc.op("pe", lambda e, q=q, i=i, pv=pv: e.transpose(out=pv[:, i, :], in_=rb[0:15, (q * 4 + i) * 128:(q * 4 + i + 1) * 128],
                                                                 identity=ident_f[0:15, 0:15]),
                          reads=["rb", "ident_f"], writes=[("ps", 7)])
                sc.op("dve", lambda e, q=q, pv=pv: e.tensor_copy(out=hpad[:, q * 4:(q + 1) * 4, 2065:2080], in_=pv),
                      reads=[("ps", 7)], writes=[("hph", q)])
            sc.barrier()
            side = [(2032, 16, 0), (2048, 16, 16)]
            norm_tiles(hpad, nmix, li, TILES[0:4], [tx, tx], [tq, tq], tr, hcol0=-16, side=side, hl=hl)
            norm_tiles(hpad, nmix, li, TILES[4:5], [tx, tx], [tq, tq], tr, hcol0=-32, side=side, hl=hl)
            sc.barrier()
            for (c_lo, dst) in ((1, o_pool_p[j]), (17, o_pool_s[j])):
                for q in range(4):
                    for i in range(4):
                        kc = q * 4 + i
                        sc.op("pe", lambda e, kc=kc, c_lo=c_lo, q=q, i=i: e.transpose(
                            out=bank(q)[0:15, i * 128:(i + 1) * 128], in_=hl[:, kc, c_lo:c_lo + 15], identity=ident_f),
                            reads=[("hl", kc, 0), ("hl", kc, 16), "ident_f"], writes=[("ps", q)])
                    sc.op("act", lambda e, q=q: e.copy(out=po[0:15, q * 512:(q + 1) * 512], in_=bank(q)[0:15, :]),
                          reads=[("ps", q)], writes=[("po", q)])
                sc.op("sp", lambda e, dst=dst: e.dma_start(out=dst, in_=po[0:15, :]), reads=[("po", q) for q in range(4)],
                      writes=[("pout", c_lo)], dma=True)
            for kc in range(KC):
                gi = kc // 4
                w = 2 << gi
                src = hpad[:, kc, :]
                bufs = [tA, tB]
                cur = None
                sh = 1
                for lv in range(gi + 1):
                    dstb = bufs[lv % 2]
                    eng = "dve" if (kc + lv) % 2 == 0 else "pool"
                    a_in = src if cur is None else cur
                    sc.op(eng, lambda e, dstb=dstb, a_in=a_in, sh=sh: e.tensor_tensor(
                        out=dstb[:, sh:HP], in0=a_in[:, sh:HP], in1=a_in[:, 0:HP - sh], op=ALU.add),
                        reads=[("hT", kc, c) for c, _ in TILES] + [("win", id(a_in)), ("hp0",), ("hp1",), ("hph", kc // 4)],
                        writes=[("win", id(dstb))])
                    cur = dstb
                    sh *= 2
                sc.op("dve", lambda e, cur=cur, src=src, w=w, kc=kc: e.scalar_tensor_tensor(
                    out=dT[:, kc, 0:TP], in0=cur[:, 16:16 + TP], scalar=1.0 / w, in1=src[:, 16:16 + TP],
                    op0=ALU.mult, op1=ALU.subtract),
                    reads=[("win", id(cur))] + [("hT", kc, c) for c, _ in TILES], writes=[("dT", kc)])
                sc.op("dve", lambda e, cur=cur, src=src, w=w, kc=kc: e.scalar_tensor_tensor(
                    out=dT[:, kc, TP:TR], in0=cur[:, 2080:2096], scalar=1.0 / w, in1=src[:, 2080:2096],
                    op0=ALU.mult, op1=ALU.subtract),
                    reads=[("win", id(cur))] + [("hT", kc, c) for c, _ in TILES], writes=[("dT", kc)])
                sc.op("dve", lambda e, cur=cur, w=w: e.tensor_tensor(out=tC[:, 0:w - 1], in0=cur[:, 16:16 + w - 1], in1=icnt[:, 0:w - 1], op=ALU.mult),
                      reads=[("win", id(cur)), "icnt"], writes=["tC"])
                sc.op("dve", lambda e, src=src, w=w, kc=kc: e.tensor_tensor(out=dT[:, kc, 0:w - 1], in0=tC[:, 0:w - 1], in1=src[:, 16:16 + w - 1], op=ALU.subtract),
                      reads=["tC"] + [("hT", kc, c) for c, _ in TILES], writes=[("dT", kc)])
            it = 0
            for gi in range(4):
                slot = gi % 2
                vw = wslot[slot].rearrange("p (a b) -> p a b", b=512)
                srcw = pool_w[j, gi].rearrange("(kc p) m -> p kc m", p=128)
                sc.op("pool", lambda e, vw=vw, srcw=srcw: e.dma_start(out=vw, in_=srcw), writes=[("w", slot)], dma=True)
                for ec in range(4):
                    mc = gi * 4 + ec
                    for (c0, n) in TILES:
                        bk = it % 6
                        xb = it % 2
                        it += 1
                        sc.op("sp", lambda e, xb=xb, mc=mc, c0=c0, n=n: e.dma_start(out=xres[xb][:, 0:n], in_=xTv[:, mc, c0:c0 + n]),
                              writes=[("xres", xb)], dma=True)
                        for cc in range(4):
                            sc.op("pe", lambda e, vw=vw, cc=cc, ec=ec, gi=gi, c0=c0, n=n, bk=bk: e.matmul(
                                bank(bk)[:, 0:n], vw[:, cc, ec * 128:(ec + 1) * 128], dT[:, gi * 4 + cc, c0:c0 + n],
                                start=(cc == 0), stop=(cc == 3)),
                                reads=[("w", slot), ("dT", gi * 4 + cc)], writes=[("ps", bk)])
                        sc.op("dve", lambda e, xb=xb, bk=bk, n=n, mc=mc: e.scalar_tensor_tensor(
                            out=xres[xb][:, 0:n], in0=bank(bk)[:, 0:n], scalar=pscale[:, mc, j:j + 1], in1=xres[xb][:, 0:n],
                            op0=ALU.mult, op1=ALU.add),
                            reads=[("xres", xb), ("ps", bk)], writes=[("xres", xb)])
                        sc.op("sp", lambda e, xb=xb, mc=mc, c0=c0, n=n: e.dma_start(out=xTv[:, mc, c0:c0 + n], in_=xres[xb][:, 0:n]),
                              reads=[("xres", xb)], writes=[("xTo", mc, c0)], dma=True)
            sc.barrier()

        def phase_gdn(li):
            j = li // 2
            Win = gdn_w_in[j]
            ar.reset()
            xTv = xT.rearrange("(kc p) t -> p kc t", p=128)
            cw = ar.alloc(F32, [128, 64, 4])
            chs = ar.alloc(F32, [128, 64, 3])
            cst = ar.alloc(F32, [128, 64, 8])
            bT = ar.alloc(F32, [32, TA])
            gT = ar.alloc(F32, [32, TA])
            alog = ar.alloc(F32, [32, 1])
            dtb = ar.alloc(F32, [32, 1])
            nega = ar.alloc(F32, [32, 1])
            gnw = ar.alloc(F32, [128, 1])
            gcTok = ar.alloc(F32, [128, NG, 32])
            nbTok = ar.alloc(F32, [128, NG, 32])
            bTok = ar.alloc(F32, [128, NG, 32])
            egc = ar.alloc(F32, [128, NG, 32])
            ekd = ar.alloc(F32, [128, NG, 32])
            glb = ar.alloc(F32, [128, NG, 32])
            negones = ar.alloc(F32, [32, 128])
            persist_off = ar.off

            rb = ar.alloc(F32, [16, QKV])
            rowbuf_box[0] = rb
            vec_to_cols(gdn_conv_w[j], 4, QKV, cw)
            vec_to_cols(st_conv[j], 3, QKV, chs)
            sc.op("sp", lambda e: e.dma_start(out=alog, in_=gdn_A_log[j].rearrange("(p o) -> p o", o=1)), writes=["alog"], dma=True)
            sc.op("sp", lambda e: e.dma_start(out=dtb, in_=gdn_dt_bias[j].rearrange("(p o) -> p o", o=1)), writes=["dtb"], dma=True)
            sc.op("sp", lambda e: e.dma_start(out=gnw, in_=gdn_norm_w[j].rearrange("(p o) -> p o", o=1)), writes=["gnw"], dma=True)
            sc.op("pool", lambda e: e.memset(negones, -1.0), writes=["negones"])
            sc.op("pool", lambda e: e.memset(cst, 0.0), writes=[("cst", mc) for mc in range(64)])
            sc.barrier()

            ar.off = persist_off
            hT = ar.alloc(BF16, [128, KC, TR])
            g2_off = ar.off
            tx = ar.alloc(F32, [128, KC, 512])
            tq = ar.alloc(F32, [128, KC, 512])
            tr = [ar.alloc(F32, [128, 512]) for _ in range(2)]
            norm_tiles(hT, nmix, li, TILES, [tx, tx], [tq, tq], tr)
            sc.barrier()

            if os.environ.get("GDN_STOP") == "1":
                sc.barrier()
                return
            ar.off = g2_off
            wslot = [ar.alloc(BF16, [128, 4096]) for _ in range(2)]
            PW = 2070
            pre = [ar.alloc(F32, [128, PW]) for _ in range(2)]
            acc = ar.alloc(F32, [128, PW])
            sil = ar.alloc(F32, [128, PW])
            sqb = ar.alloc(BF16, [128, PW])
            vout = [ar.alloc(BF16, [128, TA]) for _ in range(2)]
            rs = acc
            for b in range(2):
                sc.op("pool", lambda e, b=b: e.memset(vout[b][:, TR:TA], 0.0), writes=[("voutpad", b)])
                sc.op("pool", lambda e, b=b: e.memset(pre[b][:, 0:3], 0.0), writes=[("prepad", b)])
            PCOL = [3, 515, 1027, 1539, 2054]
            nchunks_total = 97
            itc = 0
            for mc in range(nchunks_total):
                if mc % 2 == 0:
                    slot = (mc // 2) % 2
                    ncols = min(256, IN_DIM - mc * 128)
                    wv = wload(wslot, slot, Win, KC, mc * 128, ncols)
                wi = mc % 2
                M = 128 if mc < 96 else 64
                pb = mc % 2
                kind = "q" if mc < 16 else ("k" if mc < 32 else ("v" if mc < 64 else ("z" if mc < 96 else "ba")))
                if kind in ("q", "k", "v"):
                    sc.op("dve", lambda e, pb=pb, mc=mc: e.tensor_copy(out=pre[pb][:, 2051:2054], in_=chs[:, mc, :]),
                          reads=[("cols", id(chs))], writes=[("prehist", pb)])
                for ti, (c0, n) in enumerate(TILES):
                    if kind == "ba":
                        halves = [(0, 32, bT), (32, 64, gT)]
                    else:
                        halves = [(0, M, None)]
                    for (m0, m1, dstT) in halves:
                        bk = itc % 6
                        itc += 1
                        for kc in range(KC):
                            sc.op("pe", lambda e, wv=wv, kc=kc, wi=wi, m0=m0, m1=m1, c0=c0, n=n, bk=bk: e.matmul(
                                bank(bk)[0:m1 - m0, 0:n], wv[:, kc, wi * 128 + m0:wi * 128 + m1], hT[:, kc, c0:c0 + n],
                                start=(kc == 0), stop=(kc == KC - 1)),
                                reads=[("w", slot), ("hT", kc, c0)], writes=[("ps", bk)])
                        if kind in ("q", "k", "v"):
                            sc.op("act", lambda e, pb=pb, ti=ti, n=n, bk=bk: e.copy(out=pre[pb][:, PCOL[ti]:PCOL[ti] + n], in_=bank(bk)[:, 0:n]),
                                  reads=[("ps", bk)], writes=[("pre", pb, ti)])
                        elif kind == "z":
                            zb = mc % 2
                            sc.op("act", lambda e, zb=zb, c0=c0, n=n, bk=bk: e.activation(out=vout[zb][:, c0:c0 + n], in_=bank(bk)[:, 0:n], func=AF.Silu),
                                  reads=[("ps", bk)], writes=[("vout", zb, ti)])
                        else:
                            sc.op("act", lambda e, dstT=dstT, c0=c0, n=n, bk=bk: e.copy(out=dstT[:, c0:c0 + n], in_=bank(bk)[0:32, 0:n]),
                                  reads=[("ps", bk)], writes=[("baT", id(dstT), ti)])
                if kind == "z":
                    zb = mc % 2
                    h = mc - 64
                    sc.op("sp", lambda e, zb=zb, h=h: e.dma_start(out=zT[h * 128:(h + 1) * 128, :], in_=vout[zb]),
                          reads=[("vout", zb, ti) for ti in range(5)] + [("voutpad", zb)], writes=[("zT", h)], dma=True)
                if kind in ("q", "k", "v"):
                    prk = [("pre", pb, ti) for ti in range(5)] + [("prehist", pb), ("prepad", pb)]
                    P = pre[pb]
                    sc.op("dve", lambda e, P=P, mc=mc: e.tensor_copy(out=cst[:, mc, 0:3], in_=P[:, 2048:2051]),
                          reads=prk, writes=[("cst", mc)])
                    sc.op("dve", lambda e, P=P, mc=mc: e.tensor_copy(out=cst[:, mc, 4:7], in_=P[:, 2067:2070]),
                          reads=prk, writes=[("cst", mc)])
                    L = PW - 3
                    sc.op("dve", lambda e, P=P, mc=mc: e.tensor_scalar(out=acc[:, 0:L], in0=P[:, 3:PW], scalar1=cw[:, mc, 3:4], scalar2=None, op0=ALU.mult),
                          reads=prk + [("cols", id(cw))], writes=["acc"])
                    for tap in (2, 1, 0):
                        sc.op("dve", lambda e, P=P, mc=mc, tap=tap: e.scalar_tensor_tensor(
                            out=acc[:, 0:L], in0=P[:, tap:tap + L], scalar=cw[:, mc, tap:tap + 1], in1=acc[:, 0:L],
                            op0=ALU.mult, op1=ALU.add),
                            reads=prk + ["acc"], writes=["acc"])
                    vb = mc % 2
                    if kind == "v":
                        sc.op("act", lambda e, vb=vb: e.activation(out=vout[vb][:, 0:TP], in_=acc[:, 0:TP], func=AF.Silu),
                              reads=["acc"], writes=[("vout", vb, 0)])
                        sc.op("act", lambda e, vb=vb: e.activation(out=vout[vb][:, TP:TR], in_=acc[:, 2051:2067], func=AF.Silu),
                              reads=["acc"], writes=[("vout", vb, 1)])
                    else:
                        sc.op("act", lambda e: e.activation(out=sil[:, 0:L], in_=acc[:, 0:L], func=AF.Silu),
                              reads=["acc"], writes=["sil"])
                        sc.op("act", lambda e: e.activation(out=sqb[:, 0:L], in_=sil[:, 0:L], func=AF.Square),
                              reads=["sil"], writes=["sqb"])
                        c = 0
                        while c < L:
                            n = min(512, L - c)
                            bk = itc % 6
                            itc += 1
                            sc.op("pe", lambda e, c=c, n=n, bk=bk: e.matmul(bank(bk)[:, 0:n], ones_b, sqb[:, c:c + n], start=True, stop=True),
                                  reads=["sqb", "ones_b"], writes=[("ps", bk)])
                            sc.op("act", lambda e, c=c, n=n, bk=bk: e.activation(out=rs[:, c:c + n], in_=bank(bk)[:, 0:n], func=AF.Ln, bias=EPS, scale=1.0),
                                  reads=[("ps", bk), "sil"], writes=["acc"])
                            c += n
                        rsk = ["acc"]
                        sc.op("act", lambda e: e.activation(out=rs[:, 0:L], in_=rs[:, 0:L], func=AF.Exp, scale=-0.5),
                              reads=rsk, writes=rsk)
                        qs = (128.0 ** -0.5) if kind == "q" else 1.0
                        sc.op("dve", lambda e, vb=vb, qs=qs: e.scalar_tensor_tensor(
                            out=vout[vb][:, 0:TP], in0=sil[:, 0:TP], scalar=qs, in1=rs[:, 0:TP], op0=ALU.mult, op1=ALU.mult),
                            reads=["sil"] + rsk, writes=[("vout", vb, 0)])
                        sc.op("dve", lambda e, vb=vb, qs=qs: e.scalar_tensor_tensor(
                            out=vout[vb][:, TP:TR], in0=sil[:, 2051:2067], scalar=qs, in1=rs[:, 2051:2067], op0=ALU.mult, op1=ALU.mult),
                            reads=["sil"] + rsk, writes=[("vout", vb, 1)])
                    sc.op("sp", lambda e, vb=vb, mc=mc: e.dma_start(out=qkvT[mc * 128:(mc + 1) * 128, :], in_=vout[vb]),
                          reads=[("vout", vb, 0), ("vout", vb, 1), ("voutpad", vb)], writes=[("qkvT", mc)], dma=True)
            if os.environ.get("GDN_STOP") == "2a":
                sc.barrier()
                return
            crow_p = acc[0:3, 0:2048]
            crow_s = sil[0:3, 0:2048]
            for r4 in range(4):
                for q in range(4):
                    for i in range(4):
                        mc = r4 * 16 + q * 4 + i
                        sc.op("pe", lambda e, mc=mc, q=q, i=i: e.transpose(out=bank(q)[0:4, i * 128:(i + 1) * 128], in_=cst[:, mc, 0:4], identity=ident_f),
                              reads=[("cst", mc), "ident_f"], writes=[("ps", q)])
                        sc.op("pe", lambda e, mc=mc, q=q, i=i: e.transpose(out=bank(q + 4)[0:4, i * 128:(i + 1) * 128], in_=cst[:, mc, 4:8], identity=ident_f),
                              reads=[("cst", mc), "ident_f"], writes=[("ps", q + 4)])
                    sc.op("act", lambda e, q=q: e.copy(out=crow_p[:, q * 512:(q + 1) * 512], in_=bank(q)[0:3, :]),
                          reads=[("ps", q), "acc"], writes=[("crowp", q), "acc"])
                    sc.op("dve", lambda e, q=q: e.tensor_copy(out=crow_s[:, q * 512:(q + 1) * 512], in_=bank(q + 4)[0:3, :]),
                          reads=[("ps", q + 4), "sil"], writes=[("crows", q), "sil"])
                sc.op("sp", lambda e, r4=r4: e.dma_start(out=o_conv_p[j][:, r4 * 2048:(r4 + 1) * 2048], in_=crow_p),
                      reads=[("crowp", q) for q in range(4)] + ["acc"], writes=[("ocp", r4), "acc"], dma=True)
                sc.op("sp", lambda e, r4=r4: e.dma_start(out=o_conv_s[j][:, r4 * 2048:(r4 + 1) * 2048], in_=crow_s),
                      reads=[("crows", q) for q in range(4)] + ["sil"], writes=[("ocs", r4), "sil"], dma=True)
            sc.barrier()

            if os.environ.get("GDN_STOP") == "2":
                sc.barrier()
                return
            ar.off = persist_off
            Gd = ar.alloc(F32, [32, NG, 32])
            tmpE = ar.alloc(F32, [32, TA])
            bak = [("baT", id(bT), ti) for ti in range(5)]
            gak = [("baT", id(gT), ti) for ti in range(5)]
            sc.op("act", lambda e: e.activation(out=bT[:, 0:TR], in_=bT[:, 0:TR], func=AF.Sigmoid), reads=bak, writes=["bT"])
            sc.op("pool", lambda e: e.memset(bT[:, TR:TA], 0.0), writes=["bTpad"])
            sc.op("act", lambda e: e.activation(out=nega, in_=alog, func=AF.Exp), reads=["alog"], writes=["nega"])
            sc.op("act", lambda e: e.activation(out=tmpE[:, 0:TR], in_=gT[:, 0:TR], func=AF.Exp, bias=dtb[:, 0:1], scale=1.0),
                  reads=gak + ["dtb"], writes=["tmpE"])
            sc.op("act", lambda e: e.activation(out=tmpE[:, 0:TR], in_=tmpE[:, 0:TR], func=AF.Ln, bias=1.0, scale=1.0),
                  reads=["tmpE"], writes=["tmpE"])
            sc.op("dve", lambda e: e.tensor_scalar(out=gT[:, 0:TR], in0=tmpE[:, 0:TR], scalar1=nega[:, 0:1], scalar2=-1.0, op0=ALU.mult, op1=ALU.mult),
                  reads=["tmpE", "nega"] + gak, writes=["gTg"])
            sc.op("pool", lambda e: e.memset(gT[:, TR:TA], 0.0), writes=["gTpad"])
            if os.environ.get("GDN_STOP") == "3a":
                sc.barrier()
                return
            for n in range(NG):
                sc.op("dve", lambda e, n=n: e.tensor_tensor_scan(out=tmpE[:, n * 128:(n + 1) * 128], data0=ones_f[0:32, :], data1=gT[:, n * 128:(n + 1) * 128],
                                                               initial=0.0, op0=ALU.mult, op1=ALU.add),
                      reads=["gTg", "gTpad", "ones_f", "tmpE"], writes=[("gc", n)])
            gck = [("gc", n) for n in range(NG)]
            sc.op("dve", lambda e: e.tensor_copy(out=gT, in_=tmpE), reads=gck + ["gTg", "gTpad"], writes=["gcT"])
            gcT = gT
            if os.environ.get("GDN_STOP") == "3b":
                sc.barrier()
                return
            for n in range(NG):
                sc.op("pe", lambda e, n=n: e.transpose(out=bank(6)[:, 0:32], in_=gcT[:, n * 128:(n + 1) * 128], identity=ident_f[0:32, 0:32]),
                      reads=["gcT", "ident_f"], writes=[("ps", 6)])
                sc.op("dve", lambda e, n=n: e.tensor_copy(out=gcTok[:, n, :], in_=bank(6)[:, 0:32]), reads=[("ps", 6)], writes=[("gcTok", n)])
                sc.op("pe", lambda e, n=n: e.transpose(out=bank(7)[:, 0:32], in_=bT[:, n * 128:(n + 1) * 128], identity=ident_f[0:32, 0:32]),
                      reads=["bT", "bTpad", "ident_f"], writes=[("ps", 7)])
                sc.op("act", lambda e, n=n: e.copy(out=bTok[:, n, :], in_=bank(7)[:, 0:32]), reads=[("ps", 7)], writes=[("bTok", n)])
            gtk = [("gcTok", n) for n in range(NG)]
            btk = [("bTok", n) for n in range(NG)]
            sc.op("dve", lambda e: e.tensor_scalar(out=nbTok, in0=bTok, scalar1=-1.0, scalar2=None, op0=ALU.mult), reads=btk, writes=["nbTok"])
            sc.op("act", lambda e: e.activation(out=egc, in_=gcTok, func=AF.Exp), reads=gtk, writes=["egc"])
            if os.environ.get("GDN_STOP") == "3c":
                sc.barrier()
                return
            gclast = gcT.rearrange("p (n c) -> p n c", c=128)[:, :, 127]
            for h in range(32):
                sc.op("dve", lambda e, h=h: e.tensor_scalar(out=Gd[:, :, h], in0=gclast, scalar1=ident_f[0:32, h:h + 1], scalar2=None, op0=ALU.mult),
                      reads=["gcT", "ident_f"], writes=[("Gd", h)])
            gdk = [("Gd", h) for h in range(32)]
            Gdf = Gd.rearrange("p a b -> p (a b)")
            glbf = glb.rearrange("p a b -> p (a b)")
            ekdf = ekd.rearrange("p a b -> p (a b)")
            gctf = gcTok.rearrange("p a b -> p (a b)")
            for half, (c0, c1) in enumerate(((0, 288), (288, 544))):
                sc.op("pe", lambda e, half=half, c0=c0, c1=c1: e.matmul(bank(6 + half)[:, 0:c1 - c0], ones_f[0:32, :], Gdf[:, c0:c1], start=True, stop=True),
                      reads=gdk + ["ones_f"], writes=[("ps", 6 + half)])
                sc.op("dve", lambda e, half=half, c0=c0, c1=c1: e.tensor_copy(out=glbf[:, c0:c1], in_=bank(6 + half)[:, 0:c1 - c0]),
                      reads=[("ps", 6 + half)], writes=[("glraw", half)])
            if os.environ.get("GDN_STOP") == "3d":
                sc.barrier()
                return
            sc.op("dve", lambda e: e.tensor_tensor(out=ekdf, in0=glbf, in1=gctf, op=ALU.subtract),
                  reads=[("glraw", 0), ("glraw", 1)] + gtk, writes=["ekd"])
            sc.op("act", lambda e: e.activation(out=ekdf, in_=ekdf, func=AF.Exp), reads=["ekd"], writes=["ekdx"])
            sc.op("act", lambda e: e.activation(out=glbf, in_=glbf, func=AF.Exp), reads=[("glraw", 0), ("glraw", 1), "ekd"], writes=["glb"])
            sc.barrier()

            if os.environ.get("GDN_STOP") == "3":
                sc.barrier()
                return
            ar.off = persist_off
            kTq = [[ar.alloc(BF16, [128, TA]) for _ in range(2)] for _ in range(4)]
            ogT = [ar.alloc(BF16, [128, TA]) for _ in range(2)]
            gcm = [ar.alloc(F32, [32, TA]) for _ in range(2)]
            oh = [ar.alloc(F32, [32, 128]) for _ in range(2)]
            S_p = [ar.alloc(F32, [128, 128]) for _ in range(2)]
            S_s = [ar.alloc(F32, [128, 128]) for _ in range(2)]
            Sb = [ar.alloc(BF16, [128, 128]) for _ in range(2)]
            class _L:
                def __init__(self, t):
                    self.t = t

                def __getitem__(self, n):
                    return self.t[:, n, :]
            ATt = [ar.alloc(BF16, [128, NG, 128]) for _ in range(2)]
            U7t = [ar.alloc(BF16, [128, NG, 128]) for _ in range(2)]
            nWTt = [ar.alloc(BF16, [128, NG, 128]) for _ in range(2)]
            kdtt = [ar.alloc(BF16, [128, NG, 128]) for _ in range(2)]
            vtkt = [ar.alloc(BF16, [128, NG, 128]) for _ in range(2)]
            AT, U7, nWT, kdt, vtk = [[_L(t) for t in tt] for tt in (ATt, U7t, nWTt, kdtt, vtkt)]
            GR = 4
            SG = 2
            Dm4s = [ar.alloc(F32, [128, SG, 128]) for _ in range(2)]
            DmS4s = [ar.alloc(F32, [128, SG, 128]) for _ in range(2)]
            Xh4s = [ar.alloc(BF16, [128, SG, 128]) for _ in range(2)]
            N4s = [[ar.alloc(F32, [128, SG, 128]) for _ in range(2)] for _ in range(2)]
            NT4s = [[ar.alloc(F32, [128, SG, 128]) for _ in range(2)] for _ in range(2)]
            U4s = [[ar.alloc(F32, [128, SG, 128]) for _ in range(2)] for _ in range(2)]
            ident4 = ar.alloc(F32, [128, GR, 128])
            for i in range(GR):
                sc.op("pool", lambda e, i=i: e.tensor_copy(out=ident4[:, i, :], in_=ident_f), reads=["ident_f"], writes=[("ident4", i)])
            id4k = [("ident4", i) for i in range(GR)]
            vnew = [ar.alloc(BF16, [128, 128]) for _ in range(2)]
            otmp = [ar.alloc(F32, [128, 128]) for _ in range(2)]
            otok = [ar.alloc(F32, [128, 128]) for _ in range(2)]
            osq = [ar.alloc(F32, [128, 128]) for _ in range(2)]
            onb = [ar.alloc(BF16, [128, 128]) for _ in range(2)]
            ssq = [ar.alloc(F32, [128, 1]) for _ in range(2)]

            def pq(b, q):
                return PS[:, b, q * 128:(q + 1) * 128]

            def pqb(b, q):
                return PS[:, b, q * 128:q * 128 + 64].bitcast(BF16)

            bcnt = [0]
            stopat = os.environ.get('GDN_STOP')

            def do_head(h):
                hb = h % 2
                kh = h // 2
                srcs = [qkvT[2048 + kh * 128:2048 + (kh + 1) * 128, :], qkvT[kh * 128:(kh + 1) * 128, :],
                        qkvT[4096 + h * 128:4096 + (h + 1) * 128, :], zT[h * 128:(h + 1) * 128, :]]
                kT_h, qT_h, vT_h, zs_h = kTq[0][hb], kTq[1][hb], kTq[2][hb], kTq[3][hb]
                kk_, qk_, vk_, zk_ = ("hin", 0, hb), ("hin", 1, hb), ("hin", 2, hb), ("hin", 3, hb)

                def setup():
                    for a in range(4):
                        sc.op("sp", lambda e, a=a, src=srcs[a]: e.dma_start(out=kTq[a][hb], in_=src), writes=[("hin", a, hb)], dma=True)
                    sc.op("dve", lambda e: e.tensor_scalar(out=gcm[hb], in0=gcT, scalar1=ident_f[0:32, h:h + 1], scalar2=None, op0=ALU.mult),
                          reads=["gcT", "ident_f"], writes=[("gcm", hb)])
                    sc.op("dve", lambda e: e.tensor_scalar(out=oh[hb], in0=ones_f[0:32, :], scalar1=ident_f[0:32, h:h + 1], scalar2=None, op0=ALU.mult),
                          reads=["ones_f", "ident_f"], writes=[("oh", hb)])
                def groupfn(g0, si):
                    grp = list(range(g0, min(NG, g0 + SG)))
                    G = len(grp)
                    R0, R1, R2 = 3 * si, 3 * si + 1, 3 * si + 2
                    Dm4, DmS4, Xh4, N4, NT4, U4 = Dm4s[si], DmS4s[si], Xh4s[si], N4s[si], NT4s[si], U4s[si]
                    kDm, kDmS = ("Dm4", si), ("DmS4", si)
                    bqv = lambda b: PS[:, b, :].bitcast(BF16).rearrange("p (q x) -> p q x", x=256)[:, 0:G, 0:128]

                    def b4(b):
                        return PS[:, b, 0:G * 128].rearrange("p (q x) -> p q x", x=128)
                    for i, n in enumerate(grp):
                        cs = n * 128
                        sc.op("pe", lambda e, i=i, cs=cs: e.transpose(out=pqb(R1, i), in_=kT_h[:, cs:cs + 128], identity=ident_b),
                              reads=[kk_, "ident_b"], writes=[("ps", R1)])
                    for i, n in enumerate(grp):
                        sc.op("act", lambda e, i=i, n=n: e.activation(out=Xh4[:, i, :], in_=pqb(R1, i), func=AF.Copy, scale=egc[:, n, h:h + 1]),
                              reads=[("ps", R1), "egc"], writes=[("Xh", si, i)])
                    for i, n in enumerate(grp):
                        sc.op("dve", lambda e, i=i, n=n: e.tensor_scalar(out=kdt[hb][n], in0=pqb(R1, i), scalar1=ekd[:, n, h:h + 1], scalar2=None, op0=ALU.mult),
                              reads=[("ps", R1), "ekdx"], writes=[("kdt", hb, n)])
                    for i, n in enumerate(grp):
                        cs = n * 128
                        sc.op("pe", lambda e, i=i, cs=cs: e.matmul(pq(R0, i), oh[hb], gcT[:, cs:cs + 128], start=True, stop=False),
                              reads=[("oh", hb), "gcT"], writes=[("ps", R0)])
                        sc.op("pe", lambda e, i=i, cs=cs: e.matmul(pq(R0, i), gcm[hb][:, cs:cs + 128], negones, start=False, stop=False),
                              reads=[("gcm", hb), "negones"], writes=[("ps", R0)])
                        sc.op("pe", lambda e, i=i: e.matmul(pq(R0, i), ident_b, mneg_b, start=False, stop=True),
                              reads=["ident_b", "mneg_b"], writes=[("ps", R0)])
                    sc.op("act", lambda e: e.activation(out=Dm4[:, 0:G, :], in_=b4(R0), func=AF.Exp), reads=[("ps", R0)], writes=[kDm])
                    sc.op("pool", lambda e: e.tensor_tensor(out=DmS4[:, 0:G, :], in0=Dm4[:, 0:G, :], in1=ident4[:, 0:G, :], op=ALU.subtract),
                          reads=[kDm] + id4k, writes=[kDmS])
                    for i, n in enumerate(grp):
                        sc.op("pool", lambda e, i=i, n=n: e.tensor_scalar(out=DmS4[:, i, :], in0=DmS4[:, i, :], scalar1=nbTok[:, n, h:h + 1], scalar2=None, op0=ALU.mult),
                              reads=[kDmS, "nbTok"], writes=[kDmS])
                    yield
                    for i, n in enumerate(grp):
                        cs = n * 128
                        sc.op("pe", lambda e, i=i, cs=cs: e.matmul(pq(R1, i), kT_h[:, cs:cs + 128], qT_h[:, cs:cs + 128], start=True, stop=True),
                              reads=[kk_, qk_], writes=[("ps", R1)])
                    for i, n in enumerate(grp):
                        cs = n * 128
                        sc.op("pe", lambda e, i=i, cs=cs: e.matmul(pq(R2, i), kT_h[:, cs:cs + 128], kT_h[:, cs:cs + 128], start=True, stop=True),
                              reads=[kk_], writes=[("ps", R2)])
                    sc.op("dve", lambda e: e.tensor_tensor(out=ATt[hb][:, g0:g0 + G, :], in0=b4(R1), in1=Dm4[:, 0:G, :], op=ALU.mult),
                          reads=[("ps", R1), kDm], writes=[("AT", hb, n) for n in grp])
                    sc.op("dve", lambda e: e.tensor_tensor(out=N4[0][:, 0:G, :], in0=b4(R2), in1=DmS4[:, 0:G, :], op=ALU.mult),
                          reads=[("ps", R2), kDmS], writes=[("N", si, 0)])
                    for i, n in enumerate(grp):
                        cs = n * 128
                        sc.op("pe", lambda e, i=i, cs=cs: e.transpose(out=pqb(R0, i), in_=vT_h[:, cs:cs + 128], identity=ident_b),
                              reads=[vk_, "ident_b"], writes=[("ps", R0)])
                    sc.op("act", lambda e: e.copy(out=vtkt[hb][:, g0:g0 + G, :], in_=bqv(R0)), reads=[("ps", R0)], writes=[("vtk", hb, n) for n in grp])
                    yield
                    for i, n in enumerate(grp):
                        sc.op("pe", lambda e, i=i: e.transpose(out=pq(R1, i), in_=N4[0][:, i, :], identity=ident_f),
                              reads=[("N", si, 0), "ident_f"], writes=[("ps", R1)])
                    sc.op("act", lambda e: e.copy(out=NT4[0][:, 0:G, :], in_=b4(R1)), reads=[("ps", R1)], writes=[("NT", si, 0)])
                    sc.op("pool", lambda e: e.tensor_tensor(out=U4[0][:, 0:G, :], in0=N4[0][:, 0:G, :], in1=ident4[:, 0:G, :], op=ALU.add),
                          reads=[("N", si, 0)] + id4k, writes=[("U", si, 0)])
                    yield
                    for lv in range(1, 7):
                        a, bb = (lv - 1) % 2, lv % 2
                        if lv <= 5:
                            for i, n in enumerate(grp):
                                sc.op("pe", lambda e, i=i, a=a: e.matmul(pq(R0, i), NT4[a][:, i, :], N4[a][:, i, :], start=True, stop=True),
                                      reads=[("NT", si, a), ("N", si, a)], writes=[("ps", R0)])
                        for i, n in enumerate(grp):
                            sc.op("pe", lambda e, i=i, a=a: e.matmul(pq(R1, i), N4[a][:, i, :], NT4[a][:, i, :], start=True, stop=True),
                                  reads=[("NT", si, a), ("N", si, a)], writes=[("ps", R1)])
                        yield
                        if lv <= 5:
                            sc.op("dve", lambda e, bb=bb: e.tensor_copy(out=N4[bb][:, 0:G, :], in_=b4(R0)), reads=[("ps", R0)], writes=[("N", si, bb)])
                        sc.op("act", lambda e, bb=bb: e.copy(out=NT4[bb][:, 0:G, :], in_=b4(R1)), reads=[("ps", R1)], writes=[("NT", si, bb)])
                        for i, n in enumerate(grp):
                            sc.op("pe", lambda e, i=i, a=a, bb=bb: e.matmul(pq(R2, i), NT4[bb][:, i, :], U4[a][:, i, :], start=True, stop=True),
                                  reads=[("NT", si, bb), ("U", si, a)], writes=[("ps", R2)])
                        yield
                        if lv < 6:
                            sc.op("dve", lambda e, a=a, bb=bb: e.tensor_tensor(out=U4[bb][:, 0:G, :], in0=b4(R2), in1=U4[a][:, 0:G, :], op=ALU.add),
                                  reads=[("ps", R2), ("U", si, a)], writes=[("U", si, bb)])
                        else:
                            sc.op("dve", lambda e, a=a: e.tensor_tensor(out=U7t[hb][:, g0:g0 + G, :], in0=b4(R2), in1=U4[a][:, 0:G, :], op=ALU.add),
                                  reads=[("ps", R2), ("U", si, a)], writes=[("U7", hb, n) for n in grp])
                    yield
                    for i, n in enumerate(grp):
                        sc.op("pe", lambda e, i=i, n=n: e.matmul(pq(R0, i), Xh4[:, i, :], U7[hb][n], start=True, stop=True),
                              reads=[("Xh", si, i), ("U7", hb, n)], writes=[("ps", R0)])
                    sc.op("act", lambda e: e.activation(out=nWTt[hb][:, g0:g0 + G, :], in_=b4(R0), func=AF.Copy, scale=-1.0),
                          reads=[("ps", R0)], writes=[("nWT", hb, n) for n in grp])

                def chunkfn(n):
                    cs = n * 128
                    Sf = S_p[hb] if n < 16 else S_s[hb]
                    skey = ("S", hb, 0 if n < 16 else 1)
                    if n == 0:
                        sc.op("pool", lambda e, Sf=Sf: e.memset(Sf, 0.0), writes=[skey])
                        sc.op("pool", lambda e, hb=hb: e.memset(Sb[hb], 0.0), writes=[("Sb", hb)])
                    if n == 16:
                        sc.op("sp", lambda e, Sf=Sf, h=h: e.dma_start(out=Sf, in_=st_S[j, h]), writes=[skey], dma=True)
                        sc.op("act", lambda e, Sf=Sf, hb=hb: e.copy(out=Sb[hb], in_=Sf), reads=[skey], writes=[("Sb", hb)])
                    bq = bcnt[0] % 2
                    bcnt[0] += 1
                    B0 = 6
                    B1 = B0 + 1
                    sc.op("pe", lambda e, n=n, hb=hb, B0=B0: e.matmul(pq(B0, 0), U7[hb][n], vtk[hb][n], start=True, stop=False),
                          reads=[("U7", hb, n), ("vtk", hb, n)], writes=[("pq", B0, 0)])
                    sc.op("pe", lambda e, n=n, hb=hb, B0=B0: e.matmul(pq(B0, 0), nWT[hb][n], Sb[hb], start=False, stop=True),
                          reads=[("nWT", hb, n), ("Sb", hb)], writes=[("pq", B0, 0)])
                    sc.op("act", lambda e, n=n, h=h, bq=bq, B0=B0: e.activation(out=vnew[bq], in_=pq(B0, 0), func=AF.Copy, scale=bTok[:, n, h:h + 1]),
                          reads=[("pq", B0, 0)] + btk, writes=[("vnew", bq)])
                    sc.op("pe", lambda e, cs=cs, hb=hb, B0=B0, qT_h=qT_h: e.matmul(pq(B0, 1), qT_h[:, cs:cs + 128], Sb[hb], start=True, stop=True),
                          reads=[qk_, ("Sb", hb)], writes=[("pq", B0, 1)])
                    sc.op("pe", lambda e, n=n, hb=hb, bq=bq, B0=B0: e.matmul(pq(B0, 2), AT[hb][n], vnew[bq], start=True, stop=True),
                          reads=[("AT", hb, n), ("vnew", bq)], writes=[("pq", B0, 2)])
                    sc.op("act", lambda e, n=n, h=h, bq=bq, B0=B0: e.activation(out=otmp[bq], in_=pq(B0, 1), func=AF.Copy, scale=egc[:, n, h:h + 1]),
                          reads=[("pq", B0, 1), "egc"], writes=[("otmp", bq)])
                    sc.op("dve", lambda e, bq=bq, B0=B0: e.tensor_tensor(out=otok[bq], in0=otmp[bq], in1=pq(B0, 2), op=ALU.add),
                          reads=[("otmp", bq), ("pq", B0, 2)], writes=[("otok", bq)])
                    sc.op("pe", lambda e, n=n, hb=hb, bq=bq, B1=B1: e.matmul(pq(B1, 0), kdt[hb][n], vnew[bq], start=True, stop=True),
                          reads=[("kdt", hb, n), ("vnew", bq)], writes=[("pq", B1, 0)])
                    sc.op("dve", lambda e, n=n, h=h, Sf=Sf, B1=B1: e.scalar_tensor_tensor(out=Sf, in0=Sf, scalar=glb[:, n, h:h + 1], in1=pq(B1, 0),
                                                                                   op0=ALU.mult, op1=ALU.add),
                          reads=[skey, ("pq", B1, 0), "glb"], writes=[skey])
                    sc.op("act", lambda e, Sf=Sf, hb=hb: e.copy(out=Sb[hb], in_=Sf), reads=[skey], writes=[("Sb", hb)])
                    if n == 15:
                        sc.op("sp", lambda e, Sf=Sf, h=h: e.dma_start(out=o_S_p[j, h], in_=Sf), reads=[skey], writes=[("oSp", h)], dma=True)
                    if n == 16:
                        sc.op("sp", lambda e, Sf=Sf, h=h: e.dma_start(out=o_S_s[j, h], in_=Sf), reads=[skey], writes=[("oSs", h)], dma=True)
                    sc.op("act", lambda e, bq=bq: e.activation(out=osq[bq], in_=otok[bq], func=AF.Square, accum_out=ssq[bq]),
                          reads=[("otok", bq)], writes=[("ssq", bq), ("osq", bq)])
                    sc.op("act", lambda e, bq=bq: e.activation(out=ssq[bq], in_=ssq[bq], func=AF.Ln, scale=1.0 / 128, bias=EPS),
                          reads=[("ssq", bq)], writes=[("ssq", bq)])
                    sc.op("act", lambda e, bq=bq: e.activation(out=ssq[bq], in_=ssq[bq], func=AF.Exp, scale=-0.5),
                          reads=[("ssq", bq)], writes=[("ssq", bq)])
                    sc.op("dve", lambda e, bq=bq: e.tensor_scalar(out=onb[bq], in0=otok[bq], scalar1=ssq[bq][:, 0:1], scalar2=None, op0=ALU.mult),
                          reads=[("otok", bq), ("ssq", bq)], writes=[("onb", bq)])
                    sc.op("pe", lambda e, bq=bq, B1=B1: e.transpose(out=pqb(B1, 1), in_=onb[bq], identity=ident_b),
                          reads=[("onb", bq), "ident_b"], writes=[("pq", B1, 1)])
                    sc.op("dve", lambda e, cs=cs, hb=hb, B1=B1, zs_h=zs_h: e.scalar_tensor_tensor(
                        out=ogT[hb][:, cs:cs + 128], in0=pqb(B1, 1), scalar=gnw[:, 0:1], in1=zs_h[:, cs:cs + 128], op0=ALU.mult, op1=ALU.mult),
                        reads=[("pq", B1, 1), "gnw", zk_], writes=[("ogT", hb, n)])
                    if n == NG - 1:
                        sc.op("sp", lambda e, hb=hb, h=h: e.dma_start(out=oT[h * 128:(h + 1) * 128, :], in_=ogT[hb]),
                              reads=[("ogT", hb, n) for n in range(NG)], writes=[("oT", h)], dma=True)
                return setup, groupfn, chunkfn

            heads = [do_head(h) for h in range(NV)]
            g0s = list(range(0, NG, GR))

            def run_pair(hd, g0):
                gens = [hd[1](g0, 0)]
                if g0 + SG < NG:
                    gens.append(hd[1](g0 + SG, 1))
                live = list(gens)
                while live:
                    for g in list(live):
                        try:
                            next(g)
                        except StopIteration:
                            live.remove(g)
            heads[0][0]()
            for g0 in g0s:
                run_pair(heads[0], g0)
            for h in range(NV):
                nxt = heads[h + 1] if h + 1 < NV else None
                if nxt is not None:
                    nxt[0]()
                for g0 in g0s:
                    if nxt is not None:
                        run_pair(nxt, g0)
                    for n in range(g0, min(NG, g0 + GR)):
                        heads[h][2](n)
            sc.barrier()

            if os.environ.get("GDN_STOP") == "4":
                sc.barrier()
                return
            ar.off = persist_off
            Wo = gdn_w_out[j]
            osb = ar.alloc(BF16, [128, 32, 1040])
            wslot = [ar.alloc(BF16, [128, 8192]) for _ in range(2)]
            xres = [ar.alloc(F32, [128, 512]) for _ in range(2)]
            oTv = oT.rearrange("(kc p) t -> p kc t", p=128)
            wcnt = 0
            for blk in (TILES[0:2], TILES[2:5]):
                hc0 = blk[0][0]
                ntok = sum(n for _, n in blk)
                for kc in range(32):
                    sc.op("sp", lambda e, kc=kc, hc0=hc0, ntok=ntok: e.dma_start(out=osb[:, kc, 0:ntok], in_=oTv[:, kc, hc0:hc0 + ntok]),
                          writes=[("osb", kc)], dma=True)
                it = 0
                for mp in range(8):
                    slot = wcnt % 2
                    wcnt += 1
                    vw = wslot[slot].rearrange("p (a b) -> p a b", b=256)
                    srcw = Wo[:, mp * 256:(mp + 1) * 256].rearrange("(kc p) m -> p kc m", p=128)
                    sc.op("pool", lambda e, vw=vw, srcw=srcw: e.dma_start(out=vw, in_=srcw), writes=[("w", slot)], dma=True)
                    for mi in range(2):
                        mc = mp * 2 + mi
                        for (c0, n) in blk:
                            bk = it % 6
                            xb = it % 2
                            it += 1
                            sc.op("sp", lambda e, xb=xb, mc=mc, c0=c0, n=n: e.dma_start(out=xres[xb][:, 0:n], in_=xTv[:, mc, c0:c0 + n]),
                                  writes=[("xres", xb)], dma=True)
                            for kc in range(32):
                                sc.op("pe", lambda e, vw=vw, kc=kc, mi=mi, c0=c0, n=n, bk=bk, hc0=hc0: e.matmul(
                                    bank(bk)[:, 0:n], vw[:, kc, mi * 128:(mi + 1) * 128], osb[:, kc, c0 - hc0:c0 - hc0 + n],
                                    start=(kc == 0), stop=(kc == 31)),
                                    reads=[("w", slot), ("osb", kc)], writes=[("ps", bk)])
                            sc.op("dve", lambda e, xb=xb, bk=bk, n=n: e.tensor_tensor(
                                out=xres[xb][:, 0:n], in0=xres[xb][:, 0:n], in1=bank(bk)[:, 0:n], op=ALU.add),
                                reads=[("xres", xb), ("ps", bk)], writes=[("xres", xb)])
                            sc.op("sp", lambda e, xb=xb, mc=mc, c0=c0, n=n: e.dma_start(out=xTv[:, mc, c0:c0 + n], in_=xres[xb][:, 0:n]),
                                  reads=[("xres", xb)], writes=[("xTo", mc, c0)], dma=True)
                sc.barrier()

        def phase_final():
            ar.reset()
            tmp_x = [ar.alloc(F32, [128, KC, 512]) for _ in range(2)]
            tmp_sq = [ar.alloc(F32, [128, KC, 512])] * 2
            tmp_r = [ar.alloc(F32, [128, 512]) for _ in range(2)]
            yT = ar.alloc(F32, [128, KC, 512])
            yo = [ar.alloc(F32, [128, D]) for _ in range(2)]
            xTv = xT.rearrange("(kc p) t -> p kc t", p=128)
            cnt = 0
            for ti, (c0, n) in enumerate(TILES):
                b = ti % 2
                xt, sq, rr = tmp_x[b], tmp_sq[b], tmp_r[b]
                sc.op("sp", lambda e, xt=xt, c0=c0, n=n: e.dma_start(out=xt[:, :, 0:n], in_=xTv[:, :, c0:c0 + n]),
                      writes=[("nx", b)], dma=True)
                sc.op("act", lambda e, xt=xt, sq=sq, n=n: e.activation(out=sq[:, :, 0:n], in_=xt[:, :, 0:n], func=AF.Square),
                      reads=[("nx", b)], writes=[("nsq", 0)])
                bk = 6 + b
                for kc in range(KC):
                    sc.op("pe", lambda e, sq=sq, kc=kc, n=n, bk=bk: e.matmul(
                        bank(bk)[:, 0:n], ones_f, sq[:, kc, 0:n], start=(kc == 0), stop=(kc == KC - 1)),
                        reads=[("nsq", 0), "ones_f"], writes=[("ps", bk)])
                sc.op("act", lambda e, rr=rr, n=n, bk=bk: e.activation(out=rr[:, 0:n], in_=bank(bk)[:, 0:n], func=AF.Ln,
                                                                      scale=1.0 / D, bias=EPS),
                      reads=[("ps", bk)], writes=[("nr", b)])
                sc.op("act", lambda e, rr=rr, n=n: e.activation(out=rr[:, 0:n], in_=rr[:, 0:n], func=AF.Exp, scale=-0.5),
                      reads=[("nr", b)], writes=[("nr", b)])
                for kc in range(KC):
                    sc.op("dve", lambda e, xt=xt, rr=rr, kc=kc, n=n: e.scalar_tensor_tensor(
                        out=yT[:, kc, 0:n], in0=xt[:, kc, 0:n], scalar=nfin[:, kc, 0:1],
                        in1=rr[:, 0:n], op0=ALU.mult, op1=ALU.mult),
                        reads=[("nx", b), ("nr", b)], writes=[("yT", kc)])
                ngr = (n + 127) // 128
                for gi in range(ngr):
                    rows = min(128, n - gi * 128)
                    ob = cnt % 2
                    cnt += 1
                    for q in range(4):
                        bk2 = q
                        for i in range(4):
                            kc = q * 4 + i
                            sc.op("pe", lambda e, kc=kc, gi=gi, rows=rows, bk2=bk2, i=i: e.transpose(
                                out=bank(bk2)[0:rows, i * 128:(i + 1) * 128], in_=yT[:, kc, gi * 128:gi * 128 + rows],
                                identity=ident_f),
                                reads=[("yT", kc), "ident_f"], writes=[("ps", bk2)])
                        if q % 2 == 0:
                            sc.op("act", lambda e, ob=ob, q=q, rows=rows, bk2=bk2: e.copy(
                                out=yo[ob][0:rows, q * 512:(q + 1) * 512], in_=bank(bk2)[0:rows, :]),
                                reads=[("ps", bk2)], writes=[("yo", ob, q)])
                        else:
                            sc.op("dve", lambda e, ob=ob, q=q, rows=rows, bk2=bk2: e.tensor_copy(
                                out=yo[ob][0:rows, q * 512:(q + 1) * 512], in_=bank(bk2)[0:rows, :]),
                                reads=[("ps", bk2)], writes=[("yo", ob, q)])
                    t0 = c0 + gi * 128
                    dst = y_p[t0:t0 + rows, :] if t0 < TP else y_s
                    sc.op("sp", lambda e, ob=ob, rows=rows, dst=dst: e.dma_start(out=dst, in_=yo[ob][0:rows, :]),
                          reads=[("yo", ob, q) for q in range(4)], writes=[("y", t0)], dma=True)
            sc.barrier()

        phase_input()
        for s in stages:
            if s.startswith("ffn"):
                phase_ffn(int(s[3:]))
            elif s.startswith("pool"):
                phase_pool(int(s[4:]))
            elif s.startswith("gdn"):
                phase_gdn(int(s[3:]))
        phase_final()
        sc.finalize(st)
    return nc, sc


_CACHE = {}


def kernel(**inputs):
    f32 = lambda a: np.ascontiguousarray(np.asarray(a, dtype=np.float32))
    if "nc" not in _CACHE:
        _CACHE["nc"] = build()[0]
    nc = _CACHE["nc"]
    shared = {k: f32(inputs[k]) for k in ("norm_mix_w", "norm_ffn_w", "gdn_w_in", "gdn_conv_w", "gdn_A_log",
                                          "gdn_dt_bias", "gdn_norm_w", "gdn_w_out", "pool_w", "pool_scale",
                                          "ffn_w_gu", "ffn_w_down")}
    shared["final_norm_w"] = f32(inputs["final_norm_w"]).reshape(1, D)
    xp, xs = f32(inputs["x_prompt"]), f32(inputs["x_sample"])
    sconv, sS, spool = f32(inputs["state_gdn_conv"]), f32(inputs["state_gdn_S"]), f32(inputs["state_pool"])
    in_maps = []
    for b in range(8):
        m = dict(shared)
        m["x_p"] = xp[b]
        m["x_s"] = xs[b]
        m["st_conv"] = np.ascontiguousarray(sconv[:, b])
        m["st_S"] = np.ascontiguousarray(sS[:, b])
        m["st_pool"] = np.ascontiguousarray(spool[:, b])
        in_maps.append(m)
    res = run_bass_kernel_spmd(nc, in_maps, core_ids=list(range(8)))
    r = res.results
    stack = lambda k, ax: np.stack([np.asarray(r[b][k], dtype=np.float32) for b in range(8)], axis=ax)
    return (stack("y_p", 0), stack("y_s", 0), stack("o_conv_p", 1), stack("o_S_p", 1), stack("o_pool_p", 1),
            stack("o_conv_s", 1), stack("o_S_s", 1), stack("o_pool_s", 1))
```

```python
import math
import os
from contextlib import ExitStack
import numpy as np
import concourse.bass as bass
import concourse.mybir as mybir
from concourse.alu_op_type import AluOpType as ALU
from concourse.bass_utils import run_bass_kernel_spmd

F32 = mybir.dt.float32
BF16 = mybir.dt.bfloat16
AF = mybir.ActivationFunctionType

ENGS = ("pe", "dve", "act", "pool", "sp")

D = 2048
KC = 16
TP = 2048
TS = 16
TR = TP + TS
TA = TP + 128
NG = TA // 128
NV = 32
QKV = 8192
VAL = 4096
IN_DIM = 12352
FH = 5632
FC = FH // 128
EPS = 1e-6
TILES = [(0, 512), (512, 512), (1024, 512), (1536, 512), (2048, 16)]
NEG = -1.0e30


class Sched:
    def __init__(self, nc, n_lanes=8):
        self.nc = nc
        self.ops = []
        self.last_w = {}
        self.readers = {}
        self.n_lanes = n_lanes
        self.implicit = {"pe"}

    def op(self, eng, fn, reads=(), writes=(), dma=False):
        nk = lambda k: ("ps", k[1]) if (isinstance(k, tuple) and k and k[0] == "pq") else k
        reads = [nk(k) for k in reads] + ["ALL"]
        writes = [nk(k) for k in writes]
        deps = set()
        for k in reads:
            w = self.last_w.get(k)
            if w is not None:
                deps.add(w)
            if isinstance(k, tuple) and k and k[0] == "ps":
                for r in self.readers.get(k, ()):
                    if self.ops[r]["eng"] != eng:
                        deps.add(r)
        for k in writes:
            w = self.last_w.get(k)
            if w is not None:
                deps.add(w)
            for r in self.readers.get(k, ()):
                deps.add(r)
        idx = len(self.ops)
        self.ops.append(dict(eng=eng, fn=fn, deps=deps, dma=dma))
        for k in reads:
            self.readers.setdefault(k, []).append(idx)
        for k in writes:
            self.last_w[k] = idx
            self.readers[k] = []
        return idx

    def barrier(self):
        for e in ("sp", "pe", "dve", "act", "pool"):
            self.op(e, lambda h: h.nop(), writes=["ALL"])

    def finalize(self, stack):
        nc = self.nc
        ops = self.ops
        needed = set()
        for o in ops:
            if o["eng"] in self.implicit:
                o["deps"] = {d for d in o["deps"] if ops[d]["eng"] != o["eng"] or ops[d]["dma"]}
            needed |= o["deps"]
        csem = {e: stack.enter_context(nc.semaphore(f"c_{e}")) for e in ENGS}
        lanes = {e: [stack.enter_context(nc.semaphore(f"l_{e}_{i}")) for i in range(self.n_lanes)]
                 for e in ("sp", "pool")}
        ccount = {e: 0 for e in ENGS}
        lane_cnt = {e: [0] * self.n_lanes for e in lanes}
        lane_rr = {e: 0 for e in lanes}
        token = [None] * len(ops)
        known = {e: {} for e in ENGS}
        streams = {e: [] for e in ENGS}
        for i, o in enumerate(ops):
            e = o["eng"]
            waits = {}
            for d in o["deps"]:
                s, v = token[d]
                key = id(s)
                if known[e].get(key, 0) >= v:
                    continue
                if key not in waits or waits[key][1] < v:
                    waits[key] = (s, v)
            inc = None
            if o["dma"]:
                ln = lane_rr[e]
                lane_rr[e] = (ln + 1) % self.n_lanes
                s = lanes[e][ln]
                prev = lane_cnt[e][ln]
                if prev > 0 and known[e].get(id(s), 0) < prev:
                    key = id(s)
                    if key not in waits or waits[key][1] < prev:
                        waits[key] = (s, prev)
                lane_cnt[e][ln] = prev + 16
                token[i] = (s, prev + 16)
                inc = (s, 16)
            else:
                if i in needed:
                    ccount[e] += 1
                    token[i] = (csem[e], ccount[e])
                    inc = (csem[e], 1)
                else:
                    token[i] = (csem[e], ccount[e] + 1)
            for key, (s, v) in waits.items():
                known[e][key] = v
            streams[e].append((list(waits.values()), o["fn"], inc))
        self.stats = {e: len(streams[e]) for e in ENGS}
        final_waits = []
        for e in lanes:
            for ln in range(self.n_lanes):
                if lane_cnt[e][ln] > 0:
                    final_waits.append((lanes[e][ln], lane_cnt[e][ln]))
        for e in ENGS:
            if ccount[e] > 0:
                final_waits.append((csem[e], ccount[e]))
        with nc.Block() as block:
            def mk(e):
                def body(engh):
                    for waits, fn, inc in streams[e]:
                        for s, v in waits:
                            engh.wait_ge(s, v)
                        ins = fn(engh)
                        if inc is not None:
                            ins.then_inc(inc[0], inc[1])
                    if e == "sp":
                        for s, v in final_waits:
                            engh.wait_ge(s, v)
                return body
            block.tensor(mk("pe"))
            block.vector(mk("dve"))
            block.scalar(mk("act"))
            block.gpsimd(mk("pool"))
            block.sync(mk("sp"))


class Arena:
    def __init__(self, A, nbytes):
        self.A = A
        self.size = nbytes
        self.base = 0
        self.off = 0

    def alloc(self, dt, shape, name=None):
        esz = 4 if dt == F32 else 2
        free = 1
        for s in shape[1:]:
            free *= s
        nb = free * esz
        off = self.off
        self.off += (nb + 63) // 64 * 64
        assert self.off <= self.size, f"arena overflow {self.off} > {self.size} ({name})"
        ap = self.A[0:shape[0], off // 4:(off + nb + 3) // 4]
        if dt != F32:
            ap = ap.bitcast(dt)
        if len(shape) == 3:
            ap = ap.rearrange("p (a b) -> p a b", b=shape[2])
        elif len(shape) == 4:
            ap = ap.rearrange("p (a b c) -> p a b c", b=shape[2], c=shape[3])
        return ap

    def view(self, off, dt, shape):
        save = self.off
        self.off = off
        ap = self.alloc(dt, shape)
        self.off = save
        return ap

    def mark(self):
        self.base = self.off

    def reset(self):
        self.off = self.base


def build(stages=None, debug=False):
    nc = bass.Bass("TRN2", target_bir_lowering=False)

    def din(name, shape):
        return nc.dram_tensor(name, list(shape), F32, kind="ExternalInput").ap()

    def dout(name, shape):
        return nc.dram_tensor(name, list(shape), F32, kind="ExternalOutput").ap()

    def dscr(name, shape, dt):
        return nc.dram_tensor(name, list(shape), dt, kind="Internal").ap()

    x_p = din("x_p", [TP, D])
    x_s = din("x_s", [TS, D])
    st_conv = din("st_conv", [2, 3, QKV])
    st_S = din("st_S", [2, NV, 128, 128])
    st_pool = din("st_pool", [2, 15, D])
    norm_mix_w = din("norm_mix_w", [4, D])
    norm_ffn_w = din("norm_ffn_w", [4, D])
    final_norm_w = din("final_norm_w", [1, D])
    gdn_w_in = din("gdn_w_in", [2, D, IN_DIM])
    gdn_conv_w = din("gdn_conv_w", [2, 4, QKV])
    gdn_A_log = din("gdn_A_log", [2, NV])
    gdn_dt_bias = din("gdn_dt_bias", [2, NV])
    gdn_norm_w = din("gdn_norm_w", [2, 128])
    gdn_w_out = din("gdn_w_out", [2, VAL, D])
    pool_w = din("pool_w", [2, 4, 512, 512])
    pool_scale = din("pool_scale", [2, D])
    ffn_w_gu = din("ffn_w_gu", [4, D, 2 * FH])
    ffn_w_down = din("ffn_w_down", [4, FH, D])

    y_p = dout("y_p", [TP, D])
    y_s = dout("y_s", [TS, D])
    o_conv_p = dout("o_conv_p", [2, 3, QKV])
    o_S_p = dout("o_S_p", [2, NV, 128, 128])
    o_pool_p = dout("o_pool_p", [2, 15, D])
    o_conv_s = dout("o_conv_s", [2, 3, QKV])
    o_S_s = dout("o_S_s", [2, NV, 128, 128])
    o_pool_s = dout("o_pool_s", [2, 15, D])

    xT = dscr("xT", [D, TA], F32)
    qkvT = dscr("qkvT", [QKV, TA], BF16)
    zT = dscr("zT", [VAL, TA], BF16)
    oT = dscr("oT", [VAL, TA], BF16)

    sc = Sched(nc)
    if stages is None:
        stages = ["gdn0", "ffn0", "pool1", "ffn1", "gdn2", "ffn2", "pool3", "ffn3"]

    with ExitStack() as st:
        ARENA_BYTES = 176 * 1024
        A = st.enter_context(nc.sbuf_tensor("arena", [128, ARENA_BYTES // 4], F32))
        PS = st.enter_context(nc.psum_tensor("psum", [128, 8, 512], F32))
        ar = Arena(A, ARENA_BYTES)

        def bank(i):
            return PS[:, i, :]

        ident_f = ar.alloc(F32, [128, 128])
        ident_b = ar.alloc(BF16, [128, 128])
        ones_f = ar.alloc(F32, [128, 128])
        ones_b = ar.alloc(BF16, [128, 128])
        mneg_b = ar.alloc(BF16, [128, 128])
        zero_f = ar.alloc(F32, [128, 128])
        nmix = ar.alloc(F32, [128, KC, 4])
        nffn = ar.alloc(F32, [128, KC, 4])
        nfin = ar.alloc(F32, [128, KC, 1])
        pscale = ar.alloc(F32, [128, KC, 2])

        sc.op("pool", lambda e: e.memset(ones_f, 1.0), writes=["ones_f"])
        sc.op("pool", lambda e: e.memset(ones_b, 1.0), writes=["ones_b"])
        sc.op("pool", lambda e: e.memset(zero_f, 0.0), writes=["zero_f"])
        sc.op("pool", lambda e: e.affine_select(out=ident_f, in_=ones_f, pattern=[[1, 128]],
                                                compare_op=ALU.is_equal, fill=0.0, base=0,
                                                channel_multiplier=-1),
              reads=["ones_f"], writes=["ident_f"])
        sc.op("pool", lambda e: e.tensor_copy(out=ident_b, in_=ident_f), reads=["ident_f"], writes=["ident_b"])
        sc.op("pool", lambda e: e.affine_select(out=mneg_b, in_=zero_f, pattern=[[1, 128]],
                                                compare_op=ALU.is_ge, fill=NEG, base=0,
                                                channel_multiplier=-1),
              reads=["zero_f"], writes=["mneg_b"])

        rowbuf_box = [None]

        def vec_to_cols(src, R, N, dst, bankno=7):
            nchunk = N // 128
            rowbuf = rowbuf_box[0]
            sc.op("sp", lambda e: e.dma_start(out=rowbuf[0:R, 0:N], in_=src), writes=["rowbuf"], dma=True)
            per = 512 // R
            c = 0
            while c < nchunk:
                m = min(per, nchunk - c)
                pv = bank(bankno)[:, 0:m * R].rearrange("p (a b) -> p a b", b=R)
                for i in range(m):
                    sc.op("pe", lambda e, c=c, i=i, pv=pv: e.transpose(
                        out=pv[:, i, :], in_=rowbuf[0:R, (c + i) * 128:(c + i + 1) * 128], identity=ident_f[0:R, 0:R]),
                        reads=["rowbuf", "ident_f"], writes=[("ps", bankno)])
                sc.op("dve", lambda e, c=c, m=m, pv=pv: e.tensor_copy(out=dst[:, c:c + m, :], in_=pv),
                      reads=[("ps", bankno)], writes=[("cols", id(dst))])
                c += m

        ar.mark()
        rowbuf_box[0] = ar.alloc(F32, [16, QKV])
        vec_to_cols(norm_mix_w, 4, D, nmix)
        vec_to_cols(norm_ffn_w, 4, D, nffn)
        vec_to_cols(final_norm_w, 1, D, nfin)
        vec_to_cols(pool_scale, 2, D, pscale)
        sc.barrier()

        def phase_input():
            ar.reset()
            xin = [ar.alloc(F32, [128, D]) for _ in range(2)]
            xo = [ar.alloc(F32, [128, KC, 128]) for _ in range(2)]
            for g in range(NG):
                b = g % 2
                rows = 128 if g < 16 else TS
                src = x_p[g * 128:(g + 1) * 128, :] if g < 16 else x_s
                sc.op("sp", lambda e, b=b, rows=rows, src=src: e.dma_start(out=xin[b][0:rows, :], in_=src),
                      writes=[("xin", b)], dma=True)
                for q in range(4):
                    bk = (g * 4 + q) % 4
                    for i in range(4):
                        kc = q * 4 + i
                        sc.op("pe", lambda e, b=b, rows=rows, kc=kc, bk=bk, i=i: e.transpose(
                            out=bank(bk)[:, i * 128:i * 128 + rows], in_=xin[b][0:rows, kc * 128:(kc + 1) * 128],
                            identity=ident_f[0:rows, 0:rows]),
                            reads=[("xin", b), "ident_f"], writes=[("ps", bk)])
                    eng = "act" if q % 2 == 0 else "dve"
                    pv = bank(bk).rearrange("p (a b) -> p a b", b=128)[:, :, 0:rows]
                    if eng == "act":
                        sc.op("act", lambda e, b=b, q=q, pv=pv, rows=rows: e.copy(out=xo[b][:, q * 4:(q + 1) * 4, 0:rows], in_=pv),
                              reads=[("ps", bk)], writes=[("xo", b, q)])
                    else:
                        sc.op("dve", lambda e, b=b, q=q, pv=pv, rows=rows: e.tensor_copy(out=xo[b][:, q * 4:(q + 1) * 4, 0:rows], in_=pv),
                              reads=[("ps", bk)], writes=[("xo", b, q)])
                dstv = xT.rearrange("(kc p) t -> p kc t", p=128)[:, :, g * 128:g * 128 + rows]
                sc.op("sp", lambda e, b=b, rows=rows, dstv=dstv: e.dma_start(out=dstv, in_=xo[b][:, :, 0:rows]),
                      reads=[("xo", b, q) for q in range(4)], writes=[("xT", g)], dma=True)
            sc.barrier()

        def norm_tiles(hT, wcols, widx, tiles, tmp_x, tmp_sq, tmp_r, hcol0=0, side=None, hl=None):
            xTv = xT.rearrange("(kc p) t -> p kc t", p=128)
            for ti, (c0, n) in enumerate(tiles):
                b = ti % 2
                xt, sq, rr = tmp_x[b], tmp_sq[b], tmp_r[b]
                kx, kq, kr = ("nx", id(xt)), ("nsq", id(sq)), ("nr", id(rr))
                sc.op("sp", lambda e, xt=xt, c0=c0, n=n: e.dma_start(out=xt[:, :, 0:n], in_=xTv[:, :, c0:c0 + n]),
                      writes=[kx], dma=True)
                sc.op("act", lambda e, xt=xt, sq=sq, n=n: e.activation(out=sq[:, :, 0:n], in_=xt[:, :, 0:n], func=AF.Square),
                      reads=[kx], writes=[kq])
                bk = 6 + b
                for kc in range(KC):
                    sc.op("pe", lambda e, sq=sq, kc=kc, n=n, bk=bk: e.matmul(
                        bank(bk)[:, 0:n], ones_f, sq[:, kc, 0:n], start=(kc == 0), stop=(kc == KC - 1)),
                        reads=[kq, "ones_f"], writes=[("ps", bk)])
                sc.op("act", lambda e, rr=rr, n=n, bk=bk: e.activation(out=rr[:, 0:n], in_=bank(bk)[:, 0:n], func=AF.Ln,
                                                                      scale=1.0 / D, bias=EPS),
                      reads=[("ps", bk)], writes=[kr])
                sc.op("act", lambda e, rr=rr, n=n: e.activation(out=rr[:, 0:n], in_=rr[:, 0:n], func=AF.Exp, scale=-0.5),
                      reads=[kr], writes=[kr])
                for kc in range(KC):
                    sc.op("dve", lambda e, xt=xt, rr=rr, kc=kc, c0=c0, n=n: e.scalar_tensor_tensor(
                        out=hT[:, kc, c0 - hcol0:c0 - hcol0 + n], in0=xt[:, kc, 0:n], scalar=wcols[:, kc, widx:widx + 1],
                        in1=rr[:, 0:n], op0=ALU.mult, op1=ALU.mult),
                        reads=[kx, kr], writes=[("hT", kc, c0)])
                for (ts0, cnt, dcol) in (side or []):
                    if c0 <= ts0 and ts0 + cnt <= c0 + n:
                        for kc in range(KC):
                            sc.op("dve", lambda e, xt=xt, rr=rr, kc=kc, o=ts0 - c0, cnt=cnt, dcol=dcol: e.scalar_tensor_tensor(
                                out=hl[:, kc, dcol:dcol + cnt], in0=xt[:, kc, o:o + cnt], scalar=wcols[:, kc, widx:widx + 1],
                                in1=rr[:, o:o + cnt], op0=ALU.mult, op1=ALU.mult),
                                reads=[kx, kr], writes=[("hl", kc, dcol)])

        def wload(wslot, slot, W, KCn, col0, ncols):
            view = wslot[slot][:, 0:KCn * ncols].rearrange("p (a b) -> p a b", b=ncols)
            src = W[:, col0:col0 + ncols].rearrange("(kc p) m -> p kc m", p=128)
            sc.op("pool", lambda e: e.dma_start(out=view, in_=src), writes=[("w", slot)], dma=True)
            return view

        def phase_ffn(li):
            ar.reset()
            Wgu = ffn_w_gu[li]
            Wd = ffn_w_down[li]
            NB = 1040
            hT = ar.alloc(BF16, [128, KC, NB])
            act_off = ar.off
            act = ar.alloc(BF16, [128, FC, NB])
            wslot = [ar.alloc(BF16, [128, 8192]) for _ in range(2)]
            sgt = [ar.alloc(F32, [128, 512]) for _ in range(2)]
            xres = [ar.alloc(F32, [128, 512]) for _ in range(2)]
            tx = ar.view(act_off, F32, [128, KC, 512])
            tq = ar.view(act_off + 32768, F32, [128, KC, 512])
            tr = [ar.view(act_off + 65536 + i * 2048, F32, [128, 512]) for i in range(2)]
            blocks = [TILES[0:2], TILES[2:5]]
            xTv = xT.rearrange("(kc p) t -> p kc t", p=128)
            wcnt = 0
            for blk in blocks:
                hc0 = blk[0][0]
                norm_tiles(hT, nffn, li, blk, [tx, tx], [tq, tq], tr, hcol0=hc0)
                sc.barrier()
                for jp in range(FC // 2):
                    slot = wcnt % 2
                    wcnt += 1
                    vg = wslot[slot][:, 0:KC * 256].rearrange("p (a b) -> p a b", b=256)
                    vu = wslot[slot][:, KC * 256:KC * 512].rearrange("p (a b) -> p a b", b=256)
                    srcg = Wgu[:, jp * 256:jp * 256 + 256].rearrange("(kc p) m -> p kc m", p=128)
                    srcu = Wgu[:, FH + jp * 256:FH + jp * 256 + 256].rearrange("(kc p) m -> p kc m", p=128)
                    sc.op("pool", lambda e, vg=vg, srcg=srcg: e.dma_start(out=vg, in_=srcg),
                          writes=[("w", slot, 0)], dma=True)
                    sc.op("pool", lambda e, vu=vu, srcu=srcu: e.dma_start(out=vu, in_=srcu),
                          writes=[("w", slot, 1)], dma=True)
                    for jj in range(2):
                        j = jp * 2 + jj
                        for ti, (c0, n) in enumerate(blk):
                            it = j * len(blk) + ti
                            bg = (it % 3) * 2
                            bu = bg + 1
                            for kc in range(KC):
                                sc.op("pe", lambda e, vg=vg, kc=kc, jj=jj, c0=c0, n=n, bg=bg, hc0=hc0: e.matmul(
                                    bank(bg)[:, 0:n], vg[:, kc, jj * 128:(jj + 1) * 128], hT[:, kc, c0 - hc0:c0 - hc0 + n],
                                    start=(kc == 0), stop=(kc == KC - 1)),
                                    reads=[("w", slot, 0), ("hT", kc, c0)], writes=[("ps", bg)])
                            for kc in range(KC):
                                sc.op("pe", lambda e, vu=vu, kc=kc, jj=jj, c0=c0, n=n, bu=bu, hc0=hc0: e.matmul(
                                    bank(bu)[:, 0:n], vu[:, kc, jj * 128:(jj + 1) * 128], hT[:, kc, c0 - hc0:c0 - hc0 + n],
                                    start=(kc == 0), stop=(kc == KC - 1)),
                                    reads=[("w", slot, 1), ("hT", kc, c0)], writes=[("ps", bu)])
                            sb = it % 2
                            sc.op("act", lambda e, sb=sb, bg=bg, n=n: e.activation(out=sgt[sb][:, 0:n], in_=bank(bg)[:, 0:n], func=AF.Silu),
                                  reads=[("ps", bg)], writes=[("sgt", sb)])
                            sc.op("dve", lambda e, sb=sb, bu=bu, n=n, j=j, c0=c0, hc0=hc0: e.tensor_tensor(
                                out=act[:, j, c0 - hc0:c0 - hc0 + n], in0=sgt[sb][:, 0:n], in1=bank(bu)[:, 0:n], op=ALU.mult),
                                reads=[("sgt", sb), ("ps", bu)], writes=[("act", j, c0)])
                for mc in range(KC):
                    slot = wcnt % 2
                    wcnt += 1
                    vw = wslot[slot][:, 0:FC * 128].rearrange("p (a b) -> p a b", b=128)
                    src = Wd[:, mc * 128:(mc + 1) * 128].rearrange("(kc p) m -> p kc m", p=128)
                    sc.op("pool", lambda e, vw=vw, src=src: e.dma_start(out=vw, in_=src),
                          writes=[("w", slot, 0), ("w", slot, 1)], dma=True)
                    for ti, (c0, n) in enumerate(blk):
                        it = mc * len(blk) + ti
                        bk = it % 6
                        xb = it % 2
                        sc.op("sp", lambda e, xb=xb, mc=mc, c0=c0, n=n: e.dma_start(out=xres[xb][:, 0:n], in_=xTv[:, mc, c0:c0 + n]),
                              writes=[("xres", xb)], dma=True)
                        for kc in range(FC):
                            sc.op("pe", lambda e, vw=vw, kc=kc, c0=c0, n=n, bk=bk, hc0=hc0: e.matmul(
                                bank(bk)[:, 0:n], vw[:, kc, :], act[:, kc, c0 - hc0:c0 - hc0 + n],
                                start=(kc == 0), stop=(kc == FC - 1)),
                                reads=[("w", slot, 0), ("w", slot, 1), ("act", kc, c0)], writes=[("ps", bk)])
                        sc.op("dve", lambda e, xb=xb, bk=bk, n=n: e.tensor_tensor(
                            out=xres[xb][:, 0:n], in0=xres[xb][:, 0:n], in1=bank(bk)[:, 0:n], op=ALU.add),
                            reads=[("xres", xb), ("ps", bk)], writes=[("xres", xb)])
                        sc.op("sp", lambda e, xb=xb, mc=mc, c0=c0, n=n: e.dma_start(out=xTv[:, mc, c0:c0 + n], in_=xres[xb][:, 0:n]),
                              reads=[("xres", xb)], writes=[("xTo", mc, c0)], dma=True)
                sc.barrier()

        def phase_pool(li):
            j = li // 2
            ar.reset()
            HP = 2096
            hpad = ar.alloc(BF16, [128, KC, HP])
            hl = ar.alloc(F32, [128, KC, 32])
            icnt = ar.alloc(F32, [128, 16])
            wslot = [ar.alloc(BF16, [128, 2048]) for _ in range(2)]
            xres = [ar.alloc(F32, [128, 512]) for _ in range(2)]
            po = ar.alloc(F32, [16, D])
            big_off = ar.off
            dT = ar.alloc(BF16, [128, KC, TR])
            tA = ar.alloc(F32, [128, HP])
            tB = ar.alloc(F32, [128, HP])
            tC = ar.alloc(F32, [128, 16])
            tx = ar.view(big_off, F32, [128, KC, 512])
            tq = ar.view(big_off + 32768, F32, [128, KC, 512])
            tr = [ar.view(big_off + 65536 + i * 2048, F32, [128, 512]) for i in range(2)]
            rb = ar.view(big_off, F32, [16, D])
            xTv = xT.rearrange("(kc p) t -> p kc t", p=128)
            for t in range(16):
                sc.op("pool", lambda e, t=t: e.memset(icnt[:, t:t + 1], 1.0 / (t + 1)), writes=["icnt"])
            sc.op("pool", lambda e: e.memset(hpad[:, :, 0:16], 0.0), writes=[("hp0",)])
            sc.op("pool", lambda e: e.memset(hpad[:, :, 2064:2065], 0.0), writes=[("hp1",)])
            sc.op("sp", lambda e: e.dma_start(out=rb[0:15, :], in_=st_pool[j]), writes=["rb"], dma=True)
            for q in range(4):
                pv = bank(7)[:, 0:60].rearrange("p (a b) -> p a b", b=15)
                for i in range(4):
                    sc.op("pe", lambda e, q=q, i=i, pv=pv: e.transpose(out=pv[:, i, :], in_=rb[0:15, (q * 4 + i) * 128:(q * 4 + i + 1) * 128],
                                                                 identity=ident_f[0:15, 0:15]),
                          reads=["rb", "ident_f"], writes=[("ps", 7)])
                sc.op("dve", lambda e, q=q, pv=pv: e.tensor_copy(out=hpad[:, q * 4:(q + 1) * 4, 2065:2080], in_=pv),
                      reads=[("ps", 7)], writes=[("hph", q)])
            sc.barrier()
            side = [(2032, 16, 0), (2048, 16, 16)]
            norm_tiles(hpad, nmix, li, TILES[0:4], [tx, tx], [tq, tq], tr, hcol0=-16, side=side, hl=hl)
            norm_tiles(hpad, nmix, li, TILES[4:5], [tx, tx], [tq, tq], tr, hcol0=-32, side=side, hl=hl)
            sc.barrier()
            for (c_lo, dst) in ((1, o_pool_p[j]), (17, o_pool_s[j])):
                for q in range(4):
                    for i in range(4):
                        kc = q * 4 + i
                        sc.op("pe", lambda e, kc=kc, c_lo=c_lo, q=q, i=i: e.transpose(
                            out=bank(q)[0:15, i * 128:(i + 1) * 128], in_=hl[:, kc, c_lo:c_lo + 15], identity=ident_f),
                            reads=[("hl", kc, 0), ("hl", kc, 16), "ident_f"], writes=[("ps", q)])
                    sc.op("act", lambda e, q=q: e.copy(out=po[0:15, q * 512:(q + 1) * 512], in_=bank(q)[0:15, :]),
                          reads=[("ps", q)], writes=[("po", q)])
                sc.op("sp", lambda e, dst=dst: e.dma_start(out=dst, in_=po[0:15, :]), reads=[("po", q) for q in range(4)],
                      writes=[("pout", c_lo)], dma=True)
            for kc in range(KC):
                gi = kc // 4
                w = 2 << gi
                src = hpad[:, kc, :]
                bufs = [tA, tB]
                cur = None
                sh = 1
                for lv in range(gi + 1):
                    dstb = bufs[lv % 2]
                    eng = "dve" if (kc + lv) % 2 == 0 else "pool"
                    a_in = src if cur is None else cur
                    sc.op(eng, lambda e, dstb=dstb, a_in=a_in, sh=sh: e.tensor_tensor(
                        out=dstb[:, sh:HP], in0=a_in[:, sh:HP], in1=a_in[:, 0:HP - sh], op=ALU.add),
                        reads=[("hT", kc, c) for c, _ in TILES] + [("win", id(a_in)), ("hp0",), ("hp1",), ("hph", kc // 4)],
                        writes=[("win", id(dstb))])
                    cur = dstb
                    sh *= 2
                sc.op("dve", lambda e, cur=cur, src=src, w=w, kc=kc: e.scalar_tensor_tensor(
                    out=dT[:, kc, 0:TP], in0=cur[:, 16:16 + TP], scalar=1.0 / w, in1=src[:, 16:16 + TP],
                    op0=ALU.mult, op1=ALU.subtract),
                    reads=[("win", id(cur))] + [("hT", kc, c) for c, _ in TILES], writes=[("dT", kc)])
                sc.op("dve", lambda e, cur=cur, src=src, w=w, kc=kc: e.scalar_tensor_tensor(
                    out=dT[:, kc, TP:TR], in0=cur[:, 2080:2096], scalar=1.0 / w, in1=src[:, 2080:2096],
                    op0=ALU.mult, op1=ALU.subtract),
                    reads=[("win", id(cur))] + [("hT", kc, c) for c, _ in TILES], writes=[("dT", kc)])
                sc.op("dve", lambda e, cur=cur, w=w: e.tensor_tensor(out=tC[:, 0:w - 1], in0=cur[:, 16:16 + w - 1], in1=icnt[:, 0:w - 1], op=ALU.mult),
                      reads=[("win", id(cur)), "icnt"], writes=["tC"])
                sc.op("dve", lambda e, src=src, w=w, kc=kc: e.tensor_tensor(out=dT[:, kc, 0:w - 1], in0=tC[:, 0:w - 1], in1=src[:, 16:16 + w - 1], op=ALU.subtract),
                      reads=["tC"] + [("hT", kc, c) for c, _ in TILES], writes=[("dT", kc)])
            it = 0
            for gi in range(4):
                slot = gi % 2
                vw = wslot[slot].rearrange("p (a b) -> p a b", b=512)
                srcw = pool_w[j, gi].rearrange("(kc p) m -> p kc m", p=128)
                sc.op("pool", lambda e, vw=vw, srcw=srcw: e.dma_start(out=vw, in_=srcw), writes=[("w", slot)], dma=True)
                for ec in range(4):
                    mc = gi * 4 + ec
                    for (c0, n) in TILES:
                        bk = it % 6
                        xb = it % 2
                        it += 1
                        sc.op("sp", lambda e, xb=xb, mc=mc, c0=c0, n=n: e.dma_start(out=xres[xb][:, 0:n], in_=xTv[:, mc, c0:c0 + n]),
                              writes=[("xres", xb)], dma=True)
                        for cc in range(4):
                            sc.op("pe", lambda e, vw=vw, cc=cc, ec=ec, gi=gi, c0=c0, n=n, bk=bk: e.matmul(
                                bank(bk)[:, 0:n], vw[:, cc, ec * 128:(ec + 1) * 128], dT[:, gi * 4 + cc, c0:c0 + n],
                                start=(cc == 0), stop=(cc == 3)),
                                reads=[("w", slot), ("dT", gi * 4 + cc)], writes=[("ps", bk)])
                        sc.op("dve", lambda e, xb=xb, bk=bk, n=n, mc=mc: e.scalar_tensor_tensor(
                            out=xres[xb][:, 0:n], in0=bank(bk)[:, 0:n], scalar=pscale[:, mc, j:j + 1], in1=xres[xb][:, 0:n],
                            op0=ALU.mult, op1=ALU.add),
                            reads=[("xres", xb), ("ps", bk)], writes=[("xres", xb)])
                        sc.op("sp", lambda e, xb=xb, mc=mc, c0=c0, n=n: e.dma_start(out=xTv[:, mc, c0:c0 + n], in_=xres[xb][:, 0:n]),
                              reads=[("xres", xb)], writes=[("xTo", mc, c0)], dma=True)
            sc.barrier()

        def phase_gdn(li):
            j = li // 2
            Win = gdn_w_in[j]
            ar.reset()
            xTv = xT.rearrange("(kc p) t -> p kc t", p=128)
            cw = ar.alloc(F32, [128, 64, 4])
            chs = ar.alloc(F32, [128, 64, 3])
            cst = ar.alloc(F32, [128, 64, 8])
            bT = ar.alloc(F32, [32, TA])
            gT = ar.alloc(F32, [32, TA])
            alog = ar.alloc(F32, [32, 1])
            dtb = ar.alloc(F32, [32, 1])
            nega = ar.alloc(F32, [32, 1])
            gnw = ar.alloc(F32, [128, 1])
            gcTok = ar.alloc(F32, [128, NG, 32])
            nbTok = ar.alloc(F32, [128, NG, 32])
            bTok = ar.alloc(F32, [128, NG, 32])
            egc = ar.alloc(F32, [128, NG, 32])
            ekd = ar.alloc(F32, [128, NG, 32])
            glb = ar.alloc(F32, [128, NG, 32])
            negones = ar.alloc(F32, [32, 128])
            persist_off = ar.off

            rb = ar.alloc(F32, [16, QKV])
            rowbuf_box[0] = rb
            vec_to_cols(gdn_conv_w[j], 4, QKV, cw)
            vec_to_cols(st_conv[j], 3, QKV, chs)
            sc.op("sp", lambda e: e.dma_start(out=alog, in_=gdn_A_log[j].rearrange("(p o) -> p o", o=1)), writes=["alog"], dma=True)
            sc.op("sp", lambda e: e.dma_start(out=dtb, in_=gdn_dt_bias[j].rearrange("(p o) -> p o", o=1)), writes=["dtb"], dma=True)
            sc.op("sp", lambda e: e.dma_start(out=gnw, in_=gdn_norm_w[j].rearrange("(p o) -> p o", o=1)), writes=["gnw"], dma=True)
            sc.op("pool", lambda e: e.memset(negones, -1.0), writes=["negones"])
            sc.op("pool", lambda e: e.memset(cst, 0.0), writes=[("cst", mc) for mc in range(64)])
            sc.barrier()

            ar.off = persist_off
            hT = ar.alloc(BF16, [128, KC, TR])
            g2_off = ar.off
            tx = ar.alloc(F32, [128, KC, 512])
            tq = ar.alloc(F32, [128, KC, 512])
            tr = [ar.alloc(F32, [128, 512]) for _ in range(2)]
            norm_tiles(hT, nmix, li, TILES, [tx, tx], [tq, tq], tr)
            sc.barrier()

            if os.environ.get("GDN_STOP") == "1":
                sc.barrier()
                return
            ar.off = g2_off
            wslot = [ar.alloc(BF16, [128, 4096]) for _ in range(2)]
            PW = 2070
            pre = [ar.alloc(F32, [128, PW]) for _ in range(2)]
            acc = ar.alloc(F32, [128, PW])
            sil = ar.alloc(F32, [128, PW])
            sqb = ar.alloc(BF16, [128, PW])
            vout = [ar.alloc(BF16, [128, TA]) for _ in range(2)]
            rs = acc
            for b in range(2):
                sc.op("pool", lambda e, b=b: e.memset(vout[b][:, TR:TA], 0.0), writes=[("voutpad", b)])
                sc.op("pool", lambda e, b=b: e.memset(pre[b][:, 0:3], 0.0), writes=[("prepad", b)])
            PCOL = [3, 515, 1027, 1539, 2054]
            nchunks_total = 97
            itc = 0
            for mc in range(nchunks_total):
                if mc % 2 == 0:
                    slot = (mc // 2) % 2
                    ncols = min(256, IN_DIM - mc * 128)
                    wv = wload(wslot, slot, Win, KC, mc * 128, ncols)
                wi = mc % 2
                M = 128 if mc < 96 else 64
                pb = mc % 2
                kind = "q" if mc < 16 else ("k" if mc < 32 else ("v" if mc < 64 else ("z" if mc < 96 else "ba")))
                if kind in ("q", "k", "v"):
                    sc.op("dve", lambda e, pb=pb, mc=mc: e.tensor_copy(out=pre[pb][:, 2051:2054], in_=chs[:, mc, :]),
                          reads=[("cols", id(chs))], writes=[("prehist", pb)])
                for ti, (c0, n) in enumerate(TILES):
                    if kind == "ba":
                        halves = [(0, 32, bT), (32, 64, gT)]
                    else:
                        halves = [(0, M, None)]
                    for (m0, m1, dstT) in halves:
                        bk = itc % 6
                        itc += 1
                        for kc in range(KC):
                            sc.op("pe", lambda e, wv=wv, kc=kc, wi=wi, m0=m0, m1=m1, c0=c0, n=n, bk=bk: e.matmul(
                                bank(bk)[0:m1 - m0, 0:n], wv[:, kc, wi * 128 + m0:wi * 128 + m1], hT[:, kc, c0:c0 + n],
                                start=(kc == 0), stop=(kc == KC - 1)),
                                reads=[("w", slot), ("hT", kc, c0)], writes=[("ps", bk)])
                        if kind in ("q", "k", "v"):
                            sc.op("act", lambda e, pb=pb, ti=ti, n=n, bk=bk: e.copy(out=pre[pb][:, PCOL[ti]:PCOL[ti] + n], in_=bank(bk)[:, 0:n]),
                                  reads=[("ps", bk)], writes=[("pre", pb, ti)])
                        elif kind == "z":
                            zb = mc % 2
                            sc.op("act", lambda e, zb=zb, c0=c0, n=n, bk=bk: e.activation(out=vout[zb][:, c0:c0 + n], in_=bank(bk)[:, 0:n], func=AF.Silu),
                                  reads=[("ps", bk)], writes=[("vout", zb, ti)])
                        else:
                            sc.op("act", lambda e, dstT=dstT, c0=c0, n=n, bk=bk: e.copy(out=dstT[:, c0:c0 + n], in_=bank(bk)[0:32, 0:n]),
                                  reads=[("ps", bk)], writes=[("baT", id(dstT), ti)])
                if kind == "z":
                    zb = mc % 2
                    h = mc - 64
                    sc.op("sp", lambda e, zb=zb, h=h: e.dma_start(out=zT[h * 128:(h + 1) * 128, :], in_=vout[zb]),
                          reads=[("vout", zb, ti) for ti in range(5)] + [("voutpad", zb)], writes=[("zT", h)], dma=True)
                if kind in ("q", "k", "v"):
                    prk = [("pre", pb, ti) for ti in range(5)] + [("prehist", pb), ("prepad", pb)]
                    P = pre[pb]
                    sc.op("dve", lambda e, P=P, mc=mc: e.tensor_copy(out=cst[:, mc, 0:3], in_=P[:, 2048:2051]),
                          reads=prk, writes=[("cst", mc)])
                    sc.op("dve", lambda e, P=P, mc=mc: e.tensor_copy(out=cst[:, mc, 4:7], in_=P[:, 2067:2070]),
                          reads=prk, writes=[("cst", mc)])
                    L = PW - 3
                    sc.op("dve", lambda e, P=P, mc=mc: e.tensor_scalar(out=acc[:, 0:L], in0=P[:, 3:PW], scalar1=cw[:, mc, 3:4], scalar2=None, op0=ALU.mult),
                          reads=prk + [("cols", id(cw))], writes=["acc"])
                    for tap in (2, 1, 0):
                        sc.op("dve", lambda e, P=P, mc=mc, tap=tap: e.scalar_tensor_tensor(
                            out=acc[:, 0:L], in0=P[:, tap:tap + L], scalar=cw[:, mc, tap:tap + 1], in1=acc[:, 0:L],
                            op0=ALU.mult, op1=ALU.add),
                            reads=prk + ["acc"], writes=["acc"])
                    vb = mc % 2
                    if kind == "v":
                        sc.op("act", lambda e, vb=vb: e.activation(out=vout[vb][:, 0:TP], in_=acc[:, 0:TP], func=AF.Silu),
                              reads=["acc"], writes=[("vout", vb, 0)])
                        sc.op("act", lambda e, vb=vb: e.activation(out=vout[vb][:, TP:TR], in_=acc[:, 2051:2067], func=AF.Silu),
                              reads=["acc"], writes=[("vout", vb, 1)])
                    else:
                        sc.op("act", lambda e: e.activation(out=sil[:, 0:L], in_=acc[:, 0:L], func=AF.Silu),
                              reads=["acc"], writes=["sil"])
                        sc.op("act", lambda e: e.activation(out=sqb[:, 0:L], in_=sil[:, 0:L], func=AF.Square),
                              reads=["sil"], writes=["sqb"])
                        c = 0
                        while c < L:
                            n = min(512, L - c)
                            bk = itc % 6
                            itc += 1
                            sc.op("pe", lambda e, c=c, n=n, bk=bk: e.matmul(bank(bk)[:, 0:n], ones_b, sqb[:, c:c + n], start=True, stop=True),
                                  reads=["sqb", "ones_b"], writes=[("ps", bk)])
                            sc.op("act", lambda e, c=c, n=n, bk=bk: e.activation(out=rs[:, c:c + n], in_=bank(bk)[:, 0:n], func=AF.Ln, bias=EPS, scale=1.0),
                                  reads=[("ps", bk), "sil"], writes=["acc"])
                            c += n
                        rsk = ["acc"]
                        sc.op("act", lambda e: e.activation(out=rs[:, 0:L], in_=rs[:, 0:L], func=AF.Exp, scale=-0.5),
                              reads=rsk, writes=rsk)
                        qs = (128.0 ** -0.5) if kind == "q" else 1.0
                        sc.op("dve", lambda e, vb=vb, qs=qs: e.scalar_tensor_tensor(
                            out=vout[vb][:, 0:TP], in0=sil[:, 0:TP], scalar=qs, in1=rs[:, 0:TP], op0=ALU.mult, op1=ALU.mult),
                            reads=["sil"] + rsk, writes=[("vout", vb, 0)])
                        sc.op("dve", lambda e, vb=vb, qs=qs: e.scalar_tensor_tensor(
                            out=vout[vb][:, TP:TR], in0=sil[:, 2051:2067], scalar=qs, in1=rs[:, 2051:2067], op0=ALU.mult, op1=ALU.mult),
                            reads=["sil"] + rsk, writes=[("vout", vb, 1)])
                    sc.op("sp", lambda e, vb=vb, mc=mc: e.dma_start(out=qkvT[mc * 128:(mc + 1) * 128, :], in_=vout[vb]),
                          reads=[("vout", vb, 0), ("vout", vb, 1), ("voutpad", vb)], writes=[("qkvT", mc)], dma=True)
            if os.environ.get("GDN_STOP") == "2a":
                sc.barrier()
                return
            crow_p = acc[0:3, 0:2048]
            crow_s = sil[0:3, 0:2048]
            for r4 in range(4):
                for q in range(4):
                    for i in range(4):
                        mc = r4 * 16 + q * 4 + i
                        sc.op("pe", lambda e, mc=mc, q=q, i=i: e.transpose(out=bank(q)[0:4, i * 128:(i + 1) * 128], in_=cst[:, mc, 0:4], identity=ident_f),
                              reads=[("cst", mc), "ident_f"], writes=[("ps", q)])
                        sc.op("pe", lambda e, mc=mc, q=q, i=i: e.transpose(out=bank(q + 4)[0:4, i * 128:(i + 1) * 128], in_=cst[:, mc, 4:8], identity=ident_f),
                              reads=[("cst", mc), "ident_f"], writes=[("ps", q + 4)])
                    sc.op("act", lambda e, q=q: e.copy(out=crow_p[:, q * 512:(q + 1) * 512], in_=bank(q)[0:3, :]),
                          reads=[("ps", q), "acc"], writes=[("crowp", q), "acc"])
                    sc.op("dve", lambda e, q=q: e.tensor_copy(out=crow_s[:, q * 512:(q + 1) * 512], in_=bank(q + 4)[0:3, :]),
                          reads=[("ps", q + 4), "sil"], writes=[("crows", q), "sil"])
                sc.op("sp", lambda e, r4=r4: e.dma_start(out=o_conv_p[j][:, r4 * 2048:(r4 + 1) * 2048], in_=crow_p),
                      reads=[("crowp", q) for q in range(4)] + ["acc"], writes=[("ocp", r4), "acc"], dma=True)
                sc.op("sp", lambda e, r4=r4: e.dma_start(out=o_conv_s[j][:, r4 * 2048:(r4 + 1) * 2048], in_=crow_s),
                      reads=[("crows", q) for q in range(4)] + ["sil"], writes=[("ocs", r4), "sil"], dma=True)
            sc.barrier()

            if os.environ.get("GDN_STOP") == "2":
                sc.barrier()
                return
            ar.off = persist_off
            Gd = ar.alloc(F32, [32, NG, 32])
            tmpE = ar.alloc(F32, [32, TA])
            bak = [("baT", id(bT), ti) for ti in range(5)]
            gak = [("baT", id(gT), ti) for ti in range(5)]
            sc.op("act", lambda e: e.activation(out=bT[:, 0:TR], in_=bT[:, 0:TR], func=AF.Sigmoid), reads=bak, writes=["bT"])
            sc.op("pool", lambda e: e.memset(bT[:, TR:TA], 0.0), writes=["bTpad"])
            sc.op("act", lambda e: e.activation(out=nega, in_=alog, func=AF.Exp), reads=["alog"], writes=["nega"])
            sc.op("act", lambda e: e.activation(out=tmpE[:, 0:TR], in_=gT[:, 0:TR], func=AF.Exp, bias=dtb[:, 0:1], scale=1.0),
                  reads=gak + ["dtb"], writes=["tmpE"])
            sc.op("act", lambda e: e.activation(out=tmpE[:, 0:TR], in_=tmpE[:, 0:TR], func=AF.Ln, bias=1.0, scale=1.0),
                  reads=["tmpE"], writes=["tmpE"])
            sc.op("dve", lambda e: e.tensor_scalar(out=gT[:, 0:TR], in0=tmpE[:, 0:TR], scalar1=nega[:, 0:1], scalar2=-1.0, op0=ALU.mult, op1=ALU.mult),
                  reads=["tmpE", "nega"] + gak, writes=["gTg"])
            sc.op("pool", lambda e: e.memset(gT[:, TR:TA], 0.0), writes=["gTpad"])
            if os.environ.get("GDN_STOP") == "3a":
                sc.barrier()
                return
            for n in range(NG):
                sc.op("dve", lambda e, n=n: e.tensor_tensor_scan(out=tmpE[:, n * 128:(n + 1) * 128], data0=ones_f[0:32, :], data1=gT[:, n * 128:(n + 1) * 128],
                                                               initial=0.0, op0=ALU.mult, op1=ALU.add),
                      reads=["gTg", "gTpad", "ones_f", "tmpE"], writes=[("gc", n)])
            gck = [("gc", n) for n in range(NG)]
            sc.op("dve", lambda e: e.tensor_copy(out=gT, in_=tmpE), reads=gck + ["gTg", "gTpad"], writes=["gcT"])
            gcT = gT
            if os.environ.get("GDN_STOP") == "3b":
                sc.barrier()
                return
            for n in range(NG):
                sc.op("pe", lambda e, n=n: e.transpose(out=bank(6)[:, 0:32], in_=gcT[:, n * 128:(n + 1) * 128], identity=ident_f[0:32, 0:32]),
                      reads=["gcT", "ident_f"], writes=[("ps", 6)])
                sc.op("dve", lambda e, n=n: e.tensor_copy(out=gcTok[:, n, :], in_=bank(6)[:, 0:32]), reads=[("ps", 6)], writes=[("gcTok", n)])
                sc.op("pe", lambda e, n=n: e.transpose(out=bank(7)[:, 0:32], in_=bT[:, n * 128:(n + 1) * 128], identity=ident_f[0:32, 0:32]),
                      reads=["bT", "bTpad", "ident_f"], writes=[("ps", 7)])
                sc.op("act", lambda e, n=n: e.copy(out=bTok[:, n, :], in_=bank(7)[:, 0:32]), reads=[("ps", 7)], writes=[("bTok", n)])
            gtk = [("gcTok", n) for n in range(NG)]
            btk = [("bTok", n) for n in range(NG)]
            sc.op("dve", lambda e: e.tensor_scalar(out=nbTok, in0=bTok, scalar1=-1.0, scalar2=None, op0=ALU.mult), reads=btk, writes=["nbTok"])
            sc.op("act", lambda e: e.activation(out=egc, in_=gcTok, func=AF.Exp), reads=gtk, writes=["egc"])
            if os.environ.get("GDN_STOP") == "3c":
                sc.barrier()
                return
            gclast = gcT.rearrange("p (n c) -> p n c", c=128)[:, :, 127]
            for h in range(32):
                sc.op("dve", lambda e, h=h: e.tensor_scalar(out=Gd[:, :, h], in0=gclast, scalar1=ident_f[0:32, h:h + 1], scalar2=None, op0=ALU.mult),
                      reads=["gcT", "ident_f"], writes=[("Gd", h)])
            gdk = [("Gd", h) for h in range(32)]
            Gdf = Gd.rearrange("p a b -> p (a b)")
            glbf = glb.rearrange("p a b -> p (a b)")
            ekdf = ekd.rearrange("p a b -> p (a b)")
            gctf = gcTok.rearrange("p a b -> p (a b)")
            for half, (c0, c1) in enumerate(((0, 288), (288, 544))):
                sc.op("pe", lambda e, half=half, c0=c0, c1=c1: e.matmul(bank(6 + half)[:, 0:c1 - c0], ones_f[0:32, :], Gdf[:, c0:c1], start=True, stop=True),
                      reads=gdk + ["ones_f"], writes=[("ps", 6 + half)])
                sc.op("dve", lambda e, half=half, c0=c0, c1=c1: e.tensor_copy(out=glbf[:, c0:c1], in_=bank(6 + half)[:, 0:c1 - c0]),
                      reads=[("ps", 6 + half)], writes=[("glraw", half)])
            if os.environ.get("GDN_STOP") == "3d":
                sc.barrier()
                return
            sc.op("dve", lambda e: e.tensor_tensor(out=ekdf, in0=glbf, in1=gctf, op=ALU.subtract),
                  reads=[("glraw", 0), ("glraw", 1)] + gtk, writes=["ekd"])
            sc.op("act", lambda e: e.activation(out=ekdf, in_=ekdf, func=AF.Exp), reads=["ekd"], writes=["ekdx"])
            sc.op("act", lambda e: e.activation(out=glbf, in_=glbf, func=AF.Exp), reads=[("glraw", 0), ("glraw", 1), "ekd"], writes=["glb"])
            sc.barrier()

            if os.environ.get("GDN_STOP") == "3":
                sc.barrier()
                return
            ar.off = persist_off
            kTq = [[ar.alloc(BF16, [128, TA]) for _ in range(2)] for _ in range(4)]
            ogT = [ar.alloc(BF16, [128, TA]) for _ in range(2)]
            gcm = [ar.alloc(F32, [32, TA])] * 2
            oh = [ar.alloc(F32, [32, 128])] * 2
            S_p = [ar.alloc(F32, [128, 128]) for _ in range(2)]
            S_s = [ar.alloc(F32, [128, 128]) for _ in range(2)]
            Sb = [ar.alloc(BF16, [128, 128]) for _ in range(2)]
            class _L:
                def __init__(self, t):
                    self.t = t

                def __getitem__(self, n):
                    return self.t[:, n, :]
            ATt = [ar.alloc(BF16, [128, NG, 128]) for _ in range(2)]
            U7t = [ar.alloc(BF16, [128, NG, 128]) for _ in range(2)]
            nWTt = [ar.alloc(BF16, [128, NG, 128]) for _ in range(2)]
            kdtt = [ar.alloc(BF16, [128, NG, 128]) for _ in range(2)]
            vtkt = [ar.alloc(BF16, [128, NG, 128]) for _ in range(2)]
            AT, U7, nWT, kdt, vtk = [[_L(t) for t in tt] for tt in (ATt, U7t, nWTt, kdtt, vtkt)]
            GR = 6
            SG = 3
            Dm4s = [ar.alloc(F32, [128, SG, 128]) for _ in range(2)]
            DmS4s = [ar.alloc(F32, [128, SG, 128]) for _ in range(2)]
            Xh4s = [ar.alloc(BF16, [128, SG, 128]) for _ in range(2)]
            N4s = [[ar.alloc(F32, [128, SG, 128]) for _ in range(1)] for _ in range(2)]
            NT4s = [[ar.alloc(F32, [128, SG, 128]) for _ in range(1)] for _ in range(2)]
            Nbs = [[ar.alloc(BF16, [128, SG, 128]) for _ in range(2)] for _ in range(2)]
            NTbs = [[ar.alloc(BF16, [128, SG, 128]) for _ in range(2)] for _ in range(2)]
            Ubs = [[ar.alloc(BF16, [128, SG, 128]) for _ in range(2)] for _ in range(2)]
            X0fs = [ar.alloc(F32, [128, SG, 128]) for _ in range(2)]
            ImXs = [ar.alloc(F32, [128, SG, 128]) for _ in range(2)]
            E0fs = [ar.alloc(F32, [128, SG, 128]) for _ in range(2)]
            X0Ts = [ar.alloc(F32, [128, SG, 128]) for _ in range(2)]
            print("G4 arena bytes", ar.off)
            ident4 = ar.alloc(F32, [128, 4, 128])
            for i in range(4):
                sc.op("pool", lambda e, i=i: e.tensor_copy(out=ident4[:, i, :], in_=ident_f), reads=["ident_f"], writes=[("ident4", i)])
            id4k = [("ident4", i) for i in range(4)]
            vnew = [ar.alloc(BF16, [128, 128]) for _ in range(2)]
            otmp = [ar.alloc(F32, [128, 128]) for _ in range(2)]
            otok = [ar.alloc(F32, [128, 128]) for _ in range(2)]
            osq = [ar.alloc(F32, [128, 128]) for _ in range(2)]
            onb = [ar.alloc(BF16, [128, 128]) for _ in range(2)]
            ssq = [ar.alloc(F32, [128, 1]) for _ in range(2)]

            def pq(b, q):
                return PS[:, b, q * 128:(q + 1) * 128]

            def pqb(b, q):
                return PS[:, b, q * 128:q * 128 + 64].bitcast(BF16)

            bcnt = [0]
            stopat = os.environ.get('GDN_STOP')

            def do_head(h):
                hb = h % 2
                kh = h // 2
                srcs = [qkvT[2048 + kh * 128:2048 + (kh + 1) * 128, :], qkvT[kh * 128:(kh + 1) * 128, :],
                        qkvT[4096 + h * 128:4096 + (h + 1) * 128, :], zT[h * 128:(h + 1) * 128, :]]
                kT_h, qT_h, vT_h, zs_h = kTq[0][hb], kTq[1][hb], kTq[2][hb], kTq[3][hb]
                kk_, qk_, vk_, zk_ = ("hin", 0, hb), ("hin", 1, hb), ("hin", 2, hb), ("hin", 3, hb)

                def setup():
                    for a in range(4):
                        sc.op("sp", lambda e, a=a, src=srcs[a]: e.dma_start(out=kTq[a][hb], in_=src), writes=[("hin", a, hb)], dma=True)
                    sc.op("dve", lambda e: e.tensor_scalar(out=gcm[hb], in0=gcT, scalar1=ident_f[0:32, h:h + 1], scalar2=None, op0=ALU.mult),
                          reads=["gcT", "ident_f"], writes=[("gcm", 0)])
                    sc.op("dve", lambda e: e.tensor_scalar(out=oh[hb], in0=ones_f[0:32, :], scalar1=ident_f[0:32, h:h + 1], scalar2=None, op0=ALU.mult),
                          reads=["ones_f", "ident_f"], writes=[("oh", 0)])
                def groupfn(g0, si):
                    grp = list(range(g0, min(NG, g0 + SG)))
                    G = len(grp)
                    R0, R1, R2 = 3 * si, 3 * si + 1, 3 * si + 2
                    Dm4, DmS4, Xh4, N4, NT4 = Dm4s[si], DmS4s[si], Xh4s[si], N4s[si], NT4s[si]
                    Nb, NTb, Ub, X0f, ImX, E0f, X0T = Nbs[si], NTbs[si], Ubs[si], X0fs[si], ImXs[si], E0fs[si], X0Ts[si]
                    kDm, kDmS = ("Dm4", si), ("DmS4", si)
                    bqv = lambda b: PS[:, b, :].bitcast(BF16).rearrange("p (q x) -> p q x", x=256)[:, 0:G, 0:128]

                    def b4(b):
                        return PS[:, b, 0:G * 128].rearrange("p (q x) -> p q x", x=128)
                    for i, n in enumerate(grp):
                        cs = n * 128
                        sc.op("pe", lambda e, i=i, cs=cs: e.transpose(out=pqb(R1, i), in_=kT_h[:, cs:cs + 128], identity=ident_b),
                              reads=[kk_, "ident_b"], writes=[("ps", R1)])
                    for i, n in enumerate(grp):
                        sc.op("act", lambda e, i=i, n=n: e.activation(out=Xh4[:, i, :], in_=pqb(R1, i), func=AF.Copy, scale=egc[:, n, h:h + 1]),
                              reads=[("ps", R1), "egc"], writes=[("Xh", si, i)])
                    for i, n in enumerate(grp):
                        sc.op("dve", lambda e, i=i, n=n: e.tensor_scalar(out=kdt[hb][n], in0=pqb(R1, i), scalar1=ekd[:, n, h:h + 1], scalar2=None, op0=ALU.mult),
                              reads=[("ps", R1), "ekdx"], writes=[("kdt", hb, n)])
                    for i, n in enumerate(grp):
                        cs = n * 128
                        sc.op("pe", lambda e, i=i, cs=cs: e.matmul(pq(R0, i), oh[hb], gcT[:, cs:cs + 128], start=True, stop=False),
                              reads=[("oh", 0), "gcT"], writes=[("ps", R0)])
                        sc.op("pe", lambda e, i=i, cs=cs: e.matmul(pq(R0, i), gcm[hb][:, cs:cs + 128], negones, start=False, stop=False),
                              reads=[("gcm", 0), "negones"], writes=[("ps", R0)])
                        sc.op("pe", lambda e, i=i: e.matmul(pq(R0, i), ident_b, mneg_b, start=False, stop=True),
                              reads=["ident_b", "mneg_b"], writes=[("ps", R0)])
                    sc.op("act", lambda e: e.activation(out=Dm4[:, 0:G, :], in_=b4(R0), func=AF.Exp), reads=[("ps", R0)], writes=[kDm])
                    sc.op("pool", lambda e: e.tensor_tensor(out=DmS4[:, 0:G, :], in0=Dm4[:, 0:G, :], in1=ident4[:, 0:G, :], op=ALU.subtract),
                          reads=[kDm] + id4k, writes=[kDmS])
                    for i, n in enumerate(grp):
                        sc.op("pool", lambda e, i=i, n=n: e.tensor_scalar(out=DmS4[:, i, :], in0=DmS4[:, i, :], scalar1=nbTok[:, n, h:h + 1], scalar2=None, op0=ALU.mult),
                              reads=[kDmS, "nbTok"], writes=[kDmS])
                    yield
                    for i, n in enumerate(grp):
                        cs = n * 128
                        sc.op("pe", lambda e, i=i, cs=cs: e.matmul(pq(R1, i), kT_h[:, cs:cs + 128], qT_h[:, cs:cs + 128], start=True, stop=True),
                              reads=[kk_, qk_], writes=[("ps", R1)])
                    for i, n in enumerate(grp):
                        cs = n * 128
                        sc.op("pe", lambda e, i=i, cs=cs: e.matmul(pq(R2, i), kT_h[:, cs:cs + 128], kT_h[:, cs:cs + 128], start=True, stop=True),
                              reads=[kk_], writes=[("ps", R2)])
                    sc.op("dve", lambda e: e.tensor_tensor(out=ATt[hb][:, g0:g0 + G, :], in0=b4(R1), in1=Dm4[:, 0:G, :], op=ALU.mult),
                          reads=[("ps", R1), kDm], writes=[("AT", hb, n) for n in grp])
                    sc.op("dve", lambda e: e.tensor_tensor(out=N4[0][:, 0:G, :], in0=b4(R2), in1=DmS4[:, 0:G, :], op=ALU.mult),
                          reads=[("ps", R2), kDmS], writes=[("N", si, 0)])
                    for i, n in enumerate(grp):
                        cs = n * 128
                        sc.op("pe", lambda e, i=i, cs=cs: e.transpose(out=pqb(R0, i), in_=vT_h[:, cs:cs + 128], identity=ident_b),
                              reads=[vk_, "ident_b"], writes=[("ps", R0)])
                    sc.op("act", lambda e: e.copy(out=vtkt[hb][:, g0:g0 + G, :], in_=bqv(R0)), reads=[("ps", R0)], writes=[("vtk", hb, n) for n in grp])
                    yield
                    for i, n in enumerate(grp):
                        sc.op("pe", lambda e, i=i: e.transpose(out=pq(R1, i), in_=N4[0][:, i, :], identity=ident_f),
                              reads=[("N", si, 0), "ident_f"], writes=[("ps", R1)])
                    sc.op("act", lambda e: e.copy(out=NT4[0][:, 0:G, :], in_=b4(R1)), reads=[("ps", R1)], writes=[("NT", si, 0)])
                    sc.op("pool", lambda e: e.tensor_tensor(out=Ub[0][:, 0:G, :], in0=N4[0][:, 0:G, :], in1=ident4[:, 0:G, :], op=ALU.add),
                          reads=[("N", si, 0)] + id4k, writes=[("Ub", si, 0)])
                    sc.op("pool", lambda e: e.tensor_copy(out=Nb[0][:, 0:G, :], in_=N4[0][:, 0:G, :]), reads=[("N", si, 0)], writes=[("Nb", si, 0)])
                    sc.op("pool", lambda e: e.tensor_copy(out=NTb[0][:, 0:G, :], in_=NT4[0][:, 0:G, :]), reads=[("NT", si, 0)], writes=[("NTb", si, 0)])
                    yield
                    NLV = 4
                    for lv in range(1, NLV + 1):
                        a, bb = (lv - 1) % 2, lv % 2
                        if lv < NLV:
                            for i, n in enumerate(grp):
                                sc.op("pe", lambda e, i=i, a=a: e.matmul(pq(R0, i), NTb[a][:, i, :], Nb[a][:, i, :], start=True, stop=True),
                                      reads=[("NTb", si, a), ("Nb", si, a)], writes=[("ps", R0)])
                        for i, n in enumerate(grp):
                            sc.op("pe", lambda e, i=i, a=a: e.matmul(pq(R1, i), Nb[a][:, i, :], NTb[a][:, i, :], start=True, stop=True),
                                  reads=[("NTb", si, a), ("Nb", si, a)], writes=[("ps", R1)])
                        yield
                        if lv < NLV:
                            sc.op("dve", lambda e, bb=bb: e.tensor_copy(out=Nb[bb][:, 0:G, :], in_=b4(R0)), reads=[("ps", R0)], writes=[("Nb", si, bb)])
                        sc.op("act", lambda e, bb=bb: e.copy(out=NTb[bb][:, 0:G, :], in_=b4(R1)), reads=[("ps", R1)], writes=[("NTb", si, bb)])
                        for i, n in enumerate(grp):
                            sc.op("pe", lambda e, i=i, a=a, bb=bb: e.matmul(pq(R2, i), NTb[bb][:, i, :], Ub[a][:, i, :], start=True, stop=True),
                                  reads=[("NTb", si, bb), ("Ub", si, a)], writes=[("ps", R2)])
                        yield
                        if lv < NLV:
                            sc.op("dve", lambda e, a=a, bb=bb: e.tensor_tensor(out=Ub[bb][:, 0:G, :], in0=b4(R2), in1=Ub[a][:, 0:G, :], op=ALU.add),
                                  reads=[("ps", R2), ("Ub", si, a)], writes=[("Ub", si, bb)])
                        else:
                            sc.op("dve", lambda e, a=a: e.tensor_tensor(out=X0f[:, 0:G, :], in0=b4(R2), in1=Ub[a][:, 0:G, :], op=ALU.add),
                                  reads=[("ps", R2), ("Ub", si, a)], writes=[("X0f", si)])
                    for i, n in enumerate(grp):
                        sc.op("pe", lambda e, i=i: e.matmul(pq(R0, i), NT4[0][:, i, :], X0f[:, i, :], start=True, stop=True),
                              reads=[("NT", si, 0), ("X0f", si)], writes=[("ps", R0)])
                    for i, n in enumerate(grp):
                        sc.op("pe", lambda e, i=i: e.transpose(out=pq(R1, i), in_=X0f[:, i, :], identity=ident_f),
                              reads=[("X0f", si), "ident_f"], writes=[("ps", R1)])
                    sc.op("pool", lambda e: e.tensor_tensor(out=ImX[:, 0:G, :], in0=ident4[:, 0:G, :], in1=X0f[:, 0:G, :], op=ALU.subtract),
                          reads=[("X0f", si)] + id4k, writes=[("ImX", si)])
                    yield
                    sc.op("dve", lambda e: e.tensor_tensor(out=E0f[:, 0:G, :], in0=b4(R0), in1=ImX[:, 0:G, :], op=ALU.add),
                          reads=[("ps", R0), ("ImX", si)], writes=[("E0f", si)])
                    sc.op("act", lambda e: e.copy(out=X0T[:, 0:G, :], in_=b4(R1)), reads=[("ps", R1)], writes=[("X0T", si)])
                    for i, n in enumerate(grp):
                        sc.op("pe", lambda e, i=i: e.matmul(pq(R2, i), X0T[:, i, :], E0f[:, i, :], start=True, stop=True),
                              reads=[("X0T", si), ("E0f", si)], writes=[("ps", R2)])
                    yield
                    sc.op("dve", lambda e: e.tensor_tensor(out=U7t[hb][:, g0:g0 + G, :], in0=b4(R2), in1=X0f[:, 0:G, :], op=ALU.add),
                          reads=[("ps", R2), ("X0f", si)], writes=[("U7", hb, n) for n in grp])
                    yield
                    for i, n in enumerate(grp):
                        sc.op("pe", lambda e, i=i, n=n: e.matmul(pq(R0, i), Xh4[:, i, :], U7[hb][n], start=True, stop=True),
                              reads=[("Xh", si, i), ("U7", hb, n)], writes=[("ps", R0)])
                    sc.op("act", lambda e: e.activation(out=nWTt[hb][:, g0:g0 + G, :], in_=b4(R0), func=AF.Copy, scale=-1.0),
                          reads=[("ps", R0)], writes=[("nWT", hb, n) for n in grp])

                def chunkfn(n):
                    cs = n * 128
                    Sf = S_p[hb] if n < 16 else S_s[hb]
                    skey = ("S", hb, 0 if n < 16 else 1)
                    if n == 0:
                        sc.op("pool", lambda e, Sf=Sf: e.memset(Sf, 0.0), writes=[skey])
                        sc.op("pool", lambda e, hb=hb: e.memset(Sb[hb], 0.0), writes=[("Sb", hb)])
                    if n == 16:
                        sc.op("sp", lambda e, Sf=Sf, h=h: e.dma_start(out=Sf, in_=st_S[j, h]), writes=[skey], dma=True)
                        sc.op("act", lambda e, Sf=Sf, hb=hb: e.copy(out=Sb[hb], in_=Sf), reads=[skey], writes=[("Sb", hb)])
                    bq = bcnt[0] % 2
                    bcnt[0] += 1
                    B0 = 6
                    B1 = B0 + 1
                    sc.op("pe", lambda e, n=n, hb=hb, B0=B0: e.matmul(pq(B0, 0), U7[hb][n], vtk[hb][n], start=True, stop=False),
                          reads=[("U7", hb, n), ("vtk", hb, n)], writes=[("pq", B0, 0)])
                    sc.op("pe", lambda e, n=n, hb=hb, B0=B0: e.matmul(pq(B0, 0), nWT[hb][n], Sb[hb], start=False, stop=True),
                          reads=[("nWT", hb, n), ("Sb", hb)], writes=[("pq", B0, 0)])
                    yield
                    sc.op("act", lambda e, n=n, h=h, bq=bq, B0=B0: e.activation(out=vnew[bq], in_=pq(B0, 0), func=AF.Copy, scale=bTok[:, n, h:h + 1]),
                          reads=[("pq", B0, 0)] + btk, writes=[("vnew", bq)])
                    sc.op("pe", lambda e, cs=cs, hb=hb, B0=B0, qT_h=qT_h: e.matmul(pq(B0, 1), qT_h[:, cs:cs + 128], Sb[hb], start=True, stop=True),
                          reads=[qk_, ("Sb", hb)], writes=[("pq", B0, 1)])
                    sc.op("pe", lambda e, n=n, hb=hb, bq=bq, B0=B0: e.matmul(pq(B0, 2), AT[hb][n], vnew[bq], start=True, stop=True),
                          reads=[("AT", hb, n), ("vnew", bq)], writes=[("pq", B0, 2)])
                    sc.op("pe", lambda e, n=n, hb=hb, bq=bq, B1=B1: e.matmul(pq(B1, 0), kdt[hb][n], vnew[bq], start=True, stop=True),
                          reads=[("kdt", hb, n), ("vnew", bq)], writes=[("pq", B1, 0)])
                    yield
                    sc.op("act", lambda e, n=n, h=h, bq=bq, B0=B0: e.activation(out=otmp[bq], in_=pq(B0, 1), func=AF.Copy, scale=egc[:, n, h:h + 1]),
                          reads=[("pq", B0, 1), "egc"], writes=[("otmp", bq)])
                    sc.op("dve", lambda e, bq=bq, B0=B0: e.tensor_tensor(out=otok[bq], in0=otmp[bq], in1=pq(B0, 2), op=ALU.add),
                          reads=[("otmp", bq), ("pq", B0, 2)], writes=[("otok", bq)])
                    sc.op("dve", lambda e, n=n, h=h, Sf=Sf, B1=B1: e.scalar_tensor_tensor(out=Sf, in0=Sf, scalar=glb[:, n, h:h + 1], in1=pq(B1, 0),
                                                                                   op0=ALU.mult, op1=ALU.add),
                          reads=[skey, ("pq", B1, 0), "glb"], writes=[skey])
                    sc.op("act", lambda e, Sf=Sf, hb=hb: e.copy(out=Sb[hb], in_=Sf), reads=[skey], writes=[("Sb", hb)])
                    if n == 15:
                        sc.op("sp", lambda e, Sf=Sf, h=h: e.dma_start(out=o_S_p[j, h], in_=Sf), reads=[skey], writes=[("oSp", h)], dma=True)
                    if n == 16:
                        sc.op("sp", lambda e, Sf=Sf, h=h: e.dma_start(out=o_S_s[j, h], in_=Sf), reads=[skey], writes=[("oSs", h)], dma=True)
                    yield
                    sc.op("act", lambda e, bq=bq: e.activation(out=osq[bq], in_=otok[bq], func=AF.Square, accum_out=ssq[bq]),
                          reads=[("otok", bq)], writes=[("ssq", bq), ("osq", bq)])
                    sc.op("act", lambda e, bq=bq: e.activation(out=ssq[bq], in_=ssq[bq], func=AF.Ln, scale=1.0 / 128, bias=EPS),
                          reads=[("ssq", bq)], writes=[("ssq", bq)])
                    sc.op("act", lambda e, bq=bq: e.activation(out=ssq[bq], in_=ssq[bq], func=AF.Exp, scale=-0.5),
                          reads=[("ssq", bq)], writes=[("ssq", bq)])
                    sc.op("dve", lambda e, bq=bq: e.tensor_scalar(out=onb[bq], in0=otok[bq], scalar1=ssq[bq][:, 0:1], scalar2=None, op0=ALU.mult),
                          reads=[("otok", bq), ("ssq", bq)], writes=[("onb", bq)])
                    sc.op("pe", lambda e, bq=bq, B1=B1: e.transpose(out=pqb(B1, 1), in_=onb[bq], identity=ident_b),
                          reads=[("onb", bq), "ident_b"], writes=[("pq", B1, 1)])
                    yield
                    sc.op("dve", lambda e, cs=cs, hb=hb, B1=B1, zs_h=zs_h: e.scalar_tensor_tensor(
                        out=ogT[hb][:, cs:cs + 128], in0=pqb(B1, 1), scalar=gnw[:, 0:1], in1=zs_h[:, cs:cs + 128], op0=ALU.mult, op1=ALU.mult),
                        reads=[("pq", B1, 1), "gnw", zk_], writes=[("ogT", hb, n)])
                    if n == NG - 1:
                        sc.op("sp", lambda e, hb=hb, h=h: e.dma_start(out=oT[h * 128:(h + 1) * 128, :], in_=ogT[hb]),
                              reads=[("ogT", hb, n) for n in range(NG)], writes=[("oT", h)], dma=True)
                return setup, groupfn, chunkfn

            heads = [do_head(h) for h in range(NV)]
            g0s = list(range(0, NG, GR))

            def bgen(hd, chunks):
                for n in chunks:
                    yield from hd[2](n)

            def run_pair(hd, g0, bg):
                gens = []
                if hd is not None:
                    gens.append(hd[1](g0, 0))
                    if g0 + SG < NG:
                        gens.append(hd[1](g0 + SG, 1))
                live = list(gens)
                blive = bg is not None
                while live or blive:
                    for g in list(live):
                        try:
                            next(g)
                        except StopIteration:
                            live.remove(g)
                    for _ in range(2):
                        if blive:
                            try:
                                next(bg)
                            except StopIteration:
                                blive = False
            heads[0][0]()
            for g0 in g0s:
                run_pair(heads[0], g0, None)
            for h in range(NV):
                nxt = heads[h + 1] if h + 1 < NV else None
                if nxt is not None:
                    nxt[0]()
                for g0 in g0s:
                    run_pair(nxt, g0, bgen(heads[h], range(g0, min(NG, g0 + GR))))
            sc.barrier()

            if os.environ.get("GDN_STOP") == "4":
                sc.barrier()
                return
            ar.off = persist_off
            Wo = gdn_w_out[j]
            osb = ar.alloc(BF16, [128, 32, 1040])
            wslot = [ar.alloc(BF16, [128, 8192]) for _ in range(2)]
            xres = [ar.alloc(F32, [128, 512]) for _ in range(2)]
            oTv = oT.rearrange("(kc p) t -> p kc t", p=128)
            wcnt = 0
            for blk in (TILES[0:2], TILES[2:5]):
                hc0 = blk[0][0]
                ntok = sum(n for _, n in blk)
                for kc in range(32):
                    sc.op("sp", lambda e, kc=kc, hc0=hc0, ntok=ntok: e.dma_start(out=osb[:, kc, 0:ntok], in_=oTv[:, kc, hc0:hc0 + ntok]),
                          writes=[("osb", kc)], dma=True)
                it = 0
                for mp in range(8):
                    slot = wcnt % 2
                    wcnt += 1
                    vw = wslot[slot].rearrange("p (a b) -> p a b", b=256)
                    srcw = Wo[:, mp * 256:(mp + 1) * 256].rearrange("(kc p) m -> p kc m", p=128)
                    sc.op("pool", lambda e, vw=vw, srcw=srcw: e.dma_start(out=vw, in_=srcw), writes=[("w", slot)], dma=True)
                    for mi in range(2):
                        mc = mp * 2 + mi
                        for (c0, n) in blk:
                            bk = it % 6
                            xb = it % 2
                            it += 1
                            sc.op("sp", lambda e, xb=xb, mc=mc, c0=c0, n=n: e.dma_start(out=xres[xb][:, 0:n], in_=xTv[:, mc, c0:c0 + n]),
                                  writes=[("xres", xb)], dma=True)
                            for kc in range(32):
                                sc.op("pe", lambda e, vw=vw, kc=kc, mi=mi, c0=c0, n=n, bk=bk, hc0=hc0: e.matmul(
                                    bank(bk)[:, 0:n], vw[:, kc, mi * 128:(mi + 1) * 128], osb[:, kc, c0 - hc0:c0 - hc0 + n],
                                    start=(kc == 0), stop=(kc == 31)),
                                    reads=[("w", slot), ("osb", kc)], writes=[("ps", bk)])
                            sc.op("dve", lambda e, xb=xb, bk=bk, n=n: e.tensor_tensor(
                                out=xres[xb][:, 0:n], in0=xres[xb][:, 0:n], in1=bank(bk)[:, 0:n], op=ALU.add),
                                reads=[("xres", xb), ("ps", bk)], writes=[("xres", xb)])
                            sc.op("sp", lambda e, xb=xb, mc=mc, c0=c0, n=n: e.dma_start(out=xTv[:, mc, c0:c0 + n], in_=xres[xb][:, 0:n]),
                                  reads=[("xres", xb)], writes=[("xTo", mc, c0)], dma=True)
                sc.barrier()

        def phase_final():
            ar.reset()
            tmp_x = [ar.alloc(F32, [128, KC, 512]) for _ in range(2)]
            tmp_sq = [ar.alloc(F32, [128, KC, 512])] * 2
            tmp_r = [ar.alloc(F32, [128, 512]) for _ in range(2)]
            yT = ar.alloc(F32, [128, KC, 512])
            yo = [ar.alloc(F32, [128, D]) for _ in range(2)]
            xTv = xT.rearrange("(kc p) t -> p kc t", p=128)
            cnt = 0
            for ti, (c0, n) in enumerate(TILES):
                b = ti % 2
                xt, sq, rr = tmp_x[b], tmp_sq[b], tmp_r[b]
                sc.op("sp", lambda e, xt=xt, c0=c0, n=n: e.dma_start(out=xt[:, :, 0:n], in_=xTv[:, :, c0:c0 + n]),
                      writes=[("nx", b)], dma=True)
                sc.op("act", lambda e, xt=xt, sq=sq, n=n: e.activation(out=sq[:, :, 0:n], in_=xt[:, :, 0:n], func=AF.Square),
                      reads=[("nx", b)], writes=[("nsq", 0)])
                bk = 6 + b
                for kc in range(KC):
                    sc.op("pe", lambda e, sq=sq, kc=kc, n=n, bk=bk: e.matmul(
                        bank(bk)[:, 0:n], ones_f, sq[:, kc, 0:n], start=(kc == 0), stop=(kc == KC - 1)),
                        reads=[("nsq", 0), "ones_f"], writes=[("ps", bk)])
                sc.op("act", lambda e, rr=rr, n=n, bk=bk: e.activation(out=rr[:, 0:n], in_=bank(bk)[:, 0:n], func=AF.Ln,
                                                                      scale=1.0 / D, bias=EPS),
                      reads=[("ps", bk)], writes=[("nr", b)])
                sc.op("act", lambda e, rr=rr, n=n: e.activation(out=rr[:, 0:n], in_=rr[:, 0:n], func=AF.Exp, scale=-0.5),
                      reads=[("nr", b)], writes=[("nr", b)])
                for kc in range(KC):
                    sc.op("dve", lambda e, xt=xt, rr=rr, kc=kc, n=n: e.scalar_tensor_tensor(
                        out=yT[:, kc, 0:n], in0=xt[:, kc, 0:n], scalar=nfin[:, kc, 0:1],
                        in1=rr[:, 0:n], op0=ALU.mult, op1=ALU.mult),
                        reads=[("nx", b), ("nr", b)], writes=[("yT", kc)])
                ngr = (n + 127) // 128
                for gi in range(ngr):
                    rows = min(128, n - gi * 128)
                    ob = cnt % 2
                    cnt += 1
                    for q in range(4):
                        bk2 = q
                        for i in range(4):
                            kc = q * 4 + i
                            sc.op("pe", lambda e, kc=kc, gi=gi, rows=rows, bk2=bk2, i=i: e.transpose(
                                out=bank(bk2)[0:rows, i * 128:(i + 1) * 128], in_=yT[:, kc, gi * 128:gi * 128 + rows],
                                identity=ident_f),
                                reads=[("yT", kc), "ident_f"], writes=[("ps", bk2)])
                        if q % 2 == 0:
                            sc.op("act", lambda e, ob=ob, q=q, rows=rows, bk2=bk2: e.copy(
                                out=yo[ob][0:rows, q * 512:(q + 1) * 512], in_=bank(bk2)[0:rows, :]),
                                reads=[("ps", bk2)], writes=[("yo", ob, q)])
                        else:
                            sc.op("dve", lambda e, ob=ob, q=q, rows=rows, bk2=bk2: e.tensor_copy(
                                out=yo[ob][0:rows, q * 512:(q + 1) * 512], in_=bank(bk2)[0:rows, :]),
                                reads=[("ps", bk2)], writes=[("yo", ob, q)])
                    t0 = c0 + gi * 128
                    dst = y_p[t0:t0 + rows, :] if t0 < TP else y_s
                    sc.op("sp", lambda e, ob=ob, rows=rows, dst=dst: e.dma_start(out=dst, in_=yo[ob][0:rows, :]),
                          reads=[("yo", ob, q) for q in range(4)], writes=[("y", t0)], dma=True)
            sc.barrier()

        phase_input()
        for s in stages:
            if s.startswith("ffn"):
                phase_ffn(int(s[3:]))
            elif s.startswith("pool"):
                phase_pool(int(s[4:]))
            elif s.startswith("gdn"):
                phase_gdn(int(s[3:]))
        phase_final()
        sc.finalize(st)
    return nc, sc


_CACHE = {}


def kernel(**inputs):
    f32 = lambda a: np.ascontiguousarray(np.asarray(a, dtype=np.float32))
    if "nc" not in _CACHE:
        _CACHE["nc"] = build()[0]
    nc = _CACHE["nc"]
    shared = {k: f32(inputs[k]) for k in ("norm_mix_w", "norm_ffn_w", "gdn_w_in", "gdn_conv_w", "gdn_A_log",
                                          "gdn_dt_bias", "gdn_norm_w", "gdn_w_out", "pool_w", "pool_scale",
                                          "ffn_w_gu", "ffn_w_down")}
    shared["final_norm_w"] = f32(inputs["final_norm_w"]).reshape(1, D)
    xp, xs = f32(inputs["x_prompt"]), f32(inputs["x_sample"])
    sconv, sS, spool = f32(inputs["state_gdn_conv"]), f32(inputs["state_gdn_S"]), f32(inputs["state_pool"])
    in_maps = []
    for b in range(8):
        m = dict(shared)
        m["x_p"] = xp[b]
        m["x_s"] = xs[b]
        m["st_conv"] = np.ascontiguousarray(sconv[:, b])
        m["st_S"] = np.ascontiguousarray(sS[:, b])
        m["st_pool"] = np.ascontiguousarray(spool[:, b])
        in_maps.append(m)
    res = run_bass_kernel_spmd(nc, in_maps, core_ids=list(range(8)))
    r = res.results
    stack = lambda k, ax: np.stack([np.asarray(r[b][k], dtype=np.float32) for b in range(8)], axis=ax)
    return (stack("y_p", 0), stack("y_s", 0), stack("o_conv_p", 1), stack("o_S_p", 1), stack("o_pool_p", 1),
            stack("o_conv_s", 1), stack("o_S_s", 1), stack("o_pool_s", 1))
```

```python
import math
import os
from contextlib import ExitStack
import numpy as np
import concourse.bass as bass
import concourse.mybir as mybir
from concourse.alu_op_type import AluOpType as ALU
from concourse.bass_utils import run_bass_kernel_spmd

F32 = mybir.dt.float32
BF16 = mybir.dt.bfloat16
AF = mybir.ActivationFunctionType

ENGS = ("pe", "dve", "act", "pool", "sp")

D = 2048
KC = 16
TP = 2048
TS = 16
TR = TP + TS
TA = TP + 128
NG = TA // 128
NV = 32
QKV = 8192
VAL = 4096
IN_DIM = 12352
FH = 5632
FC = FH // 128
EPS = 1e-6
TILES = [(0, 512), (512, 512), (1024, 512), (1536, 512), (2048, 16)]
NEG = -1.0e30


class Sched:
    def __init__(self, nc, n_lanes=8):
        self.nc = nc
        self.ops = []
        self.last_w = {}
        self.readers = {}
        self.n_lanes = n_lanes
        self.implicit = {"pe"}

    def op(self, eng, fn, reads=(), writes=(), dma=False):
        nk = lambda k: ("ps", k[1]) if (isinstance(k, tuple) and k and k[0] == "pq") else k
        reads = [nk(k) for k in reads] + ["ALL"]
        writes = [nk(k) for k in writes]
        deps = set()
        for k in reads:
            w = self.last_w.get(k)
            if w is not None:
                deps.add(w)
            if isinstance(k, tuple) and k and k[0] == "ps":
                for r in self.readers.get(k, ()):
                    if self.ops[r]["eng"] != eng:
                        deps.add(r)
        for k in writes:
            w = self.last_w.get(k)
            if w is not None:
                deps.add(w)
            for r in self.readers.get(k, ()):
                deps.add(r)
        idx = len(self.ops)
        self.ops.append(dict(eng=eng, fn=fn, deps=deps, dma=dma))
        for k in reads:
            self.readers.setdefault(k, []).append(idx)
        for k in writes:
            self.last_w[k] = idx
            self.readers[k] = []
        return idx

    def barrier(self):
        for e in ("sp", "pe", "dve", "act", "pool"):
            self.op(e, lambda h: h.nop(), writes=["ALL"])

    def finalize(self, stack):
        nc = self.nc
        ops = self.ops
        needed = set()
        for o in ops:
            if o["eng"] in self.implicit:
                o["deps"] = {d for d in o["deps"] if ops[d]["eng"] != o["eng"] or ops[d]["dma"]}
            needed |= o["deps"]
        csem = {e: stack.enter_context(nc.semaphore(f"c_{e}")) for e in ENGS}
        lanes = {e: [stack.enter_context(nc.semaphore(f"l_{e}_{i}")) for i in range(self.n_lanes)]
                 for e in ("sp", "pool")}
        ccount = {e: 0 for e in ENGS}
        lane_cnt = {e: [0] * self.n_lanes for e in lanes}
        lane_rr = {e: 0 for e in lanes}
        token = [None] * len(ops)
        known = {e: {} for e in ENGS}
        streams = {e: [] for e in ENGS}
        for i, o in enumerate(ops):
            e = o["eng"]
            waits = {}
            for d in o["deps"]:
                s, v = token[d]
                key = id(s)
                if known[e].get(key, 0) >= v:
                    continue
                if key not in waits or waits[key][1] < v:
                    waits[key] = (s, v)
            inc = None
            if o["dma"]:
                ln = lane_rr[e]
                lane_rr[e] = (ln + 1) % self.n_lanes
                s = lanes[e][ln]
                prev = lane_cnt[e][ln]
                if prev > 0 and known[e].get(id(s), 0) < prev:
                    key = id(s)
                    if key not in waits or waits[key][1] < prev:
                        waits[key] = (s, prev)
                lane_cnt[e][ln] = prev + 16
                token[i] = (s, prev + 16)
                inc = (s, 16)
            else:
                if i in needed:
                    ccount[e] += 1
                    token[i] = (csem[e], ccount[e])
                    inc = (csem[e], 1)
                else:
                    token[i] = (csem[e], ccount[e] + 1)
            for key, (s, v) in waits.items():
                known[e][key] = v
            streams[e].append((list(waits.values()), o["fn"], inc))
        self.stats = {e: len(streams[e]) for e in ENGS}
        final_waits = []
        for e in lanes:
            for ln in range(self.n_lanes):
                if lane_cnt[e][ln] > 0:
                    final_waits.append((lanes[e][ln], lane_cnt[e][ln]))
        for e in ENGS:
            if ccount[e] > 0:
                final_waits.append((csem[e], ccount[e]))
        with nc.Block() as block:
            def mk(e):
                def body(engh):
                    for waits, fn, inc in streams[e]:
                        for s, v in waits:
                            engh.wait_ge(s, v)
                        ins = fn(engh)
                        if inc is not None:
                            ins.then_inc(inc[0], inc[1])
                    if e == "sp":
                        for s, v in final_waits:
                            engh.wait_ge(s, v)
                return body
            block.tensor(mk("pe"))
            block.vector(mk("dve"))
            block.scalar(mk("act"))
            block.gpsimd(mk("pool"))
            block.sync(mk("sp"))


class Arena:
    def __init__(self, A, nbytes):
        self.A = A
        self.size = nbytes
        self.base = 0
        self.off = 0

    def alloc(self, dt, shape, name=None):
        esz = 4 if dt == F32 else 2
        free = 1
        for s in shape[1:]:
            free *= s
        nb = free * esz
        off = self.off
        self.off += (nb + 63) // 64 * 64
        assert self.off <= self.size, f"arena overflow {self.off} > {self.size} ({name})"
        ap = self.A[0:shape[0], off // 4:(off + nb + 3) // 4]
        if dt != F32:
            ap = ap.bitcast(dt)
        if len(shape) == 3:
            ap = ap.rearrange("p (a b) -> p a b", b=shape[2])
        elif len(shape) == 4:
            ap = ap.rearrange("p (a b c) -> p a b c", b=shape[2], c=shape[3])
        return ap

    def view(self, off, dt, shape):
        save = self.off
        self.off = off
        ap = self.alloc(dt, shape)
        self.off = save
        return ap

    def mark(self):
        self.base = self.off

    def reset(self):
        self.off = self.base


def build(stages=None, debug=False):
    nc = bass.Bass("TRN2", target_bir_lowering=False)

    def din(name, shape):
        return nc.dram_tensor(name, list(shape), F32, kind="ExternalInput").ap()

    def dout(name, shape):
        return nc.dram_tensor(name, list(shape), F32, kind="ExternalOutput").ap()

    def dscr(name, shape, dt):
        return nc.dram_tensor(name, list(shape), dt, kind="Internal").ap()

    x_p = din("x_p", [TP, D])
    x_s = din("x_s", [TS, D])
    st_conv = din("st_conv", [2, 3, QKV])
    st_S = din("st_S", [2, NV, 128, 128])
    st_pool = din("st_pool", [2, 15, D])
    norm_mix_w = din("norm_mix_w", [4, D])
    norm_ffn_w = din("norm_ffn_w", [4, D])
    final_norm_w = din("final_norm_w", [1, D])
    gdn_w_in = din("gdn_w_in", [2, D, IN_DIM])
    gdn_conv_w = din("gdn_conv_w", [2, 4, QKV])
    gdn_A_log = din("gdn_A_log", [2, NV])
    gdn_dt_bias = din("gdn_dt_bias", [2, NV])
    gdn_norm_w = din("gdn_norm_w", [2, 128])
    gdn_w_out = din("gdn_w_out", [2, VAL, D])
    pool_w = din("pool_w", [2, 4, 512, 512])
    pool_scale = din("pool_scale", [2, D])
    ffn_w_gu = din("ffn_w_gu", [4, D, 2 * FH])
    ffn_w_down = din("ffn_w_down", [4, FH, D])

    y_p = dout("y_p", [TP, D])
    y_s = dout("y_s", [TS, D])
    o_conv_p = dout("o_conv_p", [2, 3, QKV])
    o_S_p = dout("o_S_p", [2, NV, 128, 128])
    o_pool_p = dout("o_pool_p", [2, 15, D])
    o_conv_s = dout("o_conv_s", [2, 3, QKV])
    o_S_s = dout("o_S_s", [2, NV, 128, 128])
    o_pool_s = dout("o_pool_s", [2, 15, D])

    xT = dscr("xT", [D, TA], F32)
    qkvT = dscr("qkvT", [QKV, TA], BF16)
    zT = dscr("zT", [VAL, TA], BF16)
    oT = dscr("oT", [VAL, TA], BF16)

    sc = Sched(nc)
    if stages is None:
        stages = ["gdn0", "ffn0", "pool1", "ffn1", "gdn2", "ffn2", "pool3", "ffn3"]

    with ExitStack() as st:
        ARENA_BYTES = 176 * 1024
        A = st.enter_context(nc.sbuf_tensor("arena", [128, ARENA_BYTES // 4], F32))
        PS = st.enter_context(nc.psum_tensor("psum", [128, 8, 512], F32))
        ar = Arena(A, ARENA_BYTES)

        def bank(i):
            return PS[:, i, :]

        ident_f = ar.alloc(F32, [128, 128])
        ident_b = ar.alloc(BF16, [128, 128])
        ones_f = ar.alloc(F32, [128, 128])
        ones_b = ar.alloc(BF16, [128, 128])
        mneg_b = ar.alloc(BF16, [128, 128])
        zero_f = ar.alloc(F32, [128, 128])
        nmix = ar.alloc(F32, [128, KC, 4])
        nffn = ar.alloc(F32, [128, KC, 4])
        nfin = ar.alloc(F32, [128, KC, 1])
        pscale = ar.alloc(F32, [128, KC, 2])

        sc.op("pool", lambda e: e.memset(ones_f, 1.0), writes=["ones_f"])
        sc.op("pool", lambda e: e.memset(ones_b, 1.0), writes=["ones_b"])
        sc.op("pool", lambda e: e.memset(zero_f, 0.0), writes=["zero_f"])
        sc.op("pool", lambda e: e.affine_select(out=ident_f, in_=ones_f, pattern=[[1, 128]],
                                                compare_op=ALU.is_equal, fill=0.0, base=0,
                                                channel_multiplier=-1),
              reads=["ones_f"], writes=["ident_f"])
        sc.op("pool", lambda e: e.tensor_copy(out=ident_b, in_=ident_f), reads=["ident_f"], writes=["ident_b"])
        sc.op("pool", lambda e: e.affine_select(out=mneg_b, in_=zero_f, pattern=[[1, 128]],
                                                compare_op=ALU.is_ge, fill=NEG, base=0,
                                                channel_multiplier=-1),
              reads=["zero_f"], writes=["mneg_b"])

        rowbuf_box = [None]

        def vec_to_cols(src, R, N, dst, bankno=7):
            nchunk = N // 128
            rowbuf = rowbuf_box[0]
            sc.op("sp", lambda e: e.dma_start(out=rowbuf[0:R, 0:N], in_=src), writes=["rowbuf"], dma=True)
            per = 512 // R
            c = 0
            while c < nchunk:
                m = min(per, nchunk - c)
                pv = bank(bankno)[:, 0:m * R].rearrange("p (a b) -> p a b", b=R)
                for i in range(m):
                    sc.op("pe", lambda e, c=c, i=i, pv=pv: e.transpose(
                        out=pv[:, i, :], in_=rowbuf[0:R, (c + i) * 128:(c + i + 1) * 128], identity=ident_f[0:R, 0:R]),
                        reads=["rowbuf", "ident_f"], writes=[("ps", bankno)])
                sc.op("dve", lambda e, c=c, m=m, pv=pv: e.tensor_copy(out=dst[:, c:c + m, :], in_=pv),
                      reads=[("ps", bankno)], writes=[("cols", id(dst))])
                c += m

        ar.mark()
        rowbuf_box[0] = ar.alloc(F32, [16, QKV])
        vec_to_cols(norm_mix_w, 4, D, nmix)
        vec_to_cols(norm_ffn_w, 4, D, nffn)
        vec_to_cols(final_norm_w, 1, D, nfin)
        vec_to_cols(pool_scale, 2, D, pscale)
        sc.barrier()

        def phase_input():
            ar.reset()
            xin = [ar.alloc(F32, [128, D]) for _ in range(2)]
            xo = [ar.alloc(F32, [128, KC, 128]) for _ in range(2)]
            for g in range(NG):
                b = g % 2
                rows = 128 if g < 16 else TS
                src = x_p[g * 128:(g + 1) * 128, :] if g < 16 else x_s
                sc.op("sp", lambda e, b=b, rows=rows, src=src: e.dma_start(out=xin[b][0:rows, :], in_=src),
                      writes=[("xin", b)], dma=True)
                for q in range(4):
                    bk = (g * 4 + q) % 4
                    for i in range(4):
                        kc = q * 4 + i
                        sc.op("pe", lambda e, b=b, rows=rows, kc=kc, bk=bk, i=i: e.transpose(
                            out=bank(bk)[:, i * 128:i * 128 + rows], in_=xin[b][0:rows, kc * 128:(kc + 1) * 128],
                            identity=ident_f[0:rows, 0:rows]),
                            reads=[("xin", b), "ident_f"], writes=[("ps", bk)])
                    eng = "act" if q % 2 == 0 else "dve"
                    pv = bank(bk).rearrange("p (a b) -> p a b", b=128)[:, :, 0:rows]
                    if eng == "act":
                        sc.op("act", lambda e, b=b, q=q, pv=pv, rows=rows: e.copy(out=xo[b][:, q * 4:(q + 1) * 4, 0:rows], in_=pv),
                              reads=[("ps", bk)], writes=[("xo", b, q)])
                    else:
                        sc.op("dve", lambda e, b=b, q=q, pv=pv, rows=rows: e.tensor_copy(out=xo[b][:, q * 4:(q + 1) * 4, 0:rows], in_=pv),
                              reads=[("ps", bk)], writes=[("xo", b, q)])
                dstv = xT.rearrange("(kc p) t -> p kc t", p=128)[:, :, g * 128:g * 128 + rows]
                sc.op("sp", lambda e, b=b, rows=rows, dstv=dstv: e.dma_start(out=dstv, in_=xo[b][:, :, 0:rows]),
                      reads=[("xo", b, q) for q in range(4)], writes=[("xT", g)], dma=True)
            sc.barrier()

        def norm_tiles(hT, wcols, widx, tiles, tmp_x, tmp_sq, tmp_r, hcol0=0, side=None, hl=None):
            xTv = xT.rearrange("(kc p) t -> p kc t", p=128)
            for ti, (c0, n) in enumerate(tiles):
                b = ti % 2
                xt, sq, rr = tmp_x[b], tmp_sq[b], tmp_r[b]
                kx, kq, kr = ("nx", id(xt)), ("nsq", id(sq)), ("nr", id(rr))
                sc.op("sp", lambda e, xt=xt, c0=c0, n=n: e.dma_start(out=xt[:, :, 0:n], in_=xTv[:, :, c0:c0 + n]),
                      writes=[kx], dma=True)
                sc.op("act", lambda e, xt=xt, sq=sq, n=n: e.activation(out=sq[:, :, 0:n], in_=xt[:, :, 0:n], func=AF.Square),
                      reads=[kx], writes=[kq])
                bk = 6 + b
                for kc in range(KC):
                    sc.op("pe", lambda e, sq=sq, kc=kc, n=n, bk=bk: e.matmul(
                        bank(bk)[:, 0:n], ones_f, sq[:, kc, 0:n], start=(kc == 0), stop=(kc == KC - 1)),
                        reads=[kq, "ones_f"], writes=[("ps", bk)])
                sc.op("act", lambda e, rr=rr, n=n, bk=bk: e.activation(out=rr[:, 0:n], in_=bank(bk)[:, 0:n], func=AF.Ln,
                                                                      scale=1.0 / D, bias=EPS),
                      reads=[("ps", bk)], writes=[kr])
                sc.op("act", lambda e, rr=rr, n=n: e.activation(out=rr[:, 0:n], in_=rr[:, 0:n], func=AF.Exp, scale=-0.5),
                      reads=[kr], writes=[kr])
                for kc in range(KC):
                    sc.op("dve", lambda e, xt=xt, rr=rr, kc=kc, c0=c0, n=n: e.scalar_tensor_tensor(
                        out=hT[:, kc, c0 - hcol0:c0 - hcol0 + n], in0=xt[:, kc, 0:n], scalar=wcols[:, kc, widx:widx + 1],
                        in1=rr[:, 0:n], op0=ALU.mult, op1=ALU.mult),
                        reads=[kx, kr], writes=[("hT", kc, c0)])
                for (ts0, cnt, dcol) in (side or []):
                    if c0 <= ts0 and ts0 + cnt <= c0 + n:
                        for kc in range(KC):
                            sc.op("dve", lambda e, xt=xt, rr=rr, kc=kc, o=ts0 - c0, cnt=cnt, dcol=dcol: e.scalar_tensor_tensor(
                                out=hl[:, kc, dcol:dcol + cnt], in0=xt[:, kc, o:o + cnt], scalar=wcols[:, kc, widx:widx + 1],
                                in1=rr[:, o:o + cnt], op0=ALU.mult, op1=ALU.mult),
                                reads=[kx, kr], writes=[("hl", kc, dcol)])

        def wload(wslot, slot, W, KCn, col0, ncols):
            view = wslot[slot][:, 0:KCn * ncols].rearrange("p (a b) -> p a b", b=ncols)
            src = W[:, col0:col0 + ncols].rearrange("(kc p) m -> p kc m", p=128)
            sc.op("pool", lambda e: e.dma_start(out=view, in_=src), writes=[("w", slot)], dma=True)
            return view

        def phase_ffn(li):
            ar.reset()
            Wgu = ffn_w_gu[li]
            Wd = ffn_w_down[li]
            NB = 1040
            hT = ar.alloc(BF16, [128, KC, NB])
            act_off = ar.off
            act = ar.alloc(BF16, [128, FC, NB])
            wslot = [ar.alloc(BF16, [128, 8192]) for _ in range(2)]
            sgt = [ar.alloc(F32, [128, 512]) for _ in range(2)]
            xres = [ar.alloc(F32, [128, 512]) for _ in range(2)]
            tx = ar.view(act_off, F32, [128, KC, 512])
            tq = ar.view(act_off + 32768, F32, [128, KC, 512])
            tr = [ar.view(act_off + 65536 + i * 2048, F32, [128, 512]) for i in range(2)]
            blocks = [TILES[0:2], TILES[2:5]]
            xTv = xT.rearrange("(kc p) t -> p kc t", p=128)
            wcnt = 0
            for blk in blocks:
                hc0 = blk[0][0]
                norm_tiles(hT, nffn, li, blk, [tx, tx], [tq, tq], tr, hcol0=hc0)
                sc.barrier()
                for jp in range(FC // 2):
                    slot = wcnt % 2
                    wcnt += 1
                    vg = wslot[slot][:, 0:KC * 256].rearrange("p (a b) -> p a b", b=256)
                    vu = wslot[slot][:, KC * 256:KC * 512].rearrange("p (a b) -> p a b", b=256)
                    srcg = Wgu[:, jp * 256:jp * 256 + 256].rearrange("(kc p) m -> p kc m", p=128)
                    srcu = Wgu[:, FH + jp * 256:FH + jp * 256 + 256].rearrange("(kc p) m -> p kc m", p=128)
                    sc.op("pool", lambda e, vg=vg, srcg=srcg: e.dma_start(out=vg, in_=srcg),
                          writes=[("w", slot, 0)], dma=True)
                    sc.op("pool", lambda e, vu=vu, srcu=srcu: e.dma_start(out=vu, in_=srcu),
                          writes=[("w", slot, 1)], dma=True)
                    for jj in range(2):
                        j = jp * 2 + jj
                        for ti, (c0, n) in enumerate(blk):
                            it = j * len(blk) + ti
                            bg = (it % 3) * 2
                            bu = bg + 1
                            for kc in range(KC):
                                sc.op("pe", lambda e, vg=vg, kc=kc, jj=jj, c0=c0, n=n, bg=bg, hc0=hc0: e.matmul(
                                    bank(bg)[:, 0:n], vg[:, kc, jj * 128:(jj + 1) * 128], hT[:, kc, c0 - hc0:c0 - hc0 + n],
                                    start=(kc == 0), stop=(kc == KC - 1)),
                                    reads=[("w", slot, 0), ("hT", kc, c0)], writes=[("ps", bg)])
                            for kc in range(KC):
                                sc.op("pe", lambda e, vu=vu, kc=kc, jj=jj, c0=c0, n=n, bu=bu, hc0=hc0: e.matmul(
                                    bank(bu)[:, 0:n], vu[:, kc, jj * 128:(jj + 1) * 128], hT[:, kc, c0 - hc0:c0 - hc0 + n],
                                    start=(kc == 0), stop=(kc == KC - 1)),
                                    reads=[("w", slot, 1), ("hT", kc, c0)], writes=[("ps", bu)])
                            sb = it % 2
                            sc.op("act", lambda e, sb=sb, bg=bg, n=n: e.activation(out=sgt[sb][:, 0:n], in_=bank(bg)[:, 0:n], func=AF.Silu),
                                  reads=[("ps", bg)], writes=[("sgt", sb)])
                            sc.op("dve", lambda e, sb=sb, bu=bu, n=n, j=j, c0=c0, hc0=hc0: e.tensor_tensor(
                                out=act[:, j, c0 - hc0:c0 - hc0 + n], in0=sgt[sb][:, 0:n], in1=bank(bu)[:, 0:n], op=ALU.mult),
                                reads=[("sgt", sb), ("ps", bu)], writes=[("act", j, c0)])
                for mc in range(KC):
                    slot = wcnt % 2
                    wcnt += 1
                    vw = wslot[slot][:, 0:FC * 128].rearrange("p (a b) -> p a b", b=128)
                    src = Wd[:, mc * 128:(mc + 1) * 128].rearrange("(kc p) m -> p kc m", p=128)
                    sc.op("pool", lambda e, vw=vw, src=src: e.dma_start(out=vw, in_=src),
                          writes=[("w", slot, 0), ("w", slot, 1)], dma=True)
                    for ti, (c0, n) in enumerate(blk):
                        it = mc * len(blk) + ti
                        bk = it % 6
                        xb = it % 2
                        sc.op("sp", lambda e, xb=xb, mc=mc, c0=c0, n=n: e.dma_start(out=xres[xb][:, 0:n], in_=xTv[:, mc, c0:c0 + n]),
                              writes=[("xres", xb)], dma=True)
                        for kc in range(FC):
                            sc.op("pe", lambda e, vw=vw, kc=kc, c0=c0, n=n, bk=bk, hc0=hc0: e.matmul(
                                bank(bk)[:, 0:n], vw[:, kc, :], act[:, kc, c0 - hc0:c0 - hc0 + n],
                                start=(kc == 0), stop=(kc == FC - 1)),
                                reads=[("w", slot, 0), ("w", slot, 1), ("act", kc, c0)], writes=[("ps", bk)])
                        sc.op("dve", lambda e, xb=xb, bk=bk, n=n: e.tensor_tensor(
                            out=xres[xb][:, 0:n], in0=xres[xb][:, 0:n], in1=bank(bk)[:, 0:n], op=ALU.add),
                            reads=[("xres", xb), ("ps", bk)], writes=[("xres", xb)])
                        sc.op("sp", lambda e, xb=xb, mc=mc, c0=c0, n=n: e.dma_start(out=xTv[:, mc, c0:c0 + n], in_=xres[xb][:, 0:n]),
                              reads=[("xres", xb)], writes=[("xTo", mc, c0)], dma=True)
                sc.barrier()

        def phase_pool(li):
            j = li // 2
            ar.reset()
            HP = 2096
            hpad = ar.alloc(BF16, [128, KC, HP])
            hl = ar.alloc(F32, [128, KC, 32])
            icnt = ar.alloc(F32, [128, 16])
            wslot = [ar.alloc(BF16, [128, 2048]) for _ in range(2)]
            xres = [ar.alloc(F32, [128, 512]) for _ in range(2)]
            po = ar.alloc(F32, [16, D])
            big_off = ar.off
            dT = ar.alloc(BF16, [128, KC, TR])
            tA = ar.alloc(F32, [128, HP])
            tB = ar.alloc(F32, [128, HP])
            tC = ar.alloc(F32, [128, 16])
            tx = ar.view(big_off, F32, [128, KC, 512])
            tq = ar.view(big_off + 32768, F32, [128, KC, 512])
            tr = [ar.view(big_off + 65536 + i * 2048, F32, [128, 512]) for i in range(2)]
            rb = ar.view(big_off, F32, [16, D])
            xTv = xT.rearrange("(kc p) t -> p kc t", p=128)
            for t in range(16):
                sc.op("pool", lambda e, t=t: e.memset(icnt[:, t:t + 1], 1.0 / (t + 1)), writes=["icnt"])
            sc.op("pool", lambda e: e.memset(hpad[:, :, 0:16], 0.0), writes=[("hp0",)])
            sc.op("pool", lambda e: e.memset(hpad[:, :, 2064:2065], 0.0), writes=[("hp1",)])
            sc.op("sp", lambda e: e.dma_start(out=rb[0:15, :], in_=st_pool[j]), writes=["rb"], dma=True)
            for q in range(4):
                pv = bank(7)[:, 0:60].rearrange("p (a b) -> p a b", b=15)
                for i in range(4):
                    sc.op("pe", lambda e, q=q, i=i, pv=pv: e.transpose(out=pv[:, i, :], in_=rb[0:15, (q * 4 + i) * 128:(q * 4 + i + 1) * 128],
                                                                 identity=ident_f[0:15, 0:15]),
                          reads=["rb", "ident_f"], writes=[("ps", 7)])
                sc.op("dve", lambda e, q=q, pv=pv: e.tensor_copy(out=hpad[:, q * 4:(q + 1) * 4, 2065:2080], in_=pv),
                      reads=[("ps", 7)], writes=[("hph", q)])
            sc.barrier()
            side = [(2032, 16, 0), (2048, 16, 16)]
            norm_tiles(hpad, nmix, li, TILES[0:4], [tx, tx], [tq, tq], tr, hcol0=-16, side=side, hl=hl)
            norm_tiles(hpad, nmix, li, TILES[4:5], [tx, tx], [tq, tq], tr, hcol0=-32, side=side, hl=hl)
            sc.barrier()
            for (c_lo, dst) in ((1, o_pool_p[j]), (17, o_pool_s[j])):
                for q in range(4):
                    for i in range(4):
                        kc = q * 4 + i
                        sc.op("pe", lambda e, kc=kc, c_lo=c_lo, q=q, i=i: e.transpose(
                            out=bank(q)[0:15, i * 128:(i + 1) * 128], in_=hl[:, kc, c_lo:c_lo + 15], identity=ident_f),
                            reads=[("hl", kc, 0), ("hl", kc, 16), "ident_f"], writes=[("ps", q)])
                    sc.op("act", lambda e, q=q: e.copy(out=po[0:15, q * 512:(q + 1) * 512], in_=bank(q)[0:15, :]),
                          reads=[("ps", q)], writes=[("po", q)])
                sc.op("sp", lambda e, dst=dst: e.dma_start(out=dst, in_=po[0:15, :]), reads=[("po", q) for q in range(4)],
                      writes=[("pout", c_lo)], dma=True)
            for kc in range(KC):
                gi = kc // 4
                w = 2 << gi
                src = hpad[:, kc, :]
                bufs = [tA, tB]
                cur = None
                sh = 1
                for lv in range(gi + 1):
                    dstb = bufs[lv % 2]
                    eng = "dve" if (kc + lv) % 2 == 0 else "pool"
                    a_in = src if cur is None else cur
                    sc.op(eng, lambda e, dstb=dstb, a_in=a_in, sh=sh: e.tensor_tensor(
                        out=dstb[:, sh:HP], in0=a_in[:, sh:HP], in1=a_in[:, 0:HP - sh], op=ALU.add),
                        reads=[("hT", kc, c) for c, _ in TILES] + [("win", id(a_in)), ("hp0",), ("hp1",), ("hph", kc // 4)],
                        writes=[("win", id(dstb))])
                    cur = dstb
                    sh *= 2
                sc.op("dve", lambda e, cur=cur, src=src, w=w, kc=kc: e.scalar_tensor_tensor(
                    out=dT[:, kc, 0:TP], in0=cur[:, 16:16 + TP], scalar=1.0 / w, in1=src[:, 16:16 + TP],
                    op0=ALU.mult, op1=ALU.subtract),
                    reads=[("win", id(cur))] + [("hT", kc, c) for c, _ in TILES], writes=[("dT", kc)])
                sc.op("dve", lambda e, cur=cur, src=src, w=w, kc=kc: e.scalar_tensor_tensor(
                    out=dT[:, kc, TP:TR], in0=cur[:, 2080:2096], scalar=1.0 / w, in1=src[:, 2080:2096],
                    op0=ALU.mult, op1=ALU.subtract),
                    reads=[("win", id(cur))] + [("hT", kc, c) for c, _ in TILES], writes=[("dT", kc)])
                sc.op("dve", lambda e, cur=cur, w=w: e.tensor_tensor(out=tC[:, 0:w - 1], in0=cur[:, 16:16 + w - 1], in1=icnt[:, 0:w - 1], op=ALU.mult),
                      reads=[("win", id(cur)), "icnt"], writes=["tC"])
                sc.op("dve", lambda e, src=src, w=w, kc=kc: e.tensor_tensor(out=dT[:, kc, 0:w - 1], in0=tC[:, 0:w - 1], in1=src[:, 16:16 + w - 1], op=ALU.subtract),
                      reads=["tC"] + [("hT", kc, c) for c, _ in TILES], writes=[("dT", kc)])
            it = 0
            for gi in range(4):
                slot = gi % 2
                vw = wslot[slot].rearrange("p (a b) -> p a b", b=512)
                srcw = pool_w[j, gi].rearrange("(kc p) m -> p kc m", p=128)
                sc.op("pool", lambda e, vw=vw, srcw=srcw: e.dma_start(out=vw, in_=srcw), writes=[("w", slot)], dma=True)
                for ec in range(4):
                    mc = gi * 4 + ec
                    for (c0, n) in TILES:
                        bk = it % 6
                        xb = it % 2
                        it += 1
                        sc.op("sp", lambda e, xb=xb, mc=mc, c0=c0, n=n: e.dma_start(out=xres[xb][:, 0:n], in_=xTv[:, mc, c0:c0 + n]),
                              writes=[("xres", xb)], dma=True)
                        for cc in range(4):
                            sc.op("pe", lambda e, vw=vw, cc=cc, ec=ec, gi=gi, c0=c0, n=n, bk=bk: e.matmul(
                                bank(bk)[:, 0:n], vw[:, cc, ec * 128:(ec + 1) * 128], dT[:, gi * 4 + cc, c0:c0 + n],
                                start=(cc == 0), stop=(cc == 3)),
                                reads=[("w", slot), ("dT", gi * 4 + cc)], writes=[("ps", bk)])
                        sc.op("dve", lambda e, xb=xb, bk=bk, n=n, mc=mc: e.scalar_tensor_tensor(
                            out=xres[xb][:, 0:n], in0=bank(bk)[:, 0:n], scalar=pscale[:, mc, j:j + 1], in1=xres[xb][:, 0:n],
                            op0=ALU.mult, op1=ALU.add),
                            reads=[("xres", xb), ("ps", bk)], writes=[("xres", xb)])
                        sc.op("sp", lambda e, xb=xb, mc=mc, c0=c0, n=n: e.dma_start(out=xTv[:, mc, c0:c0 + n], in_=xres[xb][:, 0:n]),
                              reads=[("xres", xb)], writes=[("xTo", mc, c0)], dma=True)
            sc.barrier()

        def phase_gdn(li):
            j = li // 2
            Win = gdn_w_in[j]
            ar.reset()
            xTv = xT.rearrange("(kc p) t -> p kc t", p=128)
            cw = ar.alloc(F32, [128, 64, 4])
            chs = ar.alloc(F32, [128, 64, 3])
            cst = ar.alloc(F32, [128, 64, 8])
            bT = ar.alloc(F32, [32, TA])
            gT = ar.alloc(F32, [32, TA])
            alog = ar.alloc(F32, [32, 1])
            dtb = ar.alloc(F32, [32, 1])
            nega = ar.alloc(F32, [32, 1])
            gnw = ar.alloc(F32, [128, 1])
            gcTok = ar.alloc(F32, [128, NG, 32])
            nbTok = ar.alloc(F32, [128, NG, 32])
            bTok = ar.alloc(F32, [128, NG, 32])
            egc = ar.alloc(F32, [128, NG, 32])
            ekd = ar.alloc(F32, [128, NG, 32])
            glb = ar.alloc(F32, [128, NG, 32])
            negones = ar.alloc(F32, [32, 128])
            persist_off = ar.off

            rb = ar.alloc(F32, [16, QKV])
            rowbuf_box[0] = rb
            vec_to_cols(gdn_conv_w[j], 4, QKV, cw)
            vec_to_cols(st_conv[j], 3, QKV, chs)
            sc.op("sp", lambda e: e.dma_start(out=alog, in_=gdn_A_log[j].rearrange("(p o) -> p o", o=1)), writes=["alog"], dma=True)
            sc.op("sp", lambda e: e.dma_start(out=dtb, in_=gdn_dt_bias[j].rearrange("(p o) -> p o", o=1)), writes=["dtb"], dma=True)
            sc.op("sp", lambda e: e.dma_start(out=gnw, in_=gdn_norm_w[j].rearrange("(p o) -> p o", o=1)), writes=["gnw"], dma=True)
            sc.op("pool", lambda e: e.memset(negones, -1.0), writes=["negones"])
            sc.op("pool", lambda e: e.memset(cst, 0.0), writes=[("cst", mc) for mc in range(64)])
            sc.barrier()

            ar.off = persist_off
            hT = ar.alloc(BF16, [128, KC, TR])
            g2_off = ar.off
            tx = ar.alloc(F32, [128, KC, 512])
            tq = ar.alloc(F32, [128, KC, 512])
            tr = [ar.alloc(F32, [128, 512]) for _ in range(2)]
            norm_tiles(hT, nmix, li, TILES, [tx, tx], [tq, tq], tr)
            sc.barrier()

            if os.environ.get("GDN_STOP") == "1":
                sc.barrier()
                return
            ar.off = g2_off
            wslot = [ar.alloc(BF16, [128, 4096]) for _ in range(2)]
            PW = 2070
            pre = [ar.alloc(F32, [128, PW]) for _ in range(2)]
            acc = ar.alloc(F32, [128, PW])
            sil = ar.alloc(F32, [128, PW])
            sqb = ar.alloc(BF16, [128, PW])
            vout = [ar.alloc(BF16, [128, TA]) for _ in range(2)]
            rs = acc
            for b in range(2):
                sc.op("pool", lambda e, b=b: e.memset(vout[b][:, TR:TA], 0.0), writes=[("voutpad", b)])
                sc.op("pool", lambda e, b=b: e.memset(pre[b][:, 0:3], 0.0), writes=[("prepad", b)])
            PCOL = [3, 515, 1027, 1539, 2054]
            nchunks_total = 97
            itc = 0
            qkvp = list(range(32))
            zp = list(range(32, 48))
            porder = []
            for i in range(16):
                porder += [qkvp[2 * i], zp[i], qkvp[2 * i + 1]]
            porder.append(48)
            mc_order = []
            for pi in porder:
                for mc in (2 * pi, 2 * pi + 1):
                    if mc < nchunks_total:
                        mc_order.append(mc)
            wl = 0
            for mc in mc_order:
                if mc % 2 == 0:
                    slot = wl % 2
                    wl += 1
                    ncols = min(256, IN_DIM - mc * 128)
                    wv = wload(wslot, slot, Win, KC, mc * 128, ncols)
                wi = mc % 2
                M = 128 if mc < 96 else 64
                pb = mc % 2
                kind = "q" if mc < 16 else ("k" if mc < 32 else ("v" if mc < 64 else ("z" if mc < 96 else "ba")))
                if kind in ("q", "k", "v"):
                    sc.op("dve", lambda e, pb=pb, mc=mc: e.tensor_copy(out=pre[pb][:, 2051:2054], in_=chs[:, mc, :]),
                          reads=[("cols", id(chs))], writes=[("prehist", pb)])
                for ti, (c0, n) in enumerate(TILES):
                    if kind == "ba":
                        halves = [(0, 32, bT), (32, 64, gT)]
                    else:
                        halves = [(0, M, None)]
                    for (m0, m1, dstT) in halves:
                        bk = itc % 6
                        itc += 1
                        for kc in range(KC):
                            sc.op("pe", lambda e, wv=wv, kc=kc, wi=wi, m0=m0, m1=m1, c0=c0, n=n, bk=bk: e.matmul(
                                bank(bk)[0:m1 - m0, 0:n], wv[:, kc, wi * 128 + m0:wi * 128 + m1], hT[:, kc, c0:c0 + n],
                                start=(kc == 0), stop=(kc == KC - 1)),
                                reads=[("w", slot), ("hT", kc, c0)], writes=[("ps", bk)])
                        if kind in ("q", "k", "v"):
                            sc.op("act", lambda e, pb=pb, ti=ti, n=n, bk=bk: e.copy(out=pre[pb][:, PCOL[ti]:PCOL[ti] + n], in_=bank(bk)[:, 0:n]),
                                  reads=[("ps", bk)], writes=[("pre", pb, ti)])
                        elif kind == "z":
                            zb = mc % 2
                            sc.op("act", lambda e, zb=zb, c0=c0, n=n, bk=bk: e.activation(out=vout[zb][:, c0:c0 + n], in_=bank(bk)[:, 0:n], func=AF.Silu),
                                  reads=[("ps", bk)], writes=[("vout", zb, ti)])
                        else:
                            sc.op("act", lambda e, dstT=dstT, c0=c0, n=n, bk=bk: e.copy(out=dstT[:, c0:c0 + n], in_=bank(bk)[0:32, 0:n]),
                                  reads=[("ps", bk)], writes=[("baT", id(dstT), ti)])
                if kind == "z":
                    zb = mc % 2
                    h = mc - 64
                    sc.op("sp", lambda e, zb=zb, h=h: e.dma_start(out=zT[h * 128:(h + 1) * 128, :], in_=vout[zb]),
                          reads=[("vout", zb, ti) for ti in range(5)] + [("voutpad", zb)], writes=[("zT", h)], dma=True)
                if kind in ("q", "k", "v"):
                    prk = [("pre", pb, ti) for ti in range(5)] + [("prehist", pb), ("prepad", pb)]
                    P = pre[pb]
                    sc.op("dve", lambda e, P=P, mc=mc: e.tensor_copy(out=cst[:, mc, 0:3], in_=P[:, 2048:2051]),
                          reads=prk, writes=[("cst", mc)])
                    sc.op("dve", lambda e, P=P, mc=mc: e.tensor_copy(out=cst[:, mc, 4:7], in_=P[:, 2067:2070]),
                          reads=prk, writes=[("cst", mc)])
                    L = PW - 3
                    sc.op("dve", lambda e, P=P, mc=mc: e.tensor_scalar(out=acc[:, 0:L], in0=P[:, 3:PW], scalar1=cw[:, mc, 3:4], scalar2=None, op0=ALU.mult),
                          reads=prk + [("cols", id(cw))], writes=["acc"])
                    for tap in (2, 1, 0):
                        sc.op("dve", lambda e, P=P, mc=mc, tap=tap: e.scalar_tensor_tensor(
                            out=acc[:, 0:L], in0=P[:, tap:tap + L], scalar=cw[:, mc, tap:tap + 1], in1=acc[:, 0:L],
                            op0=ALU.mult, op1=ALU.add),
                            reads=prk + ["acc"], writes=["acc"])
                    vb = mc % 2
                    if kind == "v":
                        sc.op("act", lambda e, vb=vb: e.activation(out=vout[vb][:, 0:TP], in_=acc[:, 0:TP], func=AF.Silu),
                              reads=["acc"], writes=[("vout", vb, 0)])
                        sc.op("act", lambda e, vb=vb: e.activation(out=vout[vb][:, TP:TR], in_=acc[:, 2051:2067], func=AF.Silu),
                              reads=["acc"], writes=[("vout", vb, 1)])
                    else:
                        sc.op("act", lambda e: e.activation(out=sil[:, 0:L], in_=acc[:, 0:L], func=AF.Silu),
                              reads=["acc"], writes=["sil"])
                        sc.op("act", lambda e: e.activation(out=sqb[:, 0:L], in_=sil[:, 0:L], func=AF.Square),
                              reads=["sil"], writes=["sqb"])
                        c = 0
                        while c < L:
                            n = min(512, L - c)
                            bk = itc % 6
                            itc += 1
                            sc.op("pe", lambda e, c=c, n=n, bk=bk: e.matmul(bank(bk)[:, 0:n], ones_b, sqb[:, c:c + n], start=True, stop=True),
                                  reads=["sqb", "ones_b"], writes=[("ps", bk)])
                            sc.op("act", lambda e, c=c, n=n, bk=bk: e.activation(out=rs[:, c:c + n], in_=bank(bk)[:, 0:n], func=AF.Ln, bias=EPS, scale=1.0),
                                  reads=[("ps", bk), "sil"], writes=["acc"])
                            c += n
                        rsk = ["acc"]
                        sc.op("act", lambda e: e.activation(out=rs[:, 0:L], in_=rs[:, 0:L], func=AF.Exp, scale=-0.5),
                              reads=rsk, writes=rsk)
                        qs = (128.0 ** -0.5) if kind == "q" else 1.0
                        sc.op("dve", lambda e, vb=vb, qs=qs: e.scalar_tensor_tensor(
                            out=vout[vb][:, 0:TP], in0=sil[:, 0:TP], scalar=qs, in1=rs[:, 0:TP], op0=ALU.mult, op1=ALU.mult),
                            reads=["sil"] + rsk, writes=[("vout", vb, 0)])
                        sc.op("dve", lambda e, vb=vb, qs=qs: e.scalar_tensor_tensor(
                            out=vout[vb][:, TP:TR], in0=sil[:, 2051:2067], scalar=qs, in1=rs[:, 2051:2067], op0=ALU.mult, op1=ALU.mult),
                            reads=["sil"] + rsk, writes=[("vout", vb, 1)])
                    sc.op("sp", lambda e, vb=vb, mc=mc: e.dma_start(out=qkvT[mc * 128:(mc + 1) * 128, :], in_=vout[vb]),
                          reads=[("vout", vb, 0), ("vout", vb, 1), ("voutpad", vb)], writes=[("qkvT", mc)], dma=True)
            if os.environ.get("GDN_STOP") == "2a":
                sc.barrier()
                return
            crow_p = acc[0:3, 0:2048]
            crow_s = sil[0:3, 0:2048]
            for r4 in range(4):
                for q in range(4):
                    for i in range(4):
                        mc = r4 * 16 + q * 4 + i
                        sc.op("pe", lambda e, mc=mc, q=q, i=i: e.transpose(out=bank(q)[0:4, i * 128:(i + 1) * 128], in_=cst[:, mc, 0:4], identity=ident_f),
                              reads=[("cst", mc), "ident_f"], writes=[("ps", q)])
                        sc.op("pe", lambda e, mc=mc, q=q, i=i: e.transpose(out=bank(q + 4)[0:4, i * 128:(i + 1) * 128], in_=cst[:, mc, 4:8], identity=ident_f),
                              reads=[("cst", mc), "ident_f"], writes=[("ps", q + 4)])
                    sc.op("act", lambda e, q=q: e.copy(out=crow_p[:, q * 512:(q + 1) * 512], in_=bank(q)[0:3, :]),
                          reads=[("ps", q), "acc"], writes=[("crowp", q), "acc"])
                    sc.op("dve", lambda e, q=q: e.tensor_copy(out=crow_s[:, q * 512:(q + 1) * 512], in_=bank(q + 4)[0:3, :]),
                          reads=[("ps", q + 4), "sil"], writes=[("crows", q), "sil"])
                sc.op("sp", lambda e, r4=r4: e.dma_start(out=o_conv_p[j][:, r4 * 2048:(r4 + 1) * 2048], in_=crow_p),
                      reads=[("crowp", q) for q in range(4)] + ["acc"], writes=[("ocp", r4), "acc"], dma=True)
                sc.op("sp", lambda e, r4=r4: e.dma_start(out=o_conv_s[j][:, r4 * 2048:(r4 + 1) * 2048], in_=crow_s),
                      reads=[("crows", q) for q in range(4)] + ["sil"], writes=[("ocs", r4), "sil"], dma=True)
            sc.barrier()

            if os.environ.get("GDN_STOP") == "2":
                sc.barrier()
                return
            ar.off = persist_off
            Gd = ar.alloc(F32, [32, NG, 32])
            tmpE = ar.alloc(F32, [32, TA])
            bak = [("baT", id(bT), ti) for ti in range(5)]
            gak = [("baT", id(gT), ti) for ti in range(5)]
            sc.op("act", lambda e: e.activation(out=bT[:, 0:TR], in_=bT[:, 0:TR], func=AF.Sigmoid), reads=bak, writes=["bT"])
            sc.op("pool", lambda e: e.memset(bT[:, TR:TA], 0.0), writes=["bTpad"])
            sc.op("act", lambda e: e.activation(out=nega, in_=alog, func=AF.Exp), reads=["alog"], writes=["nega"])
            sc.op("act", lambda e: e.activation(out=tmpE[:, 0:TR], in_=gT[:, 0:TR], func=AF.Exp, bias=dtb[:, 0:1], scale=1.0),
                  reads=gak + ["dtb"], writes=["tmpE"])
            sc.op("act", lambda e: e.activation(out=tmpE[:, 0:TR], in_=tmpE[:, 0:TR], func=AF.Ln, bias=1.0, scale=1.0),
                  reads=["tmpE"], writes=["tmpE"])
            sc.op("dve", lambda e: e.tensor_scalar(out=gT[:, 0:TR], in0=tmpE[:, 0:TR], scalar1=nega[:, 0:1], scalar2=-1.0, op0=ALU.mult, op1=ALU.mult),
                  reads=["tmpE", "nega"] + gak, writes=["gTg"])
            sc.op("pool", lambda e: e.memset(gT[:, TR:TA], 0.0), writes=["gTpad"])
            if os.environ.get("GDN_STOP") == "3a":
                sc.barrier()
                return
            for n in range(NG):
                sc.op("dve", lambda e, n=n: e.tensor_tensor_scan(out=tmpE[:, n * 128:(n + 1) * 128], data0=ones_f[0:32, :], data1=gT[:, n * 128:(n + 1) * 128],
                                                               initial=0.0, op0=ALU.mult, op1=ALU.add),
                      reads=["gTg", "gTpad", "ones_f", "tmpE"], writes=[("gc", n)])
            gck = [("gc", n) for n in range(NG)]
            sc.op("dve", lambda e: e.tensor_copy(out=gT, in_=tmpE), reads=gck + ["gTg", "gTpad"], writes=["gcT"])
            gcT = gT
            if os.environ.get("GDN_STOP") == "3b":
                sc.barrier()
                return
            for n in range(NG):
                sc.op("pe", lambda e, n=n: e.transpose(out=bank(6)[:, 0:32], in_=gcT[:, n * 128:(n + 1) * 128], identity=ident_f[0:32, 0:32]),
                      reads=["gcT", "ident_f"], writes=[("ps", 6)])
                sc.op("dve", lambda e, n=n: e.tensor_copy(out=gcTok[:, n, :], in_=bank(6)[:, 0:32]), reads=[("ps", 6)], writes=[("gcTok", n)])
                sc.op("pe", lambda e, n=n: e.transpose(out=bank(7)[:, 0:32], in_=bT[:, n * 128:(n + 1) * 128], identity=ident_f[0:32, 0:32]),
                      reads=["bT", "bTpad", "ident_f"], writes=[("ps", 7)])
                sc.op("act", lambda e, n=n: e.copy(out=bTok[:, n, :], in_=bank(7)[:, 0:32]), reads=[("ps", 7)], writes=[("bTok", n)])
            gtk = [("gcTok", n) for n in range(NG)]
            btk = [("bTok", n) for n in range(NG)]
            sc.op("dve", lambda e: e.tensor_scalar(out=nbTok, in0=bTok, scalar1=-1.0, scalar2=None, op0=ALU.mult), reads=btk, writes=["nbTok"])
            sc.op("act", lambda e: e.activation(out=egc, in_=gcTok, func=AF.Exp), reads=gtk, writes=["egc"])
            if os.environ.get("GDN_STOP") == "3c":
                sc.barrier()
                return
            gclast = gcT.rearrange("p (n c) -> p n c", c=128)[:, :, 127]
            for h in range(32):
                sc.op("dve", lambda e, h=h: e.tensor_scalar(out=Gd[:, :, h], in0=gclast, scalar1=ident_f[0:32, h:h + 1], scalar2=None, op0=ALU.mult),
                      reads=["gcT", "ident_f"], writes=[("Gd", h)])
            gdk = [("Gd", h) for h in range(32)]
            Gdf = Gd.rearrange("p a b -> p (a b)")
            glbf = glb.rearrange("p a b -> p (a b)")
            ekdf = ekd.rearrange("p a b -> p (a b)")
            gctf = gcTok.rearrange("p a b -> p (a b)")
            for half, (c0, c1) in enumerate(((0, 288), (288, 544))):
                sc.op("pe", lambda e, half=half, c0=c0, c1=c1: e.matmul(bank(6 + half)[:, 0:c1 - c0], ones_f[0:32, :], Gdf[:, c0:c1], start=True, stop=True),
                      reads=gdk + ["ones_f"], writes=[("ps", 6 + half)])
                sc.op("dve", lambda e, half=half, c0=c0, c1=c1: e.tensor_copy(out=glbf[:, c0:c1], in_=bank(6 + half)[:, 0:c1 - c0]),
                      reads=[("ps", 6 + half)], writes=[("glraw", half)])
            if os.environ.get("GDN_STOP") == "3d":
                sc.barrier()
                return
            sc.op("dve", lambda e: e.tensor_tensor(out=ekdf, in0=glbf, in1=gctf, op=ALU.subtract),
                  reads=[("glraw", 0), ("glraw", 1)] + gtk, writes=["ekd"])
            sc.op("act", lambda e: e.activation(out=ekdf, in_=ekdf, func=AF.Exp), reads=["ekd"], writes=["ekdx"])
            sc.op("act", lambda e: e.activation(out=glbf, in_=glbf, func=AF.Exp), reads=[("glraw", 0), ("glraw", 1), "ekd"], writes=["glb"])
            sc.barrier()

            if os.environ.get("GDN_STOP") == "3":
                sc.barrier()
                return
            ar.off = persist_off
            kTq = [[ar.alloc(BF16, [128, TA]) for _ in range(2)] for _ in range(4)]
            ogT = [ar.alloc(BF16, [128, TA]) for _ in range(2)]
            gcm = [ar.alloc(F32, [32, TA])] * 2
            oh = [ar.alloc(F32, [32, 128])] * 2
            S_p = [ar.alloc(F32, [128, 128]) for _ in range(2)]
            S_s = [ar.alloc(F32, [128, 128]) for _ in range(2)]
            Sb = [ar.alloc(BF16, [128, 128]) for _ in range(2)]
            class _L:
                def __init__(self, t):
                    self.t = t

                def __getitem__(self, n):
                    return self.t[:, n, :]
            ATt = [ar.alloc(BF16, [128, NG, 128]) for _ in range(2)]
            U7t = [ar.alloc(BF16, [128, NG, 128]) for _ in range(2)]
            nWTt = [ar.alloc(BF16, [128, NG, 128]) for _ in range(2)]
            kdtt = [ar.alloc(BF16, [128, NG, 128]) for _ in range(2)]
            vtkt = [ar.alloc(BF16, [128, NG, 128]) for _ in range(2)]
            AT, U7, nWT, kdt, vtk = [[_L(t) for t in tt] for tt in (ATt, U7t, nWTt, kdtt, vtkt)]
            GR = 6
            SG = 3
            Dm4s = [ar.alloc(F32, [128, SG, 128]) for _ in range(2)]
            DmS4s = [ar.alloc(F32, [128, SG, 128]) for _ in range(2)]
            Xh4s = [ar.alloc(BF16, [128, SG, 128]) for _ in range(2)]
            N4s = [[ar.alloc(F32, [128, SG, 128]) for _ in range(1)] for _ in range(2)]
            NT4s = [[ar.alloc(F32, [128, SG, 128]) for _ in range(1)] for _ in range(2)]
            Nbs = [[ar.alloc(BF16, [128, SG, 128]) for _ in range(2)] for _ in range(2)]
            NTbs = [[ar.alloc(BF16, [128, SG, 128]) for _ in range(2)] for _ in range(2)]
            Ubs = [[ar.alloc(BF16, [128, SG, 128]) for _ in range(2)] for _ in range(2)]
            X0fs = [ar.alloc(F32, [128, SG, 128]) for _ in range(2)]
            ImXs = [ar.alloc(F32, [128, SG, 128]) for _ in range(2)]
            E0fs = [ar.alloc(F32, [128, SG, 128]) for _ in range(2)]
            X0Ts = [ar.alloc(F32, [128, SG, 128]) for _ in range(2)]
            print("G4 arena bytes", ar.off)
            ident4 = ar.alloc(F32, [128, 4, 128])
            for i in range(4):
                sc.op("pool", lambda e, i=i: e.tensor_copy(out=ident4[:, i, :], in_=ident_f), reads=["ident_f"], writes=[("ident4", i)])
            id4k = [("ident4", i) for i in range(4)]
            vnew = [ar.alloc(BF16, [128, 128]) for _ in range(2)]
            otmp = [ar.alloc(F32, [128, 128]) for _ in range(2)]
            otok = [ar.alloc(F32, [128, 128]) for _ in range(2)]
            osq = [ar.alloc(F32, [128, 128]) for _ in range(2)]
            onb = [ar.alloc(BF16, [128, 128]) for _ in range(2)]
            ssq = [ar.alloc(F32, [128, 1]) for _ in range(2)]

            def pq(b, q):
                return PS[:, b, q * 128:(q + 1) * 128]

            def pqb(b, q):
                return PS[:, b, q * 128:q * 128 + 64].bitcast(BF16)

            bcnt = [0]
            stopat = os.environ.get('GDN_STOP')

            def do_head(h):
                hb = h % 2
                kh = h // 2
                srcs = [qkvT[2048 + kh * 128:2048 + (kh + 1) * 128, :], qkvT[kh * 128:(kh + 1) * 128, :],
                        qkvT[4096 + h * 128:4096 + (h + 1) * 128, :], zT[h * 128:(h + 1) * 128, :]]
                kT_h, qT_h, vT_h, zs_h = kTq[0][hb], kTq[1][hb], kTq[2][hb], kTq[3][hb]
                kk_, qk_, vk_, zk_ = ("hin", 0, hb), ("hin", 1, hb), ("hin", 2, hb), ("hin", 3, hb)

                def setup():
                    for a in range(4):
                        sc.op("sp", lambda e, a=a, src=srcs[a]: e.dma_start(out=kTq[a][hb], in_=src), writes=[("hin", a, hb)], dma=True)
                    sc.op("dve", lambda e: e.tensor_scalar(out=gcm[hb], in0=gcT, scalar1=ident_f[0:32, h:h + 1], scalar2=None, op0=ALU.mult),
                          reads=["gcT", "ident_f"], writes=[("gcm", 0)])
                    sc.op("dve", lambda e: e.tensor_scalar(out=oh[hb], in0=ones_f[0:32, :], scalar1=ident_f[0:32, h:h + 1], scalar2=None, op0=ALU.mult),
                          reads=["ones_f", "ident_f"], writes=[("oh", 0)])
                def groupfn(g0, si):
                    grp = list(range(g0, min(NG, g0 + SG)))
                    G = len(grp)
                    R0, R1, R2 = 3 * si, 3 * si + 1, 3 * si + 2
                    Dm4, DmS4, Xh4, N4, NT4 = Dm4s[si], DmS4s[si], Xh4s[si], N4s[si], NT4s[si]
                    Nb, NTb, Ub, X0f, ImX, E0f, X0T = Nbs[si], NTbs[si], Ubs[si], X0fs[si], ImXs[si], E0fs[si], X0Ts[si]
                    kDm, kDmS = ("Dm4", si), ("DmS4", si)
                    bqv = lambda b: PS[:, b, :].bitcast(BF16).rearrange("p (q x) -> p q x", x=256)[:, 0:G, 0:128]

                    def b4(b):
                        return PS[:, b, 0:G * 128].rearrange("p (q x) -> p q x", x=128)
                    for i, n in enumerate(grp):
                        cs = n * 128
                        sc.op("pe", lambda e, i=i, cs=cs: e.transpose(out=pqb(R1, i), in_=kT_h[:, cs:cs + 128], identity=ident_b),
                              reads=[kk_, "ident_b"], writes=[("ps", R1)])
                    for i, n in enumerate(grp):
                        sc.op("act", lambda e, i=i, n=n: e.activation(out=Xh4[:, i, :], in_=pqb(R1, i), func=AF.Copy, scale=egc[:, n, h:h + 1]),
                              reads=[("ps", R1), "egc"], writes=[("Xh", si, i)])
                    for i, n in enumerate(grp):
                        sc.op("dve", lambda e, i=i, n=n: e.tensor_scalar(out=kdt[hb][n], in0=pqb(R1, i), scalar1=ekd[:, n, h:h + 1], scalar2=None, op0=ALU.mult),
                              reads=[("ps", R1), "ekdx"], writes=[("kdt", hb, n)])
                    for i, n in enumerate(grp):
                        cs = n * 128
                        sc.op("pe", lambda e, i=i, cs=cs: e.matmul(pq(R0, i), oh[hb], gcT[:, cs:cs + 128], start=True, stop=False),
                              reads=[("oh", 0), "gcT"], writes=[("ps", R0)])
                        sc.op("pe", lambda e, i=i, cs=cs: e.matmul(pq(R0, i), gcm[hb][:, cs:cs + 128], negones, start=False, stop=False),
                              reads=[("gcm", 0), "negones"], writes=[("ps", R0)])
                        sc.op("pe", lambda e, i=i: e.matmul(pq(R0, i), ident_b, mneg_b, start=False, stop=True),
                              reads=["ident_b", "mneg_b"], writes=[("ps", R0)])
                    sc.op("act", lambda e: e.activation(out=Dm4[:, 0:G, :], in_=b4(R0), func=AF.Exp), reads=[("ps", R0)], writes=[kDm])
                    sc.op("dve", lambda e: e.tensor_tensor(out=DmS4[:, 0:G, :], in0=Dm4[:, 0:G, :], in1=ident4[:, 0:G, :], op=ALU.subtract),
                          reads=[kDm] + id4k, writes=[kDmS])
                    yield
                    for i, n in enumerate(grp):
                        cs = n * 128
                        sc.op("pe", lambda e, i=i, cs=cs: e.matmul(pq(R1, i), kT_h[:, cs:cs + 128], qT_h[:, cs:cs + 128], start=True, stop=True),
                              reads=[kk_, qk_], writes=[("ps", R1)])
                    for i, n in enumerate(grp):
                        cs = n * 128
                        sc.op("pe", lambda e, i=i, cs=cs: e.matmul(pq(R2, i), kT_h[:, cs:cs + 128], kT_h[:, cs:cs + 128], start=True, stop=True),
                              reads=[kk_], writes=[("ps", R2)])
                    sc.op("dve", lambda e: e.tensor_tensor(out=ATt[hb][:, g0:g0 + G, :], in0=b4(R1), in1=Dm4[:, 0:G, :], op=ALU.mult),
                          reads=[("ps", R1), kDm], writes=[("AT", hb, n) for n in grp])
                    for i, n in enumerate(grp):
                        sc.op("dve", lambda e, i=i, n=n: e.scalar_tensor_tensor(out=N4[0][:, i, :], in0=pq(R2, i), scalar=nbTok[:, n, h:h + 1], in1=DmS4[:, i, :],
                                                                               op0=ALU.mult, op1=ALU.mult),
                              reads=[("ps", R2), kDmS, "nbTok"], writes=[("N", si, 0)])
                    for i, n in enumerate(grp):
                        cs = n * 128
                        sc.op("pe", lambda e, i=i, cs=cs: e.transpose(out=pqb(R0, i), in_=vT_h[:, cs:cs + 128], identity=ident_b),
                              reads=[vk_, "ident_b"], writes=[("ps", R0)])
                    sc.op("act", lambda e: e.copy(out=vtkt[hb][:, g0:g0 + G, :], in_=bqv(R0)), reads=[("ps", R0)], writes=[("vtk", hb, n) for n in grp])
                    yield
                    for i, n in enumerate(grp):
                        sc.op("pe", lambda e, i=i: e.transpose(out=pq(R1, i), in_=N4[0][:, i, :], identity=ident_f),
                              reads=[("N", si, 0), "ident_f"], writes=[("ps", R1)])
                    sc.op("act", lambda e: e.copy(out=NT4[0][:, 0:G, :], in_=b4(R1)), reads=[("ps", R1)], writes=[("NT", si, 0)])
                    sc.op("pool", lambda e: e.tensor_tensor(out=Ub[0][:, 0:G, :], in0=N4[0][:, 0:G, :], in1=ident4[:, 0:G, :], op=ALU.add),
                          reads=[("N", si, 0)] + id4k, writes=[("Ub", si, 0)])
                    sc.op("dve", lambda e: e.tensor_copy(out=Nb[0][:, 0:G, :], in_=N4[0][:, 0:G, :]), reads=[("N", si, 0)], writes=[("Nb", si, 0)])
                    sc.op("act", lambda e: e.copy(out=NTb[0][:, 0:G, :], in_=b4(R1)), reads=[("ps", R1), ("NT", si, 0)], writes=[("NTb", si, 0)])
                    sc.op("pool", lambda e: e.tensor_tensor(out=ImX[:, 0:G, :], in0=NT4[0][:, 0:G, :], in1=ident4[:, 0:G, :], op=ALU.subtract),
                          reads=[("NT", si, 0)] + id4k, writes=[("NTmI", si)])
                    yield
                    NLV = 4
                    for lv in range(1, NLV + 1):
                        a, bb = (lv - 1) % 2, lv % 2
                        if lv < NLV:
                            for i, n in enumerate(grp):
                                sc.op("pe", lambda e, i=i, a=a: e.matmul(pq(R0, i), NTb[a][:, i, :], Nb[a][:, i, :], start=True, stop=True),
                                      reads=[("NTb", si, a), ("Nb", si, a)], writes=[("ps", R0)])
                        for i, n in enumerate(grp):
                            sc.op("pe", lambda e, i=i, a=a: e.matmul(pq(R1, i), Nb[a][:, i, :], NTb[a][:, i, :], start=True, stop=True),
                                  reads=[("NTb", si, a), ("Nb", si, a)], writes=[("ps", R1)])
                        yield
                        if lv < NLV:
                            sc.op("dve", lambda e, bb=bb: e.tensor_copy(out=Nb[bb][:, 0:G, :], in_=b4(R0)), reads=[("ps", R0)], writes=[("Nb", si, bb)])
                        sc.op("act", lambda e, bb=bb: e.copy(out=NTb[bb][:, 0:G, :], in_=b4(R1)), reads=[("ps", R1)], writes=[("NTb", si, bb)])
                        for i, n in enumerate(grp):
                            sc.op("pe", lambda e, i=i, a=a, bb=bb: e.matmul(pq(R2, i), NTb[bb][:, i, :], Ub[a][:, i, :], start=True, stop=True),
                                  reads=[("NTb", si, bb), ("Ub", si, a)], writes=[("ps", R2)])
                        yield
                        if lv < NLV:
                            sc.op("dve", lambda e, a=a, bb=bb: e.tensor_tensor(out=Ub[bb][:, 0:G, :], in0=b4(R2), in1=Ub[a][:, 0:G, :], op=ALU.add),
                                  reads=[("ps", R2), ("Ub", si, a)], writes=[("Ub", si, bb)])
                        else:
                            sc.op("dve", lambda e, a=a: e.tensor_tensor(out=X0f[:, 0:G, :], in0=b4(R2), in1=Ub[a][:, 0:G, :], op=ALU.add),
                                  reads=[("ps", R2), ("Ub", si, a)], writes=[("X0f", si)])
                    for i, n in enumerate(grp):
                        sc.op("pe", lambda e, i=i: e.matmul(pq(R0, i), ImX[:, i, :], X0f[:, i, :], start=True, stop=True),
                              reads=[("NTmI", si), ("X0f", si)], writes=[("ps", R0)])
                    for i, n in enumerate(grp):
                        sc.op("pe", lambda e, i=i: e.transpose(out=pq(R1, i), in_=X0f[:, i, :], identity=ident_f),
                              reads=[("X0f", si), "ident_f"], writes=[("ps", R1)])
                    yield
                    sc.op("dve", lambda e: e.tensor_tensor(out=E0f[:, 0:G, :], in0=b4(R0), in1=ident4[:, 0:G, :], op=ALU.add),
                          reads=[("ps", R0)] + id4k, writes=[("E0f", si)])
                    sc.op("act", lambda e: e.copy(out=X0T[:, 0:G, :], in_=b4(R1)), reads=[("ps", R1)], writes=[("X0T", si)])
                    for i, n in enumerate(grp):
                        sc.op("pe", lambda e, i=i: e.matmul(pq(R2, i), X0T[:, i, :], E0f[:, i, :], start=True, stop=True),
                              reads=[("X0T", si), ("E0f", si)], writes=[("ps", R2)])
                    yield
                    sc.op("dve", lambda e: e.tensor_tensor(out=U7t[hb][:, g0:g0 + G, :], in0=b4(R2), in1=X0f[:, 0:G, :], op=ALU.add),
                          reads=[("ps", R2), ("X0f", si)], writes=[("U7", hb, n) for n in grp])
                    yield
                    for i, n in enumerate(grp):
                        sc.op("pe", lambda e, i=i, n=n: e.matmul(pq(R0, i), Xh4[:, i, :], U7[hb][n], start=True, stop=True),
                              reads=[("Xh", si, i), ("U7", hb, n)], writes=[("ps", R0)])
                    sc.op("act", lambda e: e.activation(out=nWTt[hb][:, g0:g0 + G, :], in_=b4(R0), func=AF.Copy, scale=-1.0),
                          reads=[("ps", R0)], writes=[("nWT", hb, n) for n in grp])

                def chunkfn(n):
                    cs = n * 128
                    Sf = S_p[hb] if n < 16 else S_s[hb]
                    skey = ("S", hb, 0 if n < 16 else 1)
                    if n == 0:
                        sc.op("pool", lambda e, Sf=Sf: e.memset(Sf, 0.0), writes=[skey])
                        sc.op("pool", lambda e, hb=hb: e.memset(Sb[hb], 0.0), writes=[("Sb", hb)])
                    if n == 16:
                        sc.op("sp", lambda e, Sf=Sf, h=h: e.dma_start(out=Sf, in_=st_S[j, h]), writes=[skey], dma=True)
                        sc.op("act", lambda e, Sf=Sf, hb=hb: e.copy(out=Sb[hb], in_=Sf), reads=[skey], writes=[("Sb", hb)])
                    bq = bcnt[0] % 2
                    bcnt[0] += 1
                    B0 = 6
                    B1 = B0 + 1
                    sc.op("pe", lambda e, n=n, hb=hb, B0=B0: e.matmul(pq(B0, 0), U7[hb][n], vtk[hb][n], start=True, stop=False),
                          reads=[("U7", hb, n), ("vtk", hb, n)], writes=[("pq", B0, 0)])
                    sc.op("pe", lambda e, n=n, hb=hb, B0=B0: e.matmul(pq(B0, 0), nWT[hb][n], Sb[hb], start=False, stop=True),
                          reads=[("nWT", hb, n), ("Sb", hb)], writes=[("pq", B0, 0)])
                    yield
                    sc.op("act", lambda e, n=n, h=h, bq=bq, B0=B0: e.activation(out=vnew[bq], in_=pq(B0, 0), func=AF.Copy, scale=bTok[:, n, h:h + 1]),
                          reads=[("pq", B0, 0)] + btk, writes=[("vnew", bq)])
                    sc.op("pe", lambda e, cs=cs, hb=hb, B0=B0, qT_h=qT_h: e.matmul(pq(B0, 1), qT_h[:, cs:cs + 128], Sb[hb], start=True, stop=True),
                          reads=[qk_, ("Sb", hb)], writes=[("pq", B0, 1)])
                    sc.op("pe", lambda e, n=n, hb=hb, bq=bq, B0=B0: e.matmul(pq(B0, 2), AT[hb][n], vnew[bq], start=True, stop=True),
                          reads=[("AT", hb, n), ("vnew", bq)], writes=[("pq", B0, 2)])
                    sc.op("pe", lambda e, n=n, hb=hb, bq=bq, B1=B1: e.matmul(pq(B1, 0), kdt[hb][n], vnew[bq], start=True, stop=True),
                          reads=[("kdt", hb, n), ("vnew", bq)], writes=[("pq", B1, 0)])
                    yield
                    sc.op("act", lambda e, n=n, h=h, bq=bq, B0=B0: e.activation(out=otmp[bq], in_=pq(B0, 1), func=AF.Copy, scale=egc[:, n, h:h + 1]),
                          reads=[("pq", B0, 1), "egc"], writes=[("otmp", bq)])
                    sc.op("dve", lambda e, bq=bq, B0=B0: e.tensor_tensor(out=otok[bq], in0=otmp[bq], in1=pq(B0, 2), op=ALU.add),
                          reads=[("otmp", bq), ("pq", B0, 2)], writes=[("otok", bq)])
                    sc.op("dve", lambda e, n=n, h=h, Sf=Sf, B1=B1: e.scalar_tensor_tensor(out=Sf, in0=Sf, scalar=glb[:, n, h:h + 1], in1=pq(B1, 0),
                                                                                   op0=ALU.mult, op1=ALU.add),
                          reads=[skey, ("pq", B1, 0), "glb"], writes=[skey])
                    sc.op("act", lambda e, Sf=Sf, hb=hb: e.copy(out=Sb[hb], in_=Sf), reads=[skey], writes=[("Sb", hb)])
                    if n == 15:
                        sc.op("sp", lambda e, Sf=Sf, h=h: e.dma_start(out=o_S_p[j, h], in_=Sf), reads=[skey], writes=[("oSp", h)], dma=True)
                    if n == 16:
                        sc.op("sp", lambda e, Sf=Sf, h=h: e.dma_start(out=o_S_s[j, h], in_=Sf), reads=[skey], writes=[("oSs", h)], dma=True)
                    yield
                    sc.op("act", lambda e, bq=bq: e.activation(out=osq[bq], in_=otok[bq], func=AF.Square, accum_out=ssq[bq]),
                          reads=[("otok", bq)], writes=[("ssq", bq), ("osq", bq)])
                    sc.op("act", lambda e, bq=bq: e.activation(out=ssq[bq], in_=ssq[bq], func=AF.Ln, scale=1.0 / 128, bias=EPS),
                          reads=[("ssq", bq)], writes=[("ssq", bq)])
                    sc.op("act", lambda e, bq=bq: e.activation(out=ssq[bq], in_=ssq[bq], func=AF.Exp, scale=-0.5),
                          reads=[("ssq", bq)], writes=[("ssq", bq)])
                    sc.op("dve", lambda e, bq=bq: e.tensor_scalar(out=onb[bq], in0=otok[bq], scalar1=ssq[bq][:, 0:1], scalar2=None, op0=ALU.mult),
                          reads=[("otok", bq), ("ssq", bq)], writes=[("onb", bq)])
                    sc.op("pe", lambda e, bq=bq, B1=B1: e.transpose(out=pqb(B1, 1), in_=onb[bq], identity=ident_b),
                          reads=[("onb", bq), "ident_b"], writes=[("pq", B1, 1)])
                    yield
                    sc.op("dve", lambda e, cs=cs, hb=hb, B1=B1, zs_h=zs_h: e.scalar_tensor_tensor(
                        out=ogT[hb][:, cs:cs + 128], in0=pqb(B1, 1), scalar=gnw[:, 0:1], in1=zs_h[:, cs:cs + 128], op0=ALU.mult, op1=ALU.mult),
                        reads=[("pq", B1, 1), "gnw", zk_], writes=[("ogT", hb, n)])
                    if n == NG - 1:
                        sc.op("sp", lambda e, hb=hb, h=h: e.dma_start(out=oT[h * 128:(h + 1) * 128, :], in_=ogT[hb]),
                              reads=[("ogT", hb, n) for n in range(NG)], writes=[("oT", h)], dma=True)
                return setup, groupfn, chunkfn

            heads = [do_head(h) for h in range(NV)]
            g0s = list(range(0, NG, GR))

            def bgen(hd, chunks):
                for n in chunks:
                    yield from hd[2](n)

            def run_pair(hd, g0, bg):
                gens = []
                if hd is not None:
                    gens.append(hd[1](g0, 0))
                    if g0 + SG < NG:
                        gens.append(hd[1](g0 + SG, 1))
                live = list(gens)
                blive = bg is not None
                while live or blive:
                    for g in list(live):
                        try:
                            next(g)
                        except StopIteration:
                            live.remove(g)
                    for _ in range(2):
                        if blive:
                            try:
                                next(bg)
                            except StopIteration:
                                blive = False
            heads[0][0]()
            for g0 in g0s:
                run_pair(heads[0], g0, None)
            for h in range(NV):
                nxt = heads[h + 1] if h + 1 < NV else None
                if nxt is not None:
                    nxt[0]()
                for g0 in g0s:
                    run_pair(nxt, g0, bgen(heads[h], range(g0, min(NG, g0 + GR))))
            sc.barrier()

            if os.environ.get("GDN_STOP") == "4":
                sc.barrier()
                return
            ar.off = persist_off
            Wo = gdn_w_out[j]
            osb = ar.alloc(BF16, [128, 32, 1040])
            wslot = [ar.alloc(BF16, [128, 8192]) for _ in range(2)]
            xres = [ar.alloc(F32, [128, 512]) for _ in range(2)]
            oTv = oT.rearrange("(kc p) t -> p kc t", p=128)
            wcnt = 0
            for blk in (TILES[0:2], TILES[2:5]):
                hc0 = blk[0][0]
                ntok = sum(n for _, n in blk)
                for kc in range(32):
                    sc.op("sp", lambda e, kc=kc, hc0=hc0, ntok=ntok: e.dma_start(out=osb[:, kc, 0:ntok], in_=oTv[:, kc, hc0:hc0 + ntok]),
                          writes=[("osb", kc)], dma=True)
                it = 0
                for mp in range(8):
                    slot = wcnt % 2
                    wcnt += 1
                    vw = wslot[slot].rearrange("p (a b) -> p a b", b=256)
                    srcw = Wo[:, mp * 256:(mp + 1) * 256].rearrange("(kc p) m -> p kc m", p=128)
                    sc.op("pool", lambda e, vw=vw, srcw=srcw: e.dma_start(out=vw, in_=srcw), writes=[("w", slot)], dma=True)
                    for mi in range(2):
                        mc = mp * 2 + mi
                        for (c0, n) in blk:
                            bk = it % 6
                            xb = it % 2
                            it += 1
                            sc.op("sp", lambda e, xb=xb, mc=mc, c0=c0, n=n: e.dma_start(out=xres[xb][:, 0:n], in_=xTv[:, mc, c0:c0 + n]),
                                  writes=[("xres", xb)], dma=True)
                            for kc in range(32):
                                sc.op("pe", lambda e, vw=vw, kc=kc, mi=mi, c0=c0, n=n, bk=bk, hc0=hc0: e.matmul(
                                    bank(bk)[:, 0:n], vw[:, kc, mi * 128:(mi + 1) * 128], osb[:, kc, c0 - hc0:c0 - hc0 + n],
                                    start=(kc == 0), stop=(kc == 31)),
                                    reads=[("w", slot), ("osb", kc)], writes=[("ps", bk)])
                            sc.op("dve", lambda e, xb=xb, bk=bk, n=n: e.tensor_tensor(
                                out=xres[xb][:, 0:n], in0=xres[xb][:, 0:n], in1=bank(bk)[:, 0:n], op=ALU.add),
                                reads=[("xres", xb), ("ps", bk)], writes=[("xres", xb)])
                            sc.op("sp", lambda e, xb=xb, mc=mc, c0=c0, n=n: e.dma_start(out=xTv[:, mc, c0:c0 + n], in_=xres[xb][:, 0:n]),
                                  reads=[("xres", xb)], writes=[("xTo", mc, c0)], dma=True)
                sc.barrier()

        def phase_final():
            ar.reset()
            tmp_x = [ar.alloc(F32, [128, KC, 512]) for _ in range(2)]
            tmp_sq = [ar.alloc(F32, [128, KC, 512])] * 2
            tmp_r = [ar.alloc(F32, [128, 512]) for _ in range(2)]
            yT = ar.alloc(F32, [128, KC, 512])
            yo = [ar.alloc(F32, [128, D]) for _ in range(2)]
            xTv = xT.rearrange("(kc p) t -> p kc t", p=128)
            cnt = 0
            for ti, (c0, n) in enumerate(TILES):
                b = ti % 2
                xt, sq, rr = tmp_x[b], tmp_sq[b], tmp_r[b]
                sc.op("sp", lambda e, xt=xt, c0=c0, n=n: e.dma_start(out=xt[:, :, 0:n], in_=xTv[:, :, c0:c0 + n]),
                      writes=[("nx", b)], dma=True)
                sc.op("act", lambda e, xt=xt, sq=sq, n=n: e.activation(out=sq[:, :, 0:n], in_=xt[:, :, 0:n], func=AF.Square),
                      reads=[("nx", b)], writes=[("nsq", 0)])
                bk = 6 + b
                for kc in range(KC):
                    sc.op("pe", lambda e, sq=sq, kc=kc, n=n, bk=bk: e.matmul(
                        bank(bk)[:, 0:n], ones_f, sq[:, kc, 0:n], start=(kc == 0), stop=(kc == KC - 1)),
                        reads=[("nsq", 0), "ones_f"], writes=[("ps", bk)])
                sc.op("act", lambda e, rr=rr, n=n, bk=bk: e.activation(out=rr[:, 0:n], in_=bank(bk)[:, 0:n], func=AF.Ln,
                                                                      scale=1.0 / D, bias=EPS),
                      reads=[("ps", bk)], writes=[("nr", b)])
                sc.op("act", lambda e, rr=rr, n=n: e.activation(out=rr[:, 0:n], in_=rr[:, 0:n], func=AF.Exp, scale=-0.5),
                      reads=[("nr", b)], writes=[("nr", b)])
                for kc in range(KC):
                    sc.op("dve", lambda e, xt=xt, rr=rr, kc=kc, n=n: e.scalar_tensor_tensor(
                        out=yT[:, kc, 0:n], in0=xt[:, kc, 0:n], scalar=nfin[:, kc, 0:1],
                        in1=rr[:, 0:n], op0=ALU.mult, op1=ALU.mult),
                        reads=[("nx", b), ("nr", b)], writes=[("yT", kc)])
                ngr = (n + 127) // 128
                for gi in range(ngr):
                    rows = min(128, n - gi * 128)
                    ob = cnt % 2
                    cnt += 1
                    for q in range(4):
                        bk2 = q
                        for i in range(4):
                            kc = q * 4 + i
                            sc.op("pe", lambda e, kc=kc, gi=gi, rows=rows, bk2=bk2, i=i: e.transpose(
                                out=bank(bk2)[0:rows, i * 128:(i + 1) * 128], in_=yT[:, kc, gi * 128:gi * 128 + rows],
                                identity=ident_f),
                                reads=[("yT", kc), "ident_f"], writes=[("ps", bk2)])
                        if q % 2 == 0:
                            sc.op("act", lambda e, ob=ob, q=q, rows=rows, bk2=bk2: e.copy(
                                out=yo[ob][0:rows, q * 512:(q + 1) * 512], in_=bank(bk2)[0:rows, :]),
                                reads=[("ps", bk2)], writes=[("yo", ob, q)])
                        else:
                            sc.op("dve", lambda e, ob=ob, q=q, rows=rows, bk2=bk2: e.tensor_copy(
                                out=yo[ob][0:rows, q * 512:(q + 1) * 512], in_=bank(bk2)[0:rows, :]),
                                reads=[("ps", bk2)], writes=[("yo", ob, q)])
                    t0 = c0 + gi * 128
                    dst = y_p[t0:t0 + rows, :] if t0 < TP else y_s
                    sc.op("sp", lambda e, ob=ob, rows=rows, dst=dst: e.dma_start(out=dst, in_=yo[ob][0:rows, :]),
                          reads=[("yo", ob, q) for q in range(4)], writes=[("y", t0)], dma=True)
            sc.barrier()

        phase_input()
        for s in stages:
            if s.startswith("ffn"):
                phase_ffn(int(s[3:]))
            elif s.startswith("pool"):
                phase_pool(int(s[4:]))
            elif s.startswith("gdn"):
                phase_gdn(int(s[3:]))
        phase_final()
        sc.finalize(st)
    return nc, sc


_CACHE = {}


def kernel(**inputs):
    f32 = lambda a: np.ascontiguousarray(np.asarray(a, dtype=np.float32))
    if "nc" not in _CACHE:
        _CACHE["nc"] = build()[0]
    nc = _CACHE["nc"]
    shared = {k: f32(inputs[k]) for k in ("norm_mix_w", "norm_ffn_w", "gdn_w_in", "gdn_conv_w", "gdn_A_log",
                                          "gdn_dt_bias", "gdn_norm_w", "gdn_w_out", "pool_w", "pool_scale",
                                          "ffn_w_gu", "ffn_w_down")}
    shared["final_norm_w"] = f32(inputs["final_norm_w"]).reshape(1, D)
    xp, xs = f32(inputs["x_prompt"]), f32(inputs["x_sample"])
    sconv, sS, spool = f32(inputs["state_gdn_conv"]), f32(inputs["state_gdn_S"]), f32(inputs["state_pool"])
    in_maps = []
    for b in range(8):
        m = dict(shared)
        m["x_p"] = xp[b]
        m["x_s"] = xs[b]
        m["st_conv"] = np.ascontiguousarray(sconv[:, b])
        m["st_S"] = np.ascontiguousarray(sS[:, b])
        m["st_pool"] = np.ascontiguousarray(spool[:, b])
        in_maps.append(m)
    res = run_bass_kernel_spmd(nc, in_maps, core_ids=list(range(8)))
    r = res.results
    stack = lambda k, ax: np.stack([np.asarray(r[b][k], dtype=np.float32) for b in range(8)], axis=ax)
    return (stack("y_p", 0), stack("y_s", 0), stack("o_conv_p", 1), stack("o_S_p", 1), stack("o_pool_p", 1),
            stack("o_conv_s", 1), stack("o_S_s", 1), stack("o_pool_s", 1))
```

```python
import math
import os
from contextlib import ExitStack
import numpy as np
import concourse.bass as bass
import concourse.mybir as mybir
from concourse.alu_op_type import AluOpType as ALU
from concourse.bass_utils import run_bass_kernel_spmd

F32 = mybir.dt.float32
BF16 = mybir.dt.bfloat16
AF = mybir.ActivationFunctionType

ENGS = ("pe", "dve", "act", "pool", "sp")

D = 2048
KC = 16
TP = 2048
TS = 16
TR = TP + TS
TA = TP + 128
NG = TA // 128
NV = 32
QKV = 8192
VAL = 4096
IN_DIM = 12352
FH = 5632
FC = FH // 128
EPS = 1e-6
TILES = [(0, 512), (512, 512), (1024, 512), (1536, 512), (2048, 16)]
NEG = -1.0e30


class Sched:
    def __init__(self, nc, n_lanes=8):
        self.nc = nc
        self.ops = []
        self.last_w = {}
        self.readers = {}
        self.n_lanes = n_lanes
        self.implicit = {"pe"}

    def op(self, eng, fn, reads=(), writes=(), dma=False):
        nk = lambda k: ("ps", k[1]) if (isinstance(k, tuple) and k and k[0] == "pq") else k
        reads = [nk(k) for k in reads] + ["ALL"]
        writes = [nk(k) for k in writes]
        deps = set()
        for k in reads:
            w = self.last_w.get(k)
            if w is not None:
                deps.add(w)
            if isinstance(k, tuple) and k and k[0] == "ps":
                for r in self.readers.get(k, ()):
                    if self.ops[r]["eng"] != eng:
                        deps.add(r)
        for k in writes:
            w = self.last_w.get(k)
            if w is not None:
                deps.add(w)
            for r in self.readers.get(k, ()):
                deps.add(r)
        idx = len(self.ops)
        self.ops.append(dict(eng=eng, fn=fn, deps=deps, dma=dma))
        for k in reads:
            self.readers.setdefault(k, []).append(idx)
        for k in writes:
            self.last_w[k] = idx
            self.readers[k] = []
        return idx

    def barrier(self):
        for e in ("sp", "pe", "dve", "act", "pool"):
            self.op(e, lambda h: h.nop(), writes=["ALL"])

    def finalize(self, stack):
        nc = self.nc
        ops = self.ops
        needed = set()
        for o in ops:
            if o["eng"] in self.implicit:
                o["deps"] = {d for d in o["deps"] if ops[d]["eng"] != o["eng"] or ops[d]["dma"]}
            needed |= o["deps"]
        csem = {e: stack.enter_context(nc.semaphore(f"c_{e}")) for e in ENGS}
        lanes = {e: [stack.enter_context(nc.semaphore(f"l_{e}_{i}")) for i in range(self.n_lanes)]
                 for e in ("sp", "pool")}
        ccount = {e: 0 for e in ENGS}
        lane_cnt = {e: [0] * self.n_lanes for e in lanes}
        lane_rr = {e: 0 for e in lanes}
        token = [None] * len(ops)
        known = {e: {} for e in ENGS}
        streams = {e: [] for e in ENGS}
        for i, o in enumerate(ops):
            e = o["eng"]
            waits = {}
            for d in o["deps"]:
                s, v = token[d]
                key = id(s)
                if known[e].get(key, 0) >= v:
                    continue
                if key not in waits or waits[key][1] < v:
                    waits[key] = (s, v)
            inc = None
            if o["dma"]:
                ln = lane_rr[e]
                lane_rr[e] = (ln + 1) % self.n_lanes
                s = lanes[e][ln]
                prev = lane_cnt[e][ln]
                if prev > 0 and known[e].get(id(s), 0) < prev:
                    key = id(s)
                    if key not in waits or waits[key][1] < prev:
                        waits[key] = (s, prev)
                lane_cnt[e][ln] = prev + 16
                token[i] = (s, prev + 16)
                inc = (s, 16)
            else:
                if i in needed:
                    ccount[e] += 1
                    token[i] = (csem[e], ccount[e])
                    inc = (csem[e], 1)
                else:
                    token[i] = (csem[e], ccount[e] + 1)
            for key, (s, v) in waits.items():
                known[e][key] = v
            streams[e].append((list(waits.values()), o["fn"], inc))
        self.stats = {e: len(streams[e]) for e in ENGS}
        final_waits = []
        for e in lanes:
            for ln in range(self.n_lanes):
                if lane_cnt[e][ln] > 0:
                    final_waits.append((lanes[e][ln], lane_cnt[e][ln]))
        for e in ENGS:
            if ccount[e] > 0:
                final_waits.append((csem[e], ccount[e]))
        with nc.Block() as block:
            def mk(e):
                def body(engh):
                    for waits, fn, inc in streams[e]:
                        for s, v in waits:
                            engh.wait_ge(s, v)
                        ins = fn(engh)
                        if inc is not None:
                            ins.then_inc(inc[0], inc[1])
                    if e == "sp":
                        for s, v in final_waits:
                            engh.wait_ge(s, v)
                return body
            block.tensor(mk("pe"))
            block.vector(mk("dve"))
            block.scalar(mk("act"))
            block.gpsimd(mk("pool"))
            block.sync(mk("sp"))


class Arena:
    def __init__(self, A, nbytes):
        self.A = A
        self.size = nbytes
        self.base = 0
        self.off = 0

    def alloc(self, dt, shape, name=None):
        esz = 4 if dt == F32 else 2
        free = 1
        for s in shape[1:]:
            free *= s
        nb = free * esz
        off = self.off
        self.off += (nb + 63) // 64 * 64
        assert self.off <= self.size, f"arena overflow {self.off} > {self.size} ({name})"
        ap = self.A[0:shape[0], off // 4:(off + nb + 3) // 4]
        if dt != F32:
            ap = ap.bitcast(dt)
        if len(shape) == 3:
            ap = ap.rearrange("p (a b) -> p a b", b=shape[2])
        elif len(shape) == 4:
            ap = ap.rearrange("p (a b c) -> p a b c", b=shape[2], c=shape[3])
        return ap

    def view(self, off, dt, shape):
        save = self.off
        self.off = off
        ap = self.alloc(dt, shape)
        self.off = save
        return ap

    def mark(self):
        self.base = self.off

    def reset(self):
        self.off = self.base


def build(stages=None, debug=False):
    nc = bass.Bass("TRN2", target_bir_lowering=False)

    def din(name, shape):
        return nc.dram_tensor(name, list(shape), F32, kind="ExternalInput").ap()

    def dout(name, shape):
        return nc.dram_tensor(name, list(shape), F32, kind="ExternalOutput").ap()

    def dscr(name, shape, dt):
        return nc.dram_tensor(name, list(shape), dt, kind="Internal").ap()

    x_p = din("x_p", [TP, D])
    x_s = din("x_s", [TS, D])
    st_conv = din("st_conv", [2, 3, QKV])
    st_S = din("st_S", [2, NV, 128, 128])
    st_pool = din("st_pool", [2, 15, D])
    norm_mix_w = din("norm_mix_w", [4, D])
    norm_ffn_w = din("norm_ffn_w", [4, D])
    final_norm_w = din("final_norm_w", [1, D])
    gdn_w_in = din("gdn_w_in", [2, D, IN_DIM])
    gdn_conv_w = din("gdn_conv_w", [2, 4, QKV])
    gdn_A_log = din("gdn_A_log", [2, NV])
    gdn_dt_bias = din("gdn_dt_bias", [2, NV])
    gdn_norm_w = din("gdn_norm_w", [2, 128])
    gdn_w_out = din("gdn_w_out", [2, VAL, D])
    pool_w = din("pool_w", [2, 4, 512, 512])
    pool_scale = din("pool_scale", [2, D])
    ffn_w_gu = din("ffn_w_gu", [4, D, 2 * FH])
    ffn_w_down = din("ffn_w_down", [4, FH, D])

    y_p = dout("y_p", [TP, D])
    y_s = dout("y_s", [TS, D])
    o_conv_p = dout("o_conv_p", [2, 3, QKV])
    o_S_p = dout("o_S_p", [2, NV, 128, 128])
    o_pool_p = dout("o_pool_p", [2, 15, D])
    o_conv_s = dout("o_conv_s", [2, 3, QKV])
    o_S_s = dout("o_S_s", [2, NV, 128, 128])
    o_pool_s = dout("o_pool_s", [2, 15, D])

    xT = dscr("xT", [D, TA], F32)
    qkvT = dscr("qkvT", [QKV, TA], BF16)
    zT = dscr("zT", [VAL, TA], BF16)
    oT = dscr("oT", [VAL, TA], BF16)

    sc = Sched(nc)
    if stages is None:
        stages = ["gdn0", "ffn0", "pool1", "ffn1", "gdn2", "ffn2", "pool3", "ffn3"]

    with ExitStack() as st:
        ARENA_BYTES = 176 * 1024
        A = st.enter_context(nc.sbuf_tensor("arena", [128, ARENA_BYTES // 4], F32))
        PS = st.enter_context(nc.psum_tensor("psum", [128, 8, 512], F32))
        ar = Arena(A, ARENA_BYTES)

        def bank(i):
            return PS[:, i, :]

        ident_f = ar.alloc(F32, [128, 128])
        ident_b = ar.alloc(BF16, [128, 128])
        ones_f = ar.alloc(F32, [128, 128])
        ones_b = ar.alloc(BF16, [128, 128])
        mneg_b = ar.alloc(BF16, [128, 128])
        zero_f = ar.alloc(F32, [128, 128])
        nmix = ar.alloc(F32, [128, KC, 4])
        nffn = ar.alloc(F32, [128, KC, 4])
        nfin = ar.alloc(F32, [128, KC, 1])
        pscale = ar.alloc(F32, [128, KC, 2])

        sc.op("pool", lambda e: e.memset(ones_f, 1.0), writes=["ones_f"])
        sc.op("pool", lambda e: e.memset(ones_b, 1.0), writes=["ones_b"])
        sc.op("pool", lambda e: e.memset(zero_f, 0.0), writes=["zero_f"])
        sc.op("pool", lambda e: e.affine_select(out=ident_f, in_=ones_f, pattern=[[1, 128]],
                                                compare_op=ALU.is_equal, fill=0.0, base=0,
                                                channel_multiplier=-1),
              reads=["ones_f"], writes=["ident_f"])
        sc.op("pool", lambda e: e.tensor_copy(out=ident_b, in_=ident_f), reads=["ident_f"], writes=["ident_b"])
        sc.op("pool", lambda e: e.affine_select(out=mneg_b, in_=zero_f, pattern=[[1, 128]],
                                                compare_op=ALU.is_ge, fill=NEG, base=0,
                                                channel_multiplier=-1),
              reads=["zero_f"], writes=["mneg_b"])

        rowbuf_box = [None]

        def vec_to_cols(src, R, N, dst, bankno=7):
            nchunk = N // 128
            rowbuf = rowbuf_box[0]
            sc.op("sp", lambda e: e.dma_start(out=rowbuf[0:R, 0:N], in_=src), writes=["rowbuf"], dma=True)
            per = 512 // R
            c = 0
            while c < nchunk:
                m = min(per, nchunk - c)
                pv = bank(bankno)[:, 0:m * R].rearrange("p (a b) -> p a b", b=R)
                for i in range(m):
                    sc.op("pe", lambda e, c=c, i=i, pv=pv: e.transpose(
                        out=pv[:, i, :], in_=rowbuf[0:R, (c + i) * 128:(c + i + 1) * 128], identity=ident_f[0:R, 0:R]),
                        reads=["rowbuf", "ident_f"], writes=[("ps", bankno)])
                sc.op("dve", lambda e, c=c, m=m, pv=pv: e.tensor_copy(out=dst[:, c:c + m, :], in_=pv),
                      reads=[("ps", bankno)], writes=[("cols", id(dst))])
                c += m

        ar.mark()
        rowbuf_box[0] = ar.alloc(F32, [16, QKV])
        vec_to_cols(norm_mix_w, 4, D, nmix)
        vec_to_cols(norm_ffn_w, 4, D, nffn)
        vec_to_cols(final_norm_w, 1, D, nfin)
        vec_to_cols(pool_scale, 2, D, pscale)
        sc.barrier()

        def phase_input():
            ar.reset()
            xin = [ar.alloc(F32, [128, D]) for _ in range(2)]
            xo = [ar.alloc(F32, [128, KC, 128]) for _ in range(2)]
            for g in range(NG):
                b = g % 2
                rows = 128 if g < 16 else TS
                src = x_p[g * 128:(g + 1) * 128, :] if g < 16 else x_s
                sc.op("sp", lambda e, b=b, rows=rows, src=src: e.dma_start(out=xin[b][0:rows, :], in_=src),
                      writes=[("xin", b)], dma=True)
                for q in range(4):
                    bk = (g * 4 + q) % 4
                    for i in range(4):
                        kc = q * 4 + i
                        sc.op("pe", lambda e, b=b, rows=rows, kc=kc, bk=bk, i=i: e.transpose(
                            out=bank(bk)[:, i * 128:i * 128 + rows], in_=xin[b][0:rows, kc * 128:(kc + 1) * 128],
                            identity=ident_f[0:rows, 0:rows]),
                            reads=[("xin", b), "ident_f"], writes=[("ps", bk)])
                    eng = "act" if q % 2 == 0 else "dve"
                    pv = bank(bk).rearrange("p (a b) -> p a b", b=128)[:, :, 0:rows]
                    if eng == "act":
                        sc.op("act", lambda e, b=b, q=q, pv=pv, rows=rows: e.copy(out=xo[b][:, q * 4:(q + 1) * 4, 0:rows], in_=pv),
                              reads=[("ps", bk)], writes=[("xo", b, q)])
                    else:
                        sc.op("dve", lambda e, b=b, q=q, pv=pv, rows=rows: e.tensor_copy(out=xo[b][:, q * 4:(q + 1) * 4, 0:rows], in_=pv),
                              reads=[("ps", bk)], writes=[("xo", b, q)])
                dstv = xT.rearrange("(kc p) t -> p kc t", p=128)[:, :, g * 128:g * 128 + rows]
                sc.op("sp", lambda e, b=b, rows=rows, dstv=dstv: e.dma_start(out=dstv, in_=xo[b][:, :, 0:rows]),
                      reads=[("xo", b, q) for q in range(4)], writes=[("xT", g)], dma=True)
            sc.barrier()

        def norm_tiles(hT, wcols, widx, tiles, tmp_x, tmp_sq, tmp_r, hcol0=0, side=None, hl=None):
            xTv = xT.rearrange("(kc p) t -> p kc t", p=128)
            for ti, (c0, n) in enumerate(tiles):
                b = ti % 2
                xt, sq, rr = tmp_x[b], tmp_sq[b], tmp_r[b]
                kx, kq, kr = ("nx", id(xt)), ("nsq", id(sq)), ("nr", id(rr))
                sc.op("sp", lambda e, xt=xt, c0=c0, n=n: e.dma_start(out=xt[:, :, 0:n], in_=xTv[:, :, c0:c0 + n]),
                      writes=[kx], dma=True)
                sc.op("act", lambda e, xt=xt, sq=sq, n=n: e.activation(out=sq[:, :, 0:n], in_=xt[:, :, 0:n], func=AF.Square),
                      reads=[kx], writes=[kq])
                bk = 6 + b
                for kc in range(KC):
                    sc.op("pe", lambda e, sq=sq, kc=kc, n=n, bk=bk: e.matmul(
                        bank(bk)[:, 0:n], ones_f, sq[:, kc, 0:n], start=(kc == 0), stop=(kc == KC - 1)),
                        reads=[kq, "ones_f"], writes=[("ps", bk)])
                sc.op("act", lambda e, rr=rr, n=n, bk=bk: e.activation(out=rr[:, 0:n], in_=bank(bk)[:, 0:n], func=AF.Ln,
                                                                      scale=1.0 / D, bias=EPS),
                      reads=[("ps", bk)], writes=[kr])
                sc.op("act", lambda e, rr=rr, n=n: e.activation(out=rr[:, 0:n], in_=rr[:, 0:n], func=AF.Exp, scale=-0.5),
                      reads=[kr], writes=[kr])
                for kc in range(KC):
                    sc.op("dve", lambda e, xt=xt, rr=rr, kc=kc, c0=c0, n=n: e.scalar_tensor_tensor(
                        out=hT[:, kc, c0 - hcol0:c0 - hcol0 + n], in0=xt[:, kc, 0:n], scalar=wcols[:, kc, widx:widx + 1],
                        in1=rr[:, 0:n], op0=ALU.mult, op1=ALU.mult),
                        reads=[kx, kr], writes=[("hT", kc, c0)])
                for (ts0, cnt, dcol) in (side or []):
                    if c0 <= ts0 and ts0 + cnt <= c0 + n:
                        for kc in range(KC):
                            sc.op("dve", lambda e, xt=xt, rr=rr, kc=kc, o=ts0 - c0, cnt=cnt, dcol=dcol: e.scalar_tensor_tensor(
                                out=hl[:, kc, dcol:dcol + cnt], in0=xt[:, kc, o:o + cnt], scalar=wcols[:, kc, widx:widx + 1],
                                in1=rr[:, o:o + cnt], op0=ALU.mult, op1=ALU.mult),
                                reads=[kx, kr], writes=[("hl", kc, dcol)])

        def wload(wslot, slot, W, KCn, col0, ncols):
            view = wslot[slot][:, 0:KCn * ncols].rearrange("p (a b) -> p a b", b=ncols)
            src = W[:, col0:col0 + ncols].rearrange("(kc p) m -> p kc m", p=128)
            sc.op("pool", lambda e: e.dma_start(out=view, in_=src), writes=[("w", slot)], dma=True)
            return view

        def phase_ffn(li):
            ar.reset()
            Wgu = ffn_w_gu[li]
            Wd = ffn_w_down[li]
            NB = 1040
            hT = ar.alloc(BF16, [128, KC, NB])
            act_off = ar.off
            act = ar.alloc(BF16, [128, FC, NB])
            wslot = [ar.alloc(BF16, [128, 8192]) for _ in range(2)]
            sgt = [ar.alloc(F32, [128, 512]) for _ in range(2)]
            xres = [ar.alloc(F32, [128, 512]) for _ in range(2)]
            tx = ar.view(act_off, F32, [128, KC, 512])
            tq = ar.view(act_off + 32768, F32, [128, KC, 512])
            tr = [ar.view(act_off + 65536 + i * 2048, F32, [128, 512]) for i in range(2)]
            blocks = [TILES[0:2], TILES[2:5]]
            xTv = xT.rearrange("(kc p) t -> p kc t", p=128)
            wcnt = 0
            for blk in blocks:
                hc0 = blk[0][0]
                norm_tiles(hT, nffn, li, blk, [tx, tx], [tq, tq], tr, hcol0=hc0)
                sc.barrier()
                for jp in range(FC // 2):
                    slot = wcnt % 2
                    wcnt += 1
                    vg = wslot[slot][:, 0:KC * 256].rearrange("p (a b) -> p a b", b=256)
                    vu = wslot[slot][:, KC * 256:KC * 512].rearrange("p (a b) -> p a b", b=256)
                    srcg = Wgu[:, jp * 256:jp * 256 + 256].rearrange("(kc p) m -> p kc m", p=128)
                    srcu = Wgu[:, FH + jp * 256:FH + jp * 256 + 256].rearrange("(kc p) m -> p kc m", p=128)
                    sc.op("pool", lambda e, vg=vg, srcg=srcg: e.dma_start(out=vg, in_=srcg),
                          writes=[("w", slot, 0)], dma=True)
                    sc.op("pool", lambda e, vu=vu, srcu=srcu: e.dma_start(out=vu, in_=srcu),
                          writes=[("w", slot, 1)], dma=True)
                    for jj in range(2):
                        j = jp * 2 + jj
                        for ti, (c0, n) in enumerate(blk):
                            it = j * len(blk) + ti
                            bg = (it % 3) * 2
                            bu = bg + 1
                            for kc in range(KC):
                                sc.op("pe", lambda e, vg=vg, kc=kc, jj=jj, c0=c0, n=n, bg=bg, hc0=hc0: e.matmul(
                                    bank(bg)[:, 0:n], vg[:, kc, jj * 128:(jj + 1) * 128], hT[:, kc, c0 - hc0:c0 - hc0 + n],
                                    start=(kc == 0), stop=(kc == KC - 1)),
                                    reads=[("w", slot, 0), ("hT", kc, c0)], writes=[("ps", bg)])
                            for kc in range(KC):
                                sc.op("pe", lambda e, vu=vu, kc=kc, jj=jj, c0=c0, n=n, bu=bu, hc0=hc0: e.matmul(
                                    bank(bu)[:, 0:n], vu[:, kc, jj * 128:(jj + 1) * 128], hT[:, kc, c0 - hc0:c0 - hc0 + n],
                                    start=(kc == 0), stop=(kc == KC - 1)),
                                    reads=[("w", slot, 1), ("hT", kc, c0)], writes=[("ps", bu)])
                            sb = it % 2
                            sc.op("act", lambda e, sb=sb, bg=bg, n=n: e.activation(out=sgt[sb][:, 0:n], in_=bank(bg)[:, 0:n], func=AF.Silu),
                                  reads=[("ps", bg)], writes=[("sgt", sb)])
                            sc.op("dve", lambda e, sb=sb, bu=bu, n=n, j=j, c0=c0, hc0=hc0: e.tensor_tensor(
                                out=act[:, j, c0 - hc0:c0 - hc0 + n], in0=sgt[sb][:, 0:n], in1=bank(bu)[:, 0:n], op=ALU.mult),
                                reads=[("sgt", sb), ("ps", bu)], writes=[("act", j, c0)])
                for mc in range(KC):
                    slot = wcnt % 2
                    wcnt += 1
                    vw = wslot[slot][:, 0:FC * 128].rearrange("p (a b) -> p a b", b=128)
                    src = Wd[:, mc * 128:(mc + 1) * 128].rearrange("(kc p) m -> p kc m", p=128)
                    sc.op("pool", lambda e, vw=vw, src=src: e.dma_start(out=vw, in_=src),
                          writes=[("w", slot, 0), ("w", slot, 1)], dma=True)
                    for ti, (c0, n) in enumerate(blk):
                        it = mc * len(blk) + ti
                        bk = it % 6
                        xb = it % 2
                        sc.op("sp", lambda e, xb=xb, mc=mc, c0=c0, n=n: e.dma_start(out=xres[xb][:, 0:n], in_=xTv[:, mc, c0:c0 + n]),
                              writes=[("xres", xb)], dma=True)
                        for kc in range(FC):
                            sc.op("pe", lambda e, vw=vw, kc=kc, c0=c0, n=n, bk=bk, hc0=hc0: e.matmul(
                                bank(bk)[:, 0:n], vw[:, kc, :], act[:, kc, c0 - hc0:c0 - hc0 + n],
                                start=(kc == 0), stop=(kc == FC - 1)),
                                reads=[("w", slot, 0), ("w", slot, 1), ("act", kc, c0)], writes=[("ps", bk)])
                        sc.op("dve", lambda e, xb=xb, bk=bk, n=n: e.tensor_tensor(
                            out=xres[xb][:, 0:n], in0=xres[xb][:, 0:n], in1=bank(bk)[:, 0:n], op=ALU.add),
                            reads=[("xres", xb), ("ps", bk)], writes=[("xres", xb)])
                        sc.op("sp", lambda e, xb=xb, mc=mc, c0=c0, n=n: e.dma_start(out=xTv[:, mc, c0:c0 + n], in_=xres[xb][:, 0:n]),
                              reads=[("xres", xb)], writes=[("xTo", mc, c0)], dma=True)
                sc.barrier()

        def phase_pool(li):
            j = li // 2
            ar.reset()
            HP = 2096
            hpad = ar.alloc(BF16, [128, KC, HP])
            hl = ar.alloc(F32, [128, KC, 32])
            icnt = ar.alloc(F32, [128, 16])
            wslot = [ar.alloc(BF16, [128, 2048]) for _ in range(2)]
            xres = [ar.alloc(F32, [128, 512]) for _ in range(2)]
            po = ar.alloc(F32, [16, D])
            big_off = ar.off
            dT = ar.alloc(BF16, [128, KC, TR])
            tA = ar.alloc(F32, [128, HP])
            tB = ar.alloc(F32, [128, HP])
            tC = ar.alloc(F32, [128, 16])
            tx = ar.view(big_off, F32, [128, KC, 512])
            tq = ar.view(big_off + 32768, F32, [128, KC, 512])
            tr = [ar.view(big_off + 65536 + i * 2048, F32, [128, 512]) for i in range(2)]
            rb = ar.view(big_off, F32, [16, D])
            xTv = xT.rearrange("(kc p) t -> p kc t", p=128)
            for t in range(16):
                sc.op("pool", lambda e, t=t: e.memset(icnt[:, t:t + 1], 1.0 / (t + 1)), writes=["icnt"])
            sc.op("pool", lambda e: e.memset(hpad[:, :, 0:16], 0.0), writes=[("hp0",)])
            sc.op("pool", lambda e: e.memset(hpad[:, :, 2064:2065], 0.0), writes=[("hp1",)])
            sc.op("sp", lambda e: e.dma_start(out=rb[0:15, :], in_=st_pool[j]), writes=["rb"], dma=True)
            for q in range(4):
                pv = bank(7)[:, 0:60].rearrange("p (a b) -> p a b", b=15)
                for i in range(4):
                    sc.op("pe", lambda e, q=q, i=i, pv=pv: e.transpose(out=pv[:, i, :], in_=rb[0:15, (q * 4 + i) * 128:(q * 4 + i + 1) * 128],
                                                                 identity=ident_f[0:15, 0:15]),
                          reads=["rb", "ident_f"], writes=[("ps", 7)])
                sc.op("dve", lambda e, q=q, pv=pv: e.tensor_copy(out=hpad[:, q * 4:(q + 1) * 4, 2065:2080], in_=pv),
                      reads=[("ps", 7)], writes=[("hph", q)])
            sc.barrier()
            side = [(2032, 16, 0), (2048, 16, 16)]
            norm_tiles(hpad, nmix, li, TILES[0:4], [tx, tx], [tq, tq], tr, hcol0=-16, side=side, hl=hl)
            norm_tiles(hpad, nmix, li, TILES[4:5], [tx, tx], [tq, tq], tr, hcol0=-32, side=side, hl=hl)
            sc.barrier()
            for (c_lo, dst) in ((1, o_pool_p[j]), (17, o_pool_s[j])):
                for q in range(4):
                    for i in range(4):
                        kc = q * 4 + i
                        sc.op("pe", lambda e, kc=kc, c_lo=c_lo, q=q, i=i: e.transpose(
                            out=bank(q)[0:15, i * 128:(i + 1) * 128], in_=hl[:, kc, c_lo:c_lo + 15], identity=ident_f),
                            reads=[("hl", kc, 0), ("hl", kc, 16), "ident_f"], writes=[("ps", q)])
                    sc.op("act", lambda e, q=q: e.copy(out=po[0:15, q * 512:(q + 1) * 512], in_=bank(q)[0:15, :]),
                          reads=[("ps", q)], writes=[("po", q)])
                sc.op("sp", lambda e, dst=dst: e.dma_start(out=dst, in_=po[0:15, :]), reads=[("po", q) for q in range(4)],
                      writes=[("pout", c_lo)], dma=True)
            for kc in range(KC):
                gi = kc // 4
                w = 2 << gi
                src = hpad[:, kc, :]
                bufs = [tA, tB]
                cur = None
                sh = 1
                for lv in range(gi + 1):
                    dstb = bufs[lv % 2]
                    eng = "dve" if (kc + lv) % 2 == 0 else "pool"
                    a_in = src if cur is None else cur
                    sc.op(eng, lambda e, dstb=dstb, a_in=a_in, sh=sh: e.tensor_tensor(
                        out=dstb[:, sh:HP], in0=a_in[:, sh:HP], in1=a_in[:, 0:HP - sh], op=ALU.add),
                        reads=[("hT", kc, c) for c, _ in TILES] + [("win", id(a_in)), ("hp0",), ("hp1",), ("hph", kc // 4)],
                        writes=[("win", id(dstb))])
                    cur = dstb
                    sh *= 2
                sc.op("dve", lambda e, cur=cur, src=src, w=w, kc=kc: e.scalar_tensor_tensor(
                    out=dT[:, kc, 0:TP], in0=cur[:, 16:16 + TP], scalar=1.0 / w, in1=src[:, 16:16 + TP],
                    op0=ALU.mult, op1=ALU.subtract),
                    reads=[("win", id(cur))] + [("hT", kc, c) for c, _ in TILES], writes=[("dT", kc)])
                sc.op("dve", lambda e, cur=cur, src=src, w=w, kc=kc: e.scalar_tensor_tensor(
                    out=dT[:, kc, TP:TR], in0=cur[:, 2080:2096], scalar=1.0 / w, in1=src[:, 2080:2096],
                    op0=ALU.mult, op1=ALU.subtract),
                    reads=[("win", id(cur))] + [("hT", kc, c) for c, _ in TILES], writes=[("dT", kc)])
                sc.op("dve", lambda e, cur=cur, w=w: e.tensor_tensor(out=tC[:, 0:w - 1], in0=cur[:, 16:16 + w - 1], in1=icnt[:, 0:w - 1], op=ALU.mult),
                      reads=[("win", id(cur)), "icnt"], writes=["tC"])
                sc.op("dve", lambda e, src=src, w=w, kc=kc: e.tensor_tensor(out=dT[:, kc, 0:w - 1], in0=tC[:, 0:w - 1], in1=src[:, 16:16 + w - 1], op=ALU.subtract),
                      reads=["tC"] + [("hT", kc, c) for c, _ in TILES], writes=[("dT", kc)])
            it = 0
            for gi in range(4):
                slot = gi % 2
                vw = wslot[slot].rearrange("p (a b) -> p a b", b=512)
                srcw = pool_w[j, gi].rearrange("(kc p) m -> p kc m", p=128)
                sc.op("pool", lambda e, vw=vw, srcw=srcw: e.dma_start(out=vw, in_=srcw), writes=[("w", slot)], dma=True)
                for ec in range(4):
                    mc = gi * 4 + ec
                    for (c0, n) in TILES:
                        bk = it % 6
                        xb = it % 2
                        it += 1
                        sc.op("sp", lambda e, xb=xb, mc=mc, c0=c0, n=n: e.dma_start(out=xres[xb][:, 0:n], in_=xTv[:, mc, c0:c0 + n]),
                              writes=[("xres", xb)], dma=True)
                        for cc in range(4):
                            sc.op("pe", lambda e, vw=vw, cc=cc, ec=ec, gi=gi, c0=c0, n=n, bk=bk: e.matmul(
                                bank(bk)[:, 0:n], vw[:, cc, ec * 128:(ec + 1) * 128], dT[:, gi * 4 + cc, c0:c0 + n],
                                start=(cc == 0), stop=(cc == 3)),
                                reads=[("w", slot), ("dT", gi * 4 + cc)], writes=[("ps", bk)])
                        sc.op("dve", lambda e, xb=xb, bk=bk, n=n, mc=mc: e.scalar_tensor_tensor(
                            out=xres[xb][:, 0:n], in0=bank(bk)[:, 0:n], scalar=pscale[:, mc, j:j + 1], in1=xres[xb][:, 0:n],
                            op0=ALU.mult, op1=ALU.add),
                            reads=[("xres", xb), ("ps", bk)], writes=[("xres", xb)])
                        sc.op("sp", lambda e, xb=xb, mc=mc, c0=c0, n=n: e.dma_start(out=xTv[:, mc, c0:c0 + n], in_=xres[xb][:, 0:n]),
                              reads=[("xres", xb)], writes=[("xTo", mc, c0)], dma=True)
            sc.barrier()

        def phase_gdn(li):
            j = li // 2
            Win = gdn_w_in[j]
            ar.reset()
            xTv = xT.rearrange("(kc p) t -> p kc t", p=128)
            cw = ar.alloc(F32, [128, 64, 4])
            chs = ar.alloc(F32, [128, 64, 3])
            cst = ar.alloc(F32, [128, 64, 8])
            bT = ar.alloc(F32, [32, TA])
            gT = ar.alloc(F32, [32, TA])
            alog = ar.alloc(F32, [32, 1])
            dtb = ar.alloc(F32, [32, 1])
            nega = ar.alloc(F32, [32, 1])
            gnw = ar.alloc(F32, [128, 1])
            gcTok = ar.alloc(F32, [128, NG, 32])
            nbTok = ar.alloc(F32, [128, NG, 32])
            bTok = ar.alloc(F32, [128, NG, 32])
            egc = ar.alloc(F32, [128, NG, 32])
            ekd = ar.alloc(F32, [128, NG, 32])
            glb = ar.alloc(F32, [128, NG, 32])
            negones = ar.alloc(F32, [32, 128])
            persist_off = ar.off

            rb = ar.alloc(F32, [16, QKV])
            rowbuf_box[0] = rb
            vec_to_cols(gdn_conv_w[j], 4, QKV, cw)
            vec_to_cols(st_conv[j], 3, QKV, chs)
            sc.op("sp", lambda e: e.dma_start(out=alog, in_=gdn_A_log[j].rearrange("(p o) -> p o", o=1)), writes=["alog"], dma=True)
            sc.op("sp", lambda e: e.dma_start(out=dtb, in_=gdn_dt_bias[j].rearrange("(p o) -> p o", o=1)), writes=["dtb"], dma=True)
            sc.op("sp", lambda e: e.dma_start(out=gnw, in_=gdn_norm_w[j].rearrange("(p o) -> p o", o=1)), writes=["gnw"], dma=True)
            sc.op("pool", lambda e: e.memset(negones, -1.0), writes=["negones"])
            sc.op("pool", lambda e: e.memset(cst, 0.0), writes=[("cst", mc) for mc in range(64)])
            sc.barrier()

            ar.off = persist_off
            hT = ar.alloc(BF16, [128, KC, TR])
            g2_off = ar.off
            tx = ar.alloc(F32, [128, KC, 512])
            tq = ar.alloc(F32, [128, KC, 512])
            tr = [ar.alloc(F32, [128, 512]) for _ in range(2)]
            norm_tiles(hT, nmix, li, TILES, [tx, tx], [tq, tq], tr)
            sc.barrier()

            if os.environ.get("GDN_STOP") == "1":
                sc.barrier()
                return
            ar.off = g2_off
            wslot = [ar.alloc(BF16, [128, 4096]) for _ in range(2)]
            PW = 2070
            pre = [ar.alloc(F32, [128, PW]) for _ in range(2)]
            acc = ar.alloc(F32, [128, PW])
            sil = ar.alloc(F32, [128, PW])
            sqb = ar.alloc(BF16, [128, PW])
            vout = [ar.alloc(BF16, [128, TA]) for _ in range(2)]
            rs = acc
            for b in range(2):
                sc.op("pool", lambda e, b=b: e.memset(vout[b][:, TR:TA], 0.0), writes=[("voutpad", b)])
                sc.op("pool", lambda e, b=b: e.memset(pre[b][:, 0:3], 0.0), writes=[("prepad", b)])
            PCOL = [3, 515, 1027, 1539, 2054]
            nchunks_total = 97
            itc = 0
            qkvp = list(range(32))
            zp = list(range(32, 48))
            porder = []
            for i in range(16):
                porder += [qkvp[2 * i], zp[i], qkvp[2 * i + 1]]
            porder.append(48)
            mc_order = []
            for pi in porder:
                for mc in (2 * pi, 2 * pi + 1):
                    if mc < nchunks_total:
                        mc_order.append(mc)
            wl = 0
            for mc in mc_order:
                if mc % 2 == 0:
                    slot = wl % 2
                    wl += 1
                    ncols = min(256, IN_DIM - mc * 128)
                    wv = wload(wslot, slot, Win, KC, mc * 128, ncols)
                wi = mc % 2
                M = 128 if mc < 96 else 64
                pb = mc % 2
                kind = "q" if mc < 16 else ("k" if mc < 32 else ("v" if mc < 64 else ("z" if mc < 96 else "ba")))
                if kind in ("q", "k", "v"):
                    sc.op("dve", lambda e, pb=pb, mc=mc: e.tensor_copy(out=pre[pb][:, 2051:2054], in_=chs[:, mc, :]),
                          reads=[("cols", id(chs))], writes=[("prehist", pb)])
                for ti, (c0, n) in enumerate(TILES):
                    if kind == "ba":
                        halves = [(0, 32, bT), (32, 64, gT)]
                    else:
                        halves = [(0, M, None)]
                    for (m0, m1, dstT) in halves:
                        bk = itc % 6
                        itc += 1
                        for kc in range(KC):
                            sc.op("pe", lambda e, wv=wv, kc=kc, wi=wi, m0=m0, m1=m1, c0=c0, n=n, bk=bk: e.matmul(
                                bank(bk)[0:m1 - m0, 0:n], wv[:, kc, wi * 128 + m0:wi * 128 + m1], hT[:, kc, c0:c0 + n],
                                start=(kc == 0), stop=(kc == KC - 1)),
                                reads=[("w", slot), ("hT", kc, c0)], writes=[("ps", bk)])
                        if kind in ("q", "k", "v"):
                            sc.op("act", lambda e, pb=pb, ti=ti, n=n, bk=bk: e.copy(out=pre[pb][:, PCOL[ti]:PCOL[ti] + n], in_=bank(bk)[:, 0:n]),
                                  reads=[("ps", bk)], writes=[("pre", pb, ti)])
                        elif kind == "z":
                            zb = mc % 2
                            sc.op("act", lambda e, zb=zb, c0=c0, n=n, bk=bk: e.activation(out=vout[zb][:, c0:c0 + n], in_=bank(bk)[:, 0:n], func=AF.Silu),
                                  reads=[("ps", bk)], writes=[("vout", zb, ti)])
                        else:
                            sc.op("act", lambda e, dstT=dstT, c0=c0, n=n, bk=bk: e.copy(out=dstT[:, c0:c0 + n], in_=bank(bk)[0:32, 0:n]),
                                  reads=[("ps", bk)], writes=[("baT", id(dstT), ti)])
                if kind == "z":
                    zb = mc % 2
                    h = mc - 64
                    sc.op("sp", lambda e, zb=zb, h=h: e.dma_start(out=zT[h * 128:(h + 1) * 128, :], in_=vout[zb]),
                          reads=[("vout", zb, ti) for ti in range(5)] + [("voutpad", zb)], writes=[("zT", h)], dma=True)
                if kind in ("q", "k", "v"):
                    prk = [("pre", pb, ti) for ti in range(5)] + [("prehist", pb), ("prepad", pb)]
                    P = pre[pb]
                    sc.op("dve", lambda e, P=P, mc=mc: e.tensor_copy(out=cst[:, mc, 0:3], in_=P[:, 2048:2051]),
                          reads=prk, writes=[("cst", mc)])
                    sc.op("dve", lambda e, P=P, mc=mc: e.tensor_copy(out=cst[:, mc, 4:7], in_=P[:, 2067:2070]),
                          reads=prk, writes=[("cst", mc)])
                    L = PW - 3
                    sc.op("dve", lambda e, P=P, mc=mc: e.tensor_scalar(out=acc[:, 0:L], in0=P[:, 3:PW], scalar1=cw[:, mc, 3:4], scalar2=None, op0=ALU.mult),
                          reads=prk + [("cols", id(cw))], writes=["acc"])
                    for tap in (2, 1, 0):
                        sc.op("dve", lambda e, P=P, mc=mc, tap=tap: e.scalar_tensor_tensor(
                            out=acc[:, 0:L], in0=P[:, tap:tap + L], scalar=cw[:, mc, tap:tap + 1], in1=acc[:, 0:L],
                            op0=ALU.mult, op1=ALU.add),
                            reads=prk + ["acc"], writes=["acc"])
                    vb = mc % 2
                    if kind == "v":
                        sc.op("act", lambda e, vb=vb: e.activation(out=vout[vb][:, 0:TP], in_=acc[:, 0:TP], func=AF.Silu),
                              reads=["acc"], writes=[("vout", vb, 0)])
                        sc.op("act", lambda e, vb=vb: e.activation(out=vout[vb][:, TP:TR], in_=acc[:, 2051:2067], func=AF.Silu),
                              reads=["acc"], writes=[("vout", vb, 1)])
                    else:
                        sc.op("act", lambda e: e.activation(out=sil[:, 0:L], in_=acc[:, 0:L], func=AF.Silu),
                              reads=["acc"], writes=["sil"])
                        sc.op("act", lambda e: e.activation(out=sqb[:, 0:L], in_=sil[:, 0:L], func=AF.Square),
                              reads=["sil"], writes=["sqb"])
                        c = 0
                        while c < L:
                            n = min(512, L - c)
                            bk = itc % 6
                            itc += 1
                            sc.op("pe", lambda e, c=c, n=n, bk=bk: e.matmul(bank(bk)[:, 0:n], ones_b, sqb[:, c:c + n], start=True, stop=True),
                                  reads=["sqb", "ones_b"], writes=[("ps", bk)])
                            sc.op("act", lambda e, c=c, n=n, bk=bk: e.activation(out=rs[:, c:c + n], in_=bank(bk)[:, 0:n], func=AF.Ln, bias=EPS, scale=1.0),
                                  reads=[("ps", bk), "sil"], writes=["acc"])
                            c += n
                        rsk = ["acc"]
                        sc.op("act", lambda e: e.activation(out=rs[:, 0:L], in_=rs[:, 0:L], func=AF.Exp, scale=-0.5),
                              reads=rsk, writes=rsk)
                        qs = (128.0 ** -0.5) if kind == "q" else 1.0
                        sc.op("dve", lambda e, vb=vb, qs=qs: e.scalar_tensor_tensor(
                            out=vout[vb][:, 0:TP], in0=sil[:, 0:TP], scalar=qs, in1=rs[:, 0:TP], op0=ALU.mult, op1=ALU.mult),
                            reads=["sil"] + rsk, writes=[("vout", vb, 0)])
                        sc.op("dve", lambda e, vb=vb, qs=qs: e.scalar_tensor_tensor(
                            out=vout[vb][:, TP:TR], in0=sil[:, 2051:2067], scalar=qs, in1=rs[:, 2051:2067], op0=ALU.mult, op1=ALU.mult),
                            reads=["sil"] + rsk, writes=[("vout", vb, 1)])
                    sc.op("sp", lambda e, vb=vb, mc=mc: e.dma_start(out=qkvT[mc * 128:(mc + 1) * 128, :], in_=vout[vb]),
                          reads=[("vout", vb, 0), ("vout", vb, 1), ("voutpad", vb)], writes=[("qkvT", mc)], dma=True)
            if os.environ.get("GDN_STOP") == "2a":
                sc.barrier()
                return
            crow_p = acc[0:3, 0:2048]
            crow_s = sil[0:3, 0:2048]
            for r4 in range(4):
                for q in range(4):
                    for i in range(4):
                        mc = r4 * 16 + q * 4 + i
                        sc.op("pe", lambda e, mc=mc, q=q, i=i: e.transpose(out=bank(q)[0:4, i * 128:(i + 1) * 128], in_=cst[:, mc, 0:4], identity=ident_f),
                              reads=[("cst", mc), "ident_f"], writes=[("ps", q)])
                        sc.op("pe", lambda e, mc=mc, q=q, i=i: e.transpose(out=bank(q + 4)[0:4, i * 128:(i + 1) * 128], in_=cst[:, mc, 4:8], identity=ident_f),
                              reads=[("cst", mc), "ident_f"], writes=[("ps", q + 4)])
                    sc.op("act", lambda e, q=q: e.copy(out=crow_p[:, q * 512:(q + 1) * 512], in_=bank(q)[0:3, :]),
                          reads=[("ps", q), "acc"], writes=[("crowp", q), "acc"])
                    sc.op("dve", lambda e, q=q: e.tensor_copy(out=crow_s[:, q * 512:(q + 1) * 512], in_=bank(q + 4)[0:3, :]),
                          reads=[("ps", q + 4), "sil"], writes=[("crows", q), "sil"])
                sc.op("sp", lambda e, r4=r4: e.dma_start(out=o_conv_p[j][:, r4 * 2048:(r4 + 1) * 2048], in_=crow_p),
                      reads=[("crowp", q) for q in range(4)] + ["acc"], writes=[("ocp", r4), "acc"], dma=True)
                sc.op("sp", lambda e, r4=r4: e.dma_start(out=o_conv_s[j][:, r4 * 2048:(r4 + 1) * 2048], in_=crow_s),
                      reads=[("crows", q) for q in range(4)] + ["sil"], writes=[("ocs", r4), "sil"], dma=True)
            sc.barrier()

            if os.environ.get("GDN_STOP") == "2":
                sc.barrier()
                return
            ar.off = persist_off
            Gd = ar.alloc(F32, [32, NG, 32])
            tmpE = ar.alloc(F32, [32, TA])
            bak = [("baT", id(bT), ti) for ti in range(5)]
            gak = [("baT", id(gT), ti) for ti in range(5)]
            sc.op("act", lambda e: e.activation(out=bT[:, 0:TR], in_=bT[:, 0:TR], func=AF.Sigmoid), reads=bak, writes=["bT"])
            sc.op("pool", lambda e: e.memset(bT[:, TR:TA], 0.0), writes=["bTpad"])
            sc.op("act", lambda e: e.activation(out=nega, in_=alog, func=AF.Exp), reads=["alog"], writes=["nega"])
            sc.op("act", lambda e: e.activation(out=tmpE[:, 0:TR], in_=gT[:, 0:TR], func=AF.Exp, bias=dtb[:, 0:1], scale=1.0),
                  reads=gak + ["dtb"], writes=["tmpE"])
            sc.op("act", lambda e: e.activation(out=tmpE[:, 0:TR], in_=tmpE[:, 0:TR], func=AF.Ln, bias=1.0, scale=1.0),
                  reads=["tmpE"], writes=["tmpE"])
            sc.op("dve", lambda e: e.tensor_scalar(out=gT[:, 0:TR], in0=tmpE[:, 0:TR], scalar1=nega[:, 0:1], scalar2=-1.0, op0=ALU.mult, op1=ALU.mult),
                  reads=["tmpE", "nega"] + gak, writes=["gTg"])
            sc.op("pool", lambda e: e.memset(gT[:, TR:TA], 0.0), writes=["gTpad"])
            if os.environ.get("GDN_STOP") == "3a":
                sc.barrier()
                return
            for n in range(NG):
                sc.op("dve", lambda e, n=n: e.tensor_tensor_scan(out=tmpE[:, n * 128:(n + 1) * 128], data0=ones_f[0:32, :], data1=gT[:, n * 128:(n + 1) * 128],
                                                               initial=0.0, op0=ALU.mult, op1=ALU.add),
                      reads=["gTg", "gTpad", "ones_f", "tmpE"], writes=[("gc", n)])
            gck = [("gc", n) for n in range(NG)]
            sc.op("dve", lambda e: e.tensor_copy(out=gT, in_=tmpE), reads=gck + ["gTg", "gTpad"], writes=["gcT"])
            gcT = gT
            if os.environ.get("GDN_STOP") == "3b":
                sc.barrier()
                return
            for n in range(NG):
                sc.op("pe", lambda e, n=n: e.transpose(out=bank(6)[:, 0:32], in_=gcT[:, n * 128:(n + 1) * 128], identity=ident_f[0:32, 0:32]),
                      reads=["gcT", "ident_f"], writes=[("ps", 6)])
                sc.op("dve", lambda e, n=n: e.tensor_copy(out=gcTok[:, n, :], in_=bank(6)[:, 0:32]), reads=[("ps", 6)], writes=[("gcTok", n)])
                sc.op("pe", lambda e, n=n: e.transpose(out=bank(7)[:, 0:32], in_=bT[:, n * 128:(n + 1) * 128], identity=ident_f[0:32, 0:32]),
                      reads=["bT", "bTpad", "ident_f"], writes=[("ps", 7)])
                sc.op("act", lambda e, n=n: e.copy(out=bTok[:, n, :], in_=bank(7)[:, 0:32]), reads=[("ps", 7)], writes=[("bTok", n)])
            gtk = [("gcTok", n) for n in range(NG)]
            btk = [("bTok", n) for n in range(NG)]
            sc.op("dve", lambda e: e.tensor_scalar(out=nbTok, in0=bTok, scalar1=-1.0, scalar2=None, op0=ALU.mult), reads=btk, writes=["nbTok"])
            sc.op("act", lambda e: e.activation(out=egc, in_=gcTok, func=AF.Exp), reads=gtk, writes=["egc"])
            if os.environ.get("GDN_STOP") == "3c":
                sc.barrier()
                return
            gclast = gcT.rearrange("p (n c) -> p n c", c=128)[:, :, 127]
            for h in range(32):
                sc.op("dve", lambda e, h=h: e.tensor_scalar(out=Gd[:, :, h], in0=gclast, scalar1=ident_f[0:32, h:h + 1], scalar2=None, op0=ALU.mult),
                      reads=["gcT", "ident_f"], writes=[("Gd", h)])
            gdk = [("Gd", h) for h in range(32)]
            Gdf = Gd.rearrange("p a b -> p (a b)")
            glbf = glb.rearrange("p a b -> p (a b)")
            ekdf = ekd.rearrange("p a b -> p (a b)")
            gctf = gcTok.rearrange("p a b -> p (a b)")
            for half, (c0, c1) in enumerate(((0, 288), (288, 544))):
                sc.op("pe", lambda e, half=half, c0=c0, c1=c1: e.matmul(bank(6 + half)[:, 0:c1 - c0], ones_f[0:32, :], Gdf[:, c0:c1], start=True, stop=True),
                      reads=gdk + ["ones_f"], writes=[("ps", 6 + half)])
                sc.op("dve", lambda e, half=half, c0=c0, c1=c1: e.tensor_copy(out=glbf[:, c0:c1], in_=bank(6 + half)[:, 0:c1 - c0]),
                      reads=[("ps", 6 + half)], writes=[("glraw", half)])
            if os.environ.get("GDN_STOP") == "3d":
                sc.barrier()
                return
            sc.op("dve", lambda e: e.tensor_tensor(out=ekdf, in0=glbf, in1=gctf, op=ALU.subtract),
                  reads=[("glraw", 0), ("glraw", 1)] + gtk, writes=["ekd"])
            sc.op("act", lambda e: e.activation(out=ekdf, in_=ekdf, func=AF.Exp), reads=["ekd"], writes=["ekdx"])
            sc.op("act", lambda e: e.activation(out=glbf, in_=glbf, func=AF.Exp), reads=[("glraw", 0), ("glraw", 1), "ekd"], writes=["glb"])
            sc.barrier()

            if os.environ.get("GDN_STOP") == "3":
                sc.barrier()
                return
            ar.off = persist_off
            kTq = [[ar.alloc(BF16, [128, TA]) for _ in range(2)] for _ in range(4)]
            ogT = [ar.alloc(BF16, [128, TA]) for _ in range(2)]
            gcm = [ar.alloc(F32, [32, TA])] * 2
            oh = [ar.alloc(F32, [32, 128])] * 2
            S_p = [ar.alloc(F32, [128, 128]) for _ in range(2)]
            S_s = [ar.alloc(F32, [128, 128]) for _ in range(2)]
            Sb = [ar.alloc(BF16, [128, 128]) for _ in range(2)]
            class _L:
                def __init__(self, t):
                    self.t = t

                def __getitem__(self, n):
                    return self.t[:, n, :]
            ATt = [ar.alloc(BF16, [128, NG, 128]) for _ in range(2)]
            U7t = [ar.alloc(BF16, [128, NG, 128]) for _ in range(2)]
            nWTt = [ar.alloc(BF16, [128, NG, 128]) for _ in range(2)]
            kdtt = [ar.alloc(BF16, [128, NG, 128]) for _ in range(2)]
            vtkt = [ar.alloc(BF16, [128, NG, 128]) for _ in range(2)]
            AT, U7, nWT, kdt, vtk = [[_L(t) for t in tt] for tt in (ATt, U7t, nWTt, kdtt, vtkt)]
            GR = 6
            SG = 3
            Dm4s = [ar.alloc(F32, [128, SG, 128]) for _ in range(2)]
            DmS4s = [ar.alloc(F32, [128, SG, 128]) for _ in range(2)]
            Xh4s = [ar.alloc(BF16, [128, SG, 128]) for _ in range(2)]
            N4s = [[ar.alloc(F32, [128, SG, 128]) for _ in range(1)] for _ in range(2)]
            NT4s = [[ar.alloc(F32, [128, SG, 128]) for _ in range(1)] for _ in range(2)]
            Nbs = [[ar.alloc(BF16, [128, SG, 128]) for _ in range(2)] for _ in range(2)]
            NTbs = [[ar.alloc(BF16, [128, SG, 128]) for _ in range(2)] for _ in range(2)]
            Ubs = [[ar.alloc(BF16, [128, SG, 128]) for _ in range(2)] for _ in range(2)]
            X0fs = [ar.alloc(F32, [128, SG, 128]) for _ in range(2)]
            ImXs = [ar.alloc(F32, [128, SG, 128]) for _ in range(2)]
            E0fs = [ar.alloc(F32, [128, SG, 128]) for _ in range(2)]
            X0Ts = [ar.alloc(F32, [128, SG, 128]) for _ in range(2)]
            print("G4 arena bytes", ar.off)
            ident4 = ar.alloc(F32, [128, 4, 128])
            for i in range(4):
                sc.op("pool", lambda e, i=i: e.tensor_copy(out=ident4[:, i, :], in_=ident_f), reads=["ident_f"], writes=[("ident4", i)])
            id4k = [("ident4", i) for i in range(4)]
            vnew = [ar.alloc(BF16, [128, 128]) for _ in range(2)]
            otmp = [ar.alloc(F32, [128, 128]) for _ in range(2)]
            otok = [ar.alloc(F32, [128, 128]) for _ in range(2)]
            osq = [ar.alloc(F32, [128, 128]) for _ in range(2)]
            onb = [ar.alloc(BF16, [128, 128]) for _ in range(2)]
            ssq = [ar.alloc(F32, [128, 1]) for _ in range(2)]

            def pq(b, q):
                return PS[:, b, q * 128:(q + 1) * 128]

            def pqb(b, q):
                return PS[:, b, q * 128:q * 128 + 64].bitcast(BF16)

            bcnt = [0]
            stopat = os.environ.get('GDN_STOP')

            def do_head(h):
                hb = h % 2
                kh = h // 2
                srcs = [qkvT[2048 + kh * 128:2048 + (kh + 1) * 128, :], qkvT[kh * 128:(kh + 1) * 128, :],
                        qkvT[4096 + h * 128:4096 + (h + 1) * 128, :], zT[h * 128:(h + 1) * 128, :]]
                kT_h, qT_h, vT_h, zs_h = kTq[0][hb], kTq[1][hb], kTq[2][hb], kTq[3][hb]
                kk_, qk_, vk_, zk_ = ("hin", 0, hb), ("hin", 1, hb), ("hin", 2, hb), ("hin", 3, hb)

                def setup():
                    for a in range(4):
                        sc.op("sp", lambda e, a=a, src=srcs[a]: e.dma_start(out=kTq[a][hb], in_=src), writes=[("hin", a, hb)], dma=True)
                    sc.op("dve", lambda e: e.tensor_scalar(out=gcm[hb], in0=gcT, scalar1=ident_f[0:32, h:h + 1], scalar2=None, op0=ALU.mult),
                          reads=["gcT", "ident_f"], writes=[("gcm", 0)])
                    sc.op("dve", lambda e: e.tensor_scalar(out=oh[hb], in0=ones_f[0:32, :], scalar1=ident_f[0:32, h:h + 1], scalar2=None, op0=ALU.mult),
                          reads=["ones_f", "ident_f"], writes=[("oh", 0)])
                def groupfn(g0, si):
                    grp = list(range(g0, min(NG, g0 + SG)))
                    G = len(grp)
                    R0, R1, R2 = 3 * si, 3 * si + 1, 3 * si + 2
                    Dm4, DmS4, Xh4, N4, NT4 = Dm4s[si], DmS4s[si], Xh4s[si], N4s[si], NT4s[si]
                    Nb, NTb, Ub, X0f, ImX, E0f, X0T = Nbs[si], NTbs[si], Ubs[si], X0fs[si], ImXs[si], E0fs[si], X0Ts[si]
                    kDm, kDmS = ("Dm4", si), ("DmS4", si)
                    bqv = lambda b: PS[:, b, :].bitcast(BF16).rearrange("p (q x) -> p q x", x=256)[:, 0:G, 0:128]

                    def b4(b):
                        return PS[:, b, 0:G * 128].rearrange("p (q x) -> p q x", x=128)
                    for i, n in enumerate(grp):
                        cs = n * 128
                        sc.op("pe", lambda e, i=i, cs=cs: e.transpose(out=pqb(R1, i), in_=kT_h[:, cs:cs + 128], identity=ident_b),
                              reads=[kk_, "ident_b"], writes=[("ps", R1)])
                    for i, n in enumerate(grp):
                        sc.op("act", lambda e, i=i, n=n: e.activation(out=Xh4[:, i, :], in_=pqb(R1, i), func=AF.Copy, scale=egc[:, n, h:h + 1]),
                              reads=[("ps", R1), "egc"], writes=[("Xh", si, i)])
                    for i, n in enumerate(grp):
                        sc.op("dve", lambda e, i=i, n=n: e.tensor_scalar(out=kdt[hb][n], in0=pqb(R1, i), scalar1=ekd[:, n, h:h + 1], scalar2=None, op0=ALU.mult),
                              reads=[("ps", R1), "ekdx"], writes=[("kdt", hb, n)])
                    for i, n in enumerate(grp):
                        cs = n * 128
                        sc.op("pe", lambda e, i=i, cs=cs: e.matmul(pq(R0, i), oh[hb], gcT[:, cs:cs + 128], start=True, stop=False),
                              reads=[("oh", 0), "gcT"], writes=[("ps", R0)])
                        sc.op("pe", lambda e, i=i, cs=cs: e.matmul(pq(R0, i), gcm[hb][:, cs:cs + 128], negones, start=False, stop=False),
                              reads=[("gcm", 0), "negones"], writes=[("ps", R0)])
                        sc.op("pe", lambda e, i=i: e.matmul(pq(R0, i), ident_b, mneg_b, start=False, stop=True),
                              reads=["ident_b", "mneg_b"], writes=[("ps", R0)])
                    sc.op("act", lambda e: e.activation(out=Dm4[:, 0:G, :], in_=b4(R0), func=AF.Exp), reads=[("ps", R0)], writes=[kDm])
                    sc.op("dve", lambda e: e.tensor_tensor(out=DmS4[:, 0:G, :], in0=Dm4[:, 0:G, :], in1=ident4[:, 0:G, :], op=ALU.subtract),
                          reads=[kDm] + id4k, writes=[kDmS])
                    yield
                    for i, n in enumerate(grp):
                        cs = n * 128
                        sc.op("pe", lambda e, i=i, cs=cs: e.matmul(pq(R1, i), kT_h[:, cs:cs + 128], qT_h[:, cs:cs + 128], start=True, stop=True),
                              reads=[kk_, qk_], writes=[("ps", R1)])
                    for i, n in enumerate(grp):
                        cs = n * 128
                        sc.op("pe", lambda e, i=i, cs=cs: e.matmul(pq(R2, i), kT_h[:, cs:cs + 128], kT_h[:, cs:cs + 128], start=True, stop=True),
                              reads=[kk_], writes=[("ps", R2)])
                    sc.op("dve", lambda e: e.tensor_tensor(out=ATt[hb][:, g0:g0 + G, :], in0=b4(R1), in1=Dm4[:, 0:G, :], op=ALU.mult),
                          reads=[("ps", R1), kDm], writes=[("AT", hb, n) for n in grp])
                    for i, n in enumerate(grp):
                        sc.op("dve", lambda e, i=i, n=n: e.scalar_tensor_tensor(out=N4[0][:, i, :], in0=pq(R2, i), scalar=nbTok[:, n, h:h + 1], in1=DmS4[:, i, :],
                                                                               op0=ALU.mult, op1=ALU.mult),
                              reads=[("ps", R2), kDmS, "nbTok"], writes=[("N", si, 0)])
                    for i, n in enumerate(grp):
                        cs = n * 128
                        sc.op("pe", lambda e, i=i, cs=cs: e.transpose(out=pqb(R0, i), in_=vT_h[:, cs:cs + 128], identity=ident_b),
                              reads=[vk_, "ident_b"], writes=[("ps", R0)])
                    sc.op("act", lambda e: e.copy(out=vtkt[hb][:, g0:g0 + G, :], in_=bqv(R0)), reads=[("ps", R0)], writes=[("vtk", hb, n) for n in grp])
                    yield
                    for i, n in enumerate(grp):
                        sc.op("pe", lambda e, i=i: e.transpose(out=pq(R1, i), in_=N4[0][:, i, :], identity=ident_f),
                              reads=[("N", si, 0), "ident_f"], writes=[("ps", R1)])
                    sc.op("act", lambda e: e.copy(out=NT4[0][:, 0:G, :], in_=b4(R1)), reads=[("ps", R1)], writes=[("NT", si, 0)])
                    sc.op("pool", lambda e: e.tensor_tensor(out=Ub[0][:, 0:G, :], in0=N4[0][:, 0:G, :], in1=ident4[:, 0:G, :], op=ALU.add),
                          reads=[("N", si, 0)] + id4k, writes=[("Ub", si, 0)])
                    sc.op("dve", lambda e: e.tensor_copy(out=Nb[0][:, 0:G, :], in_=N4[0][:, 0:G, :]), reads=[("N", si, 0)], writes=[("Nb", si, 0)])
                    sc.op("act", lambda e: e.copy(out=NTb[0][:, 0:G, :], in_=b4(R1)), reads=[("ps", R1), ("NT", si, 0)], writes=[("NTb", si, 0)])
                    sc.op("pool", lambda e: e.tensor_tensor(out=ImX[:, 0:G, :], in0=NT4[0][:, 0:G, :], in1=ident4[:, 0:G, :], op=ALU.subtract),
                          reads=[("NT", si, 0)] + id4k, writes=[("NTmI", si)])
                    yield
                    NLV = 4
                    for lv in range(1, NLV + 1):
                        a, bb = (lv - 1) % 2, lv % 2
                        if lv < NLV:
                            for i, n in enumerate(grp):
                                sc.op("pe", lambda e, i=i, a=a: e.matmul(pq(R0, i), NTb[a][:, i, :], Nb[a][:, i, :], start=True, stop=True),
                                      reads=[("NTb", si, a), ("Nb", si, a)], writes=[("ps", R0)])
                        for i, n in enumerate(grp):
                            sc.op("pe", lambda e, i=i, a=a: e.matmul(pq(R1, i), Nb[a][:, i, :], NTb[a][:, i, :], start=True, stop=True),
                                  reads=[("NTb", si, a), ("Nb", si, a)], writes=[("ps", R1)])
                        yield
                        if lv < NLV:
                            sc.op("dve", lambda e, bb=bb: e.tensor_copy(out=Nb[bb][:, 0:G, :], in_=b4(R0)), reads=[("ps", R0)], writes=[("Nb", si, bb)])
                        sc.op("act", lambda e, bb=bb: e.copy(out=NTb[bb][:, 0:G, :], in_=b4(R1)), reads=[("ps", R1)], writes=[("NTb", si, bb)])
                        for i, n in enumerate(grp):
                            sc.op("pe", lambda e, i=i, a=a, bb=bb: e.matmul(pq(R2, i), NTb[bb][:, i, :], Ub[a][:, i, :], start=True, stop=True),
                                  reads=[("NTb", si, bb), ("Ub", si, a)], writes=[("ps", R2)])
                        yield
                        if lv < NLV:
                            sc.op("dve", lambda e, a=a, bb=bb: e.tensor_tensor(out=Ub[bb][:, 0:G, :], in0=b4(R2), in1=Ub[a][:, 0:G, :], op=ALU.add),
                                  reads=[("ps", R2), ("Ub", si, a)], writes=[("Ub", si, bb)])
                        else:
                            sc.op("dve", lambda e, a=a: e.tensor_tensor(out=X0f[:, 0:G, :], in0=b4(R2), in1=Ub[a][:, 0:G, :], op=ALU.add),
                                  reads=[("ps", R2), ("Ub", si, a)], writes=[("X0f", si)])
                    for i, n in enumerate(grp):
                        sc.op("pe", lambda e, i=i: e.matmul(pq(R0, i), ImX[:, i, :], X0f[:, i, :], start=True, stop=True),
                              reads=[("NTmI", si), ("X0f", si)], writes=[("ps", R0)])
                    for i, n in enumerate(grp):
                        sc.op("pe", lambda e, i=i: e.transpose(out=pq(R1, i), in_=X0f[:, i, :], identity=ident_f),
                              reads=[("X0f", si), "ident_f"], writes=[("ps", R1)])
                    yield
                    sc.op("dve", lambda e: e.tensor_tensor(out=E0f[:, 0:G, :], in0=b4(R0), in1=ident4[:, 0:G, :], op=ALU.add),
                          reads=[("ps", R0)] + id4k, writes=[("E0f", si)])
                    sc.op("act", lambda e: e.copy(out=X0T[:, 0:G, :], in_=b4(R1)), reads=[("ps", R1)], writes=[("X0T", si)])
                    for i, n in enumerate(grp):
                        sc.op("pe", lambda e, i=i: e.matmul(pq(R2, i), X0T[:, i, :], E0f[:, i, :], start=True, stop=True),
                              reads=[("X0T", si), ("E0f", si)], writes=[("ps", R2)])
                    yield
                    sc.op("dve", lambda e: e.tensor_tensor(out=U7t[hb][:, g0:g0 + G, :], in0=b4(R2), in1=X0f[:, 0:G, :], op=ALU.add),
                          reads=[("ps", R2), ("X0f", si)], writes=[("U7", hb, n) for n in grp])
                    yield
                    for i, n in enumerate(grp):
                        sc.op("pe", lambda e, i=i, n=n: e.matmul(pq(R0, i), Xh4[:, i, :], U7[hb][n], start=True, stop=True),
                              reads=[("Xh", si, i), ("U7", hb, n)], writes=[("ps", R0)])
                    sc.op("act", lambda e: e.activation(out=nWTt[hb][:, g0:g0 + G, :], in_=b4(R0), func=AF.Copy, scale=-1.0),
                          reads=[("ps", R0)], writes=[("nWT", hb, n) for n in grp])

                def chunkfn(n):
                    cs = n * 128
                    Sf = S_p[hb] if n < 16 else S_s[hb]
                    skey = ("S", hb, 0 if n < 16 else 1)
                    if n == 0:
                        sc.op("pool", lambda e, Sf=Sf: e.memset(Sf, 0.0), writes=[skey])
                        sc.op("pool", lambda e, hb=hb: e.memset(Sb[hb], 0.0), writes=[("Sb", hb)])
                    if n == 16:
                        sc.op("sp", lambda e, Sf=Sf, h=h: e.dma_start(out=Sf, in_=st_S[j, h]), writes=[skey], dma=True)
                        sc.op("act", lambda e, Sf=Sf, hb=hb: e.copy(out=Sb[hb], in_=Sf), reads=[skey], writes=[("Sb", hb)])
                    bq = bcnt[0] % 2
                    bcnt[0] += 1
                    B0 = 6
                    B1 = B0 + 1
                    sc.op("pe", lambda e, n=n, hb=hb, B0=B0: e.matmul(pq(B0, 0), U7[hb][n], vtk[hb][n], start=True, stop=False),
                          reads=[("U7", hb, n), ("vtk", hb, n)], writes=[("pq", B0, 0)])
                    sc.op("pe", lambda e, n=n, hb=hb, B0=B0: e.matmul(pq(B0, 0), nWT[hb][n], Sb[hb], start=False, stop=True),
                          reads=[("nWT", hb, n), ("Sb", hb)], writes=[("pq", B0, 0)])
                    yield
                    sc.op("act", lambda e, n=n, h=h, bq=bq, B0=B0: e.activation(out=vnew[bq], in_=pq(B0, 0), func=AF.Copy, scale=bTok[:, n, h:h + 1]),
                          reads=[("pq", B0, 0)] + btk, writes=[("vnew", bq)])
                    sc.op("pe", lambda e, cs=cs, hb=hb, B0=B0, qT_h=qT_h: e.matmul(pq(B0, 1), qT_h[:, cs:cs + 128], Sb[hb], start=True, stop=True),
                          reads=[qk_, ("Sb", hb)], writes=[("pq", B0, 1)])
                    sc.op("pe", lambda e, n=n, hb=hb, bq=bq, B0=B0: e.matmul(pq(B0, 2), AT[hb][n], vnew[bq], start=True, stop=True),
                          reads=[("AT", hb, n), ("vnew", bq)], writes=[("pq", B0, 2)])
                    sc.op("pe", lambda e, n=n, hb=hb, bq=bq, B1=B1: e.matmul(pq(B1, 0), kdt[hb][n], vnew[bq], start=True, stop=True),
                          reads=[("kdt", hb, n), ("vnew", bq)], writes=[("pq", B1, 0)])
                    yield
                    sc.op("act", lambda e, n=n, h=h, bq=bq, B0=B0: e.activation(out=otmp[bq], in_=pq(B0, 1), func=AF.Copy, scale=egc[:, n, h:h + 1]),
                          reads=[("pq", B0, 1), "egc"], writes=[("otmp", bq)])
                    sc.op("dve", lambda e, bq=bq, B0=B0: e.tensor_tensor(out=otok[bq], in0=otmp[bq], in1=pq(B0, 2), op=ALU.add),
                          reads=[("otmp", bq), ("pq", B0, 2)], writes=[("otok", bq)])
                    sc.op("dve", lambda e, n=n, h=h, Sf=Sf, B1=B1, hb=hb: e.scalar_tensor_tensor(out=Sb[hb], in0=Sf, scalar=glb[:, n, h:h + 1], in1=pq(B1, 0),
                                                                                          op0=ALU.mult, op1=ALU.add),
                          reads=[skey, ("pq", B1, 0), "glb"], writes=[("Sb", hb)])
                    sc.op("dve", lambda e, n=n, h=h, Sf=Sf, B1=B1: e.scalar_tensor_tensor(out=Sf, in0=Sf, scalar=glb[:, n, h:h + 1], in1=pq(B1, 0),
                                                                                   op0=ALU.mult, op1=ALU.add),
                          reads=[skey, ("pq", B1, 0), "glb"], writes=[skey])
                    if n == 15:
                        sc.op("sp", lambda e, Sf=Sf, h=h: e.dma_start(out=o_S_p[j, h], in_=Sf), reads=[skey], writes=[("oSp", h)], dma=True)
                    if n == 16:
                        sc.op("sp", lambda e, Sf=Sf, h=h: e.dma_start(out=o_S_s[j, h], in_=Sf), reads=[skey], writes=[("oSs", h)], dma=True)
                    yield
                    sc.op("act", lambda e, bq=bq: e.activation(out=osq[bq], in_=otok[bq], func=AF.Square, accum_out=ssq[bq]),
                          reads=[("otok", bq)], writes=[("ssq", bq), ("osq", bq)])
                    sc.op("act", lambda e, bq=bq: e.activation(out=ssq[bq], in_=ssq[bq], func=AF.Ln, scale=1.0 / 128, bias=EPS),
                          reads=[("ssq", bq)], writes=[("ssq", bq)])
                    sc.op("act", lambda e, bq=bq: e.activation(out=ssq[bq], in_=ssq[bq], func=AF.Exp, scale=-0.5),
                          reads=[("ssq", bq)], writes=[("ssq", bq)])
                    sc.op("dve", lambda e, bq=bq: e.tensor_scalar(out=onb[bq], in0=otok[bq], scalar1=ssq[bq][:, 0:1], scalar2=None, op0=ALU.mult),
                          reads=[("otok", bq), ("ssq", bq)], writes=[("onb", bq)])
                    sc.op("pe", lambda e, bq=bq, B1=B1: e.transpose(out=pqb(B1, 1), in_=onb[bq], identity=ident_b),
                          reads=[("onb", bq), "ident_b"], writes=[("pq", B1, 1)])
                    yield
                    sc.op("dve", lambda e, cs=cs, hb=hb, B1=B1, zs_h=zs_h: e.scalar_tensor_tensor(
                        out=ogT[hb][:, cs:cs + 128], in0=pqb(B1, 1), scalar=gnw[:, 0:1], in1=zs_h[:, cs:cs + 128], op0=ALU.mult, op1=ALU.mult),
                        reads=[("pq", B1, 1), "gnw", zk_], writes=[("ogT", hb, n)])
                    if n == NG - 1:
                        sc.op("sp", lambda e, hb=hb, h=h: e.dma_start(out=oT[h * 128:(h + 1) * 128, :], in_=ogT[hb]),
                              reads=[("ogT", hb, n) for n in range(NG)], writes=[("oT", h)], dma=True)
                return setup, groupfn, chunkfn

            heads = [do_head(h) for h in range(NV)]
            g0s = list(range(0, NG, GR))

            def bgen(hd, chunks):
                for n in chunks:
                    yield from hd[2](n)

            def run_pair(hd, g0, bg):
                gens = []
                if hd is not None:
                    gens.append(hd[1](g0, 0))
                    if g0 + SG < NG:
                        gens.append(hd[1](g0 + SG, 1))
                live = list(gens)
                blive = bg is not None
                while live or blive:
                    for g in list(live):
                        try:
                            next(g)
                        except StopIteration:
                            live.remove(g)
                    for _ in range(2):
                        if blive:
                            try:
                                next(bg)
                            except StopIteration:
                                blive = False
            heads[0][0]()
            for g0 in g0s:
                run_pair(heads[0], g0, None)
            for h in range(NV):
                nxt = heads[h + 1] if h + 1 < NV else None
                if nxt is not None:
                    nxt[0]()
                for g0 in g0s:
                    run_pair(nxt, g0, bgen(heads[h], range(g0, min(NG, g0 + GR))))
            sc.barrier()

            if os.environ.get("GDN_STOP") == "4":
                sc.barrier()
                return
            ar.off = persist_off
            Wo = gdn_w_out[j]
            osb = ar.alloc(BF16, [128, 32, 1040])
            wslot = [ar.alloc(BF16, [128, 8192]) for _ in range(2)]
            xres = [ar.alloc(F32, [128, 512]) for _ in range(2)]
            oTv = oT.rearrange("(kc p) t -> p kc t", p=128)
            wcnt = 0
            for blk in (TILES[0:2], TILES[2:5]):
                hc0 = blk[0][0]
                ntok = sum(n for _, n in blk)
                for kc in range(32):
                    sc.op("sp", lambda e, kc=kc, hc0=hc0, ntok=ntok: e.dma_start(out=osb[:, kc, 0:ntok], in_=oTv[:, kc, hc0:hc0 + ntok]),
                          writes=[("osb", kc)], dma=True)
                it = 0
                for mp in range(8):
                    slot = wcnt % 2
                    wcnt += 1
                    vw = wslot[slot].rearrange("p (a b) -> p a b", b=256)
                    srcw = Wo[:, mp * 256:(mp + 1) * 256].rearrange("(kc p) m -> p kc m", p=128)
                    sc.op("pool", lambda e, vw=vw, srcw=srcw: e.dma_start(out=vw, in_=srcw), writes=[("w", slot)], dma=True)
                    for mi in range(2):
                        mc = mp * 2 + mi
                        for (c0, n) in blk:
                            bk = it % 6
                            xb = it % 2
                            it += 1
                            sc.op("sp", lambda e, xb=xb, mc=mc, c0=c0, n=n: e.dma_start(out=xres[xb][:, 0:n], in_=xTv[:, mc, c0:c0 + n]),
                                  writes=[("xres", xb)], dma=True)
                            for kc in range(32):
                                sc.op("pe", lambda e, vw=vw, kc=kc, mi=mi, c0=c0, n=n, bk=bk, hc0=hc0: e.matmul(
                                    bank(bk)[:, 0:n], vw[:, kc, mi * 128:(mi + 1) * 128], osb[:, kc, c0 - hc0:c0 - hc0 + n],
                                    start=(kc == 0), stop=(kc == 31)),
                                    reads=[("w", slot), ("osb", kc)], writes=[("ps", bk)])
                            sc.op("dve", lambda e, xb=xb, bk=bk, n=n: e.tensor_tensor(
                                out=xres[xb][:, 0:n], in0=xres[xb][:, 0:n], in1=bank(bk)[:, 0:n], op=ALU.add),
                                reads=[("xres", xb), ("ps", bk)], writes=[("xres", xb)])
                            sc.op("sp", lambda e, xb=xb, mc=mc, c0=c0, n=n: e.dma_start(out=xTv[:, mc, c0:c0 + n], in_=xres[xb][:, 0:n]),
                                  reads=[("xres", xb)], writes=[("xTo", mc, c0)], dma=True)
                sc.barrier()

        def phase_final():
            ar.reset()
            tmp_x = [ar.alloc(F32, [128, KC, 512]) for _ in range(2)]
            tmp_sq = [ar.alloc(F32, [128, KC, 512])] * 2
            tmp_r = [ar.alloc(F32, [128, 512]) for _ in range(2)]
            yT = ar.alloc(F32, [128, KC, 512])
            yo = [ar.alloc(F32, [128, D]) for _ in range(2)]
            xTv = xT.rearrange("(kc p) t -> p kc t", p=128)
            cnt = 0
            for ti, (c0, n) in enumerate(TILES):
                b = ti % 2
                xt, sq, rr = tmp_x[b], tmp_sq[b], tmp_r[b]
                sc.op("sp", lambda e, xt=xt, c0=c0, n=n: e.dma_start(out=xt[:, :, 0:n], in_=xTv[:, :, c0:c0 + n]),
                      writes=[("nx", b)], dma=True)
                sc.op("act", lambda e, xt=xt, sq=sq, n=n: e.activation(out=sq[:, :, 0:n], in_=xt[:, :, 0:n], func=AF.Square),
                      reads=[("nx", b)], writes=[("nsq", 0)])
                bk = 6 + b
                for kc in range(KC):
                    sc.op("pe", lambda e, sq=sq, kc=kc, n=n, bk=bk: e.matmul(
                        bank(bk)[:, 0:n], ones_f, sq[:, kc, 0:n], start=(kc == 0), stop=(kc == KC - 1)),
                        reads=[("nsq", 0), "ones_f"], writes=[("ps", bk)])
                sc.op("act", lambda e, rr=rr, n=n, bk=bk: e.activation(out=rr[:, 0:n], in_=bank(bk)[:, 0:n], func=AF.Ln,
                                                                      scale=1.0 / D, bias=EPS),
                      reads=[("ps", bk)], writes=[("nr", b)])
                sc.op("act", lambda e, rr=rr, n=n: e.activation(out=rr[:, 0:n], in_=rr[:, 0:n], func=AF.Exp, scale=-0.5),
                      reads=[("nr", b)], writes=[("nr", b)])
                for kc in range(KC):
                    sc.op("dve", lambda e, xt=xt, rr=rr, kc=kc, n=n: e.scalar_tensor_tensor(
                        out=yT[:, kc, 0:n], in0=xt[:, kc, 0:n], scalar=nfin[:, kc, 0:1],
                        in1=rr[:, 0:n], op0=ALU.mult, op1=ALU.mult),
                        reads=[("nx", b), ("nr", b)], writes=[("yT", kc)])
                ngr = (n + 127) // 128
                for gi in range(ngr):
                    rows = min(128, n - gi * 128)
                    ob = cnt % 2
                    cnt += 1
                    for q in range(4):
                        bk2 = q
                        for i in range(4):
                            kc = q * 4 + i
                            sc.op("pe", lambda e, kc=kc, gi=gi, rows=rows, bk2=bk2, i=i: e.transpose(
                                out=bank(bk2)[0:rows, i * 128:(i + 1) * 128], in_=yT[:, kc, gi * 128:gi * 128 + rows],
                                identity=ident_f),
                                reads=[("yT", kc), "ident_f"], writes=[("ps", bk2)])
                        if q % 2 == 0:
                            sc.op("act", lambda e, ob=ob, q=q, rows=rows, bk2=bk2: e.copy(
                                out=yo[ob][0:rows, q * 512:(q + 1) * 512], in_=bank(bk2)[0:rows, :]),
                                reads=[("ps", bk2)], writes=[("yo", ob, q)])
                        else:
                            sc.op("dve", lambda e, ob=ob, q=q, rows=rows, bk2=bk2: e.tensor_copy(
                                out=yo[ob][0:rows, q * 512:(q + 1) * 512], in_=bank(bk2)[0:rows, :]),
                                reads=[("ps", bk2)], writes=[("yo", ob, q)])
                    t0 = c0 + gi * 128
                    dst = y_p[t0:t0 + rows, :] if t0 < TP else y_s
                    sc.op("sp", lambda e, ob=ob, rows=rows, dst=dst: e.dma_start(out=dst, in_=yo[ob][0:rows, :]),
                          reads=[("yo", ob, q) for q in range(4)], writes=[("y", t0)], dma=True)
            sc.barrier()

        phase_input()
        for s in stages:
            if s.startswith("ffn"):
                phase_ffn(int(s[3:]))
            elif s.startswith("pool"):
                phase_pool(int(s[4:]))
            elif s.startswith("gdn"):
                phase_gdn(int(s[3:]))
        phase_final()
        sc.finalize(st)
    return nc, sc


_CACHE = {}


def kernel(**inputs):
    f32 = lambda a: np.ascontiguousarray(np.asarray(a, dtype=np.float32))
    if "nc" not in _CACHE:
        _CACHE["nc"] = build()[0]
    nc = _CACHE["nc"]
    shared = {k: f32(inputs[k]) for k in ("norm_mix_w", "norm_ffn_w", "gdn_w_in", "gdn_conv_w", "gdn_A_log",
                                          "gdn_dt_bias", "gdn_norm_w", "gdn_w_out", "pool_w", "pool_scale",
                                          "ffn_w_gu", "ffn_w_down")}
    shared["final_norm_w"] = f32(inputs["final_norm_w"]).reshape(1, D)
    xp, xs = f32(inputs["x_prompt"]), f32(inputs["x_sample"])
    sconv, sS, spool = f32(inputs["state_gdn_conv"]), f32(inputs["state_gdn_S"]), f32(inputs["state_pool"])
    in_maps = []
    for b in range(8):
        m = dict(shared)
        m["x_p"] = xp[b]
        m["x_s"] = xs[b]
        m["st_conv"] = np.ascontiguousarray(sconv[:, b])
        m["st_S"] = np.ascontiguousarray(sS[:, b])
        m["st_pool"] = np.ascontiguousarray(spool[:, b])
        in_maps.append(m)
    res = run_bass_kernel_spmd(nc, in_maps, core_ids=list(range(8)))
    r = res.results
    stack = lambda k, ax: np.stack([np.asarray(r[b][k], dtype=np.float32) for b in range(8)], axis=ax)
    return (stack("y_p", 0), stack("y_s", 0), stack("o_conv_p", 1), stack("o_S_p", 1), stack("o_pool_p", 1),
            stack("o_conv_s", 1), stack("o_S_s", 1), stack("o_pool_s", 1))
```

```python
import math
import os
from contextlib import ExitStack
import numpy as np
import concourse.bass as bass
import concourse.mybir as mybir
from concourse.alu_op_type import AluOpType as ALU
from concourse.bass_utils import run_bass_kernel_spmd

F32 = mybir.dt.float32
BF16 = mybir.dt.bfloat16
AF = mybir.ActivationFunctionType

ENGS = ("pe", "dve", "act", "pool", "sp")

D = 2048
KC = 16
TP = 2048
TS = 16
TR = TP + TS
TA = TP + 128
NG = TA // 128
NV = 32
QKV = 8192
VAL = 4096
IN_DIM = 12352
FH = 5632
FC = FH // 128
EPS = 1e-6
TILES = [(0, 512), (512, 512), (1024, 512), (1536, 512), (2048, 16)]
NEG = -1.0e30


class Sched:
    def __init__(self, nc, n_lanes=8):
        self.nc = nc
        self.ops = []
        self.last_w = {}
        self.readers = {}
        self.n_lanes = n_lanes
        self.implicit = {"pe"}

    def op(self, eng, fn, reads=(), writes=(), dma=False):
        nk = lambda k: ("ps", k[1]) if (isinstance(k, tuple) and k and k[0] == "pq") else k
        reads = [nk(k) for k in reads] + ["ALL"]
        writes = [nk(k) for k in writes]
        deps = set()
        for k in reads:
            w = self.last_w.get(k)
            if w is not None:
                deps.add(w)
            if isinstance(k, tuple) and k and k[0] == "ps":
                for r in self.readers.get(k, ()):
                    if self.ops[r]["eng"] != eng:
                        deps.add(r)
        for k in writes:
            w = self.last_w.get(k)
            if w is not None:
                deps.add(w)
            for r in self.readers.get(k, ()):
                deps.add(r)
        idx = len(self.ops)
        self.ops.append(dict(eng=eng, fn=fn, deps=deps, dma=dma))
        for k in reads:
            self.readers.setdefault(k, []).append(idx)
        for k in writes:
            self.last_w[k] = idx
            self.readers[k] = []
        return idx

    def barrier(self):
        for e in ("sp", "pe", "dve", "act", "pool"):
            self.op(e, lambda h: h.nop(), writes=["ALL"])

    def finalize(self, stack):
        nc = self.nc
        ops = self.ops
        needed = set()
        for o in ops:
            if o["eng"] in self.implicit:
                o["deps"] = {d for d in o["deps"] if ops[d]["eng"] != o["eng"] or ops[d]["dma"]}
            needed |= o["deps"]
        csem = {e: stack.enter_context(nc.semaphore(f"c_{e}")) for e in ENGS}
        lanes = {e: [stack.enter_context(nc.semaphore(f"l_{e}_{i}")) for i in range(self.n_lanes)]
                 for e in ("sp", "pool")}
        ccount = {e: 0 for e in ENGS}
        lane_cnt = {e: [0] * self.n_lanes for e in lanes}
        lane_rr = {e: 0 for e in lanes}
        token = [None] * len(ops)
        known = {e: {} for e in ENGS}
        streams = {e: [] for e in ENGS}
        for i, o in enumerate(ops):
            e = o["eng"]
            waits = {}
            for d in o["deps"]:
                s, v = token[d]
                key = id(s)
                if known[e].get(key, 0) >= v:
                    continue
                if key not in waits or waits[key][1] < v:
                    waits[key] = (s, v)
            inc = None
            if o["dma"]:
                ln = lane_rr[e]
                lane_rr[e] = (ln + 1) % self.n_lanes
                s = lanes[e][ln]
                prev = lane_cnt[e][ln]
                if prev > 0 and known[e].get(id(s), 0) < prev:
                    key = id(s)
                    if key not in waits or waits[key][1] < prev:
                        waits[key] = (s, prev)
                lane_cnt[e][ln] = prev + 16
                token[i] = (s, prev + 16)
                inc = (s, 16)
            else:
                if i in needed:
                    ccount[e] += 1
                    token[i] = (csem[e], ccount[e])
                    inc = (csem[e], 1)
                else:
                    token[i] = (csem[e], ccount[e] + 1)
            for key, (s, v) in waits.items():
                known[e][key] = v
            streams[e].append((list(waits.values()), o["fn"], inc))
        self.stats = {e: len(streams[e]) for e in ENGS}
        final_waits = []
        for e in lanes:
            for ln in range(self.n_lanes):
                if lane_cnt[e][ln] > 0:
                    final_waits.append((lanes[e][ln], lane_cnt[e][ln]))
        for e in ENGS:
            if ccount[e] > 0:
                final_waits.append((csem[e], ccount[e]))
        with nc.Block() as block:
            def mk(e):
                def body(engh):
                    for waits, fn, inc in streams[e]:
                        for s, v in waits:
                            engh.wait_ge(s, v)
                        ins = fn(engh)
                        if inc is not None:
                            ins.then_inc(inc[0], inc[1])
                    if e == "sp":
                        for s, v in final_waits:
                            engh.wait_ge(s, v)
                return body
            block.tensor(mk("pe"))
            block.vector(mk("dve"))
            block.scalar(mk("act"))
            block.gpsimd(mk("pool"))
            block.sync(mk("sp"))


class Arena:
    def __init__(self, A, nbytes):
        self.A = A
        self.size = nbytes
        self.base = 0
        self.off = 0

    def alloc(self, dt, shape, name=None):
        esz = 4 if dt == F32 else 2
        free = 1
        for s in shape[1:]:
            free *= s
        nb = free * esz
        off = self.off
        self.off += (nb + 63) // 64 * 64
        assert self.off <= self.size, f"arena overflow {self.off} > {self.size} ({name})"
        ap = self.A[0:shape[0], off // 4:(off + nb + 3) // 4]
        if dt != F32:
            ap = ap.bitcast(dt)
        if len(shape) == 3:
            ap = ap.rearrange("p (a b) -> p a b", b=shape[2])
        elif len(shape) == 4:
            ap = ap.rearrange("p (a b c) -> p a b c", b=shape[2], c=shape[3])
        return ap

    def view(self, off, dt, shape):
        save = self.off
        self.off = off
        ap = self.alloc(dt, shape)
        self.off = save
        return ap

    def mark(self):
        self.base = self.off

    def reset(self):
        self.off = self.base


def build(stages=None, debug=False):
    nc = bass.Bass("TRN2", target_bir_lowering=False)

    def din(name, shape):
        return nc.dram_tensor(name, list(shape), F32, kind="ExternalInput").ap()

    def dout(name, shape):
        return nc.dram_tensor(name, list(shape), F32, kind="ExternalOutput").ap()

    def dscr(name, shape, dt):
        return nc.dram_tensor(name, list(shape), dt, kind="Internal").ap()

    x_p = din("x_p", [TP, D])
    x_s = din("x_s", [TS, D])
    st_conv = din("st_conv", [2, 3, QKV])
    st_S = din("st_S", [2, NV, 128, 128])
    st_pool = din("st_pool", [2, 15, D])
    norm_mix_w = din("norm_mix_w", [4, D])
    norm_ffn_w = din("norm_ffn_w", [4, D])
    final_norm_w = din("final_norm_w", [1, D])
    gdn_w_in = din("gdn_w_in", [2, D, IN_DIM])
    gdn_conv_w = din("gdn_conv_w", [2, 4, QKV])
    gdn_A_log = din("gdn_A_log", [2, NV])
    gdn_dt_bias = din("gdn_dt_bias", [2, NV])
    gdn_norm_w = din("gdn_norm_w", [2, 128])
    gdn_w_out = din("gdn_w_out", [2, VAL, D])
    pool_w = din("pool_w", [2, 4, 512, 512])
    pool_scale = din("pool_scale", [2, D])
    ffn_w_gu = din("ffn_w_gu", [4, D, 2 * FH])
    ffn_w_down = din("ffn_w_down", [4, FH, D])

    y_p = dout("y_p", [TP, D])
    y_s = dout("y_s", [TS, D])
    o_conv_p = dout("o_conv_p", [2, 3, QKV])
    o_S_p = dout("o_S_p", [2, NV, 128, 128])
    o_pool_p = dout("o_pool_p", [2, 15, D])
    o_conv_s = dout("o_conv_s", [2, 3, QKV])
    o_S_s = dout("o_S_s", [2, NV, 128, 128])
    o_pool_s = dout("o_pool_s", [2, 15, D])

    xT = dscr("xT", [D, TA], F32)
    qkvT = dscr("qkvT", [QKV, TA], BF16)
    zT = dscr("zT", [VAL, TA], BF16)
    oT = dscr("oT", [VAL, TA], BF16)

    sc = Sched(nc)
    if stages is None:
        stages = ["gdn0", "ffn0", "pool1", "ffn1", "gdn2", "ffn2", "pool3", "ffn3"]

    with ExitStack() as st:
        ARENA_BYTES = 176 * 1024
        A = st.enter_context(nc.sbuf_tensor("arena", [128, ARENA_BYTES // 4], F32))
        PS = st.enter_context(nc.psum_tensor("psum", [128, 8, 512], F32))
        ar = Arena(A, ARENA_BYTES)

        def bank(i):
            return PS[:, i, :]

        ident_f = ar.alloc(F32, [128, 128])
        ident_b = ar.alloc(BF16, [128, 128])
        ones_f = ar.alloc(F32, [128, 128])
        ones_b = ar.alloc(BF16, [128, 128])
        mneg_b = ar.alloc(BF16, [128, 128])
        zero_f = ar.alloc(F32, [128, 128])
        nmix = ar.alloc(F32, [128, KC, 4])
        nffn = ar.alloc(F32, [128, KC, 4])
        nfin = ar.alloc(F32, [128, KC, 1])
        pscale = ar.alloc(F32, [128, KC, 2])

        sc.op("pool", lambda e: e.memset(ones_f, 1.0), writes=["ones_f"])
        sc.op("pool", lambda e: e.memset(ones_b, 1.0), writes=["ones_b"])
        sc.op("pool", lambda e: e.memset(zero_f, 0.0), writes=["zero_f"])
        sc.op("pool", lambda e: e.affine_select(out=ident_f, in_=ones_f, pattern=[[1, 128]],
                                                compare_op=ALU.is_equal, fill=0.0, base=0,
                                                channel_multiplier=-1),
              reads=["ones_f"], writes=["ident_f"])
        sc.op("pool", lambda e: e.tensor_copy(out=ident_b, in_=ident_f), reads=["ident_f"], writes=["ident_b"])
        sc.op("pool", lambda e: e.affine_select(out=mneg_b, in_=zero_f, pattern=[[1, 128]],
                                                compare_op=ALU.is_ge, fill=NEG, base=0,
                                                channel_multiplier=-1),
              reads=["zero_f"], writes=["mneg_b"])

        rowbuf_box = [None]

        def vec_to_cols(src, R, N, dst, bankno=7):
            nchunk = N // 128
            rowbuf = rowbuf_box[0]
            sc.op("sp", lambda e: e.dma_start(out=rowbuf[0:R, 0:N], in_=src), writes=["rowbuf"], dma=True)
            per = 512 // R
            c = 0
            while c < nchunk:
                m = min(per, nchunk - c)
                pv = bank(bankno)[:, 0:m * R].rearrange("p (a b) -> p a b", b=R)
                for i in range(m):
                    sc.op("pe", lambda e, c=c, i=i, pv=pv: e.transpose(
                        out=pv[:, i, :], in_=rowbuf[0:R, (c + i) * 128:(c + i + 1) * 128], identity=ident_f[0:R, 0:R]),
                        reads=["rowbuf", "ident_f"], writes=[("ps", bankno)])
                sc.op("dve", lambda e, c=c, m=m, pv=pv: e.tensor_copy(out=dst[:, c:c + m, :], in_=pv),
                      reads=[("ps", bankno)], writes=[("cols", id(dst))])
                c += m

        ar.mark()
        rowbuf_box[0] = ar.alloc(F32, [16, QKV])
        vec_to_cols(norm_mix_w, 4, D, nmix)
        vec_to_cols(norm_ffn_w, 4, D, nffn)
        vec_to_cols(final_norm_w, 1, D, nfin)
        vec_to_cols(pool_scale, 2, D, pscale)
        sc.barrier()

        def phase_input():
            ar.reset()
            xin = [ar.alloc(F32, [128, D]) for _ in range(2)]
            xo = [ar.alloc(F32, [128, KC, 128]) for _ in range(2)]
            for g in range(NG):
                b = g % 2
                rows = 128 if g < 16 else TS
                src = x_p[g * 128:(g + 1) * 128, :] if g < 16 else x_s
                sc.op("sp", lambda e, b=b, rows=rows, src=src: e.dma_start(out=xin[b][0:rows, :], in_=src),
                      writes=[("xin", b)], dma=True)
                for q in range(4):
                    bk = (g * 4 + q) % 4
                    for i in range(4):
                        kc = q * 4 + i
                        sc.op("pe", lambda e, b=b, rows=rows, kc=kc, bk=bk, i=i: e.transpose(
                            out=bank(bk)[:, i * 128:i * 128 + rows], in_=xin[b][0:rows, kc * 128:(kc + 1) * 128],
                            identity=ident_f[0:rows, 0:rows]),
                            reads=[("xin", b), "ident_f"], writes=[("ps", bk)])
                    eng = "act" if q % 2 == 0 else "dve"
                    pv = bank(bk).rearrange("p (a b) -> p a b", b=128)[:, :, 0:rows]
                    if eng == "act":
                        sc.op("act", lambda e, b=b, q=q, pv=pv, rows=rows: e.copy(out=xo[b][:, q * 4:(q + 1) * 4, 0:rows], in_=pv),
                              reads=[("ps", bk)], writes=[("xo", b, q)])
                    else:
                        sc.op("dve", lambda e, b=b, q=q, pv=pv, rows=rows: e.tensor_copy(out=xo[b][:, q * 4:(q + 1) * 4, 0:rows], in_=pv),
                              reads=[("ps", bk)], writes=[("xo", b, q)])
                dstv = xT.rearrange("(kc p) t -> p kc t", p=128)[:, :, g * 128:g * 128 + rows]
                sc.op("sp", lambda e, b=b, rows=rows, dstv=dstv: e.dma_start(out=dstv, in_=xo[b][:, :, 0:rows]),
                      reads=[("xo", b, q) for q in range(4)], writes=[("xT", g)], dma=True)
            sc.barrier()

        def norm_tiles(hT, wcols, widx, tiles, tmp_x, tmp_sq, tmp_r, hcol0=0, side=None, hl=None):
            xTv = xT.rearrange("(kc p) t -> p kc t", p=128)
            for ti, (c0, n) in enumerate(tiles):
                b = ti % 2
                xt, sq, rr = tmp_x[b], tmp_sq[b], tmp_r[b]
                kx, kq, kr = ("nx", id(xt)), ("nsq", id(sq)), ("nr", id(rr))
                sc.op("sp", lambda e, xt=xt, c0=c0, n=n: e.dma_start(out=xt[:, :, 0:n], in_=xTv[:, :, c0:c0 + n]),
                      writes=[kx], dma=True)
                sc.op("act", lambda e, xt=xt, sq=sq, n=n: e.activation(out=sq[:, :, 0:n], in_=xt[:, :, 0:n], func=AF.Square),
                      reads=[kx], writes=[kq])
                bk = 6 + b
                for kc in range(KC):
                    sc.op("pe", lambda e, sq=sq, kc=kc, n=n, bk=bk: e.matmul(
                        bank(bk)[:, 0:n], ones_f, sq[:, kc, 0:n], start=(kc == 0), stop=(kc == KC - 1)),
                        reads=[kq, "ones_f"], writes=[("ps", bk)])
                sc.op("act", lambda e, rr=rr, n=n, bk=bk: e.activation(out=rr[:, 0:n], in_=bank(bk)[:, 0:n], func=AF.Ln,
                                                                      scale=1.0 / D, bias=EPS),
                      reads=[("ps", bk)], writes=[kr])
                sc.op("act", lambda e, rr=rr, n=n: e.activation(out=rr[:, 0:n], in_=rr[:, 0:n], func=AF.Exp, scale=-0.5),
                      reads=[kr], writes=[kr])
                for kc in range(KC):
                    sc.op("dve", lambda e, xt=xt, rr=rr, kc=kc, c0=c0, n=n: e.scalar_tensor_tensor(
                        out=hT[:, kc, c0 - hcol0:c0 - hcol0 + n], in0=xt[:, kc, 0:n], scalar=wcols[:, kc, widx:widx + 1],
                        in1=rr[:, 0:n], op0=ALU.mult, op1=ALU.mult),
                        reads=[kx, kr], writes=[("hT", kc, c0)])
                for (ts0, cnt, dcol) in (side or []):
                    if c0 <= ts0 and ts0 + cnt <= c0 + n:
                        for kc in range(KC):
                            sc.op("dve", lambda e, xt=xt, rr=rr, kc=kc, o=ts0 - c0, cnt=cnt, dcol=dcol: e.scalar_tensor_tensor(
                                out=hl[:, kc, dcol:dcol + cnt], in0=xt[:, kc, o:o + cnt], scalar=wcols[:, kc, widx:widx + 1],
                                in1=rr[:, o:o + cnt], op0=ALU.mult, op1=ALU.mult),
                                reads=[kx, kr], writes=[("hl", kc, dcol)])

        def wload(wslot, slot, W, KCn, col0, ncols):
            view = wslot[slot][:, 0:KCn * ncols].rearrange("p (a b) -> p a b", b=ncols)
            src = W[:, col0:col0 + ncols].rearrange("(kc p) m -> p kc m", p=128)
            sc.op("pool", lambda e: e.dma_start(out=view, in_=src), writes=[("w", slot)], dma=True)
            return view

        def phase_ffn(li):
            ar.reset()
            Wgu = ffn_w_gu[li]
            Wd = ffn_w_down[li]
            NB = 1040
            hT = ar.alloc(BF16, [128, KC, NB])
            act_off = ar.off
            act = ar.alloc(BF16, [128, FC, NB])
            wslot = [ar.alloc(BF16, [128, 8192]) for _ in range(2)]
            sgt = [ar.alloc(F32, [128, 512]) for _ in range(2)]
            xres = [ar.alloc(F32, [128, 512]) for _ in range(2)]
            tx = ar.view(act_off, F32, [128, KC, 512])
            tq = ar.view(act_off + 32768, F32, [128, KC, 512])
            tr = [ar.view(act_off + 65536 + i * 2048, F32, [128, 512]) for i in range(2)]
            blocks = [TILES[0:2], TILES[2:5]]
            xTv = xT.rearrange("(kc p) t -> p kc t", p=128)
            wcnt = 0
            for blk in blocks:
                hc0 = blk[0][0]
                norm_tiles(hT, nffn, li, blk, [tx, tx], [tq, tq], tr, hcol0=hc0)
                sc.barrier()
                for jp in range(FC // 2):
                    slot = wcnt % 2
                    wcnt += 1
                    vg = wslot[slot][:, 0:KC * 256].rearrange("p (a b) -> p a b", b=256)
                    vu = wslot[slot][:, KC * 256:KC * 512].rearrange("p (a b) -> p a b", b=256)
                    srcg = Wgu[:, jp * 256:jp * 256 + 256].rearrange("(kc p) m -> p kc m", p=128)
                    srcu = Wgu[:, FH + jp * 256:FH + jp * 256 + 256].rearrange("(kc p) m -> p kc m", p=128)
                    sc.op("pool", lambda e, vg=vg, srcg=srcg: e.dma_start(out=vg, in_=srcg),
                          writes=[("w", slot, 0)], dma=True)
                    sc.op("pool", lambda e, vu=vu, srcu=srcu: e.dma_start(out=vu, in_=srcu),
                          writes=[("w", slot, 1)], dma=True)
                    for jj in range(2):
                        j = jp * 2 + jj
                        for ti, (c0, n) in enumerate(blk):
                            it = j * len(blk) + ti
                            bg = (it % 3) * 2
                            bu = bg + 1
                            for kc in range(KC):
                                sc.op("pe", lambda e, vg=vg, kc=kc, jj=jj, c0=c0, n=n, bg=bg, hc0=hc0: e.matmul(
                                    bank(bg)[:, 0:n], vg[:, kc, jj * 128:(jj + 1) * 128], hT[:, kc, c0 - hc0:c0 - hc0 + n],
                                    start=(kc == 0), stop=(kc == KC - 1)),
                                    reads=[("w", slot, 0), ("hT", kc, c0)], writes=[("ps", bg)])
                            for kc in range(KC):
                                sc.op("pe", lambda e, vu=vu, kc=kc, jj=jj, c0=c0, n=n, bu=bu, hc0=hc0: e.matmul(
                                    bank(bu)[:, 0:n], vu[:, kc, jj * 128:(jj + 1) * 128], hT[:, kc, c0 - hc0:c0 - hc0 + n],
                                    start=(kc == 0), stop=(kc == KC - 1)),
                                    reads=[("w", slot, 1), ("hT", kc, c0)], writes=[("ps", bu)])
                            sb = it % 2
                            sc.op("act", lambda e, sb=sb, bg=bg, n=n: e.activation(out=sgt[sb][:, 0:n], in_=bank(bg)[:, 0:n], func=AF.Silu),
                                  reads=[("ps", bg)], writes=[("sgt", sb)])
                            sc.op("dve", lambda e, sb=sb, bu=bu, n=n, j=j, c0=c0, hc0=hc0: e.tensor_tensor(
                                out=act[:, j, c0 - hc0:c0 - hc0 + n], in0=sgt[sb][:, 0:n], in1=bank(bu)[:, 0:n], op=ALU.mult),
                                reads=[("sgt", sb), ("ps", bu)], writes=[("act", j, c0)])
                for mc in range(KC):
                    slot = wcnt % 2
                    wcnt += 1
                    vw = wslot[slot][:, 0:FC * 128].rearrange("p (a b) -> p a b", b=128)
                    src = Wd[:, mc * 128:(mc + 1) * 128].rearrange("(kc p) m -> p kc m", p=128)
                    sc.op("pool", lambda e, vw=vw, src=src: e.dma_start(out=vw, in_=src),
                          writes=[("w", slot, 0), ("w", slot, 1)], dma=True)
                    for ti, (c0, n) in enumerate(blk):
                        it = mc * len(blk) + ti
                        bk = it % 6
                        xb = it % 2
                        sc.op("sp", lambda e, xb=xb, mc=mc, c0=c0, n=n: e.dma_start(out=xres[xb][:, 0:n], in_=xTv[:, mc, c0:c0 + n]),
                              writes=[("xres", xb)], dma=True)
                        for kc in range(FC):
                            sc.op("pe", lambda e, vw=vw, kc=kc, c0=c0, n=n, bk=bk, hc0=hc0: e.matmul(
                                bank(bk)[:, 0:n], vw[:, kc, :], act[:, kc, c0 - hc0:c0 - hc0 + n],
                                start=(kc == 0), stop=(kc == FC - 1)),
                                reads=[("w", slot, 0), ("w", slot, 1), ("act", kc, c0)], writes=[("ps", bk)])
                        sc.op("dve", lambda e, xb=xb, bk=bk, n=n: e.tensor_tensor(
                            out=xres[xb][:, 0:n], in0=xres[xb][:, 0:n], in1=bank(bk)[:, 0:n], op=ALU.add),
                            reads=[("xres", xb), ("ps", bk)], writes=[("xres", xb)])
                        sc.op("sp", lambda e, xb=xb, mc=mc, c0=c0, n=n: e.dma_start(out=xTv[:, mc, c0:c0 + n], in_=xres[xb][:, 0:n]),
                              reads=[("xres", xb)], writes=[("xTo", mc, c0)], dma=True)
                sc.barrier()

        def phase_pool(li):
            j = li // 2
            ar.reset()
            HP = 2096
            hpad = ar.alloc(BF16, [128, KC, HP])
            hl = ar.alloc(F32, [128, KC, 32])
            icnt = ar.alloc(F32, [128, 16])
            wslot = [ar.alloc(BF16, [128, 2048]) for _ in range(2)]
            xres = [ar.alloc(F32, [128, 512]) for _ in range(2)]
            po = ar.alloc(F32, [16, D])
            big_off = ar.off
            dT = ar.alloc(BF16, [128, KC, TR])
            tA = ar.alloc(F32, [128, HP])
            tB = ar.alloc(F32, [128, HP])
            tC = ar.alloc(F32, [128, 16])
            tx = ar.view(big_off, F32, [128, KC, 512])
            tq = ar.view(big_off + 32768, F32, [128, KC, 512])
            tr = [ar.view(big_off + 65536 + i * 2048, F32, [128, 512]) for i in range(2)]
            rb = ar.view(big_off, F32, [16, D])
            xTv = xT.rearrange("(kc p) t -> p kc t", p=128)
            for t in range(16):
                sc.op("pool", lambda e, t=t: e.memset(icnt[:, t:t + 1], 1.0 / (t + 1)), writes=["icnt"])
            sc.op("pool", lambda e: e.memset(hpad[:, :, 0:16], 0.0), writes=[("hp0",)])
            sc.op("pool", lambda e: e.memset(hpad[:, :, 2064:2065], 0.0), writes=[("hp1",)])
            sc.op("sp", lambda e: e.dma_start(out=rb[0:15, :], in_=st_pool[j]), writes=["rb"], dma=True)
            for q in range(4):
                pv = bank(7)[:, 0:60].rearrange("p (a b) -> p a b", b=15)
                for i in range(4):
                    sc.op("pe", lambda e, q=q, i=i, pv=pv: e.transpose(out=pv[:, i, :], in_=rb[0:15, (q * 4 + i) * 128:(q * 4 + i + 1) * 128],
                                                                 identity=ident_f[0:15, 0:15]),
                          reads=["rb", "ident_f"], writes=[("ps", 7)])
                sc.op("dve", lambda e, q=q, pv=pv: e.tensor_copy(out=hpad[:, q * 4:(q + 1) * 4, 2065:2080], in_=pv),
                      reads=[("ps", 7)], writes=[("hph", q)])
            sc.barrier()
            side = [(2032, 16, 0), (2048, 16, 16)]
            norm_tiles(hpad, nmix, li, TILES[0:4], [tx, tx], [tq, tq], tr, hcol0=-16, side=side, hl=hl)
            norm_tiles(hpad, nmix, li, TILES[4:5], [tx, tx], [tq, tq], tr, hcol0=-32, side=side, hl=hl)
            sc.barrier()
            for (c_lo, dst) in ((1, o_pool_p[j]), (17, o_pool_s[j])):
                for q in range(4):
                    for i in range(4):
                        kc = q * 4 + i
                        sc.op("pe", lambda e, kc=kc, c_lo=c_lo, q=q, i=i: e.transpose(
                            out=bank(q)[0:15, i * 128:(i + 1) * 128], in_=hl[:, kc, c_lo:c_lo + 15], identity=ident_f),
                            reads=[("hl", kc, 0), ("hl", kc, 16), "ident_f"], writes=[("ps", q)])
                    sc.op("act", lambda e, q=q: e.copy(out=po[0:15, q * 512:(q + 1) * 512], in_=bank(q)[0:15, :]),
                          reads=[("ps", q)], writes=[("po", q)])
                sc.op("sp", lambda e, dst=dst: e.dma_start(out=dst, in_=po[0:15, :]), reads=[("po", q) for q in range(4)],
                      writes=[("pout", c_lo)], dma=True)
            for kc in range(KC):
                gi = kc // 4
                w = 2 << gi
                src = hpad[:, kc, :]
                bufs = [tA, tB]
                cur = None
                sh = 1
                for lv in range(gi + 1):
                    dstb = bufs[lv % 2]
                    eng = "dve" if (kc + lv) % 2 == 0 else "pool"
                    a_in = src if cur is None else cur
                    sc.op(eng, lambda e, dstb=dstb, a_in=a_in, sh=sh: e.tensor_tensor(
                        out=dstb[:, sh:HP], in0=a_in[:, sh:HP], in1=a_in[:, 0:HP - sh], op=ALU.add),
                        reads=[("hT", kc, c) for c, _ in TILES] + [("win", id(a_in)), ("hp0",), ("hp1",), ("hph", kc // 4)],
                        writes=[("win", id(dstb))])
                    cur = dstb
                    sh *= 2
                sc.op("dve", lambda e, cur=cur, src=src, w=w, kc=kc: e.scalar_tensor_tensor(
                    out=dT[:, kc, 0:TP], in0=cur[:, 16:16 + TP], scalar=1.0 / w, in1=src[:, 16:16 + TP],
                    op0=ALU.mult, op1=ALU.subtract),
                    reads=[("win", id(cur))] + [("hT", kc, c) for c, _ in TILES], writes=[("dT", kc)])
                sc.op("dve", lambda e, cur=cur, src=src, w=w, kc=kc: e.scalar_tensor_tensor(
                    out=dT[:, kc, TP:TR], in0=cur[:, 2080:2096], scalar=1.0 / w, in1=src[:, 2080:2096],
                    op0=ALU.mult, op1=ALU.subtract),
                    reads=[("win", id(cur))] + [("hT", kc, c) for c, _ in TILES], writes=[("dT", kc)])
                sc.op("dve", lambda e, cur=cur, w=w: e.tensor_tensor(out=tC[:, 0:w - 1], in0=cur[:, 16:16 + w - 1], in1=icnt[:, 0:w - 1], op=ALU.mult),
                      reads=[("win", id(cur)), "icnt"], writes=["tC"])
                sc.op("dve", lambda e, src=src, w=w, kc=kc: e.tensor_tensor(out=dT[:, kc, 0:w - 1], in0=tC[:, 0:w - 1], in1=src[:, 16:16 + w - 1], op=ALU.subtract),
                      reads=["tC"] + [("hT", kc, c) for c, _ in TILES], writes=[("dT", kc)])
            it = 0
            for gi in range(4):
                slot = gi % 2
                vw = wslot[slot].rearrange("p (a b) -> p a b", b=512)
                srcw = pool_w[j, gi].rearrange("(kc p) m -> p kc m", p=128)
                sc.op("pool", lambda e, vw=vw, srcw=srcw: e.dma_start(out=vw, in_=srcw), writes=[("w", slot)], dma=True)
                for ec in range(4):
                    mc = gi * 4 + ec
                    for (c0, n) in TILES:
                        bk = it % 6
                        xb = it % 2
                        it += 1
                        sc.op("sp", lambda e, xb=xb, mc=mc, c0=c0, n=n: e.dma_start(out=xres[xb][:, 0:n], in_=xTv[:, mc, c0:c0 + n]),
                              writes=[("xres", xb)], dma=True)
                        for cc in range(4):
                            sc.op("pe", lambda e, vw=vw, cc=cc, ec=ec, gi=gi, c0=c0, n=n, bk=bk: e.matmul(
                                bank(bk)[:, 0:n], vw[:, cc, ec * 128:(ec + 1) * 128], dT[:, gi * 4 + cc, c0:c0 + n],
                                start=(cc == 0), stop=(cc == 3)),
                                reads=[("w", slot), ("dT", gi * 4 + cc)], writes=[("ps", bk)])
                        sc.op("dve", lambda e, xb=xb, bk=bk, n=n, mc=mc: e.scalar_tensor_tensor(
                            out=xres[xb][:, 0:n], in0=bank(bk)[:, 0:n], scalar=pscale[:, mc, j:j + 1], in1=xres[xb][:, 0:n],
                            op0=ALU.mult, op1=ALU.add),
                            reads=[("xres", xb), ("ps", bk)], writes=[("xres", xb)])
                        sc.op("sp", lambda e, xb=xb, mc=mc, c0=c0, n=n: e.dma_start(out=xTv[:, mc, c0:c0 + n], in_=xres[xb][:, 0:n]),
                              reads=[("xres", xb)], writes=[("xTo", mc, c0)], dma=True)
            sc.barrier()

        def phase_gdn(li):
            j = li // 2
            Win = gdn_w_in[j]
            ar.reset()
            xTv = xT.rearrange("(kc p) t -> p kc t", p=128)
            cw = ar.alloc(F32, [128, 64, 4])
            chs = ar.alloc(F32, [128, 64, 3])
            cst = ar.alloc(F32, [128, 64, 8])
            bT = ar.alloc(F32, [32, TA])
            gT = ar.alloc(F32, [32, TA])
            alog = ar.alloc(F32, [32, 1])
            dtb = ar.alloc(F32, [32, 1])
            nega = ar.alloc(F32, [32, 1])
            gnw = ar.alloc(F32, [128, 1])
            gcTok = ar.alloc(F32, [128, NG, 32])
            nbTok = ar.alloc(F32, [128, NG, 32])
            bTok = ar.alloc(F32, [128, NG, 32])
            egc = ar.alloc(F32, [128, NG, 32])
            ekd = ar.alloc(F32, [128, NG, 32])
            glb = ar.alloc(F32, [128, NG, 32])
            negones = ar.alloc(F32, [32, 128])
            persist_off = ar.off

            rb = ar.alloc(F32, [16, QKV])
            rowbuf_box[0] = rb
            vec_to_cols(gdn_conv_w[j], 4, QKV, cw)
            vec_to_cols(st_conv[j], 3, QKV, chs)
            sc.op("sp", lambda e: e.dma_start(out=alog, in_=gdn_A_log[j].rearrange("(p o) -> p o", o=1)), writes=["alog"], dma=True)
            sc.op("sp", lambda e: e.dma_start(out=dtb, in_=gdn_dt_bias[j].rearrange("(p o) -> p o", o=1)), writes=["dtb"], dma=True)
            sc.op("sp", lambda e: e.dma_start(out=gnw, in_=gdn_norm_w[j].rearrange("(p o) -> p o", o=1)), writes=["gnw"], dma=True)
            sc.op("pool", lambda e: e.memset(negones, -1.0), writes=["negones"])
            sc.op("pool", lambda e: e.memset(cst, 0.0), writes=[("cst", mc) for mc in range(64)])
            sc.barrier()

            ar.off = persist_off
            hT = ar.alloc(BF16, [128, KC, TR])
            g2_off = ar.off
            tx = ar.alloc(F32, [128, KC, 512])
            tq = ar.alloc(F32, [128, KC, 512])
            tr = [ar.alloc(F32, [128, 512]) for _ in range(2)]
            norm_tiles(hT, nmix, li, TILES, [tx, tx], [tq, tq], tr)
            sc.barrier()

            if os.environ.get("GDN_STOP") == "1":
                sc.barrier()
                return
            ar.off = g2_off
            wslot = [ar.alloc(BF16, [128, 4096]) for _ in range(2)]
            PW = 2070
            pre = [ar.alloc(F32, [128, PW]) for _ in range(2)]
            acc = ar.alloc(F32, [128, PW])
            sil = ar.alloc(F32, [128, PW])
            sqb = ar.alloc(BF16, [128, PW])
            vout = [ar.alloc(BF16, [128, TA]) for _ in range(2)]
            rs = acc
            for b in range(2):
                sc.op("pool", lambda e, b=b: e.memset(vout[b][:, TR:TA], 0.0), writes=[("voutpad", b)])
                sc.op("pool", lambda e, b=b: e.memset(pre[b][:, 0:3], 0.0), writes=[("prepad", b)])
            PCOL = [3, 515, 1027, 1539, 2054]
            nchunks_total = 97
            itc = 0
            qkvp = list(range(32))
            zp = list(range(32, 48))
            porder = []
            for i in range(16):
                porder += [qkvp[2 * i], zp[i], qkvp[2 * i + 1]]
            porder.append(48)
            mc_order = []
            for pi in porder:
                for mc in (2 * pi, 2 * pi + 1):
                    if mc < nchunks_total:
                        mc_order.append(mc)
            wl = 0
            for mc in mc_order:
                if mc % 2 == 0:
                    slot = wl % 2
                    wl += 1
                    ncols = min(256, IN_DIM - mc * 128)
                    wv = wload(wslot, slot, Win, KC, mc * 128, ncols)
                wi = mc % 2
                M = 128 if mc < 96 else 64
                pb = mc % 2
                kind = "q" if mc < 16 else ("k" if mc < 32 else ("v" if mc < 64 else ("z" if mc < 96 else "ba")))
                if kind in ("q", "k", "v"):
                    sc.op("dve", lambda e, pb=pb, mc=mc: e.tensor_copy(out=pre[pb][:, 2051:2054], in_=chs[:, mc, :]),
                          reads=[("cols", id(chs))], writes=[("prehist", pb)])
                for ti, (c0, n) in enumerate(TILES):
                    if kind == "ba":
                        halves = [(0, 32, bT), (32, 64, gT)]
                    else:
                        halves = [(0, M, None)]
                    for (m0, m1, dstT) in halves:
                        bk = itc % 6
                        itc += 1
                        for kc in range(KC):
                            sc.op("pe", lambda e, wv=wv, kc=kc, wi=wi, m0=m0, m1=m1, c0=c0, n=n, bk=bk: e.matmul(
                                bank(bk)[0:m1 - m0, 0:n], wv[:, kc, wi * 128 + m0:wi * 128 + m1], hT[:, kc, c0:c0 + n],
                                start=(kc == 0), stop=(kc == KC - 1)),
                                reads=[("w", slot), ("hT", kc, c0)], writes=[("ps", bk)])
                        if kind in ("q", "k", "v"):
                            sc.op("act", lambda e, pb=pb, ti=ti, n=n, bk=bk: e.copy(out=pre[pb][:, PCOL[ti]:PCOL[ti] + n], in_=bank(bk)[:, 0:n]),
                                  reads=[("ps", bk)], writes=[("pre", pb, ti)])
                        elif kind == "z":
                            zb = mc % 2
                            sc.op("act", lambda e, zb=zb, c0=c0, n=n, bk=bk: e.activation(out=vout[zb][:, c0:c0 + n], in_=bank(bk)[:, 0:n], func=AF.Silu),
                                  reads=[("ps", bk)], writes=[("vout", zb, ti)])
                        else:
                            sc.op("act", lambda e, dstT=dstT, c0=c0, n=n, bk=bk: e.copy(out=dstT[:, c0:c0 + n], in_=bank(bk)[0:32, 0:n]),
                                  reads=[("ps", bk)], writes=[("baT", id(dstT), ti)])
                if kind == "z":
                    zb = mc % 2
                    h = mc - 64
                    sc.op("sp", lambda e, zb=zb, h=h: e.dma_start(out=zT[h * 128:(h + 1) * 128, :], in_=vout[zb]),
                          reads=[("vout", zb, ti) for ti in range(5)] + [("voutpad", zb)], writes=[("zT", h)], dma=True)
                if kind in ("q", "k", "v"):
                    prk = [("pre", pb, ti) for ti in range(5)] + [("prehist", pb), ("prepad", pb)]
                    P = pre[pb]
                    sc.op("dve", lambda e, P=P, mc=mc: e.tensor_copy(out=cst[:, mc, 0:3], in_=P[:, 2048:2051]),
                          reads=prk, writes=[("cst", mc)])
                    sc.op("dve", lambda e, P=P, mc=mc: e.tensor_copy(out=cst[:, mc, 4:7], in_=P[:, 2067:2070]),
                          reads=prk, writes=[("cst", mc)])
                    L = PW - 3
                    sc.op("dve", lambda e, P=P, mc=mc: e.tensor_scalar(out=acc[:, 0:L], in0=P[:, 3:PW], scalar1=cw[:, mc, 3:4], scalar2=None, op0=ALU.mult),
                          reads=prk + [("cols", id(cw))], writes=["acc"])
                    for tap in (2, 1, 0):
                        sc.op("dve", lambda e, P=P, mc=mc, tap=tap: e.scalar_tensor_tensor(
                            out=acc[:, 0:L], in0=P[:, tap:tap + L], scalar=cw[:, mc, tap:tap + 1], in1=acc[:, 0:L],
                            op0=ALU.mult, op1=ALU.add),
                            reads=prk + ["acc"], writes=["acc"])
                    vb = mc % 2
                    if kind == "v":
                        sc.op("act", lambda e, vb=vb: e.activation(out=vout[vb][:, 0:TP], in_=acc[:, 0:TP], func=AF.Silu),
                              reads=["acc"], writes=[("vout", vb, 0)])
                        sc.op("act", lambda e, vb=vb: e.activation(out=vout[vb][:, TP:TR], in_=acc[:, 2051:2067], func=AF.Silu),
                              reads=["acc"], writes=[("vout", vb, 1)])
                    else:
                        sc.op("act", lambda e: e.activation(out=sil[:, 0:L], in_=acc[:, 0:L], func=AF.Silu),
                              reads=["acc"], writes=["sil"])
                        sc.op("act", lambda e: e.activation(out=sqb[:, 0:L], in_=sil[:, 0:L], func=AF.Square),
                              reads=["sil"], writes=["sqb"])
                        c = 0
                        while c < L:
                            n = min(512, L - c)
                            bk = itc % 6
                            itc += 1
                            sc.op("pe", lambda e, c=c, n=n, bk=bk: e.matmul(bank(bk)[:, 0:n], ones_b, sqb[:, c:c + n], start=True, stop=True),
                                  reads=["sqb", "ones_b"], writes=[("ps", bk)])
                            sc.op("act", lambda e, c=c, n=n, bk=bk: e.activation(out=rs[:, c:c + n], in_=bank(bk)[:, 0:n], func=AF.Ln, bias=EPS, scale=1.0),
                                  reads=[("ps", bk), "sil"], writes=["acc"])
                            c += n
                        rsk = ["acc"]
                        sc.op("act", lambda e: e.activation(out=rs[:, 0:L], in_=rs[:, 0:L], func=AF.Exp, scale=-0.5),
                              reads=rsk, writes=rsk)
                        qs = (128.0 ** -0.5) if kind == "q" else 1.0
                        sc.op("dve", lambda e, vb=vb, qs=qs: e.scalar_tensor_tensor(
                            out=vout[vb][:, 0:TP], in0=sil[:, 0:TP], scalar=qs, in1=rs[:, 0:TP], op0=ALU.mult, op1=ALU.mult),
                            reads=["sil"] + rsk, writes=[("vout", vb, 0)])
                        sc.op("dve", lambda e, vb=vb, qs=qs: e.scalar_tensor_tensor(
                            out=vout[vb][:, TP:TR], in0=sil[:, 2051:2067], scalar=qs, in1=rs[:, 2051:2067], op0=ALU.mult, op1=ALU.mult),
                            reads=["sil"] + rsk, writes=[("vout", vb, 1)])
                    sc.op("sp", lambda e, vb=vb, mc=mc: e.dma_start(out=qkvT[mc * 128:(mc + 1) * 128, :], in_=vout[vb]),
                          reads=[("vout", vb, 0), ("vout", vb, 1), ("voutpad", vb)], writes=[("qkvT", mc)], dma=True)
            if os.environ.get("GDN_STOP") == "2a":
                sc.barrier()
                return
            crow_p = acc[0:3, 0:2048]
            crow_s = sil[0:3, 0:2048]
            for r4 in range(4):
                for q in range(4):
                    for i in range(4):
                        mc = r4 * 16 + q * 4 + i
                        sc.op("pe", lambda e, mc=mc, q=q, i=i: e.transpose(out=bank(q)[0:4, i * 128:(i + 1) * 128], in_=cst[:, mc, 0:4], identity=ident_f),
                              reads=[("cst", mc), "ident_f"], writes=[("ps", q)])
                        sc.op("pe", lambda e, mc=mc, q=q, i=i: e.transpose(out=bank(q + 4)[0:4, i * 128:(i + 1) * 128], in_=cst[:, mc, 4:8], identity=ident_f),
                              reads=[("cst", mc), "ident_f"], writes=[("ps", q + 4)])
                    sc.op("act", lambda e, q=q: e.copy(out=crow_p[:, q * 512:(q + 1) * 512], in_=bank(q)[0:3, :]),
                          reads=[("ps", q), "acc"], writes=[("crowp", q), "acc"])
                    sc.op("dve", lambda e, q=q: e.tensor_copy(out=crow_s[:, q * 512:(q + 1) * 512], in_=bank(q + 4)[0:3, :]),
                          reads=[("ps", q + 4), "sil"], writes=[("crows", q), "sil"])
                sc.op("sp", lambda e, r4=r4: e.dma_start(out=o_conv_p[j][:, r4 * 2048:(r4 + 1) * 2048], in_=crow_p),
                      reads=[("crowp", q) for q in range(4)] + ["acc"], writes=[("ocp", r4), "acc"], dma=True)
                sc.op("sp", lambda e, r4=r4: e.dma_start(out=o_conv_s[j][:, r4 * 2048:(r4 + 1) * 2048], in_=crow_s),
                      reads=[("crows", q) for q in range(4)] + ["sil"], writes=[("ocs", r4), "sil"], dma=True)
            sc.barrier()

            if os.environ.get("GDN_STOP") == "2":
                sc.barrier()
                return
            ar.off = persist_off
            Gd = ar.alloc(F32, [32, NG, 32])
            tmpE = ar.alloc(F32, [32, TA])
            bak = [("baT", id(bT), ti) for ti in range(5)]
            gak = [("baT", id(gT), ti) for ti in range(5)]
            sc.op("act", lambda e: e.activation(out=bT[:, 0:TR], in_=bT[:, 0:TR], func=AF.Sigmoid), reads=bak, writes=["bT"])
            sc.op("pool", lambda e: e.memset(bT[:, TR:TA], 0.0), writes=["bTpad"])
            sc.op("act", lambda e: e.activation(out=nega, in_=alog, func=AF.Exp), reads=["alog"], writes=["nega"])
            sc.op("act", lambda e: e.activation(out=tmpE[:, 0:TR], in_=gT[:, 0:TR], func=AF.Exp, bias=dtb[:, 0:1], scale=1.0),
                  reads=gak + ["dtb"], writes=["tmpE"])
            sc.op("act", lambda e: e.activation(out=tmpE[:, 0:TR], in_=tmpE[:, 0:TR], func=AF.Ln, bias=1.0, scale=1.0),
                  reads=["tmpE"], writes=["tmpE"])
            sc.op("dve", lambda e: e.tensor_scalar(out=gT[:, 0:TR], in0=tmpE[:, 0:TR], scalar1=nega[:, 0:1], scalar2=-1.0, op0=ALU.mult, op1=ALU.mult),
                  reads=["tmpE", "nega"] + gak, writes=["gTg"])
            sc.op("pool", lambda e: e.memset(gT[:, TR:TA], 0.0), writes=["gTpad"])
            if os.environ.get("GDN_STOP") == "3a":
                sc.barrier()
                return
            for n in range(NG):
                sc.op("dve", lambda e, n=n: e.tensor_tensor_scan(out=tmpE[:, n * 128:(n + 1) * 128], data0=ones_f[0:32, :], data1=gT[:, n * 128:(n + 1) * 128],
                                                               initial=0.0, op0=ALU.mult, op1=ALU.add),
                      reads=["gTg", "gTpad", "ones_f", "tmpE"], writes=[("gc", n)])
            gck = [("gc", n) for n in range(NG)]
            sc.op("dve", lambda e: e.tensor_copy(out=gT, in_=tmpE), reads=gck + ["gTg", "gTpad"], writes=["gcT"])
            gcT = gT
            if os.environ.get("GDN_STOP") == "3b":
                sc.barrier()
                return
            for n in range(NG):
                sc.op("pe", lambda e, n=n: e.transpose(out=bank(6)[:, 0:32], in_=gcT[:, n * 128:(n + 1) * 128], identity=ident_f[0:32, 0:32]),
                      reads=["gcT", "ident_f"], writes=[("ps", 6)])
                sc.op("dve", lambda e, n=n: e.tensor_copy(out=gcTok[:, n, :], in_=bank(6)[:, 0:32]), reads=[("ps", 6)], writes=[("gcTok", n)])
                sc.op("pe", lambda e, n=n: e.transpose(out=bank(7)[:, 0:32], in_=bT[:, n * 128:(n + 1) * 128], identity=ident_f[0:32, 0:32]),
                      reads=["bT", "bTpad", "ident_f"], writes=[("ps", 7)])
                sc.op("act", lambda e, n=n: e.copy(out=bTok[:, n, :], in_=bank(7)[:, 0:32]), reads=[("ps", 7)], writes=[("bTok", n)])
            gtk = [("gcTok", n) for n in range(NG)]
            btk = [("bTok", n) for n in range(NG)]
            sc.op("dve", lambda e: e.tensor_scalar(out=nbTok, in0=bTok, scalar1=-1.0, scalar2=None, op0=ALU.mult), reads=btk, writes=["nbTok"])
            sc.op("act", lambda e: e.activation(out=egc, in_=gcTok, func=AF.Exp), reads=gtk, writes=["egc"])
            if os.environ.get("GDN_STOP") == "3c":
                sc.barrier()
                return
            gclast = gcT.rearrange("p (n c) -> p n c", c=128)[:, :, 127]
            for h in range(32):
                sc.op("dve", lambda e, h=h: e.tensor_scalar(out=Gd[:, :, h], in0=gclast, scalar1=ident_f[0:32, h:h + 1], scalar2=None, op0=ALU.mult),
                      reads=["gcT", "ident_f"], writes=[("Gd", h)])
            gdk = [("Gd", h) for h in range(32)]
            Gdf = Gd.rearrange("p a b -> p (a b)")
            glbf = glb.rearrange("p a b -> p (a b)")
            ekdf = ekd.rearrange("p a b -> p (a b)")
            gctf = gcTok.rearrange("p a b -> p (a b)")
            for half, (c0, c1) in enumerate(((0, 288), (288, 544))):
                sc.op("pe", lambda e, half=half, c0=c0, c1=c1: e.matmul(bank(6 + half)[:, 0:c1 - c0], ones_f[0:32, :], Gdf[:, c0:c1], start=True, stop=True),
                      reads=gdk + ["ones_f"], writes=[("ps", 6 + half)])
                sc.op("dve", lambda e, half=half, c0=c0, c1=c1: e.tensor_copy(out=glbf[:, c0:c1], in_=bank(6 + half)[:, 0:c1 - c0]),
                      reads=[("ps", 6 + half)], writes=[("glraw", half)])
            if os.environ.get("GDN_STOP") == "3d":
                sc.barrier()
                return
            sc.op("dve", lambda e: e.tensor_tensor(out=ekdf, in0=glbf, in1=gctf, op=ALU.subtract),
                  reads=[("glraw", 0), ("glraw", 1)] + gtk, writes=["ekd"])
            sc.op("act", lambda e: e.activation(out=ekdf, in_=ekdf, func=AF.Exp), reads=["ekd"], writes=["ekdx"])
            sc.op("act", lambda e: e.activation(out=glbf, in_=glbf, func=AF.Exp), reads=[("glraw", 0), ("glraw", 1), "ekd"], writes=["glb"])
            sc.barrier()

            if os.environ.get("GDN_STOP") == "3":
                sc.barrier()
                return
            ar.off = persist_off
            kTq = [[ar.alloc(BF16, [128, TA]) for _ in range(2)] for _ in range(4)]
            ogT = [ar.alloc(BF16, [128, TA]) for _ in range(2)]
            gcm = [ar.alloc(F32, [32, TA])] * 2
            oh = [ar.alloc(F32, [32, 128])] * 2
            S_p = [ar.alloc(F32, [128, 128]) for _ in range(2)]
            S_s = [ar.alloc(F32, [128, 128]) for _ in range(2)]
            Sb = [ar.alloc(BF16, [128, 128]) for _ in range(2)]
            class _L:
                def __init__(self, t):
                    self.t = t

                def __getitem__(self, n):
                    return self.t[:, n, :]
            ATt = [ar.alloc(BF16, [128, NG, 128]) for _ in range(2)]
            U7t = [ar.alloc(BF16, [128, NG, 128]) for _ in range(2)]
            nWTt = [ar.alloc(BF16, [128, NG, 128]) for _ in range(2)]
            kdtt = [ar.alloc(BF16, [128, NG, 128]) for _ in range(2)]
            vtkt = [ar.alloc(BF16, [128, NG, 128]) for _ in range(2)]
            AT, U7, nWT, kdt, vtk = [[_L(t) for t in tt] for tt in (ATt, U7t, nWTt, kdtt, vtkt)]
            GR = 6
            SG = 3
            Dm4s = [ar.alloc(F32, [128, SG, 128]) for _ in range(2)]
            DmS4s = [ar.alloc(F32, [128, SG, 128]) for _ in range(2)]
            Xh4s = [ar.alloc(BF16, [128, SG, 128]) for _ in range(2)]
            N4s = [[ar.alloc(F32, [128, SG, 128]) for _ in range(1)] for _ in range(2)]
            NT4s = [[ar.alloc(F32, [128, SG, 128]) for _ in range(1)] for _ in range(2)]
            Nbs = [[ar.alloc(BF16, [128, SG, 128]) for _ in range(2)] for _ in range(2)]
            NTbs = [[ar.alloc(BF16, [128, SG, 128]) for _ in range(2)] for _ in range(2)]
            Ubs = [[ar.alloc(BF16, [128, SG, 128]) for _ in range(2)] for _ in range(2)]
            X0fs = [ar.alloc(F32, [128, SG, 128]) for _ in range(2)]
            ImXs = [ar.alloc(F32, [128, SG, 128]) for _ in range(2)]
            E0fs = [ar.alloc(F32, [128, SG, 128]) for _ in range(2)]
            X0Ts = [ar.alloc(F32, [128, SG, 128]) for _ in range(2)]
            print("G4 arena bytes", ar.off)
            ident4 = ar.alloc(F32, [128, 4, 128])
            for i in range(4):
                sc.op("pool", lambda e, i=i: e.tensor_copy(out=ident4[:, i, :], in_=ident_f), reads=["ident_f"], writes=[("ident4", i)])
            id4k = [("ident4", i) for i in range(4)]
            vnew = [ar.alloc(BF16, [128, 128]) for _ in range(2)]
            otmp = [ar.alloc(F32, [128, 128]) for _ in range(2)]
            otok = [ar.alloc(F32, [128, 128]) for _ in range(2)]
            osq = [ar.alloc(F32, [128, 128]) for _ in range(2)]
            onb = [ar.alloc(BF16, [128, 128]) for _ in range(2)]
            ssq = [ar.alloc(F32, [128, 1]) for _ in range(2)]

            def pq(b, q):
                return PS[:, b, q * 128:(q + 1) * 128]

            def pqb(b, q):
                return PS[:, b, q * 128:q * 128 + 64].bitcast(BF16)

            bcnt = [0]
            stopat = os.environ.get('GDN_STOP')

            def do_head(h):
                hb = h % 2
                kh = h // 2
                srcs = [qkvT[2048 + kh * 128:2048 + (kh + 1) * 128, :], qkvT[kh * 128:(kh + 1) * 128, :],
                        qkvT[4096 + h * 128:4096 + (h + 1) * 128, :], zT[h * 128:(h + 1) * 128, :]]
                kT_h, qT_h, vT_h, zs_h = kTq[0][hb], kTq[1][hb], kTq[2][hb], kTq[3][hb]
                kk_, qk_, vk_, zk_ = ("hin", 0, hb), ("hin", 1, hb), ("hin", 2, hb), ("hin", 3, hb)

                def setup():
                    for a in range(4):
                        sc.op("sp", lambda e, a=a, src=srcs[a]: e.dma_start(out=kTq[a][hb], in_=src), writes=[("hin", a, hb)], dma=True)
                    sc.op("dve", lambda e: e.tensor_scalar(out=gcm[hb], in0=gcT, scalar1=ident_f[0:32, h:h + 1], scalar2=None, op0=ALU.mult),
                          reads=["gcT", "ident_f"], writes=[("gcm", 0)])
                    sc.op("dve", lambda e: e.tensor_scalar(out=oh[hb], in0=ones_f[0:32, :], scalar1=ident_f[0:32, h:h + 1], scalar2=None, op0=ALU.mult),
                          reads=["ones_f", "ident_f"], writes=[("oh", 0)])
                def groupfn(g0, si):
                    grp = list(range(g0, min(NG, g0 + SG)))
                    G = len(grp)
                    R0, R1, R2 = 3 * si, 3 * si + 1, 3 * si + 2
                    Dm4, DmS4, Xh4, N4, NT4 = Dm4s[si], DmS4s[si], Xh4s[si], N4s[si], NT4s[si]
                    Nb, NTb, Ub, X0f, ImX, E0f, X0T = Nbs[si], NTbs[si], Ubs[si], X0fs[si], ImXs[si], E0fs[si], X0Ts[si]
                    kDm, kDmS = ("Dm4", si), ("DmS4", si)
                    bqv = lambda b: PS[:, b, :].bitcast(BF16).rearrange("p (q x) -> p q x", x=256)[:, 0:G, 0:128]

                    def b4(b):
                        return PS[:, b, 0:G * 128].rearrange("p (q x) -> p q x", x=128)
                    for i, n in enumerate(grp):
                        cs = n * 128
                        sc.op("pe", lambda e, i=i, cs=cs: e.transpose(out=pqb(R1, i), in_=kT_h[:, cs:cs + 128], identity=ident_b),
                              reads=[kk_, "ident_b"], writes=[("ps", R1)])
                    for i, n in enumerate(grp):
                        sc.op("act", lambda e, i=i, n=n: e.activation(out=Xh4[:, i, :], in_=pqb(R1, i), func=AF.Copy, scale=egc[:, n, h:h + 1]),
                              reads=[("ps", R1), "egc"], writes=[("Xh", si, i)])
                    for i, n in enumerate(grp):
                        sc.op("dve", lambda e, i=i, n=n: e.tensor_scalar(out=kdt[hb][n], in0=pqb(R1, i), scalar1=ekd[:, n, h:h + 1], scalar2=None, op0=ALU.mult),
                              reads=[("ps", R1), "ekdx"], writes=[("kdt", hb, n)])
                    for i, n in enumerate(grp):
                        cs = n * 128
                        sc.op("pe", lambda e, i=i, cs=cs: e.matmul(pq(R0, i), oh[hb], gcT[:, cs:cs + 128], start=True, stop=False),
                              reads=[("oh", 0), "gcT"], writes=[("ps", R0)])
                        sc.op("pe", lambda e, i=i, cs=cs: e.matmul(pq(R0, i), gcm[hb][:, cs:cs + 128], negones, start=False, stop=False),
                              reads=[("gcm", 0), "negones"], writes=[("ps", R0)])
                        sc.op("pe", lambda e, i=i: e.matmul(pq(R0, i), ident_b, mneg_b, start=False, stop=True),
                              reads=["ident_b", "mneg_b"], writes=[("ps", R0)])
                    sc.op("act", lambda e: e.activation(out=Dm4[:, 0:G, :], in_=b4(R0), func=AF.Exp), reads=[("ps", R0)], writes=[kDm])
                    sc.op("dve", lambda e: e.tensor_tensor(out=DmS4[:, 0:G, :], in0=Dm4[:, 0:G, :], in1=ident4[:, 0:G, :], op=ALU.subtract),
                          reads=[kDm] + id4k, writes=[kDmS])
                    yield
                    for i, n in enumerate(grp):
                        cs = n * 128
                        sc.op("pe", lambda e, i=i, cs=cs: e.matmul(pq(R1, i), kT_h[:, cs:cs + 128], qT_h[:, cs:cs + 128], start=True, stop=True),
                              reads=[kk_, qk_], writes=[("ps", R1)])
                    for i, n in enumerate(grp):
                        cs = n * 128
                        sc.op("pe", lambda e, i=i, cs=cs: e.matmul(pq(R2, i), kT_h[:, cs:cs + 128], kT_h[:, cs:cs + 128], start=True, stop=True),
                              reads=[kk_], writes=[("ps", R2)])
                    sc.op("dve", lambda e: e.tensor_tensor(out=ATt[hb][:, g0:g0 + G, :], in0=b4(R1), in1=Dm4[:, 0:G, :], op=ALU.mult),
                          reads=[("ps", R1), kDm], writes=[("AT", hb, n) for n in grp])
                    for i, n in enumerate(grp):
                        sc.op("dve", lambda e, i=i, n=n: e.scalar_tensor_tensor(out=N4[0][:, i, :], in0=pq(R2, i), scalar=nbTok[:, n, h:h + 1], in1=DmS4[:, i, :],
                                                                               op0=ALU.mult, op1=ALU.mult),
                              reads=[("ps", R2), kDmS, "nbTok"], writes=[("N", si, 0)])
                    for i, n in enumerate(grp):
                        cs = n * 128
                        sc.op("pe", lambda e, i=i, cs=cs: e.transpose(out=pqb(R0, i), in_=vT_h[:, cs:cs + 128], identity=ident_b),
                              reads=[vk_, "ident_b"], writes=[("ps", R0)])
                    sc.op("act", lambda e: e.copy(out=vtkt[hb][:, g0:g0 + G, :], in_=bqv(R0)), reads=[("ps", R0)], writes=[("vtk", hb, n) for n in grp])
                    yield
                    for i, n in enumerate(grp):
                        sc.op("pe", lambda e, i=i: e.transpose(out=pq(R1, i), in_=N4[0][:, i, :], identity=ident_f),
                              reads=[("N", si, 0), "ident_f"], writes=[("ps", R1)])
                    sc.op("act", lambda e: e.copy(out=NT4[0][:, 0:G, :], in_=b4(R1)), reads=[("ps", R1)], writes=[("NT", si, 0)])
                    sc.op("pool", lambda e: e.tensor_tensor(out=Ub[0][:, 0:G, :], in0=N4[0][:, 0:G, :], in1=ident4[:, 0:G, :], op=ALU.add),
                          reads=[("N", si, 0)] + id4k, writes=[("Ub", si, 0)])
                    sc.op("dve", lambda e: e.tensor_copy(out=Nb[0][:, 0:G, :], in_=N4[0][:, 0:G, :]), reads=[("N", si, 0)], writes=[("Nb", si, 0)])
                    sc.op("act", lambda e: e.copy(out=NTb[0][:, 0:G, :], in_=b4(R1)), reads=[("ps", R1), ("NT", si, 0)], writes=[("NTb", si, 0)])
                    sc.op("pool", lambda e: e.tensor_tensor(out=ImX[:, 0:G, :], in0=NT4[0][:, 0:G, :], in1=ident4[:, 0:G, :], op=ALU.subtract),
                          reads=[("NT", si, 0)] + id4k, writes=[("NTmI", si)])
                    yield
                    NLV = 4
                    for lv in range(1, NLV + 1):
                        a, bb = (lv - 1) % 2, lv % 2
                        if lv < NLV:
                            for i, n in enumerate(grp):
                                sc.op("pe", lambda e, i=i, a=a: e.matmul(pq(R0, i), NTb[a][:, i, :], Nb[a][:, i, :], start=True, stop=True),
                                      reads=[("NTb", si, a), ("Nb", si, a)], writes=[("ps", R0)])
                        for i, n in enumerate(grp):
                            sc.op("pe", lambda e, i=i, a=a: e.matmul(pq(R1, i), Nb[a][:, i, :], NTb[a][:, i, :], start=True, stop=True),
                                  reads=[("NTb", si, a), ("Nb", si, a)], writes=[("ps", R1)])
                        yield
                        if lv < NLV:
                            sc.op("dve", lambda e, bb=bb: e.tensor_copy(out=Nb[bb][:, 0:G, :], in_=b4(R0)), reads=[("ps", R0)], writes=[("Nb", si, bb)])
                        sc.op("act", lambda e, bb=bb: e.copy(out=NTb[bb][:, 0:G, :], in_=b4(R1)), reads=[("ps", R1)], writes=[("NTb", si, bb)])
                        for i, n in enumerate(grp):
                            sc.op("pe", lambda e, i=i, a=a, bb=bb: e.matmul(pq(R2, i), NTb[bb][:, i, :], Ub[a][:, i, :], start=True, stop=True),
                                  reads=[("NTb", si, bb), ("Ub", si, a)], writes=[("ps", R2)])
                        yield
                        if lv < NLV:
                            sc.op("dve", lambda e, a=a, bb=bb: e.tensor_tensor(out=Ub[bb][:, 0:G, :], in0=b4(R2), in1=Ub[a][:, 0:G, :], op=ALU.add),
                                  reads=[("ps", R2), ("Ub", si, a)], writes=[("Ub", si, bb)])
                        else:
                            sc.op("dve", lambda e, a=a: e.tensor_tensor(out=X0f[:, 0:G, :], in0=b4(R2), in1=Ub[a][:, 0:G, :], op=ALU.add),
                                  reads=[("ps", R2), ("Ub", si, a)], writes=[("X0f", si)])
                    for i, n in enumerate(grp):
                        sc.op("pe", lambda e, i=i: e.matmul(pq(R0, i), ImX[:, i, :], X0f[:, i, :], start=True, stop=True),
                              reads=[("NTmI", si), ("X0f", si)], writes=[("ps", R0)])
                    for i, n in enumerate(grp):
                        sc.op("pe", lambda e, i=i: e.transpose(out=pq(R1, i), in_=X0f[:, i, :], identity=ident_f),
                              reads=[("X0f", si), "ident_f"], writes=[("ps", R1)])
                    yield
                    sc.op("dve", lambda e: e.tensor_tensor(out=E0f[:, 0:G, :], in0=b4(R0), in1=ident4[:, 0:G, :], op=ALU.add),
                          reads=[("ps", R0)] + id4k, writes=[("E0f", si)])
                    sc.op("act", lambda e: e.copy(out=X0T[:, 0:G, :], in_=b4(R1)), reads=[("ps", R1)], writes=[("X0T", si)])
                    for i, n in enumerate(grp):
                        sc.op("pe", lambda e, i=i: e.matmul(pq(R2, i), X0T[:, i, :], E0f[:, i, :], start=True, stop=True),
                              reads=[("X0T", si), ("E0f", si)], writes=[("ps", R2)])
                    yield
                    sc.op("dve", lambda e: e.tensor_tensor(out=U7t[hb][:, g0:g0 + G, :], in0=b4(R2), in1=X0f[:, 0:G, :], op=ALU.add),
                          reads=[("ps", R2), ("X0f", si)], writes=[("U7", hb, n) for n in grp])
                    yield
                    for i, n in enumerate(grp):
                        sc.op("pe", lambda e, i=i, n=n: e.matmul(pq(R0, i), Xh4[:, i, :], U7[hb][n], start=True, stop=True),
                              reads=[("Xh", si, i), ("U7", hb, n)], writes=[("ps", R0)])
                    sc.op("act", lambda e: e.activation(out=nWTt[hb][:, g0:g0 + G, :], in_=b4(R0), func=AF.Copy, scale=-1.0),
                          reads=[("ps", R0)], writes=[("nWT", hb, n) for n in grp])

                def chunkfn(n):
                    cs = n * 128
                    Sf = S_p[hb] if n < 16 else S_s[hb]
                    skey = ("S", hb, 0 if n < 16 else 1)
                    if n == 0:
                        sc.op("pool", lambda e, Sf=Sf: e.memset(Sf, 0.0), writes=[skey])
                        sc.op("pool", lambda e, hb=hb: e.memset(Sb[hb], 0.0), writes=[("Sb", hb)])
                    if n == 16:
                        sc.op("sp", lambda e, Sf=Sf, h=h: e.dma_start(out=Sf, in_=st_S[j, h]), writes=[skey], dma=True)
                        sc.op("act", lambda e, Sf=Sf, hb=hb: e.copy(out=Sb[hb], in_=Sf), reads=[skey], writes=[("Sb", hb)])
                    bq = bcnt[0] % 2
                    bcnt[0] += 1
                    B0 = 6
                    B1 = B0 + 1
                    sc.op("pe", lambda e, n=n, hb=hb, B0=B0: e.matmul(pq(B0, 0), U7[hb][n], vtk[hb][n], start=True, stop=False),
                          reads=[("U7", hb, n), ("vtk", hb, n)], writes=[("pq", B0, 0)])
                    sc.op("pe", lambda e, n=n, hb=hb, B0=B0: e.matmul(pq(B0, 0), nWT[hb][n], Sb[hb], start=False, stop=True),
                          reads=[("nWT", hb, n), ("Sb", hb)], writes=[("pq", B0, 0)])
                    yield
                    sc.op("dve", lambda e, n=n, h=h, bq=bq, B0=B0: e.tensor_scalar(out=vnew[bq], in0=pq(B0, 0), scalar1=bTok[:, n, h:h + 1], scalar2=None, op0=ALU.mult),
                          reads=[("pq", B0, 0)] + btk, writes=[("vnew", bq)])
                    sc.op("pe", lambda e, cs=cs, hb=hb, B0=B0, qT_h=qT_h: e.matmul(pq(B0, 1), qT_h[:, cs:cs + 128], Sb[hb], start=True, stop=True),
                          reads=[qk_, ("Sb", hb)], writes=[("pq", B0, 1)])
                    sc.op("pe", lambda e, n=n, hb=hb, bq=bq, B0=B0: e.matmul(pq(B0, 2), AT[hb][n], vnew[bq], start=True, stop=True),
                          reads=[("AT", hb, n), ("vnew", bq)], writes=[("pq", B0, 2)])
                    sc.op("pe", lambda e, n=n, hb=hb, bq=bq, B1=B1: e.matmul(pq(B1, 0), kdt[hb][n], vnew[bq], start=True, stop=True),
                          reads=[("kdt", hb, n), ("vnew", bq)], writes=[("pq", B1, 0)])
                    yield
                    sc.op("act", lambda e, n=n, h=h, bq=bq, B0=B0: e.activation(out=otmp[bq], in_=pq(B0, 1), func=AF.Copy, scale=egc[:, n, h:h + 1]),
                          reads=[("pq", B0, 1), "egc"], writes=[("otmp", bq)])
                    sc.op("dve", lambda e, bq=bq, B0=B0: e.tensor_tensor(out=otok[bq], in0=otmp[bq], in1=pq(B0, 2), op=ALU.add),
                          reads=[("otmp", bq), ("pq", B0, 2)], writes=[("otok", bq)])
                    sc.op("dve", lambda e, n=n, h=h, Sf=Sf, B1=B1, hb=hb: e.scalar_tensor_tensor(out=Sb[hb], in0=Sf, scalar=glb[:, n, h:h + 1], in1=pq(B1, 0),
                                                                                          op0=ALU.mult, op1=ALU.add),
                          reads=[skey, ("pq", B1, 0), "glb"], writes=[("Sb", hb)])
                    sc.op("dve", lambda e, n=n, h=h, Sf=Sf, B1=B1: e.scalar_tensor_tensor(out=Sf, in0=Sf, scalar=glb[:, n, h:h + 1], in1=pq(B1, 0),
                                                                                   op0=ALU.mult, op1=ALU.add),
                          reads=[skey, ("pq", B1, 0), "glb"], writes=[skey])
                    if n == 15:
                        sc.op("sp", lambda e, Sf=Sf, h=h: e.dma_start(out=o_S_p[j, h], in_=Sf), reads=[skey], writes=[("oSp", h)], dma=True)
                    if n == 16:
                        sc.op("sp", lambda e, Sf=Sf, h=h: e.dma_start(out=o_S_s[j, h], in_=Sf), reads=[skey], writes=[("oSs", h)], dma=True)
                    yield
                    sc.op("act", lambda e, bq=bq: e.activation(out=osq[bq], in_=otok[bq], func=AF.Square, accum_out=ssq[bq]),
                          reads=[("otok", bq)], writes=[("ssq", bq), ("osq", bq)])
                    sc.op("act", lambda e, bq=bq: e.activation(out=ssq[bq], in_=ssq[bq], func=AF.Ln, scale=1.0 / 128, bias=EPS),
                          reads=[("ssq", bq)], writes=[("ssq", bq)])
                    sc.op("act", lambda e, bq=bq: e.activation(out=ssq[bq], in_=ssq[bq], func=AF.Exp, scale=-0.5),
                          reads=[("ssq", bq)], writes=[("ssq", bq)])
                    sc.op("dve", lambda e, bq=bq: e.tensor_scalar(out=onb[bq], in0=otok[bq], scalar1=ssq[bq][:, 0:1], scalar2=None, op0=ALU.mult),
                          reads=[("otok", bq), ("ssq", bq)], writes=[("onb", bq)])
                    sc.op("pe", lambda e, bq=bq, B1=B1: e.transpose(out=pqb(B1, 1), in_=onb[bq], identity=ident_b),
                          reads=[("onb", bq), "ident_b"], writes=[("pq", B1, 1)])
                    yield
                    sc.op("dve", lambda e, cs=cs, hb=hb, B1=B1, zs_h=zs_h: e.scalar_tensor_tensor(
                        out=ogT[hb][:, cs:cs + 128], in0=pqb(B1, 1), scalar=gnw[:, 0:1], in1=zs_h[:, cs:cs + 128], op0=ALU.mult, op1=ALU.mult),
                        reads=[("pq", B1, 1), "gnw", zk_], writes=[("ogT", hb, n)])
                    if n == NG - 1:
                        sc.op("sp", lambda e, hb=hb, h=h: e.dma_start(out=oT[h * 128:(h + 1) * 128, :], in_=ogT[hb]),
                              reads=[("ogT", hb, n) for n in range(NG)], writes=[("oT", h)], dma=True)
                return setup, groupfn, chunkfn

            heads = [do_head(h) for h in range(NV)]
            g0s = list(range(0, NG, GR))

            def bgen(hd, chunks):
                for n in chunks:
                    yield from hd[2](n)

            def run_pair(hd, g0, bg):
                gens = []
                if hd is not None:
                    gens.append(hd[1](g0, 0))
                    if g0 + SG < NG:
                        gens.append(hd[1](g0 + SG, 1))
                live = list(gens)
                blive = bg is not None
                while live or blive:
                    for g in list(live):
                        try:
                            next(g)
                        except StopIteration:
                            live.remove(g)
                    for _ in range(2):
                        if blive:
                            try:
                                next(bg)
                            except StopIteration:
                                blive = False
            heads[0][0]()
            for g0 in g0s:
                run_pair(heads[0], g0, None)
            for h in range(NV):
                nxt = heads[h + 1] if h + 1 < NV else None
                if nxt is not None:
                    nxt[0]()
                for g0 in g0s:
                    run_pair(nxt, g0, bgen(heads[h], range(g0, min(NG, g0 + GR))))
            sc.barrier()

            if os.environ.get("GDN_STOP") == "4":
                sc.barrier()
                return
            ar.off = persist_off
            Wo = gdn_w_out[j]
            osb = ar.alloc(BF16, [128, 32, 1040])
            wslot = [ar.alloc(BF16, [128, 8192]) for _ in range(2)]
            xres = [ar.alloc(F32, [128, 512]) for _ in range(2)]
            oTv = oT.rearrange("(kc p) t -> p kc t", p=128)
            wcnt = 0
            for blk in (TILES[0:2], TILES[2:5]):
                hc0 = blk[0][0]
                ntok = sum(n for _, n in blk)
                for kc in range(32):
                    sc.op("sp", lambda e, kc=kc, hc0=hc0, ntok=ntok: e.dma_start(out=osb[:, kc, 0:ntok], in_=oTv[:, kc, hc0:hc0 + ntok]),
                          writes=[("osb", kc)], dma=True)
                it = 0
                for mp in range(8):
                    slot = wcnt % 2
                    wcnt += 1
                    vw = wslot[slot].rearrange("p (a b) -> p a b", b=256)
                    srcw = Wo[:, mp * 256:(mp + 1) * 256].rearrange("(kc p) m -> p kc m", p=128)
                    sc.op("pool", lambda e, vw=vw, srcw=srcw: e.dma_start(out=vw, in_=srcw), writes=[("w", slot)], dma=True)
                    for mi in range(2):
                        mc = mp * 2 + mi
                        for (c0, n) in blk:
                            bk = it % 6
                            xb = it % 2
                            it += 1
                            sc.op("sp", lambda e, xb=xb, mc=mc, c0=c0, n=n: e.dma_start(out=xres[xb][:, 0:n], in_=xTv[:, mc, c0:c0 + n]),
                                  writes=[("xres", xb)], dma=True)
                            for kc in range(32):
                                sc.op("pe", lambda e, vw=vw, kc=kc, mi=mi, c0=c0, n=n, bk=bk, hc0=hc0: e.matmul(
                                    bank(bk)[:, 0:n], vw[:, kc, mi * 128:(mi + 1) * 128], osb[:, kc, c0 - hc0:c0 - hc0 + n],
                                    start=(kc == 0), stop=(kc == 31)),
                                    reads=[("w", slot), ("osb", kc)], writes=[("ps", bk)])
                            sc.op("dve", lambda e, xb=xb, bk=bk, n=n: e.tensor_tensor(
                                out=xres[xb][:, 0:n], in0=xres[xb][:, 0:n], in1=bank(bk)[:, 0:n], op=ALU.add),
                                reads=[("xres", xb), ("ps", bk)], writes=[("xres", xb)])
                            sc.op("sp", lambda e, xb=xb, mc=mc, c0=c0, n=n: e.dma_start(out=xTv[:, mc, c0:c0 + n], in_=xres[xb][:, 0:n]),
                                  reads=[("xres", xb)], writes=[("xTo", mc, c0)], dma=True)
                sc.barrier()

        def phase_final():
            ar.reset()
            tmp_x = [ar.alloc(F32, [128, KC, 512]) for _ in range(2)]
            tmp_sq = [ar.alloc(F32, [128, KC, 512])] * 2
            tmp_r = [ar.alloc(F32, [128, 512]) for _ in range(2)]
            yT = ar.alloc(F32, [128, KC, 512])
            yo = [ar.alloc(F32, [128, D]) for _ in range(2)]
            xTv = xT.rearrange("(kc p) t -> p kc t", p=128)
            cnt = 0
            for ti, (c0, n) in enumerate(TILES):
                b = ti % 2
                xt, sq, rr = tmp_x[b], tmp_sq[b], tmp_r[b]
                sc.op("sp", lambda e, xt=xt, c0=c0, n=n: e.dma_start(out=xt[:, :, 0:n], in_=xTv[:, :, c0:c0 + n]),
                      writes=[("nx", b)], dma=True)
                sc.op("act", lambda e, xt=xt, sq=sq, n=n: e.activation(out=sq[:, :, 0:n], in_=xt[:, :, 0:n], func=AF.Square),
                      reads=[("nx", b)], writes=[("nsq", 0)])
                bk = 6 + b
                for kc in range(KC):
                    sc.op("pe", lambda e, sq=sq, kc=kc, n=n, bk=bk: e.matmul(
                        bank(bk)[:, 0:n], ones_f, sq[:, kc, 0:n], start=(kc == 0), stop=(kc == KC - 1)),
                        reads=[("nsq", 0), "ones_f"], writes=[("ps", bk)])
                sc.op("act", lambda e, rr=rr, n=n, bk=bk: e.activation(out=rr[:, 0:n], in_=bank(bk)[:, 0:n], func=AF.Ln,
                                                                      scale=1.0 / D, bias=EPS),
                      reads=[("ps", bk)], writes=[("nr", b)])
                sc.op("act", lambda e, rr=rr, n=n: e.activation(out=rr[:, 0:n], in_=rr[:, 0:n], func=AF.Exp, scale=-0.5),
                      reads=[("nr", b)], writes=[("nr", b)])
                for kc in range(KC):
                    sc.op("dve", lambda e, xt=xt, rr=rr, kc=kc, n=n: e.scalar_tensor_tensor(
                        out=yT[:, kc, 0:n], in0=xt[:, kc, 0:n], scalar=nfin[:, kc, 0:1],
                        in1=rr[:, 0:n], op0=ALU.mult, op1=ALU.mult),
                        reads=[("nx", b), ("nr", b)], writes=[("yT", kc)])
                ngr = (n + 127) // 128
                for gi in range(ngr):
                    rows = min(128, n - gi * 128)
                    ob = cnt % 2
                    cnt += 1
                    for q in range(4):
                        bk2 = q
                        for i in range(4):
                            kc = q * 4 + i
                            sc.op("pe", lambda e, kc=kc, gi=gi, rows=rows, bk2=bk2, i=i: e.transpose(
                                out=bank(bk2)[0:rows, i * 128:(i + 1) * 128], in_=yT[:, kc, gi * 128:gi * 128 + rows],
                                identity=ident_f),
                                reads=[("yT", kc), "ident_f"], writes=[("ps", bk2)])
                        if q % 2 == 0:
                            sc.op("act", lambda e, ob=ob, q=q, rows=rows, bk2=bk2: e.copy(
                                out=yo[ob][0:rows, q * 512:(q + 1) * 512], in_=bank(bk2)[0:rows, :]),
                                reads=[("ps", bk2)], writes=[("yo", ob, q)])
                        else:
                            sc.op("dve", lambda e, ob=ob, q=q, rows=rows, bk2=bk2: e.tensor_copy(
                                out=yo[ob][0:rows, q * 512:(q + 1) * 512], in_=bank(bk2)[0:rows, :]),
                                reads=[("ps", bk2)], writes=[("yo", ob, q)])
                    t0 = c0 + gi * 128
                    dst = y_p[t0:t0 + rows, :] if t0 < TP else y_s
                    sc.op("sp", lambda e, ob=ob, rows=rows, dst=dst: e.dma_start(out=dst, in_=yo[ob][0:rows, :]),
                          reads=[("yo", ob, q) for q in range(4)], writes=[("y", t0)], dma=True)
            sc.barrier()

        phase_input()
        for s in stages:
            if s.startswith("ffn"):
                phase_ffn(int(s[3:]))
            elif s.startswith("pool"):
                phase_pool(int(s[4:]))
            elif s.startswith("gdn"):
                phase_gdn(int(s[3:]))
        phase_final()
        sc.finalize(st)
    return nc, sc


_CACHE = {}


def kernel(**inputs):
    f32 = lambda a: np.ascontiguousarray(np.asarray(a, dtype=np.float32))
    if "nc" not in _CACHE:
        _CACHE["nc"] = build()[0]
    nc = _CACHE["nc"]
    shared = {k: f32(inputs[k]) for k in ("norm_mix_w", "norm_ffn_w", "gdn_w_in", "gdn_conv_w", "gdn_A_log",
                                          "gdn_dt_bias", "gdn_norm_w", "gdn_w_out", "pool_w", "pool_scale",
                                          "ffn_w_gu", "ffn_w_down")}
    shared["final_norm_w"] = f32(inputs["final_norm_w"]).reshape(1, D)
    xp, xs = f32(inputs["x_prompt"]), f32(inputs["x_sample"])
    sconv, sS, spool = f32(inputs["state_gdn_conv"]), f32(inputs["state_gdn_S"]), f32(inputs["state_pool"])
    in_maps = []
    for b in range(8):
        m = dict(shared)
        m["x_p"] = xp[b]
        m["x_s"] = xs[b]
        m["st_conv"] = np.ascontiguousarray(sconv[:, b])
        m["st_S"] = np.ascontiguousarray(sS[:, b])
        m["st_pool"] = np.ascontiguousarray(spool[:, b])
        in_maps.append(m)
    res = run_bass_kernel_spmd(nc, in_maps, core_ids=list(range(8)))
    r = res.results
    stack = lambda k, ax: np.stack([np.asarray(r[b][k], dtype=np.float32) for b in range(8)], axis=ax)
    return (stack("y_p", 0), stack("y_s", 0), stack("o_conv_p", 1), stack("o_S_p", 1), stack("o_pool_p", 1),
            stack("o_conv_s", 1), stack("o_S_s", 1), stack("o_pool_s", 1))
```
